# Optimizing a Trainium2 kernel written in Bass

```python
import math
import jax, jax.numpy as jnp
from jax import lax
import numpy as np

D_MODEL = 1024
BATCH = 2
SEQ = 8192
DEPTH = 4

HEAD_DIM = 128
N_Q_HEADS = 8
N_KV_HEADS = 2
Q_PER_KV = N_Q_HEADS // N_KV_HEADS
ATTN_WIDTH = N_Q_HEADS * HEAD_DIM
KV_WIDTH = N_KV_HEADS * HEAD_DIM
WINDOW = 128
BLOCK = 128
ROPE_THETA = 10000.0

D_RNN = D_MODEL
N_RNN_BLOCKS = 8
RNN_BLOCK_W = D_RNN // N_RNN_BLOCKS
CONV_W = 4
LRU_C = 8.0

N_IN = ATTN_WIDTH + 2 * KV_WIDTH + 2 * D_RNN + 2 * D_MODEL
SPLITS = [ATTN_WIDTH,
          ATTN_WIDTH + KV_WIDTH,
          ATTN_WIDTH + 2 * KV_WIDTH,
          ATTN_WIDTH + 2 * KV_WIDTH + D_RNN,
          ATTN_WIDTH + 2 * KV_WIDTH + 2 * D_RNN,
          ATTN_WIDTH + 2 * KV_WIDTH + 2 * D_RNN + D_MODEL]

N_GROUPS = 4
EXPERTS_PER_GROUP = 8
N_EXPERTS = N_GROUPS * EXPERTS_PER_GROUP
TOP_K = 2
D_EXPERT = 512
MOE_BLOCK = 128

ALPHA = (2 * DEPTH) ** 0.25
BETA = (8 * DEPTH) ** -0.25
LN_EPS = 1e-5

kernel_name = "hybrid_gqa_rglru_hmoe_encoder"


def layer_norm(x, g, b):
    xf = x.astype(jnp.float32)
    mu = xf.mean(-1, keepdims=True)
    var = jnp.square(xf - mu).mean(-1, keepdims=True)
    y = (xf - mu) * lax.rsqrt(var + LN_EPS) * g.astype(jnp.float32) + b.astype(jnp.float32)
    return y.astype(x.dtype)


def rope(t):
    S = t.shape[1]
    inv = ROPE_THETA ** (-jnp.arange(0, HEAD_DIM, 2, dtype=jnp.float32) / HEAD_DIM)
    ang = jnp.arange(S, dtype=jnp.float32)[:, None] * inv[None, :]
    cos = jnp.cos(ang)[None, :, None, :]
    sin = jnp.sin(ang)[None, :, None, :]
    tf = t.astype(jnp.float32)
    t1, t2 = tf[..., : HEAD_DIM // 2], tf[..., HEAD_DIM // 2:]
    return jnp.concatenate([t1 * cos - t2 * sin, t2 * cos + t1 * sin], axis=-1).astype(t.dtype)


def window_attention(q, k, v, sink):
    B, S = q.shape[0], q.shape[1]
    nb = S // BLOCK
    qb = q.reshape(B, nb, BLOCK, N_KV_HEADS, Q_PER_KV, HEAD_DIM)

    def neighbours(t):
        tp = jnp.pad(t, ((0, 0), (BLOCK, BLOCK), (0, 0), (0, 0)))
        tp = tp.reshape(B, nb + 2, BLOCK, N_KV_HEADS, HEAD_DIM)
        return jnp.concatenate([tp[:, :-2], tp[:, 1:-1], tp[:, 2:]], axis=2)

    kw, vw = neighbours(k), neighbours(v)
    s = jnp.einsum('bnqhgd,bnkhd->bnhgqk', qb, kw,
                   preferred_element_type=jnp.float32) * (HEAD_DIM ** -0.5)
    qi = jnp.arange(BLOCK)[:, None]
    kj = jnp.arange(3 * BLOCK)[None, :]
    rel = kj - BLOCK - qi
    kpos = jnp.arange(nb)[:, None, None] * BLOCK + kj[None] - BLOCK
    mask = (jnp.abs(rel) <= WINDOW)[None] & (kpos >= 0) & (kpos < S)
    s = jnp.where(mask[None, :, None, None], s, -1e30)
    sk = sink.astype(jnp.float32).reshape(1, 1, N_KV_HEADS, Q_PER_KV, 1, 1)
    m = jnp.maximum(s.max(-1, keepdims=True), sk)
    p = jnp.exp(s - m)
    denom = p.sum(-1, keepdims=True) + jnp.exp(sk - m)
    p = (p / denom).astype(v.dtype)
    o = jnp.einsum('bnhgqk,bnkhd->bnqhgd', p, vw)
    return o.reshape(B, S, ATTN_WIDTH)


def centred_conv(x, w, b):
    S = x.shape[1]
    left = CONV_W // 2
    xp = jnp.pad(x, ((0, 0), (left, CONV_W - 1 - left), (0, 0)))
    y = xp[:, 0:S] * w[0]
    for tap in range(1, CONV_W):
        y = y + xp[:, tap:tap + S] * w[tap]
    return y + b


def block_diag(x, w, b):
    B, S = x.shape[0], x.shape[1]
    y = jnp.einsum('bsnc,ncd->bsnd', x.reshape(B, S, N_RNN_BLOCKS, RNN_BLOCK_W), w)
    return y.reshape(B, S, D_RNN) + b


def linear_scan(a, u):
    def comb(left, right):
        a1, b1 = left
        a2, b2 = right
        return a1 * a2, a2 * b1 + b2
    _, h = lax.associative_scan(comb, (a, u), axis=1)
    return h


def rg_lru(xc, w_r, b_r, w_i, b_i, lam, reverse):
    r = jax.nn.sigmoid(block_diag(xc, w_r, b_r).astype(jnp.float32))
    i = jax.nn.sigmoid(block_diag(xc, w_i, b_i).astype(jnp.float32))
    log_a = -LRU_C * r * jax.nn.softplus(-lam.astype(jnp.float32))
    a = jnp.exp(log_a)
    mult = jnp.sqrt(-jnp.expm1(2.0 * log_a))
    u = xc.astype(jnp.float32) * i * mult
    if reverse:
        h = jnp.flip(linear_scan(jnp.flip(a, 1), jnp.flip(u, 1)), 1)
    else:
        h = linear_scan(a, u)
    return h


def mixer(x, w_in, w_sink, w_conv, b_conv, w_rec_gate, b_rec_gate, w_in_gate, b_in_gate,
          lru_lambda, w_attn_o, w_rnn_o, w_out):
    B, S = x.shape[0], x.shape[1]
    z = x @ w_in
    q, k, v, xr, yr, ga, gr = jnp.split(z, SPLITS, axis=-1)
    q = rope(q.reshape(B, S, N_Q_HEADS, HEAD_DIM))
    k = rope(k.reshape(B, S, N_KV_HEADS, HEAD_DIM))
    v = v.reshape(B, S, N_KV_HEADS, HEAD_DIM)
    y_attn = window_attention(q, k, v, w_sink) @ w_attn_o

    xc = centred_conv(xr, w_conv, b_conv)
    h = (rg_lru(xc, w_rec_gate[0], b_rec_gate[0], w_in_gate[0], b_in_gate[0], lru_lambda[0], False)
         + rg_lru(xc, w_rec_gate[1], b_rec_gate[1], w_in_gate[1], b_in_gate[1], lru_lambda[1], True))
    y_rnn = (h.astype(x.dtype) * jax.nn.gelu(yr, approximate=True)) @ w_rnn_o

    merged = jax.nn.sigmoid(ga) * y_attn + jax.nn.sigmoid(gr) * y_rnn
    return merged @ w_out


def hier_moe(x, w_rg, b_rg, w_re, b_re, w_g, w_u, w_d):
    B, S, D = x.shape
    T = B * S
    xt = x.reshape(T, D)
    g_prob = jax.nn.softmax((xt @ w_rg + b_rg).astype(jnp.float32), axis=-1)
    g_val, g_idx = lax.top_k(g_prob, 1)
    e_logits = (xt @ w_re + b_re).astype(jnp.float32).reshape(T, N_GROUPS, EXPERTS_PER_GROUP)
    e_in_group = jnp.take_along_axis(e_logits, g_idx[:, :, None], axis=1)[:, 0]
    top_l, top_i = lax.top_k(e_in_group, TOP_K)
    gate = jax.nn.softmax(top_l, axis=-1) * g_val

    eid = (g_idx * EXPERTS_PER_GROUP + top_i).reshape(-1)
    tok = jnp.repeat(jnp.arange(T, dtype=jnp.int32), TOP_K)
    wt = gate.reshape(-1).astype(x.dtype)
    n_assign = T * TOP_K
    order = jnp.argsort(eid)
    se = eid[order]
    counts = jnp.bincount(eid, length=N_EXPERTS)
    starts = jnp.cumsum(counts) - counts
    pcounts = (counts + MOE_BLOCK - 1) // MOE_BLOCK * MOE_BLOCK
    pends = jnp.cumsum(pcounts)
    pstarts = pends - pcounts
    dest = pstarts[se] + jnp.arange(n_assign, dtype=jnp.int32) - starts[se]
    n_blocks = -(-n_assign // MOE_BLOCK) + N_EXPERTS
    n_rows = n_blocks * MOE_BLOCK
    buf_tok = jnp.full((n_rows,), T, jnp.int32).at[dest].set(tok[order])
    buf_w = jnp.zeros((n_rows,), x.dtype).at[dest].set(wt[order])
    block_e = jnp.minimum(jnp.searchsorted(pends, jnp.arange(n_blocks) * MOE_BLOCK, side='right'),
                          N_EXPERTS - 1)
    x_pad = jnp.concatenate([xt, jnp.zeros((1, D), xt.dtype)], axis=0)
    xb = x_pad[buf_tok].reshape(n_blocks, MOE_BLOCK, D)

    def expert_block(args):
        xblk, e = args
        hid = jax.nn.silu(xblk @ w_g[e]) * (xblk @ w_u[e])
        return hid @ w_d[e]

    yb = lax.map(expert_block, (xb, block_e)).reshape(n_rows, D) * buf_w[:, None]
    out = jnp.zeros((T + 1, D), x.dtype).at[buf_tok].add(yb)[:T]
    return out.reshape(B, S, D)


def setup_inputs(seed: int = 0) -> dict:
    key = jax.random.key(seed)
    ks = jax.random.split(key, 24)
    f32 = jnp.float32

    def nrm(k, shape, scale):
        return jax.random.normal(k, shape, f32) * scale

    x = nrm(ks[0], (BATCH, SEQ, D_MODEL), 1.0)
    col_scale = jnp.concatenate([jnp.ones((ATTN_WIDTH + KV_WIDTH,), f32),
                                 jnp.full((KV_WIDTH,), BETA, f32),
                                 jnp.ones((2 * D_RNN + 2 * D_MODEL,), f32)])
    w_in = nrm(ks[1], (DEPTH, D_MODEL, N_IN), D_MODEL ** -0.5) * col_scale
    w_sink = nrm(ks[2], (DEPTH, N_Q_HEADS), 0.5)
    w_conv = nrm(ks[3], (DEPTH, CONV_W, D_RNN), CONV_W ** -0.5)
    b_conv = nrm(ks[4], (DEPTH, D_RNN), 0.02)
    w_rec_gate = nrm(ks[5], (DEPTH, 2, N_RNN_BLOCKS, RNN_BLOCK_W, RNN_BLOCK_W), RNN_BLOCK_W ** -0.5)
    b_rec_gate = nrm(ks[6], (DEPTH, 2, D_RNN), 0.02)
    w_in_gate = nrm(ks[7], (DEPTH, 2, N_RNN_BLOCKS, RNN_BLOCK_W, RNN_BLOCK_W), RNN_BLOCK_W ** -0.5)
    b_in_gate = nrm(ks[8], (DEPTH, 2, D_RNN), 0.02)
    u = jax.random.uniform(ks[9], (DEPTH, 2, D_RNN), f32, 0.9, 0.999)
    s = u ** (1.0 / LRU_C)
    lru_lambda = jnp.log(s) - jnp.log1p(-s)
    w_attn_o = nrm(ks[10], (DEPTH, ATTN_WIDTH, D_MODEL), ATTN_WIDTH ** -0.5 * BETA)
    w_rnn_o = nrm(ks[11], (DEPTH, D_RNN, D_MODEL), D_RNN ** -0.5 * BETA)
    w_out = nrm(ks[12], (DEPTH, D_MODEL, D_MODEL), D_MODEL ** -0.5 * BETA)
    ln_g = 1.0 + nrm(ks[13], (DEPTH, 2, D_MODEL), 0.02)
    ln_b = nrm(ks[14], (DEPTH, 2, D_MODEL), 0.02)
    w_router_group = nrm(ks[15], (DEPTH, D_MODEL, N_GROUPS), D_MODEL ** -0.5)
    b_router_group = nrm(ks[16], (DEPTH, N_GROUPS), 0.01)
    w_router_expert = nrm(ks[17], (DEPTH, D_MODEL, N_EXPERTS), D_MODEL ** -0.5)
    b_router_expert = nrm(ks[18], (DEPTH, N_EXPERTS), 0.01)
    w_exp_gate = nrm(ks[19], (DEPTH, N_EXPERTS, D_MODEL, D_EXPERT), D_MODEL ** -0.5)
    w_exp_up = nrm(ks[20], (DEPTH, N_EXPERTS, D_MODEL, D_EXPERT), D_MODEL ** -0.5 * BETA)
    w_exp_down = nrm(ks[21], (DEPTH, N_EXPERTS, D_EXPERT, D_MODEL), D_EXPERT ** -0.5 * BETA)
    return {"x": x, "w_in": w_in, "w_sink": w_sink, "w_conv": w_conv, "b_conv": b_conv,
            "w_rec_gate": w_rec_gate, "b_rec_gate": b_rec_gate,
            "w_in_gate": w_in_gate, "b_in_gate": b_in_gate, "lru_lambda": lru_lambda,
            "w_attn_o": w_attn_o, "w_rnn_o": w_rnn_o, "w_out": w_out,
            "ln_g": ln_g, "ln_b": ln_b,
            "w_router_group": w_router_group, "b_router_group": b_router_group,
            "w_router_expert": w_router_expert, "b_router_expert": b_router_expert,
            "w_exp_gate": w_exp_gate, "w_exp_up": w_exp_up, "w_exp_down": w_exp_down}


def reference(x, w_in, w_sink, w_conv, b_conv, w_rec_gate, b_rec_gate, w_in_gate, b_in_gate,
              lru_lambda, w_attn_o, w_rnn_o, w_out, ln_g, ln_b,
              w_router_group, b_router_group, w_router_expert, b_router_expert,
              w_exp_gate, w_exp_up, w_exp_down):
    for l in range(DEPTH):
        mix = mixer(x, w_in[l], w_sink[l], w_conv[l], b_conv[l], w_rec_gate[l], b_rec_gate[l],
                    w_in_gate[l], b_in_gate[l], lru_lambda[l], w_attn_o[l], w_rnn_o[l], w_out[l])
        x = layer_norm(ALPHA * x + mix, ln_g[l, 0], ln_b[l, 0])
        ffn = hier_moe(x, w_router_group[l], b_router_group[l], w_router_expert[l], b_router_expert[l],
                       w_exp_gate[l], w_exp_up[l], w_exp_down[l])
        x = layer_norm(ALPHA * x + ffn, ln_g[l, 1], ln_b[l, 1])
    return x
```

```python
import contextlib
import numpy as np
import concourse.bass as bass
import concourse.mybir as mybir
from concourse.bass_utils import run_bass_kernel_spmd

F32 = mybir.dt.float32
BF16 = mybir.dt.bfloat16
I32 = mybir.dt.int32
U32 = mybir.dt.uint32
AF = mybir.ActivationFunctionType
ALU = mybir.AluOpType
AX = mybir.AxisListType


class Prog:
    ENG = ("pe", "dve", "act", "pool", "sp")

    def __init__(self, nc, n_slots=8):
        self.nc = nc
        self.es = contextlib.ExitStack()
        self.q = {e: [] for e in self.ENG}
        self.cnt = {e: 0 for e in self.ENG}
        self.sem = {e: self.es.enter_context(nc.semaphore("s_" + e)) for e in self.ENG}
        self.n_slots = n_slots
        self.pool_slots = 2
        self.dq = ("sp", "act", "pool")
        self.dsem = {q: [self.es.enter_context(nc.semaphore("d_%s%d" % (q, i))) for i in range(n_slots)]
                     for q in self.dq}
        self.dn = {q: 0 for q in self.dq}
        self.sems = {}
        for e in self.ENG:
            self.sems[("c", e)] = self.sem[e]
        for q in self.dq:
            for i in range(n_slots):
                self.sems[("d", q, i)] = self.dsem[q][i]
        self.lastw = {}
        self.readers = {}
        self.waited = {e: {} for e in self.ENG}
        self.n_ops = 0
        self.exclusive = set()
        self.csem = []
        self.ph = None
        self.latest = {}
        self._cidx = {}

    def sb(self, name, shape, dtype):
        st = self.ph if self.ph is not None else self.es
        return st.enter_context(self.nc.sbuf_tensor(name, list(shape), dtype))

    def ps(self, name, shape, dtype):
        st = self.ph if self.ph is not None else self.es
        return st.enter_context(self.nc.psum_tensor(name, list(shape), dtype))

    def core_idx(self, e, which):
        key = id(e)
        if key not in self._cidx:
            pid = e.partition_id()
            vals = {}
            for name, off in (("j", 0), ("jl", 3), ("jr", 5)):
                vals[name] = e.snap((pid + off) % 4, min_val=0, max_val=3)
            self._cidx[key] = vals
        return self._cidx[key][which]

    def begin_phase(self):
        assert self.ph is None
        self.ph = contextlib.ExitStack()
        self.exclusive = set()

    def _barrier(self):
        for e in self.ENG:
            waits = []
            for sk, v in self.latest.items():
                if sk == ("c", e):
                    continue
                if self.waited[e].get(sk, 0) < v:
                    self.waited[e][sk] = v
                    waits.append((sk, v))
            if waits:
                self.q[e].append((waits, None, None, 0))
        self.lastw = {}
        self.readers = {}

    def _emit_block(self):
        nc = self.nc
        engobj = {"pe": "tensor", "dve": "vector", "act": "scalar", "pool": "gpsimd", "sp": "sync"}
        with nc.Block() as block:
            for e in self.ENG:
                items = self.q[e]

                def body(eng, items=items):
                    for waits, fn, sk, amt in items:
                        for wsk, v in waits:
                            eng.wait_ge(self.sems[wsk], v)
                        if fn is not None:
                            ins = fn(eng)
                            if amt is None:
                                ins.then_inc(self.sems[sk])
                            else:
                                ins.then_inc(self.sems[sk], amt)
                getattr(block, engobj[e])(body)
        self.q = {e: [] for e in self.ENG}

    def end_phase(self):
        self._barrier()
        self._emit_block()
        self.ph.close()
        self.ph = None

    def _deps(self, eng, reads, writes, is_dma):
        need = {}

        def add(d):
            for sk, v in d.items():
                if need.get(sk, 0) < v:
                    need[sk] = v
        for k in reads:
            if k in self.lastw:
                add(self.lastw[k])
        for k in writes:
            if k in self.lastw:
                add(self.lastw[k])
            if k in self.readers:
                add(self.readers[k])
        out = []
        for sk, v in need.items():
            if (not is_dma) and eng == "pe" and sk == ("c", "pe"):
                continue
            if self.waited[eng].get(sk, 0) >= v:
                continue
            self.waited[eng][sk] = v
            out.append((sk, v))
        return out

    def _mark(self, reads, writes, tok):
        sk, v = tok
        if self.latest.get(sk, 0) < v:
            self.latest[sk] = v
        for k in reads:
            r = self.readers.setdefault(k, {})
            if r.get(sk, 0) < v:
                r[sk] = v
        for k in writes:
            self.lastw[k] = {sk: v}
            self.readers[k] = {}

    def _excl(self, reads, writes):
        ex = [k for k in reads if k in self.exclusive]
        if ex:
            writes = list(writes) + [k for k in ex if k not in writes]
        return reads, writes

    def op(self, eng, fn, reads=(), writes=()):
        reads, writes = self._excl(reads, writes)
        waits = self._deps(eng, reads, writes, False)
        self.cnt[eng] += 1
        tok = (("c", eng), self.cnt[eng])
        self.q[eng].append((waits, fn, tok[0], 1))
        self._mark(reads, writes, tok)
        self.n_ops += 1

    def dma(self, out, in_, reads=(), writes=(), queue="sp", fn=None):
        n = self.dn[queue]
        self.dn[queue] += 1
        ns = self.n_slots if queue != "pool" else min(self.n_slots, self.pool_slots)
        slot = n % ns
        sk = ("d", queue, slot)
        waits = self._deps(queue, reads, writes, True)
        prev = 16 * (n // ns)
        if prev > 0 and self.waited[queue].get(sk, 0) < prev:
            self.waited[queue][sk] = prev
            waits.append((sk, prev))
        if fn is None:
            def fn(e, out=out, in_=in_):
                return e.dma_start(out=out, in_=in_)
        tok = (sk, prev + 16)
        self.q[queue].append((waits, fn, sk, 16))
        self._mark(reads, writes, tok)
        self.n_ops += 1

    def mm(self, out, lhsT, rhs, start, stop, reads, writes):
        self.op("pe", lambda e: e.matmul(out, lhsT=lhsT, rhs=rhs, start=start, stop=stop), reads, writes)

    def tr(self, out, in_, ident, reads, writes):
        self.op("pe", lambda e: e.transpose(out=out, in_=in_, identity=ident), reads, writes)

    def act(self, out, in_, func, reads, writes, scale=1.0, bias=None, accum_out=None):
        kw = {}
        if bias is not None:
            kw["bias"] = bias
        if accum_out is not None:
            kw["accum_out"] = accum_out
        self.op("act", lambda e: e.activation(out=out, in_=in_, func=func, scale=scale, **kw), reads, writes)

    def ts(self, eng, out, in0, s1, s2, op0, op1, reads, writes, accum_out=None):
        kw = {}
        if accum_out is not None:
            kw["accum_out"] = accum_out
        if op1 is None:
            self.op(eng, lambda e: e.tensor_scalar(out=out, in0=in0, scalar1=s1, scalar2=None, op0=op0, **kw), reads, writes)
        else:
            self.op(eng, lambda e: e.tensor_scalar(out=out, in0=in0, scalar1=s1, scalar2=s2, op0=op0, op1=op1, **kw),
                    reads, writes)

    def tt(self, eng, out, in0, in1, op, reads, writes):
        self.op(eng, lambda e: e.tensor_tensor(out=out, in0=in0, in1=in1, op=op), reads, writes)

    def stt(self, out, in0, scalar, in1, op0, op1, reads, writes):
        self.op("dve", lambda e: e.scalar_tensor_tensor(out=out, in0=in0, scalar=scalar, in1=in1, op0=op0, op1=op1),
                reads, writes)

    def scan(self, out, d0, d1, init, reads, writes):
        self.op("dve", lambda e: e.tensor_tensor_scan(out=out, data0=d0, data1=d1, initial=init, op0=ALU.mult, op1=ALU.add),
                reads, writes)

    def copy(self, eng, out, in_, reads, writes):
        self.op(eng, lambda e: e.tensor_copy(out=out, in_=in_), reads, writes)

    def memset(self, eng, ap, val, writes):
        self.op(eng, lambda e: e.memset(ap, val), (), writes)

    def coll(self, kind, groups, in_ap, out_ap, reads=(), writes=()):
        idx = len(self.csem)
        sem = self.es.enter_context(self.nc.semaphore("cc%d" % idx))
        self.csem.append(sem)
        sk = ("k", idx)
        self.sems[sk] = sem
        waits = self._deps("pool", reads, writes, True)

        def fn(e):
            return e.collective_compute(kind, ALU.bypass, replica_groups=groups, ins=[in_ap], outs=[out_ap])
        self.q["pool"].append((waits, fn, sk, None))
        self._mark(reads, writes, (sk, 1))
        self.n_ops += 1

    def finish(self, final_keys):
        self._barrier()
        self._emit_block()
        if self.ph is not None:
            self.ph.close()
            self.ph = None
        self.es.close()


def _rev(ap2d, n):
    apl = [list(s) for s in ap2d.ap]
    assert len(apl) == 2 and apl[1][1] == n and apl[1][0] == 1, apl
    from concourse.ap import AP
    return AP(ap2d.tensor, ap2d.offset + (n - 1), [apl[0], [-1, n]])


class PsumRing:
    def __init__(self, p, n=8, name="ps"):
        self.p = p
        self.t = [p.ps("%s%d" % (name, i), [128, 512], F32) for i in range(n)]
        for i in range(n):
            p.exclusive.add((name, i))
        self.i = 0
        self.n = n
        self.name = name

    def next(self):
        i = self.i
        self.i = (self.i + 1) % self.n
        return self.t[i], (self.name, i)


S_LEN = 8192
NSM = 11
GELU_C = 1.5957691216057308


def build_rnn(x_dtype=F32):
    nc = bass.Bass("TRN2", target_bir_lowering=False)
    xT = nc.dram_tensor("xT", [1024, S_LEN], x_dtype, kind="ExternalInput").ap()
    w_r = nc.dram_tensor("w_r", [1024, 512], F32, kind="ExternalInput").ap()
    w_g = nc.dram_tensor("w_g", [128, 8, 128], F32, kind="ExternalInput").ap()
    small = nc.dram_tensor("small", [128, 2, NSM], F32, kind="ExternalInput").ap()
    hgT = nc.dram_tensor("hgT", [256, S_LEN], BF16, kind="ExternalOutput").ap()
    p = Prog(nc)
    emit_rnn(p, xT, w_r, w_g, small, hgT)
    p.finish([("hg_out", b, c) for b in range(2) for c in range(4)])
    return nc


def emit_rnn(p, xT, w_r, w_g, small, hgT, pfx="r"):
    T = S_LEN
    CH = 2048
    NCH = T // CH
    wb = p.sb(pfx + "wb", [128, 8, 512], BF16)
    wg = p.sb(pfx + "wg", [128, 8, 128], BF16)
    sm = p.sb(pfx + "sm", [128, 2, NSM], F32)
    cl = p.sb(pfx + "cl", [128, 2, 2, 2], F32)
    zt = p.sb(pfx + "zt", [128, 4], F32)
    pt = p.sb(pfx + "pt", [128, 4], F32)
    xb = [p.sb(pfx + "xb%d" % i, [128, 8, 512], BF16) for i in range(2)]
    xr_full = p.sb(pfx + "xrf", [128, T + 4], F32)
    gy = p.sb(pfx + "gy", [128, T], BF16)
    xc = p.sb(pfx + "xc", [128, T], F32)
    xcb = p.sb(pfx + "xcb", [128, T], BF16)
    g1 = [p.sb(pfx + "g1_%d" % i, [128, 512], F32) for i in range(2)]
    g2 = [p.sb(pfx + "g2_%d" % i, [128, 512], F32) for i in range(2)]
    rt = p.sb(pfx + "rt", [128, CH], F32)
    it = p.sb(pfx + "it", [128, CH], F32)
    at = p.sb(pfx + "at", [128, CH], F32)
    hb = [p.sb(pfx + "hb%d" % i, [128, CH], F32) for i in range(2)]
    hgb = [p.sb(pfx + "hgb%d" % i, [128, CH], BF16) for i in range(2)]
    ring = PsumRing(p, 8, pfx + "ps")

    p.dma(wb[:], w_r.rearrange("(c p) n -> p c n", p=128), writes=["wb"], queue="pool")
    p.dma(wg[:], w_g, writes=["wg"], queue="pool")
    p.dma(sm[:], small, writes=["sm"])
    for b in range(2):
        for d in range(2):
            j = b * 2 + d
            p.act(zt[:, j:j + 1], sm[:, b, 5 + 3 * d + 2: 5 + 3 * d + 3], AF.Exp, ["sm"], ["zt"], scale=-1.0)
    p.ts("dve", pt[:], zt[:], -1.0 / 6, 1.0 / 5, ALU.mult, ALU.add, ["zt"], ["pt"])
    for cst in (-1.0 / 4, 1.0 / 3, -1.0 / 2, 1.0):
        p.tt("dve", pt[:], pt[:], zt[:], ALU.mult, ["pt", "zt"], ["pt"])
        p.ts("dve", pt[:], pt[:], cst, None, ALU.add, None, ["pt"], ["pt"])
    p.tt("dve", pt[:], pt[:], zt[:], ALU.mult, ["pt", "zt"], ["pt"])
    for b in range(2):
        for d in range(2):
            j = b * 2 + d
            p.ts("dve", cl[:, b, d, 0:1], pt[:, j:j + 1], -8.0, None, ALU.mult, None, ["pt"], ["cl"])
            p.ts("dve", cl[:, b, d, 1:2], pt[:, j:j + 1], -16.0, None, ALU.mult, None, ["pt"], ["cl"])
    chunked = (xT.shape[0] == 4096 + 128)
    if chunked:
        xTv = xT[128:128 + 4096, :].rearrange("(k r c p) n -> p r k c n", k=4, r=4, c=2, p=128)
    else:
        xTv = xT.rearrange("(c p) n -> p c n", p=128)

    def xrk(c):
        return [("xr", t) for t in range(4 * c, 4 * c + 4)]

    for blk in range(2):
        p.memset("pool", xr_full[:, 0:2], 0.0, [("xr", -1)])
        p.memset("pool", xr_full[:, T + 2:T + 4], 0.0, [("xr", 16)])
        for t in range(T // 512):
            xbt = xb[t % 2]
            xbk = ("xb", t % 2)
            if chunked:
                for k in range(4):
                    p.dma(xbt[:, 2 * k:2 * k + 2, :], xTv[:, t // 4, k, :, (t % 4) * 512:(t % 4 + 1) * 512],
                          reads=["xT_all"], writes=[xbk])
            else:
                p.dma(xbt[:], xTv[:, :, t * 512:(t + 1) * 512], writes=[xbk], queue="pool")
            for m in range(2):
                pst, psk = ring.next()
                col = (0 if m == 0 else 256) + blk * 128
                for k in range(8):
                    p.mm(pst[:], wb[:, k, col:col + 128], xbt[:, k, :], k == 0, k == 7, ["wb", xbk], [psk])
                if m == 0:
                    p.act(xr_full[:, 2 + t * 512: 2 + (t + 1) * 512], pst[:], AF.Copy, [psk], [("xr", t)])
                else:
                    a1, a2 = g1[t % 2], g2[t % 2]
                    k1, k2 = ("g1", t % 2), ("g2", t % 2)
                    p.act(a1[:], pst[:], AF.Square, [psk], [k1])
                    p.ts("dve", a1[:], a1[:], 0.044715, 1.0, ALU.mult, ALU.add, [k1], [k1])
                    p.tt("dve", a1[:], a1[:], pst[:], ALU.mult, [k1, psk], [k1])
                    p.act(a2[:], a1[:], AF.Sigmoid, [k1], [k2], scale=GELU_C)
                    p.tt("dve", gy[:, t * 512:(t + 1) * 512], a2[:], pst[:], ALU.mult, [k2, psk], [("gy", t)])
        for c in range(NCH):
            o = c * CH
            rk = [("xr", t) for t in range(4 * c - 1, 4 * c + 5)]
            p.ts("dve", xc[:, o:o + CH], xr_full[:, o:o + CH], sm[:, blk, 0:1], sm[:, blk, 4:5], ALU.mult, ALU.add,
                 rk + ["sm"], [("xc", c)])
            for tap in range(1, 4):
                p.stt(xc[:, o:o + CH], xr_full[:, o + tap:o + tap + CH], sm[:, blk, tap:tap + 1], xc[:, o:o + CH],
                      ALU.mult, ALU.add, rk + ["sm", ("xc", c)], [("xc", c)])
            p.copy("pool", xcb[:, o:o + CH], xc[:, o:o + CH], [("xc", c)], [("xcb", c)])
        hf = xr_full
        for d in range(2):
            order = list(range(NCH)) if d == 0 else list(range(NCH - 1, -1, -1))
            prev = None
            for ci, c in enumerate(order):
                o = c * CH
                for s in range(CH // 512):
                    for g in range(2):
                        pst, psk = ring.next()
                        p.mm(pst[:], wg[:, blk * 4 + d * 2 + g, :], xcb[:, o + s * 512: o + (s + 1) * 512], True, True,
                             ["wg", ("xcb", c)], [psk])
                        dst = rt if g == 0 else it
                        p.act(dst[:, s * 512:(s + 1) * 512], pst[:], AF.Sigmoid, [psk, "sm"],
                              [("rt" if g == 0 else "it", s)], bias=sm[:, blk, 5 + 3 * d + g: 5 + 3 * d + g + 1])
                rtk = [("rt", s) for s in range(4)]
                itk = [("it", s) for s in range(4)]
                p.act(at[:], rt[:], AF.Exp, rtk + ["cl"], ["at"], scale=cl[:, blk, d, 0:1])
                p.act(rt[:], rt[:], AF.Exp, rtk + ["cl"], rtk, scale=cl[:, blk, d, 1:2])
                p.act(rt[:], rt[:], AF.Sqrt, rtk, rtk, scale=-1.0, bias=1.0)
                p.tt("pool", it[:], it[:], xc[:, o:o + CH], ALU.mult, itk + [("xc", c)], itk)
                p.tt("pool", it[:], it[:], rt[:], ALU.mult, itk + rtk, itk)
                if d == 0:
                    init = 0.0 if ci == 0 else hf[:, 2 + o - 1: 2 + o]
                    p.scan(hf[:, 2 + o: 2 + o + CH], at[:], it[:], init, ["at"] + itk + [("xr", 4 * c - 1)], xrk(c))
                else:
                    hbt = hb[ci % 2]
                    init = 0.0 if ci == 0 else prev[:, 0:1]
                    p.scan(_rev(hbt[:], CH), _rev(at[:], CH), _rev(it[:], CH), init,
                           ["at"] + itk + [("hb", (ci + 1) % 2)], [("hb", ci % 2)])
                    prev = hbt
                    hg = hgb[ci % 2]
                    p.tt("pool", at[:], hbt[:], hf[:, 2 + o: 2 + o + CH], ALU.add, [("hb", ci % 2), "at"] + xrk(c), ["at"])
                    p.tt("dve", hg[:], at[:], gy[:, o:o + CH], ALU.mult, ["at"] + [("gy", t) for t in range(4 * c, 4 * c + 4)],
                         [("hgb", ci % 2)])
                    hdst = hgT[c, blk * 128:(blk + 1) * 128, :] if len(hgT.shape) == 3 else hgT[blk * 128:(blk + 1) * 128, o:o + CH]
                    p.dma(hdst, hg[:], reads=[("hgb", ci % 2)], writes=[("hg_out", blk, c)])


def pack_rnn_inputs(l, j, w_in, w_conv, b_conv, w_rec_gate, b_rec_gate, w_in_gate, b_in_gate, lru_lambda):
    XR0 = 1024 + 512
    YR0 = XR0 + 1024
    c0 = 2 * j * 128
    w_r = np.concatenate([w_in[l][:, XR0 + c0: XR0 + c0 + 256], w_in[l][:, YR0 + c0: YR0 + c0 + 256]], axis=1)
    w_g = np.empty((128, 8, 128), np.float32)
    small = np.empty((128, 2, NSM), np.float32)
    for b in range(2):
        cb = 2 * j + b
        sl = slice(cb * 128, (cb + 1) * 128)
        small[:, b, 0:4] = w_conv[l][:, sl].T
        small[:, b, 4] = b_conv[l][sl]
        for d in range(2):
            w_g[:, b * 4 + d * 2 + 0, :] = w_rec_gate[l, d, cb]
            w_g[:, b * 4 + d * 2 + 1, :] = w_in_gate[l, d, cb]
            small[:, b, 5 + 3 * d + 0] = b_rec_gate[l, d][sl]
            small[:, b, 5 + 3 * d + 1] = b_in_gate[l, d][sl]
            small[:, b, 5 + 3 * d + 2] = lru_lambda[l, d][sl]
    return {"w_r": np.ascontiguousarray(w_r), "w_g": w_g, "small": small}


TOK = 2048
HALO = 128
TH = TOK + 2 * HALO
ATT_SCALE = 128 ** -0.5


def build_attn(x_dtype=F32):
    nc = bass.Bass("TRN2", target_bir_lowering=False)
    d = {}
    d["xT"] = nc.dram_tensor("xT", [1024, TH], x_dtype, kind="ExternalInput").ap()
    d["w_qkv"] = nc.dram_tensor("w_qkv", [1024, 1536], F32, kind="ExternalInput").ap()
    d["cosT"] = nc.dram_tensor("cosT", [128, TH], F32, kind="ExternalInput").ap()
    d["sinT"] = nc.dram_tensor("sinT", [128, TH], F32, kind="ExternalInput").ap()
    d["masks"] = nc.dram_tensor("masks", [128, 3, 384], F32, kind="ExternalInput").ap()
    d["cmats"] = nc.dram_tensor("cmats", [128, 2, 128], F32, kind="ExternalInput").ap()
    d["sink"] = nc.dram_tensor("sink", [128, 8], F32, kind="ExternalInput").ap()
    d["oT"] = nc.dram_tensor("oT", [1024, TOK], BF16, kind="ExternalOutput").ap()
    p = Prog(nc)
    emit_attn(p, d)
    p.finish([("o_out", h, g) for h in range(8) for g in range(4)])
    return nc


def emit_attn(p, d, pfx="a"):
    xb = p.sb(pfx + "xb", [128, 8, TH], BF16)
    wq = [p.sb(pfx + "wq%d" % i, [128, 8, 128], BF16) for i in range(3)]
    wv = p.sb(pfx + "wv", [128, 8, 256], BF16)
    cosT = p.sb(pfx + "cos", [128, TH], F32)
    sinT = p.sb(pfx + "sin", [128, TH], F32)
    masks = p.sb(pfx + "masks", [128, 3, 384], BF16)
    cm0 = p.sb(pfx + "cm0", [128, 128], BF16)
    cm1 = p.sb(pfx + "cm1", [128, 128], BF16)
    sink = p.sb(pfx + "sink", [128, 8], F32)
    nsink = p.sb(pfx + "nsink", [128, 8], F32)
    qT = p.sb(pfx + "qT", [128, 8, TOK], BF16)
    kT = p.sb(pfx + "kT", [128, 2, TH], BF16)
    V = p.sb(pfx + "V", [128, TH // 128, 256], BF16)
    qraw = [p.sb(pfx + "qraw%d" % i, [128, 512], BF16) for i in range(2)]
    r1 = [p.sb(pfx + "r1_%d" % i, [128, 512], F32) for i in range(2)]
    r2 = [p.sb(pfx + "r2_%d" % i, [128, 512], F32) for i in range(2)]
    P = [p.sb(pfx + "P%d" % i, [128, 384], BF16) for i in range(2)]
    PT = [p.sb(pfx + "PT%d" % i, [128, 384], BF16) for i in range(2)]
    D = [p.sb(pfx + "D%d" % i, [128, 128], BF16) for i in range(2)]
    cols = [p.sb(pfx + "cols%d" % i, [128, 8], F32) for i in range(2)]
    ring = PsumRing(p, 6, pfx + "ps")
    oring = PsumRing(p, 2, pfx + "po")
    ident = cm0[:]
    pswap = cm1[:]

    if "xT_own" in d:
        own = d["xT_own"].rearrange("(c p) n -> p c n", p=128)
        allv = d["xT_all"][128:128 + 4096, :].rearrange("(k r c p) n -> p r k c n", k=4, r=4, c=2, p=128)
        nc_ = p.nc

        allr = d["xT_all"][128:128 + 4096, :].rearrange("(k r q) n -> r k q n", k=4, r=4, q=256)

        def halo(e, left):
            jn = p.core_idx(e, "jl" if left else "jr")
            if left:
                return e.dma_start(out=d["halo_l"].rearrange("(k q) n -> k q n", k=4), in_=allr[jn, :, :, TOK - HALO:TOK])
            return e.dma_start(out=d["halo_r"].rearrange("(k q) n -> k q n", k=4), in_=allr[jn, :, :, 0:HALO])
        p.dma(None, None, reads=["xT_all"], writes=["halo_l"], fn=lambda e: halo(e, True))
        p.dma(None, None, reads=["xT_all"], writes=["halo_r"], fn=lambda e: halo(e, False))
        p.dma(xb[:, :, 0:HALO], d["halo_l"].rearrange("(c p) n -> p c n", p=128), reads=["halo_l"],
              writes=[("xbh", 0, k) for k in range(4)])
        p.dma(xb[:, :, HALO + TOK:TH], d["halo_r"].rearrange("(c p) n -> p c n", p=128), reads=["halo_r"],
              writes=[("xbh", 1, k) for k in range(4)])
        for t0 in range(0, TOK, 512):
            keys = sorted(set([(HALO + t0) // 512, (HALO + t0 + 511) // 512]))
            p.dma(xb[:, :, HALO + t0:HALO + t0 + 512], own[:, :, t0:t0 + 512], reads=["xT_own"],
                  writes=[("xb", k) for k in keys])
    else:
        xTv = d["xT"].rearrange("(c p) n -> p c n", p=128)
        for t0 in range(0, TH, 512):
            w = min(512, TH - t0)
            p.dma(xb[:, :, t0:t0 + w], xTv[:, :, t0:t0 + w], writes=[("xb", t0 // 512)], queue="pool")

    def xbk(a, b):
        ks = [("xb", t) for t in range(a // 512, (b - 1) // 512 + 1)]
        if a < HALO:
            ks += [("xbh", 0, k) for k in range(4)]
        if b > HALO + TOK:
            ks += [("xbh", 1, k) for k in range(4)]
        return ks
    p.dma(wv[:], d["w_qkv"].rearrange("(c p) n -> p c n", p=128)[:, :, 1280:1536], writes=["wv"], queue="pool")
    p.dma(masks[:], d["masks"], writes=["masks"], queue="pool")
    p.dma(cm0[:], d["cmats"][:, 0, :], writes=["cm"], queue="pool")
    p.dma(cm1[:], d["cmats"][:, 1, :], writes=["cm"], queue="pool")
    p.dma(cosT[:], d["cosT"], writes=["cos"])
    p.dma(sinT[:], d["sinT"], writes=["sin"])
    p.dma(sink[:], d["sink"], writes=["sink"])
    p.ts("dve", nsink[:], sink[:], -1.0, None, ALU.mult, None, ["sink"], ["nsink"])
    wqv = d["w_qkv"].rearrange("(c p) n -> p c n", p=128)
    for tt in range(TH // 128):
        pst, psk = ring.next()
        for k in range(8):
            p.mm(pst[:, 0:256], xb[:, k, tt * 128:(tt + 1) * 128], wv[:, k, :], k == 0, k == 7, xbk(tt * 128, (tt + 1) * 128) + ["wv"], [psk])
        p.copy("dve" if tt % 2 else "act", V[:, tt, :], pst[:, 0:256], [psk], [("V", tt)]) if tt % 2 else \
            p.act(V[:, tt, :], pst[:, 0:256], AF.Copy, [psk], [("V", tt)])
    it = 0
    for m in range(10):
        wt = wq[m % 3]
        wk = ("wq", m % 3)
        p.dma(wt[:], wqv[:, :, m * 128:(m + 1) * 128], writes=[wk], queue="pool")
        isq = m < 8
        ntok = TOK if isq else TH
        off = HALO if isq else 0
        t0 = 0
        while t0 < ntok:
            w = min(512, ntok - t0)
            pst, psk = ring.next()
            for k in range(8):
                p.mm(pst[:, 0:w], wt[:, k, :], xb[:, k, off + t0: off + t0 + w], k == 0, k == 7, xbk(off + t0, off + t0 + w) + [wk], [psk])
            qr = qraw[it % 2]
            qk = ("qraw", it % 2)
            p.act(qr[:, 0:w], pst[:, 0:w], AF.Copy, [psk], [qk])
            ps2, ps2k = ring.next()
            p.mm(ps2[:, 0:w], pswap, qr[:, 0:w], True, True, ["cm", qk], [ps2k])
            a1, a2 = r1[it % 2], r2[it % 2]
            k1, k2 = ("r1", it % 2), ("r2", it % 2)
            p.tt("dve", a1[:, 0:w], cosT[:, off + t0: off + t0 + w], pst[:, 0:w], ALU.mult, [psk, "cos"], [k1])
            p.tt("dve", a2[:, 0:w], sinT[:, off + t0: off + t0 + w], ps2[:, 0:w], ALU.mult, [ps2k, "sin"], [k2])
            if isq:
                dst = qT[:, m, t0:t0 + w]
                dk = [("qT", m, t0 // 512)]
            else:
                dst = kT[:, m - 8, t0:t0 + w]
                dk = [("kT", m - 8, t0 // 512)]
            p.tt("dve", dst, a1[:, 0:w], a2[:, 0:w], ALU.add, [k1, k2], dk)
            it += 1
            t0 += w
    n = 0
    for grp in range(4):
        for h in range(8):
            g = h // 4
            po, pok = oring.next()
            for qi in range(4):
                qb = grp * 4 + qi
                mi = 0 if qb == 0 else (2 if qb == 15 else 1)
                pss, pssk = ring.next()
                kkeys = [("kT", g, t) for t in sorted(set([(qb * 128) // 512, (qb * 128 + 383) // 512]))]
                p.mm(pss[:, 0:384], qT[:, h, qb * 128:(qb + 1) * 128], kT[:, g, qb * 128: qb * 128 + 384], True, False,
                     [("qT", h, grp)] + kkeys, [pssk])
                p.mm(pss[:, 0:384], ident, masks[:, mi, :], False, True, ["cm", "masks"], [pssk])
                cl = cols[n % 2]
                ck = ("cols", n % 2)
                p.op("dve", lambda e, cl=cl, pss=pss: e.reduce_max(out=cl[:, 0:1], in_=pss[:, 0:384], axis=AX.X), [pssk], [ck])
                p.ts("dve", cl[:, 1:2], cl[:, 0:1], -ATT_SCALE, nsink[:, h:h + 1], ALU.mult, ALU.min, [ck, "nsink"], [ck])
                Pt = P[n % 2]
                pk = ("P", n % 2)
                p.act(Pt[:], pss[:, 0:384], AF.Exp, [pssk, ck], [pk, ck], scale=ATT_SCALE, bias=cl[:, 1:2],
                      accum_out=cl[:, 2:3])
                p.act(cl[:, 3:4], cl[:, 1:2], AF.Exp, [ck, "sink"], [ck], bias=sink[:, h:h + 1])
                p.tt("dve", cl[:, 4:5], cl[:, 2:3], cl[:, 3:4], ALU.add, [ck], [ck])
                p.op("dve", lambda e, cl=cl: e.reciprocal(out=cl[:, 5:6], in_=cl[:, 4:5]), [ck], [ck])
                Dt = D[n % 2]
                dk = ("D", n % 2)
                p.ts("dve", Dt[:], ident, cl[:, 5:6], None, ALU.mult, None, ["cm", ck], [dk])
                ppt, pptk = ring.next()
                for kb in range(3):
                    p.mm(ppt[:, kb * 128:(kb + 1) * 128], Pt[:, kb * 128:(kb + 1) * 128], Dt[:], True, True, [pk, dk], [pptk])
                PTt = PT[n % 2]
                ptk = ("PT", n % 2)
                p.act(PTt[:], ppt[:, 0:384], AF.Copy, [pptk], [ptk])
                for kb in range(3):
                    p.mm(po[:, qi * 128:(qi + 1) * 128], V[:, qb + kb, g * 128:(g + 1) * 128], PTt[:, kb * 128:(kb + 1) * 128],
                         kb == 0, kb == 2, [ptk, ("V", qb + kb)], [pok])
                n += 1
            p.copy("dve", qT[:, h, grp * 512:(grp + 1) * 512], po[:], [pok], [("qT", h, grp)])
            p.dma(d["oT"][h * 128:(h + 1) * 128, grp * 512:(grp + 1) * 512], qT[:, h, grp * 512:(grp + 1) * 512],
                  reads=[("qT", h, grp)], writes=[("o_out", h, grp)])


def attn_consts(j):
    inv = (10000.0 ** (-np.arange(0, 128, 2, dtype=np.float32) / 128)).astype(np.float32)
    pos = (j * TOK - HALO + np.arange(TH)).astype(np.float32)
    ang = pos[:, None] * inv[None, :]
    cos = np.cos(ang).astype(np.float32).T
    sin = np.sin(ang).astype(np.float32).T
    cosT = np.concatenate([cos, cos], axis=0)
    sinT = np.concatenate([-sin, sin], axis=0)
    qi = np.arange(128)[:, None]
    kj = np.arange(384)[None, :]
    rel = kj - 128 - qi
    base = np.where(np.abs(rel) <= 128, 0.0, -30000.0).astype(np.float32)
    first = base.copy(); first[:, 0:128] = -30000.0
    last = base.copy(); last[:, 256:384] = -30000.0
    masks = np.stack([first if j == 0 else base, base, last if j == 3 else base], axis=1)
    ident = np.eye(128, dtype=np.float32)
    swap = np.zeros((128, 128), np.float32)
    mm = np.arange(128)
    swap[(mm + 64) % 128, mm] = 1.0
    cmats = np.stack([ident, swap], axis=1)
    return {"cosT": np.ascontiguousarray(cosT), "sinT": np.ascontiguousarray(sinT),
            "masks": np.ascontiguousarray(masks), "cmats": np.ascontiguousarray(cmats)}


def halo_xT(x_b, j):
    out = np.zeros((1024, TH), x_b.dtype)
    lo, hi = j * TOK - HALO, (j + 1) * TOK + HALO
    slo, shi = max(lo, 0), min(hi, S_LEN)
    out[:, slo - lo: shi - lo] = x_b[slo:shi].T
    return out


ALPHA = 8 ** 0.25
LN_EPS = 1e-5
N_EXP = 32
CAP = 256
NSLOT = N_EXP * CAP
NROWS = NSLOT + 128


def build_mix(x_dtype=F32):
    nc = bass.Bass("TRN2", target_bir_lowering=False)
    d = {}
    d["xT"] = nc.dram_tensor("xT", [1024, TOK], x_dtype, kind="ExternalInput").ap()
    d["x_tok"] = nc.dram_tensor("x_tok", [TOK, 1024], F32, kind="ExternalInput").ap()
    d["oT"] = nc.dram_tensor("oT", [1024, TOK], BF16, kind="ExternalInput").ap()
    d["hgT"] = nc.dram_tensor("hgT", [1024, TOK], BF16, kind="ExternalInput").ap()
    d["w4"] = nc.dram_tensor("w4", [4, 1024, 1024], F32, kind="ExternalInput").ap()
    d["w_out"] = nc.dram_tensor("w_out", [1024, 1024], F32, kind="ExternalInput").ap()
    d["ln"] = nc.dram_tensor("ln", [128, 2, 1024], F32, kind="ExternalInput").ap()
    d["w_rt"] = nc.dram_tensor("w_rt", [1024, 36], F32, kind="ExternalInput").ap()
    d["b_rt"] = nc.dram_tensor("b_rt", [128, 36], F32, kind="ExternalInput").ap()
    d["cst"] = nc.dram_tensor("cst", [128, 3, 128], F32, kind="ExternalInput").ap()
    d["cst2"] = nc.dram_tensor("cst2", [128, 40], F32, kind="ExternalInput").ap()
    d["x1"] = nc.dram_tensor("x1", [TOK, 1024], F32, kind="ExternalOutput").ap()
    d["xdisp"] = nc.dram_tensor("xdisp", [NROWS, 1024], BF16, kind="ExternalOutput").ap()
    d["slots"] = nc.dram_tensor("slots", [TOK, 2], I32, kind="ExternalOutput").ap()
    d["gates"] = nc.dram_tensor("gates", [TOK, 2], F32, kind="ExternalOutput").ap()
    p = Prog(nc)
    fk = emit_mix(p, d)
    p.finish(fk)
    return nc


def emit_ln(p, y, out, lnt, which, stat, reads, writes, tag):
    st6, mv, rs = stat
    sk = ("lnstat", tag)
    for hh in range(2):
        p.op("dve", lambda e, hh=hh: e.bn_stats(out=st6[:, hh * 6:(hh + 1) * 6], in_=y[:, hh * 512:(hh + 1) * 512]),
             reads + [sk], [sk])
    p.op("dve", lambda e: e.bn_aggr(out=mv[:, 0:2], in_=st6[:, 0:12]), [sk], [sk])
    p.act(rs[:, 0:1], mv[:, 1:2], AF.Sqrt, [sk], [sk], bias=LN_EPS)
    p.op("dve", lambda e: e.reciprocal(out=rs[:, 1:2], in_=rs[:, 0:1]), [sk], [sk])
    p.ts("dve", out, y, mv[:, 0:1], rs[:, 1:2], ALU.subtract, ALU.mult, reads + [sk], writes)
    p.tt("pool", out, out, lnt[:, which, 0, :], ALU.mult, writes + ["ln"], writes)
    p.tt("pool", out, out, lnt[:, which, 1, :], ALU.add, writes + ["ln"], writes)


def emit_mix(p, d, pfx="m"):
    ot = [p.sb(pfx + "ot%d" % i, [128, 8, 512], BF16) for i in range(1)] * 2
    hg = [p.sb(pfx + "hg%d" % i, [128, 8, 512], BF16) for i in range(1)] * 2
    xb = [p.sb(pfx + "xb%d" % i, [128, 8, 512], BF16) for i in range(1)] * 2
    w4r = p.sb(pfx + "w4r", [128, 4, 8, 1024], BF16)
    wo = p.sb(pfx + "wo", [128, 8, 1024], BF16)
    mg = [p.sb(pfx + "mg%d" % i, [128, 8, 512], BF16) for i in range(2)]
    t1 = [p.sb(pfx + "t1_%d" % i, [128, 512], F32) for i in range(2)]
    t2 = [p.sb(pfx + "t2_%d" % i, [128, 512], F32) for i in range(2)]
    lnt = p.sb(pfx + "ln", [128, 1, 2, 1024], F32)
    wrt = p.sb(pfx + "wrt", [128, 8, 36], F32)
    brt = p.sb(pfx + "brt", [128, 36], F32)
    cst = p.sb(pfx + "cst", [128, 3, 128], F32)
    cstb = p.sb(pfx + "cstb", [128, 2, 128], BF16)
    cst2 = p.sb(pfx + "cst2", [128, 40], F32)
    zero = p.sb(pfx + "zero", [128, 1024], BF16)
    xt = [p.sb(pfx + "xt%d" % i, [128, 1024], F32) for i in range(2)]
    y = [p.sb(pfx + "y%d" % i, [128, 1024], F32) for i in range(2)]
    x1 = [p.sb(pfx + "x1_%d" % i, [128, 1024], F32) for i in range(2)]
    x1b = [p.sb(pfx + "x1b%d" % i, [128, 1024], BF16) for i in range(2)]
    x1T = [p.sb(pfx + "x1T%d" % i, [128, 8, 128], F32) for i in range(2)]
    st6 = p.sb(pfx + "st6", [128, 12], F32)
    mv = p.sb(pfx + "mv", [128, 2], F32)
    rs = p.sb(pfx + "rs", [128, 2], F32)
    rt = [p.sb(pfx + "rt%d" % i, [128, 64], F32) for i in range(2)]
    E = [p.sb(pfx + "E%d" % i, [128, 3, 32], F32) for i in range(2)]
    Eb = [p.sb(pfx + "Eb%d" % i, [128, 32], BF16) for i in range(2)]
    base = p.sb(pfx + "base", [128, 32], F32)
    sl = [p.sb(pfx + "sl%d" % i, [128, 2], I32) for i in range(2)]
    gt = [p.sb(pfx + "gt%d" % i, [128, 2], F32) for i in range(2)]
    i8 = [p.sb(pfx + "i8_%d" % i, [128, 8], U32) for i in range(2)]
    ring = PsumRing(p, 8, pfx + "ps")
    ident = cst[:, 0, :]
    iota32 = cst2[:, 0:32]
    iota4 = cst2[:, 32:36]
    trash = cst2[:, 36:37]

    p.dma(lnt[:, 0, :, :], d["ln"], writes=["ln"])
    p.dma(wrt[:], d["w_rt"].rearrange("(c p) n -> p c n", p=128), writes=["wrt"])
    p.dma(brt[:], d["b_rt"], writes=["brt"])
    p.dma(cst[:], d["cst"], writes=["cst"])
    p.dma(cst2[:], d["cst2"], writes=["cst2"])
    p.copy("dve", cstb[:, 0, :], cst[:, 1, :], ["cst"], ["cstb"])
    p.copy("dve", cstb[:, 1, :], cst[:, 2, :], ["cst"], ["cstb"])
    p.memset("pool", zero[:], 0.0, ["zero"])
    p.memset("pool", base[:], 0.0, ["base"])
    zk = []
    for r0 in range(0, NROWS, 1024):
        nr = min(1024, NROWS - r0)
        p.dma(d["xdisp"][r0:r0 + nr, :].rearrange("(a p) n -> p a n", p=128),
              zero[:].partition_broadcast(128) if False else zero[:, None, :].to_broadcast([128, nr // 128, 1024]),
              reads=["zero"], writes=[("xdz", r0)])
        zk.append(("xdz", r0))
    for q in range(4):
        for c0 in range(0, 1024, 512):
            p.dma(w4r[:, q, :, c0:c0 + 512], d["w4"][q].rearrange("(c p) n -> p c n", p=128)[:, :, c0:c0 + 512],
                  writes=[("w4r", q, c0)], queue="pool")
    w4k = [[("w4r", q, 0), ("w4r", q, 512)] for q in range(4)]
    for c0 in range(0, 1024, 512):
        p.dma(wo[:, :, c0:c0 + 512], d["w_out"].rearrange("(c p) n -> p c n", p=128)[:, :, c0:c0 + 512], writes=[("wo", c0)], queue="pool")
    oTv = d["oT"].rearrange("(c p) n -> p c n", p=128)
    hgv = d["hgT"].rearrange("(c p) n -> p c n", p=128) if "hgT" in d else None
    xTv = (d["xT_own"] if "xT_own" in d else d["xT"]).rearrange("(c p) n -> p c n", p=128)
    fin = []
    wn = 0
    tile_i = 0
    for T in range(TOK // 512):
        b = T % 2
        p.dma(ot[b][:], oTv[:, :, T * 512:(T + 1) * 512], writes=[("ot", 0)])
        if "hg_all" in d:
            if T == 0:
                hat = d["hg_all"][128:128 + 4096, :].rearrange("(t q) n -> t q n", t=4)
                p.dma(None, None, reads=["hg_all"], writes=["hg_mine"], queue="act",
                      fn=lambda e: e.dma_start(out=d["hg_mine"], in_=hat[p.core_idx(e, "j")]))
            p.dma(hg[b][:], d["hg_mine"].rearrange("(c p) n -> p c n", p=128)[:, :, T * 512:(T + 1) * 512],
                  reads=["hg_mine"], writes=[("hg", 0)])
            p.dma(xb[b][:], xTv[:, :, T * 512:(T + 1) * 512], reads=["xT_own"], writes=[("xb", 0)])
        else:
            p.dma(hg[b][:], hgv[:, :, T * 512:(T + 1) * 512], writes=[("hg", 0)])
            p.dma(xb[b][:], xTv[:, :, T * 512:(T + 1) * 512], writes=[("xb", 0)], queue="pool")
        for m in range(8):
            banks = [ring.next() for _ in range(4)]
            srcs = [ot[b], hg[b], xb[b], xb[b]]
            skeys = [("ot", 0), ("hg", 0), ("xb", 0), ("xb", 0)]
            for q in range(4):
                pst, psk = banks[q]
                for k in range(8):
                    p.mm(pst[:], w4r[:, q, k, m * 128:(m + 1) * 128], srcs[q][:, k, :], k == 0, k == 7,
                         w4k[q] + [skeys[q]], [psk])
            a1, a2 = t1[m % 2], t2[m % 2]
            k1, k2 = ("t1", m % 2), ("t2", m % 2)
            p.act(a1[:], banks[2][0][:], AF.Sigmoid, [banks[2][1]], [k1])
            p.act(a2[:], banks[3][0][:], AF.Sigmoid, [banks[3][1]], [k2])
            p.tt("dve", a1[:], a1[:], banks[0][0][:], ALU.mult, [k1, banks[0][1]], [k1])
            p.tt("dve", a2[:], a2[:], banks[1][0][:], ALU.mult, [k2, banks[1][1]], [k2])
            p.tt("pool", mg[b][:, m, :], a1[:], a2[:], ALU.add, [k1, k2], [("mg", b, m)])
        mgk = [("mg", b, m) for m in range(8)]
        for s in range(4):
            tb = tile_i % 2
            tok0 = T * 512 + s * 128
            p.dma(xt[tb][:], d["x_tok"][tok0:tok0 + 128, :], writes=[("xt", tb)])
            for hh in range(2):
                pst, psk = ring.next()
                for k in range(8):
                    p.mm(pst[:], mg[b][:, k, s * 128:(s + 1) * 128], wo[:, k, hh * 512:(hh + 1) * 512], k == 0, k == 7,
                         mgk + [("wo", hh * 512)], [psk])
                p.stt(y[tb][:, hh * 512:(hh + 1) * 512], xt[tb][:, hh * 512:(hh + 1) * 512], ALPHA, pst[:], ALU.mult, ALU.add,
                      [("xt", tb), psk], [("y", tb, hh)])
            x1k = ("x1", tb)
            emit_ln(p, y[tb][:], x1[tb][:], lnt, 0, (st6, mv, rs), [("y", tb, 0), ("y", tb, 1)], [x1k], "a")
            p.dma(d["x1"][tok0:tok0 + 128, :], x1[tb][:], reads=[x1k], writes=[("x1o", tile_i)])
            fin.append(("x1o", tile_i))
            p.act(x1b[tb][:], x1[tb][:], AF.Copy, [x1k], [("x1b", tb)])
            for hh in range(2):
                pst, psk = ring.next()
                for c in range(4):
                    k = hh * 4 + c
                    p.tr(pst[:, c * 128:(c + 1) * 128], x1[tb][:, k * 128:(k + 1) * 128], ident, [x1k, "cst"], [psk])
                p.copy("dve", x1T[tb][:, hh * 4:(hh + 1) * 4, :], pst[:].rearrange("p (c n) -> p c n", c=4), [psk],
                       [("x1T", tb, hh)])
            pl, plk = ring.next()
            for k in range(8):
                p.mm(pl[:, 0:36], x1T[tb][:, k, :], wrt[:, k, :], k == 0, k == 7, [("x1T", tb, 0), ("x1T", tb, 1), "wrt"], [plk])
            r = rt[tb]
            rk = ("rt", tb)
            p.tt("dve", r[:, 0:36], pl[:, 0:36], brt[:], ALU.add, [plk, "brt", rk], [rk])
            p.op("dve", lambda e, r=r: e.reduce_max(out=r[:, 36:37], in_=r[:, 0:4], axis=AX.X), [rk], [rk])
            p.ts("dve", r[:, 37:38], r[:, 36:37], -1.0, None, ALU.mult, None, [rk], [rk])
            p.act(r[:, 44:48], r[:, 0:4], AF.Exp, [rk], [rk], bias=r[:, 37:38], accum_out=r[:, 38:39])
            p.op("dve", lambda e, r=r: e.reciprocal(out=r[:, 39:40], in_=r[:, 38:39]), [rk], [rk])
            p.ts("dve", r[:, 40:44], r[:, 0:4], r[:, 36:37], None, ALU.is_equal, None, [rk], [rk])
            p.ts("dve", r[:, 48:56], r[:, 4:12], r[:, 40:41], None, ALU.mult, None, [rk], [rk])
            for g in range(1, 4):
                p.stt(r[:, 48:56], r[:, 4 + 8 * g:12 + 8 * g], r[:, 40 + g:41 + g], r[:, 48:56], ALU.mult, ALU.add, [rk], [rk])
            Et = E[tb]
            ek = ("E", tb)
            p.tt("dve", Et[:, 2, 0:4], r[:, 40:44], iota4, ALU.mult, [rk, "cst2", ek], [ek])
            p.op("dve", lambda e, r=r, Et=Et: e.reduce_sum(out=r[:, 58:59], in_=Et[:, 2, 0:4], axis=AX.X), [rk, ek], [rk])
            m8 = r[:, 48:56]
            i8t = i8[tb]
            p.op("dve", lambda e, r=r, Et=Et: e.max(out=Et[:, 2, 8:16], in_=r[:, 48:56]), [rk, ek], [ek])
            p.op("dve", lambda e, r=r, Et=Et, i8t=i8t: e.max_index(out=i8t[:], in_max=Et[:, 2, 8:16], in_values=r[:, 48:56]),
                 [rk, ek], [("i8", tb)])
            g = gt[tb]
            gk = ("gt", tb)
            p.tt("dve", r[:, 56:57], Et[:, 2, 8:9], Et[:, 2, 9:10], ALU.subtract, [ek, rk], [rk])
            p.act(r[:, 57:58], r[:, 56:57], AF.Sigmoid, [rk], [rk])
            p.tt("dve", g[:, 0:1], r[:, 57:58], r[:, 39:40], ALU.mult, [rk, gk], [gk])
            p.tt("dve", g[:, 1:2], r[:, 39:40], g[:, 0:1], ALU.subtract, [rk, gk], [gk])
            p.copy("dve", r[:, 59:61], i8t[:, 0:2], [("i8", tb), rk], [rk])
            p.stt(r[:, 59:61], r[:, 58:59].to_broadcast([128, 2]), 8.0, r[:, 59:61], ALU.mult, ALU.add, [rk], [rk])
            for kk in range(2):
                p.ts("dve", Et[:, kk, :], iota32, r[:, 59 + kk:60 + kk], None, ALU.is_equal, None, ["cst2", rk, ek], [ek])
            p.tt("dve", Eb[tb][:], Et[:, 0, :], Et[:, 1, :], ALU.add, [ek], [("Eb", tb)])
            pc, pck = ring.next()
            p.mm(pc[:, 0:32], cstb[:, 0, :], Eb[tb][:], True, True, ["cstb", ("Eb", tb)], [pck])
            p.mm(pc[:, 32:64], cstb[:, 1, :], Eb[tb][:], True, True, ["cstb", ("Eb", tb)], [pck])
            p.tt("dve", Et[:, 2, :], pc[:, 0:32], base[:], ALU.add, [pck, "base", ek], [ek])
            for kk in range(2):
                p.tt("dve", Et[:, kk, :], Et[:, kk, :], Et[:, 2, :], ALU.mult, [ek], [ek])
                p.op("dve", lambda e, r=r, Et=Et, kk=kk: e.reduce_sum(out=r[:, 61 + kk:62 + kk], in_=Et[:, kk, :], axis=AX.X),
                     [ek, rk], [rk])
            p.tt("dve", base[:], base[:], pc[:, 32:64], ALU.add, [pck, "base"], ["base"])
            for kk in range(2):
                p.ts("dve", r[:, 63:64], r[:, 61 + kk:62 + kk], float(CAP), None, ALU.is_lt, None, [rk], [rk])
                p.stt(r[:, 61 + kk:62 + kk], r[:, 59 + kk:60 + kk], float(CAP), r[:, 61 + kk:62 + kk], ALU.mult, ALU.add,
                      [rk], [rk])
                p.tt("dve", r[:, 61 + kk:62 + kk], r[:, 61 + kk:62 + kk], trash, ALU.subtract, [rk, "cst2"], [rk])
                p.tt("dve", r[:, 61 + kk:62 + kk], r[:, 61 + kk:62 + kk], r[:, 63:64], ALU.mult, [rk], [rk])
                p.tt("dve", r[:, 61 + kk:62 + kk], r[:, 61 + kk:62 + kk], trash, ALU.add, [rk, "cst2"], [rk])
                p.tt("dve", g[:, kk:kk + 1], g[:, kk:kk + 1], r[:, 63:64], ALU.mult, [rk, gk], [gk])
            slt = sl[tb]
            slk = ("sl", tb)
            p.copy("dve", slt[:], r[:, 61:63], [rk], [slk])
            p.dma(d["slots"][tok0:tok0 + 128, :], slt[:], reads=[slk], writes=[("slo", tile_i)])
            p.dma(d["gates"][tok0:tok0 + 128, :], g[:], reads=[gk], writes=[("gto", tile_i)])
            fin += [("slo", tile_i), ("gto", tile_i)]
            for kk in range(2):
                p.dma(None, None, reads=[("x1b", tb), slk] + zk, writes=[("xdo", tile_i, kk)], queue="pool",
                      fn=lambda e, tb=tb, kk=kk, slt=slt: e.indirect_dma_start(
                          out=d["xdisp"], out_offset=bass.IndirectOffsetOnAxis(ap=slt[:, kk:kk + 1], axis=0),
                          in_=x1b[tb][:, :], in_offset=None))
                fin.append(("xdo", tile_i, kk))
            tile_i += 1
    return fin


def mix_consts():
    ident = np.eye(128, dtype=np.float32)
    tp = np.arange(128)[:, None]
    t = np.arange(128)[None, :]
    lower = (tp < t).astype(np.float32)
    ones = np.ones((128, 128), np.float32)
    cst = np.stack([ident, lower, ones], axis=1)
    cst2 = np.zeros((128, 40), np.float32)
    cst2[:, 0:32] = np.arange(32, dtype=np.float32)[None, :]
    cst2[:, 32:36] = np.arange(4, dtype=np.float32)[None, :]
    cst2[:, 36] = NSLOT + np.arange(128)
    return {"cst": np.ascontiguousarray(cst), "cst2": cst2}


def build_moe():
    nc = bass.Bass("TRN2", target_bir_lowering=False)
    d = {}
    d["xdisp"] = nc.dram_tensor("xdisp", [NROWS, 1024], BF16, kind="ExternalInput").ap()
    d["x1"] = nc.dram_tensor("x1", [TOK, 1024], F32, kind="ExternalInput").ap()
    d["slots"] = nc.dram_tensor("slots", [TOK, 2], I32, kind="ExternalInput").ap()
    d["gates"] = nc.dram_tensor("gates", [TOK, 2], F32, kind="ExternalInput").ap()
    d["w_g"] = nc.dram_tensor("w_g", [N_EXP, 1024, 512], F32, kind="ExternalInput").ap()
    d["w_u"] = nc.dram_tensor("w_u", [N_EXP, 1024, 512], F32, kind="ExternalInput").ap()
    d["w_d"] = nc.dram_tensor("w_d", [N_EXP, 512, 1024], F32, kind="ExternalInput").ap()
    d["ln"] = nc.dram_tensor("ln", [128, 2, 1024], F32, kind="ExternalInput").ap()
    d["ident"] = nc.dram_tensor("ident", [128, 128], F32, kind="ExternalInput").ap()
    d["ydisp"] = nc.dram_tensor("ydisp", [NROWS, 1024], F32).ap()
    d["x2"] = nc.dram_tensor("x2", [TOK, 1024], F32, kind="ExternalOutput").ap()
    p = Prog(nc)
    fk = emit_moe(p, d)
    p.finish(fk)
    return nc


def emit_moe(p, d, pfx="e"):
    wg = [p.sb(pfx + "wg%d" % i, [128, 8, 512], BF16) for i in range(2)]
    wu = [p.sb(pfx + "wu%d" % i, [128, 8, 512], BF16) for i in range(2)]
    wd = [p.sb(pfx + "wd%d" % i, [128, 4, 1024], BF16) for i in range(2)]
    xe = [p.sb(pfx + "xe%d" % i, [128, 1024], BF16) for i in range(2)]
    xeT = [p.sb(pfx + "xeT%d" % i, [128, 8, CAP], BF16) for i in range(2)]
    hT = [p.sb(pfx + "hT%d" % i, [128, 4, CAP], BF16) for i in range(2)]
    sg = [p.sb(pfx + "sg%d" % i, [128, CAP], F32) for i in range(2)]
    yt = [p.sb(pfx + "yt%d" % i, [128, 1024], F32) for i in range(2)]
    ident = p.sb(pfx + "ident", [128, 128], BF16)
    lnt = p.sb(pfx + "ln", [128, 1, 2, 1024], F32)
    zero = p.sb(pfx + "zero", [128, 1024], F32)
    sl = [p.sb(pfx + "sl%d" % i, [128, 2], I32) for i in range(2)]
    gt = [p.sb(pfx + "gt%d" % i, [128, 2], F32) for i in range(2)]
    x1t = [p.sb(pfx + "x1t%d" % i, [128, 1024], F32) for i in range(2)]
    ya = [p.sb(pfx + "ya%d" % i, [128, 1024], F32) for i in range(2)]
    yb = [p.sb(pfx + "yb%d" % i, [128, 1024], F32) for i in range(2)]
    yo = [p.sb(pfx + "yo%d" % i, [128, 1024], F32) for i in range(2)]
    st6 = p.sb(pfx + "st6", [128, 12], F32)
    mv = p.sb(pfx + "mv", [128, 2], F32)
    rs = p.sb(pfx + "rs", [128, 2], F32)
    if "xT_next" in d:
        xTn = [p.sb(pfx + "xTn%d" % i, [128, 8, 512], BF16) for i in range(2)]
        identf = p.sb(pfx + "identf", [128, 128], F32)
        p.dma(identf[:], d["ident"], writes=["identf"])
    ring = PsumRing(p, 6, pfx + "ps")
    ptr = [p.ps(pfx + "ptr%d" % i, [128, 1024], BF16) for i in range(2)]
    for i in range(2):
        p.exclusive.add((pfx + "ptr", i))

    p.dma(ident[:], d["ident"], writes=["ident"], queue="pool")
    p.dma(lnt[:, 0, :, :], d["ln"], writes=["ln"])
    p.memset("pool", zero[:], 0.0, ["zero"])
    p.dma(d["ydisp"][NSLOT:NROWS, :], zero[:], reads=["zero"], writes=[("yd", -1, 0)])
    ydk = [("yd", -1, 0)]
    nb = 0
    nstg = 0
    stg = [p.sb(pfx + "stg%d" % i, [128, 4096], F32) for i in range(3)]
    for e in range(N_EXP):
        b = e % 2
        wk = ("w", b)
        for wi, (wsrc, wdst, wkey) in enumerate(((d["w_g"][e], wg[b], ("wg", b)), (d["w_u"][e], wu[b], ("wu", b)),
                                                 (d["w_d"][e], wd[b], ("wd", b)))):
            sgi = nstg % 3
            nstg += 1
            nchunk = 4 if wi == 2 else 8
            p.dma(stg[sgi][:].rearrange("p (c n) -> p c n", c=nchunk), wsrc.rearrange("(c p) n -> p c n", p=128),
                  writes=[("stg", sgi)])
            dflat = wdst[:].rearrange("p c n -> p (c n)")
            if wi == 1:
                p.copy("pool", dflat, stg[sgi][:], [("stg", sgi)], [wkey])
            else:
                p.act(dflat, stg[sgi][:], AF.Copy, [("stg", sgi)], [wkey])
        for blk in range(CAP // 128):
            xb_ = xe[nb % 2]
            xk = ("xe", nb % 2)
            r0 = e * CAP + blk * 128
            p.dma(xb_[:], d["xdisp"][r0:r0 + 128, :], writes=[xk])
            pt = ptr[nb % 2]
            ptk = (pfx + "ptr", nb % 2)
            for k in range(8):
                p.tr(pt[:, k * 128:(k + 1) * 128], xb_[:, k * 128:(k + 1) * 128], ident[:], [xk, "ident"], [ptk])
            p.copy("dve" if blk else "act", xeT[b][:, :, blk * 128:(blk + 1) * 128], pt[:].rearrange("p (c n) -> p c n", c=8),
                   [ptk], [("xeT", b, blk)]) if blk else \
                p.act(xeT[b][:, :, blk * 128:(blk + 1) * 128], pt[:].rearrange("p (c n) -> p c n", c=8), AF.Copy,
                      [ptk], [("xeT", b, blk)])
            nb += 1
        xtk = [("xeT", b, blk) for blk in range(CAP // 128)]
        for m in range(4):
            pg, pgk = ring.next()
            pu, puk = ring.next()
            for k in range(8):
                p.mm(pg[:, 0:CAP], wg[b][:, k, m * 128:(m + 1) * 128], xeT[b][:, k, :], k == 0, k == 7, [("wg", b)] + xtk, [pgk])
            for k in range(8):
                p.mm(pu[:, 0:CAP], wu[b][:, k, m * 128:(m + 1) * 128], xeT[b][:, k, :], k == 0, k == 7, [("wu", b)] + xtk, [puk])
            s_ = sg[m % 2]
            sk = ("sg", m % 2)
            p.act(s_[:], pg[:, 0:CAP], AF.Silu, [pgk], [sk])
            p.tt("dve", hT[b][:, m, :], s_[:], pu[:, 0:CAP], ALU.mult, [sk, puk], [("hT", b, m)])
        htk = [("hT", b, m) for m in range(4)]
        for blk in range(CAP // 128):
            y_ = yt[blk % 2]
            yk = ("yt", blk % 2)
            for hh in range(2):
                py, pyk = ring.next()
                for k in range(4):
                    p.mm(py[:], hT[b][:, k, blk * 128:(blk + 1) * 128], wd[b][:, k, hh * 512:(hh + 1) * 512], k == 0, k == 3,
                         htk + [("wd", b)], [pyk])
                if hh == 0:
                    p.act(y_[:, 0:512], py[:], AF.Copy, [pyk], [yk])
                else:
                    p.copy("dve", y_[:, 512:1024], py[:], [pyk], [yk])
            r0 = e * CAP + blk * 128
            p.dma(d["ydisp"][r0:r0 + 128, :], y_[:], reads=[yk], writes=[("yd", e, blk)])
            ydk.append(("yd", e, blk))
    fin = []
    for t in range(TOK // 128):
        b = t % 2
        tok0 = t * 128
        p.dma(sl[b][:], d["slots"][tok0:tok0 + 128, :], writes=[("sl", b)])
        p.dma(gt[b][:], d["gates"][tok0:tok0 + 128, :], writes=[("gt", b)])
        p.dma(x1t[b][:], d["x1"][tok0:tok0 + 128, :], writes=[("x1t", b)])
        for kk, dst in enumerate((ya[b], yb[b])):
            p.dma(None, None, reads=[("sl", b)] + ydk, writes=[("yab", b, kk)], queue="pool",
                  fn=lambda e, dst=dst, b=b, kk=kk: e.indirect_dma_start(
                      out=dst[:, :], out_offset=None, in_=d["ydisp"],
                      in_offset=bass.IndirectOffsetOnAxis(ap=sl[b][:, kk:kk + 1], axis=0)))
        fk_ = ("f", b)
        p.ts("dve", ya[b][:], ya[b][:], gt[b][:, 0:1], None, ALU.mult, None, [("yab", b, 0), ("gt", b)], [("yab", b, 0)])
        p.stt(ya[b][:], yb[b][:], gt[b][:, 1:2], ya[b][:], ALU.mult, ALU.add, [("yab", b, 0), ("yab", b, 1), ("gt", b)],
              [("yab", b, 0)])
        p.stt(ya[b][:], x1t[b][:], ALPHA, ya[b][:], ALU.mult, ALU.add, [("yab", b, 0), ("x1t", b)], [("yab", b, 0)])
        emit_ln(p, ya[b][:], yo[b][:], lnt, 0, (st6, mv, rs), [("yab", b, 0)], [("yo", b)], "b")
        p.dma(d["x2"][tok0:tok0 + 128, :], yo[b][:], reads=[("yo", b)], writes=[("x2o", t)])
        fin.append(("x2o", t))
        if "xT_next" in d:
            xn = xTn[(t // 4) % 2]
            xnk = ("xTn", (t // 4) % 2)
            for hh in range(2):
                pst, psk = ring.next()
                for c in range(4):
                    k = hh * 4 + c
                    p.tr(pst[:, c * 128:(c + 1) * 128], yo[b][:, k * 128:(k + 1) * 128], identf[:], [("yo", b), "identf"], [psk])
                p.copy("dve" if hh else "pool", xn[:, hh * 4:(hh + 1) * 4, (t % 4) * 128:(t % 4 + 1) * 128],
                       pst[:].rearrange("p (c n) -> p c n", c=4), [psk], [xnk]) if hh else \
                    p.act(xn[:, hh * 4:(hh + 1) * 4, (t % 4) * 128:(t % 4 + 1) * 128],
                          pst[:].rearrange("p (c n) -> p c n", c=4), AF.Copy, [psk], [xnk])
            if t % 4 == 3:
                T4 = t // 4
                p.dma(d["xT_next"].rearrange("(c p) n -> p c n", p=128)[:, :, T4 * 512:(T4 + 1) * 512], xn[:],
                      reads=[xnk], writes=[("xTn_out", T4)])
    return fin


_PROGS = {}


def _prog(name, builder):
    if name not in _PROGS:
        _PROGS[name] = builder()
    return _PROGS[name]


def _run(nc, in_maps):
    res = run_bass_kernel_spmd(nc, in_maps, core_ids=list(range(8)))
    return res.results


def kernel_unfused(x, w_in, w_sink, w_conv, b_conv, w_rec_gate, b_rec_gate, w_in_gate, b_in_gate, lru_lambda,
                   w_attn_o, w_rnn_o, w_out, ln_g, ln_b, w_router_group, b_router_group, w_router_expert, b_router_expert,
                   w_exp_gate, w_exp_up, w_exp_down):
    f = lambda a: np.asarray(a, dtype=np.float32)
    x = f(x)
    w_in, w_sink, w_conv, b_conv = f(w_in), f(w_sink), f(w_conv), f(b_conv)
    w_rec_gate, b_rec_gate, w_in_gate, b_in_gate, lru_lambda = f(w_rec_gate), f(b_rec_gate), f(w_in_gate), f(b_in_gate), f(lru_lambda)
    w_attn_o, w_rnn_o, w_out, ln_g, ln_b = f(w_attn_o), f(w_rnn_o), f(w_out), f(ln_g), f(ln_b)
    w_router_group, b_router_group = f(w_router_group), f(b_router_group)
    w_router_expert, b_router_expert = f(w_router_expert), f(b_router_expert)
    w_exp_gate, w_exp_up, w_exp_down = f(w_exp_gate), f(w_exp_up), f(w_exp_down)
    depth = w_in.shape[0]
    nc_r = _prog("rnn", build_rnn)
    nc_a = _prog("attn", build_attn)
    nc_m = _prog("mix", build_mix)
    nc_e = _prog("moe", build_moe)
    aconst = [attn_consts(j) for j in range(4)]
    mconst = mix_consts()
    ident = np.eye(128, dtype=np.float32)
    cores = [(c // 4, c % 4) for c in range(8)]
    for l in range(depth):
        xTs = [np.ascontiguousarray(x[b].T) for b in range(2)]
        maps = []
        for (b, j) in cores:
            m = pack_rnn_inputs(l, j, w_in, w_conv, b_conv, w_rec_gate, b_rec_gate, w_in_gate, b_in_gate, lru_lambda)
            m["xT"] = xTs[b]
            maps.append(m)
        res = _run(nc_r, maps)
        hgT = [np.concatenate([np.asarray(res[b * 4 + j]["hgT"]) for j in range(4)], axis=0) for b in range(2)]
        w_qkv = np.ascontiguousarray(w_in[l][:, 0:1536])
        sink = np.ascontiguousarray(np.broadcast_to(w_sink[l][None, :], (128, 8)))
        maps = []
        for (b, j) in cores:
            m = dict(aconst[j])
            m["xT"] = halo_xT(x[b], j)
            m["w_qkv"] = w_qkv
            m["sink"] = sink
            maps.append(m)
        res = _run(nc_a, maps)
        oT = [np.asarray(res[c]["oT"]) for c in range(8)]
        w4 = np.ascontiguousarray(np.stack([w_attn_o[l], w_rnn_o[l], w_in[l][:, 3584:4608], w_in[l][:, 4608:5632]]))
        ln1 = np.ascontiguousarray(np.broadcast_to(np.stack([ln_g[l, 0], ln_b[l, 0]])[None], (128, 2, 1024)))
        w_rt = np.ascontiguousarray(np.concatenate([w_router_group[l], w_router_expert[l]], axis=1))
        b_rt = np.ascontiguousarray(np.broadcast_to(np.concatenate([b_router_group[l], b_router_expert[l]])[None], (128, 36)))
        wo = np.ascontiguousarray(w_out[l])
        maps = []
        for c, (b, j) in enumerate(cores):
            m = dict(mconst)
            m["xT"] = np.ascontiguousarray(xTs[b][:, j * TOK:(j + 1) * TOK])
            m["x_tok"] = np.ascontiguousarray(x[b][j * TOK:(j + 1) * TOK])
            m["oT"] = oT[c]
            m["hgT"] = np.ascontiguousarray(hgT[b][:, j * TOK:(j + 1) * TOK])
            m["w4"] = w4
            m["w_out"] = wo
            m["ln"] = ln1
            m["w_rt"] = w_rt
            m["b_rt"] = b_rt
            maps.append(m)
        res = _run(nc_m, maps)
        ln2 = np.ascontiguousarray(np.broadcast_to(np.stack([ln_g[l, 1], ln_b[l, 1]])[None], (128, 2, 1024)))
        wg_, wu_, wd_ = np.ascontiguousarray(w_exp_gate[l]), np.ascontiguousarray(w_exp_up[l]), np.ascontiguousarray(w_exp_down[l])
        maps = []
        for c in range(8):
            maps.append({"xdisp": np.asarray(res[c]["xdisp"]), "x1": np.asarray(res[c]["x1"]),
                         "slots": np.asarray(res[c]["slots"]), "gates": np.asarray(res[c]["gates"]),
                         "w_g": wg_, "w_u": wu_, "w_d": wd_, "ln": ln2, "ident": ident})
        res = _run(nc_e, maps)
        x = np.stack([np.concatenate([np.asarray(res[b * 4 + j]["x2"]) for j in range(4)], axis=0) for b in range(2)])
    return np.ascontiguousarray(x.astype(np.float32))


GROUPS4 = [[0, 1, 2, 3], [4, 5, 6, 7]]


def build_fused(depth=4):
    nc = bass.Bass("TRN2", target_bir_lowering=False)
    L = depth

    def ext(name, shape, dt=F32):
        return nc.dram_tensor(name, list(shape), dt, kind="ExternalInput").ap()

    def internal(name, shape, dt):
        return nc.dram_tensor(name, list(shape), dt).ap()
    I = {}
    I["x_tok0"] = ext("x_tok0", [TOK, 1024])
    I["xT0"] = ext("xT0", [1024, TOK])
    I["w_r"] = ext("w_r", [L, 1024, 512])
    I["w_gt"] = ext("w_gt", [L, 128, 8, 128])
    I["small"] = ext("small", [L, 128, 2, NSM])
    I["w_qkv"] = ext("w_qkv", [L, 1024, 1536])
    I["cosT"] = ext("cosT", [128, TH])
    I["sinT"] = ext("sinT", [128, TH])
    I["masks"] = ext("masks", [128, 3, 384])
    I["cmats"] = ext("cmats", [128, 2, 128])
    I["sink"] = ext("sink", [L, 128, 8])
    I["w4"] = ext("w4", [L, 4, 1024, 1024])
    I["w_out"] = ext("w_out", [L, 1024, 1024])
    I["ln1"] = ext("ln1", [L, 128, 2, 1024])
    I["ln2"] = ext("ln2", [L, 128, 2, 1024])
    I["w_rt"] = ext("w_rt", [L, 1024, 36])
    I["b_rt"] = ext("b_rt", [L, 128, 36])
    I["cst"] = ext("cst", [128, 3, 128])
    I["cst2"] = ext("cst2", [128, 40])
    I["w_eg"] = ext("w_eg", [L, N_EXP, 1024, 512])
    I["w_eu"] = ext("w_eu", [L, N_EXP, 1024, 512])
    I["w_ed"] = ext("w_ed", [L, N_EXP, 512, 1024])
    I["ident"] = ext("ident", [128, 128])
    out = nc.dram_tensor("out", [TOK, 1024], F32, kind="ExternalOutput").ap()
    xT_own = [internal("xT_own%d" % i, [1024, TOK], BF16) for i in range(2)]
    xT_all = [internal("xT_all%d" % i, [128 + 4096, TOK], BF16) for i in range(2)]
    x_tok_i = [internal("x_tok_i%d" % i, [TOK, 1024], F32) for i in range(2)]
    hg_own = [internal("hg_own%d" % i, [4, 256, TOK], BF16) for i in range(2)]
    hg_all = [internal("hg_all%d" % i, [128 + 4096, TOK], BF16) for i in range(2)]
    halo_l = internal("halo_l", [1024, HALO], BF16)
    halo_r = internal("halo_r", [1024, HALO], BF16)
    hg_mine = internal("hg_mine", [1024, TOK], BF16)
    oT = internal("oT_i", [1024, TOK], BF16)
    x1 = internal("x1_i", [TOK, 1024], F32)
    xdisp = internal("xdisp_i", [NROWS, 1024], BF16)
    slots = internal("slots_i", [TOK, 2], I32)
    gates = internal("gates_i", [TOK, 2], F32)
    ydisp = internal("ydisp_i", [NROWS, 1024], F32)

    p = Prog(nc)
    p.begin_phase()
    st = [p.sb("pro%d" % i, [128, 8, 512], BF16) for i in range(2)]
    src = I["xT0"].rearrange("(c p) n -> p c n", p=128)
    dst = xT_own[0].rearrange("(c p) n -> p c n", p=128)
    for t in range(TOK // 512):
        p.dma(st[t % 2][:], src[:, :, t * 512:(t + 1) * 512], writes=[("pro", t % 2)], queue="pool")
        p.dma(dst[:, :, t * 512:(t + 1) * 512], st[t % 2][:], reads=[("pro", t % 2)], writes=[("xT_own_w", t)])
    for k in range(4):
        p.coll("AllGather", GROUPS4, xT_own[0][k * 256:(k + 1) * 256, :], xT_all[0][128 + k * 1024:128 + (k + 1) * 1024, :],
               reads=[("xT_own_w", t) for t in range(TOK // 512)], writes=[("xT_all", k)])
    p.end_phase()
    for l in range(L):
        par = l % 2
        last = (l == L - 1)
        p.begin_phase()
        emit_attn(p, {"xT_own": xT_own[par], "xT_all": xT_all[par], "w_qkv": I["w_qkv"][l], "cosT": I["cosT"],
                      "sinT": I["sinT"], "masks": I["masks"], "cmats": I["cmats"], "sink": I["sink"][l], "oT": oT,
                      "halo_l": halo_l, "halo_r": halo_r},
                  pfx="a%d" % l)
        p.end_phase()
        p.begin_phase()
        emit_rnn(p, xT_all[par], I["w_r"][l], I["w_gt"][l], I["small"][l], hg_own[par], pfx="r%d" % l)
        for t in range(4):
            p.coll("AllGather", GROUPS4, hg_own[par][t], hg_all[par][128 + t * 1024:128 + (t + 1) * 1024, :],
                   reads=[("hg_out", b, t) for b in range(2)], writes=[("hg_all", t)])
        p.end_phase()
        p.begin_phase()
        emit_mix(p, {"xT_own": xT_own[par], "x_tok": (I["x_tok0"] if l == 0 else x_tok_i[par]), "oT": oT,
                     "hg_all": hg_all[par], "hg_mine": hg_mine, "w4": I["w4"][l], "w_out": I["w_out"][l], "ln": I["ln1"][l],
                     "w_rt": I["w_rt"][l], "b_rt": I["b_rt"][l], "cst": I["cst"], "cst2": I["cst2"],
                     "x1": x1, "xdisp": xdisp, "slots": slots, "gates": gates}, pfx="m%d" % l)
        p.end_phase()
        p.begin_phase()
        dd = {"xdisp": xdisp, "x1": x1, "slots": slots, "gates": gates, "w_g": I["w_eg"][l], "w_u": I["w_eu"][l],
              "w_d": I["w_ed"][l], "ln": I["ln2"][l], "ident": I["ident"], "ydisp": ydisp,
              "x2": (out if last else x_tok_i[1 - par])}
        if not last:
            dd["xT_next"] = xT_own[1 - par]
        emit_moe(p, dd, pfx="e%d" % l)
        if not last:
            for k in range(4):
                p.coll("AllGather", GROUPS4, xT_own[1 - par][k * 256:(k + 1) * 256, :],
                       xT_all[1 - par][128 + k * 1024:128 + (k + 1) * 1024, :],
                       reads=[("xTn_out", t) for t in range(4)], writes=[("xT_all", k)])
        p.end_phase()
    p.finish([])
    return nc


def fused_inputs(depth, x, w_in, w_sink, w_conv, b_conv, w_rec_gate, b_rec_gate, w_in_gate, b_in_gate, lru_lambda,
                 w_attn_o, w_rnn_o, w_out, ln_g, ln_b, w_router_group, b_router_group, w_router_expert, b_router_expert,
                 w_exp_gate, w_exp_up, w_exp_down):
    L = depth
    ca = np.ascontiguousarray
    shared = {}
    shared["w_qkv"] = ca(w_in[:L, :, 0:1536])
    shared["sink"] = ca(np.broadcast_to(w_sink[:L, None, :], (L, 128, 8)))
    shared["w4"] = ca(np.stack([np.stack([w_attn_o[l], w_rnn_o[l], w_in[l][:, 3584:4608], w_in[l][:, 4608:5632]]) for l in range(L)]))
    shared["w_out"] = ca(w_out[:L])
    shared["ln1"] = ca(np.broadcast_to(np.stack([ln_g[:L, 0], ln_b[:L, 0]], axis=1)[:, None], (L, 128, 2, 1024)))
    shared["ln2"] = ca(np.broadcast_to(np.stack([ln_g[:L, 1], ln_b[:L, 1]], axis=1)[:, None], (L, 128, 2, 1024)))
    shared["w_rt"] = ca(np.concatenate([w_router_group[:L], w_router_expert[:L]], axis=2))
    shared["b_rt"] = ca(np.broadcast_to(np.concatenate([b_router_group[:L], b_router_expert[:L]], axis=1)[:, None, :], (L, 128, 36)))
    shared["w_eg"] = ca(w_exp_gate[:L])
    shared["w_eu"] = ca(w_exp_up[:L])
    shared["w_ed"] = ca(w_exp_down[:L])
    shared["ident"] = np.eye(128, dtype=np.float32)
    shared.update(mix_consts())
    rn = []
    for j in range(4):
        packs = [pack_rnn_inputs(l, j, w_in, w_conv, b_conv, w_rec_gate, b_rec_gate, w_in_gate, b_in_gate, lru_lambda)
                 for l in range(L)]
        rn.append({"w_r": ca(np.stack([q["w_r"] for q in packs])), "w_gt": ca(np.stack([q["w_g"] for q in packs])),
                   "small": ca(np.stack([q["small"] for q in packs]))})
    maps = []
    for c in range(8):
        b, j = c // 4, c % 4
        m = dict(shared)
        m.update(rn[j])
        m.update(attn_consts(j))
        xs = x[b][j * TOK:(j + 1) * TOK]
        m["x_tok0"] = ca(xs)
        m["xT0"] = ca(xs.T)
        maps.append(m)
    return maps


def kernel_fused(depth, **inp):
    key = "fused%d" % depth
    nc = _prog(key, lambda: build_fused(depth))
    names = ["x", "w_in", "w_sink", "w_conv", "b_conv", "w_rec_gate", "b_rec_gate", "w_in_gate", "b_in_gate", "lru_lambda",
             "w_attn_o", "w_rnn_o", "w_out", "ln_g", "ln_b", "w_router_group", "b_router_group", "w_router_expert",
             "b_router_expert", "w_exp_gate", "w_exp_up", "w_exp_down"]
    args = [np.asarray(inp[n], dtype=np.float32) for n in names]
    maps = fused_inputs(depth, *args)
    res = _run(nc, maps)
    x = np.stack([np.concatenate([np.asarray(res[b * 4 + j]["out"]) for j in range(4)], axis=0) for b in range(2)])
    return np.ascontiguousarray(x.astype(np.float32))


def kernel(**inputs):
    return kernel_fused(4, **inputs)
```

```python
import contextlib
import numpy as np
import concourse.bass as bass
import concourse.mybir as mybir
from concourse.bass_utils import run_bass_kernel_spmd

F32 = mybir.dt.float32
BF16 = mybir.dt.bfloat16
I32 = mybir.dt.int32
U32 = mybir.dt.uint32
AF = mybir.ActivationFunctionType
ALU = mybir.AluOpType
AX = mybir.AxisListType


class Prog:
    ENG = ("pe", "dve", "act", "pool", "sp")

    def __init__(self, nc, n_slots=8):
        self.nc = nc
        self.es = contextlib.ExitStack()
        self.q = {e: [] for e in self.ENG}
        self.cnt = {e: 0 for e in self.ENG}
        self.sem = {e: self.es.enter_context(nc.semaphore("s_" + e)) for e in self.ENG}
        self.n_slots = n_slots
        self.pool_slots = 2
        self.dq = ("sp", "act", "pool")
        self.dsem = {q: [self.es.enter_context(nc.semaphore("d_%s%d" % (q, i))) for i in range(n_slots)]
                     for q in self.dq}
        self.dn = {q: 0 for q in self.dq}
        self.sems = {}
        for e in self.ENG:
            self.sems[("c", e)] = self.sem[e]
        for q in self.dq:
            for i in range(n_slots):
                self.sems[("d", q, i)] = self.dsem[q][i]
        self.lastw = {}
        self.readers = {}
        self.waited = {e: {} for e in self.ENG}
        self.n_ops = 0
        self.exclusive = set()
        self.csem = []
        self.ph = None
        self.latest = {}
        self._cidx = {}

    def sb(self, name, shape, dtype):
        st = self.ph if self.ph is not None else self.es
        return st.enter_context(self.nc.sbuf_tensor(name, list(shape), dtype))

    def ps(self, name, shape, dtype):
        st = self.ph if self.ph is not None else self.es
        return st.enter_context(self.nc.psum_tensor(name, list(shape), dtype))

    def core_idx(self, e, which):
        key = id(e)
        if key not in self._cidx:
            pid = e.partition_id()
            vals = {}
            for name, off in (("j", 0), ("jl", 3), ("jr", 5)):
                vals[name] = e.snap((pid + off) % 4, min_val=0, max_val=3)
            self._cidx[key] = vals
        return self._cidx[key][which]

    def begin_phase(self):
        assert self.ph is None
        self.ph = contextlib.ExitStack()
        self.exclusive = set()

    def _barrier(self):
        for e in self.ENG:
            waits = []
            for sk, v in self.latest.items():
                if sk == ("c", e):
                    continue
                if self.waited[e].get(sk, 0) < v:
                    self.waited[e][sk] = v
                    waits.append((sk, v))
            if waits:
                self.q[e].append((waits, None, None, 0))
        self.lastw = {}
        self.readers = {}

    def _emit_block(self):
        nc = self.nc
        engobj = {"pe": "tensor", "dve": "vector", "act": "scalar", "pool": "gpsimd", "sp": "sync"}
        with nc.Block() as block:
            for e in self.ENG:
                items = self.q[e]

                def body(eng, items=items):
                    for waits, fn, sk, amt in items:
                        for wsk, v in waits:
                            eng.wait_ge(self.sems[wsk], v)
                        if fn is not None:
                            ins = fn(eng)
                            if amt is None:
                                ins.then_inc(self.sems[sk])
                            else:
                                ins.then_inc(self.sems[sk], amt)
                getattr(block, engobj[e])(body)
        self.q = {e: [] for e in self.ENG}

    def end_phase(self):
        self._barrier()
        self._emit_block()
        self.ph.close()
        self.ph = None

    def _deps(self, eng, reads, writes, is_dma):
        need = {}

        def add(d):
            for sk, v in d.items():
                if need.get(sk, 0) < v:
                    need[sk] = v
        for k in reads:
            if k in self.lastw:
                add(self.lastw[k])
        for k in writes:
            if k in self.lastw:
                add(self.lastw[k])
            if k in self.readers:
                add(self.readers[k])
        out = []
        for sk, v in need.items():
            if (not is_dma) and eng == "pe" and sk == ("c", "pe"):
                continue
            if self.waited[eng].get(sk, 0) >= v:
                continue
            self.waited[eng][sk] = v
            out.append((sk, v))
        return out

    def _mark(self, reads, writes, tok):
        sk, v = tok
        if self.latest.get(sk, 0) < v:
            self.latest[sk] = v
        for k in reads:
            r = self.readers.setdefault(k, {})
            if r.get(sk, 0) < v:
                r[sk] = v
        for k in writes:
            self.lastw[k] = {sk: v}
            self.readers[k] = {}

    def _excl(self, reads, writes):
        ex = [k for k in reads if k in self.exclusive]
        if ex:
            writes = list(writes) + [k for k in ex if k not in writes]
        return reads, writes

    def op(self, eng, fn, reads=(), writes=()):
        reads, writes = self._excl(reads, writes)
        waits = self._deps(eng, reads, writes, False)
        self.cnt[eng] += 1
        tok = (("c", eng), self.cnt[eng])
        self.q[eng].append((waits, fn, tok[0], 1))
        self._mark(reads, writes, tok)
        self.n_ops += 1

    def dma(self, out, in_, reads=(), writes=(), queue="sp", fn=None):
        n = self.dn[queue]
        self.dn[queue] += 1
        ns = self.n_slots if queue != "pool" else min(self.n_slots, self.pool_slots)
        slot = n % ns
        sk = ("d", queue, slot)
        waits = self._deps(queue, reads, writes, True)
        prev = 16 * (n // ns)
        if prev > 0 and self.waited[queue].get(sk, 0) < prev:
            self.waited[queue][sk] = prev
            waits.append((sk, prev))
        if fn is None:
            def fn(e, out=out, in_=in_):
                return e.dma_start(out=out, in_=in_)
        tok = (sk, prev + 16)
        self.q[queue].append((waits, fn, sk, 16))
        self._mark(reads, writes, tok)
        self.n_ops += 1

    def mm(self, out, lhsT, rhs, start, stop, reads, writes):
        self.op("pe", lambda e: e.matmul(out, lhsT=lhsT, rhs=rhs, start=start, stop=stop), reads, writes)

    def tr(self, out, in_, ident, reads, writes):
        self.op("pe", lambda e: e.transpose(out=out, in_=in_, identity=ident), reads, writes)

    def act(self, out, in_, func, reads, writes, scale=1.0, bias=None, accum_out=None):
        kw = {}
        if bias is not None:
            kw["bias"] = bias
        if accum_out is not None:
            kw["accum_out"] = accum_out
        self.op("act", lambda e: e.activation(out=out, in_=in_, func=func, scale=scale, **kw), reads, writes)

    def ts(self, eng, out, in0, s1, s2, op0, op1, reads, writes, accum_out=None):
        kw = {}
        if accum_out is not None:
            kw["accum_out"] = accum_out
        if op1 is None:
            self.op(eng, lambda e: e.tensor_scalar(out=out, in0=in0, scalar1=s1, scalar2=None, op0=op0, **kw), reads, writes)
        else:
            self.op(eng, lambda e: e.tensor_scalar(out=out, in0=in0, scalar1=s1, scalar2=s2, op0=op0, op1=op1, **kw),
                    reads, writes)

    def tt(self, eng, out, in0, in1, op, reads, writes):
        self.op(eng, lambda e: e.tensor_tensor(out=out, in0=in0, in1=in1, op=op), reads, writes)

    def stt(self, out, in0, scalar, in1, op0, op1, reads, writes):
        self.op("dve", lambda e: e.scalar_tensor_tensor(out=out, in0=in0, scalar=scalar, in1=in1, op0=op0, op1=op1),
                reads, writes)

    def scan(self, out, d0, d1, init, reads, writes):
        self.op("dve", lambda e: e.tensor_tensor_scan(out=out, data0=d0, data1=d1, initial=init, op0=ALU.mult, op1=ALU.add),
                reads, writes)

    def copy(self, eng, out, in_, reads, writes):
        self.op(eng, lambda e: e.tensor_copy(out=out, in_=in_), reads, writes)

    def memset(self, eng, ap, val, writes):
        self.op(eng, lambda e: e.memset(ap, val), (), writes)

    def coll(self, kind, groups, in_ap, out_ap, reads=(), writes=()):
        idx = len(self.csem)
        sem = self.es.enter_context(self.nc.semaphore("cc%d" % idx))
        self.csem.append(sem)
        sk = ("k", idx)
        self.sems[sk] = sem
        waits = self._deps("pool", reads, writes, True)

        def fn(e):
            return e.collective_compute(kind, ALU.bypass, replica_groups=groups, ins=[in_ap], outs=[out_ap])
        self.q["pool"].append((waits, fn, sk, None))
        self._mark(reads, writes, (sk, 1))
        self.n_ops += 1

    def finish(self, final_keys):
        self._barrier()
        self._emit_block()
        if self.ph is not None:
            self.ph.close()
            self.ph = None
        self.es.close()


def _rev(ap2d, n):
    apl = [list(s) for s in ap2d.ap]
    assert len(apl) == 2 and apl[1][1] == n and apl[1][0] == 1, apl
    from concourse.ap import AP
    return AP(ap2d.tensor, ap2d.offset + (n - 1), [apl[0], [-1, n]])


class PsumRing:
    def __init__(self, p, n=8, name="ps"):
        self.p = p
        self.t = [p.ps("%s%d" % (name, i), [128, 512], F32) for i in range(n)]
        for i in range(n):
            p.exclusive.add((name, i))
        self.i = 0
        self.n = n
        self.name = name

    def next(self):
        i = self.i
        self.i = (self.i + 1) % self.n
        return self.t[i], (self.name, i)


S_LEN = 8192
NSM = 11
GELU_C = 1.5957691216057308


def build_rnn(x_dtype=F32):
    nc = bass.Bass("TRN2", target_bir_lowering=False)
    xT = nc.dram_tensor("xT", [1024, S_LEN], x_dtype, kind="ExternalInput").ap()
    w_r = nc.dram_tensor("w_r", [1024, 512], F32, kind="ExternalInput").ap()
    w_g = nc.dram_tensor("w_g", [128, 8, 128], F32, kind="ExternalInput").ap()
    small = nc.dram_tensor("small", [128, 2, NSM], F32, kind="ExternalInput").ap()
    hgT = nc.dram_tensor("hgT", [256, S_LEN], BF16, kind="ExternalOutput").ap()
    p = Prog(nc)
    emit_rnn(p, xT, w_r, w_g, small, hgT)
    p.finish([("hg_out", b, c) for b in range(2) for c in range(4)])
    return nc


def emit_rnn(p, xT, w_r, w_g, small, hgT, pfx="r"):
    T = S_LEN
    CH = 2048
    NCH = T // CH
    wb = p.sb(pfx + "wb", [128, 8, 512], BF16)
    wg = p.sb(pfx + "wg", [128, 8, 128], BF16)
    sm = p.sb(pfx + "sm", [128, 2, NSM], F32)
    cl = p.sb(pfx + "cl", [128, 2, 2, 2], F32)
    zt = p.sb(pfx + "zt", [128, 4], F32)
    pt = p.sb(pfx + "pt", [128, 4], F32)
    xb = [p.sb(pfx + "xb%d" % i, [128, 8, 512], BF16) for i in range(2)]
    xr_full = p.sb(pfx + "xrf", [128, T + 4], F32)
    gy = p.sb(pfx + "gy", [128, T], BF16)
    xc = p.sb(pfx + "xc", [128, T], F32)
    xcb = p.sb(pfx + "xcb", [128, T], BF16)
    g1 = [p.sb(pfx + "g1_%d" % i, [128, 512], F32) for i in range(2)]
    g2 = [p.sb(pfx + "g2_%d" % i, [128, 512], F32) for i in range(2)]
    rt = p.sb(pfx + "rt", [128, CH], F32)
    it = p.sb(pfx + "it", [128, CH], F32)
    at = p.sb(pfx + "at", [128, CH], F32)
    hb = [p.sb(pfx + "hb%d" % i, [128, CH], F32) for i in range(2)]
    hgb = [p.sb(pfx + "hgb%d" % i, [128, CH], BF16) for i in range(2)]
    ring = PsumRing(p, 8, pfx + "ps")

    p.dma(wb[:], w_r.rearrange("(c p) n -> p c n", p=128), writes=["wb"], queue="pool")
    p.dma(wg[:], w_g, writes=["wg"], queue="pool")
    p.dma(sm[:], small, writes=["sm"])
    for b in range(2):
        for d in range(2):
            j = b * 2 + d
            p.act(zt[:, j:j + 1], sm[:, b, 5 + 3 * d + 2: 5 + 3 * d + 3], AF.Exp, ["sm"], ["zt"], scale=-1.0)
    p.ts("dve", pt[:], zt[:], -1.0 / 6, 1.0 / 5, ALU.mult, ALU.add, ["zt"], ["pt"])
    for cst in (-1.0 / 4, 1.0 / 3, -1.0 / 2, 1.0):
        p.tt("dve", pt[:], pt[:], zt[:], ALU.mult, ["pt", "zt"], ["pt"])
        p.ts("dve", pt[:], pt[:], cst, None, ALU.add, None, ["pt"], ["pt"])
    p.tt("dve", pt[:], pt[:], zt[:], ALU.mult, ["pt", "zt"], ["pt"])
    for b in range(2):
        for d in range(2):
            j = b * 2 + d
            p.ts("dve", cl[:, b, d, 0:1], pt[:, j:j + 1], -8.0, None, ALU.mult, None, ["pt"], ["cl"])
            p.ts("dve", cl[:, b, d, 1:2], pt[:, j:j + 1], -16.0, None, ALU.mult, None, ["pt"], ["cl"])
    chunked = (xT.shape[0] == 4096 + 128)
    if chunked:
        xTv = xT[128:128 + 4096, :].rearrange("(k r c p) n -> p r k c n", k=4, r=4, c=2, p=128)
    else:
        xTv = xT.rearrange("(c p) n -> p c n", p=128)

    def xrk(c):
        return [("xr", t) for t in range(4 * c, 4 * c + 4)]

    for blk in range(2):
        p.memset("pool", xr_full[:, 0:2], 0.0, [("xr", -1)])
        p.memset("pool", xr_full[:, T + 2:T + 4], 0.0, [("xr", 16)])
        for t in range(T // 512):
            xbt = xb[t % 2]
            xbk = ("xb", t % 2)
            if chunked:
                for k in range(4):
                    p.dma(xbt[:, 2 * k:2 * k + 2, :], xTv[:, t // 4, k, :, (t % 4) * 512:(t % 4 + 1) * 512],
                          reads=["xT_all"], writes=[xbk])
            else:
                p.dma(xbt[:], xTv[:, :, t * 512:(t + 1) * 512], writes=[xbk], queue="pool")
            for m in range(2):
                pst, psk = ring.next()
                col = (0 if m == 0 else 256) + blk * 128
                for k in range(8):
                    p.mm(pst[:], wb[:, k, col:col + 128], xbt[:, k, :], k == 0, k == 7, ["wb", xbk], [psk])
                if m == 0:
                    p.act(xr_full[:, 2 + t * 512: 2 + (t + 1) * 512], pst[:], AF.Copy, [psk], [("xr", t)])
                else:
                    a1, a2 = g1[t % 2], g2[t % 2]
                    k1, k2 = ("g1", t % 2), ("g2", t % 2)
                    p.act(a1[:], pst[:], AF.Square, [psk], [k1])
                    p.ts("dve", a1[:], a1[:], 0.044715, 1.0, ALU.mult, ALU.add, [k1], [k1])
                    p.tt("dve", a1[:], a1[:], pst[:], ALU.mult, [k1, psk], [k1])
                    p.act(a2[:], a1[:], AF.Sigmoid, [k1], [k2], scale=GELU_C)
                    p.tt("dve", gy[:, t * 512:(t + 1) * 512], a2[:], pst[:], ALU.mult, [k2, psk], [("gy", t)])
        for c in range(NCH):
            o = c * CH
            rk = [("xr", t) for t in range(4 * c - 1, 4 * c + 5)]
            p.ts("dve", xc[:, o:o + CH], xr_full[:, o:o + CH], sm[:, blk, 0:1], sm[:, blk, 4:5], ALU.mult, ALU.add,
                 rk + ["sm"], [("xc", c)])
            for tap in range(1, 4):
                p.stt(xc[:, o:o + CH], xr_full[:, o + tap:o + tap + CH], sm[:, blk, tap:tap + 1], xc[:, o:o + CH],
                      ALU.mult, ALU.add, rk + ["sm", ("xc", c)], [("xc", c)])
            p.copy("pool", xcb[:, o:o + CH], xc[:, o:o + CH], [("xc", c)], [("xcb", c)])
        hf = xr_full
        for d in range(2):
            order = list(range(NCH)) if d == 0 else list(range(NCH - 1, -1, -1))
            prev = None
            for ci, c in enumerate(order):
                o = c * CH
                for s in range(CH // 512):
                    for g in range(2):
                        pst, psk = ring.next()
                        p.mm(pst[:], wg[:, blk * 4 + d * 2 + g, :], xcb[:, o + s * 512: o + (s + 1) * 512], True, True,
                             ["wg", ("xcb", c)], [psk])
                        dst = rt if g == 0 else it
                        p.act(dst[:, s * 512:(s + 1) * 512], pst[:], AF.Sigmoid, [psk, "sm"],
                              [("rt" if g == 0 else "it", s)], bias=sm[:, blk, 5 + 3 * d + g: 5 + 3 * d + g + 1])
                rtk = [("rt", s) for s in range(4)]
                itk = [("it", s) for s in range(4)]
                p.act(at[:], rt[:], AF.Exp, rtk + ["cl"], ["at"], scale=cl[:, blk, d, 0:1])
                p.act(rt[:], rt[:], AF.Exp, rtk + ["cl"], rtk, scale=cl[:, blk, d, 1:2])
                p.act(rt[:], rt[:], AF.Sqrt, rtk, rtk, scale=-1.0, bias=1.0)
                p.tt("pool", it[:], it[:], xc[:, o:o + CH], ALU.mult, itk + [("xc", c)], itk)
                p.tt("pool", it[:], it[:], rt[:], ALU.mult, itk + rtk, itk)
                if d == 0:
                    init = 0.0 if ci == 0 else hf[:, 2 + o - 1: 2 + o]
                    p.scan(hf[:, 2 + o: 2 + o + CH], at[:], it[:], init, ["at"] + itk + [("xr", 4 * c - 1)], xrk(c))
                else:
                    hbt = hb[ci % 2]
                    init = 0.0 if ci == 0 else prev[:, 0:1]
                    p.scan(_rev(hbt[:], CH), _rev(at[:], CH), _rev(it[:], CH), init,
                           ["at"] + itk + [("hb", (ci + 1) % 2)], [("hb", ci % 2)])
                    prev = hbt
                    hg = hgb[ci % 2]
                    p.tt("pool", at[:], hbt[:], hf[:, 2 + o: 2 + o + CH], ALU.add, [("hb", ci % 2), "at"] + xrk(c), ["at"])
                    p.tt("dve", hg[:], at[:], gy[:, o:o + CH], ALU.mult, ["at"] + [("gy", t) for t in range(4 * c, 4 * c + 4)],
                         [("hgb", ci % 2)])
                    hdst = hgT[c, blk * 128:(blk + 1) * 128, :] if len(hgT.shape) == 3 else hgT[blk * 128:(blk + 1) * 128, o:o + CH]
                    p.dma(hdst, hg[:], reads=[("hgb", ci % 2)], writes=[("hg_out", blk, c)])


def pack_rnn_inputs(l, j, w_in, w_conv, b_conv, w_rec_gate, b_rec_gate, w_in_gate, b_in_gate, lru_lambda):
    XR0 = 1024 + 512
    YR0 = XR0 + 1024
    c0 = 2 * j * 128
    w_r = np.concatenate([w_in[l][:, XR0 + c0: XR0 + c0 + 256], w_in[l][:, YR0 + c0: YR0 + c0 + 256]], axis=1)
    w_g = np.empty((128, 8, 128), np.float32)
    small = np.empty((128, 2, NSM), np.float32)
    for b in range(2):
        cb = 2 * j + b
        sl = slice(cb * 128, (cb + 1) * 128)
        small[:, b, 0:4] = w_conv[l][:, sl].T
        small[:, b, 4] = b_conv[l][sl]
        for d in range(2):
            w_g[:, b * 4 + d * 2 + 0, :] = w_rec_gate[l, d, cb]
            w_g[:, b * 4 + d * 2 + 1, :] = w_in_gate[l, d, cb]
            small[:, b, 5 + 3 * d + 0] = b_rec_gate[l, d][sl]
            small[:, b, 5 + 3 * d + 1] = b_in_gate[l, d][sl]
            small[:, b, 5 + 3 * d + 2] = lru_lambda[l, d][sl]
    return {"w_r": np.ascontiguousarray(w_r), "w_g": w_g, "small": small}


TOK = 2048
HALO = 128
TH = TOK + 2 * HALO
ATT_SCALE = 128 ** -0.5


def build_attn(x_dtype=F32):
    nc = bass.Bass("TRN2", target_bir_lowering=False)
    d = {}
    d["xT"] = nc.dram_tensor("xT", [1024, TH], x_dtype, kind="ExternalInput").ap()
    d["w_qkv"] = nc.dram_tensor("w_qkv", [1024, 1536], F32, kind="ExternalInput").ap()
    d["cosT"] = nc.dram_tensor("cosT", [128, TH], F32, kind="ExternalInput").ap()
    d["sinT"] = nc.dram_tensor("sinT", [128, TH], F32, kind="ExternalInput").ap()
    d["masks"] = nc.dram_tensor("masks", [128, 3, 384], F32, kind="ExternalInput").ap()
    d["cmats"] = nc.dram_tensor("cmats", [128, 2, 128], F32, kind="ExternalInput").ap()
    d["sink"] = nc.dram_tensor("sink", [128, 8], F32, kind="ExternalInput").ap()
    d["oT"] = nc.dram_tensor("oT", [1024, TOK], BF16, kind="ExternalOutput").ap()
    p = Prog(nc)
    emit_attn(p, d)
    p.finish([("o_out", h, g) for h in range(8) for g in range(4)])
    return nc


def emit_attn(p, d, pfx="a"):
    xb = p.sb(pfx + "xb", [128, 8, TH], BF16)
    wq = [p.sb(pfx + "wq%d" % i, [128, 8, 128], BF16) for i in range(3)]
    wv = p.sb(pfx + "wv", [128, 8, 256], BF16)
    cosT = p.sb(pfx + "cos", [128, TH], F32)
    sinT = p.sb(pfx + "sin", [128, TH], F32)
    masks = p.sb(pfx + "masks", [128, 3, 384], BF16)
    cm0 = p.sb(pfx + "cm0", [128, 128], BF16)
    cm1 = p.sb(pfx + "cm1", [128, 128], BF16)
    sink = p.sb(pfx + "sink", [128, 8], F32)
    nsink = p.sb(pfx + "nsink", [128, 8], F32)
    qT = p.sb(pfx + "qT", [128, 8, TOK], BF16)
    kT = p.sb(pfx + "kT", [128, 2, TH], BF16)
    V = p.sb(pfx + "V", [128, TH // 128, 256], BF16)
    qraw = [p.sb(pfx + "qraw%d" % i, [128, 512], BF16) for i in range(2)]
    r1 = [p.sb(pfx + "r1_%d" % i, [128, 512], F32) for i in range(2)]
    r2 = [p.sb(pfx + "r2_%d" % i, [128, 512], F32) for i in range(2)]
    P = [p.sb(pfx + "P%d" % i, [128, 384], BF16) for i in range(2)]
    PT = [p.sb(pfx + "PT%d" % i, [128, 384], BF16) for i in range(2)]
    D = [p.sb(pfx + "D%d" % i, [128, 128], BF16) for i in range(2)]
    cols = [p.sb(pfx + "cols%d" % i, [128, 8], F32) for i in range(2)]
    ring = PsumRing(p, 6, pfx + "ps")
    oring = PsumRing(p, 2, pfx + "po")
    ident = cm0[:]
    pswap = cm1[:]

    if "xT_own" in d:
        own = d["xT_own"].rearrange("(c p) n -> p c n", p=128)
        allv = d["xT_all"][128:128 + 4096, :].rearrange("(k r c p) n -> p r k c n", k=4, r=4, c=2, p=128)
        nc_ = p.nc

        allr = d["xT_all"][128:128 + 4096, :].rearrange("(k r q) n -> r k q n", k=4, r=4, q=256)

        def halo(e, left):
            jn = p.core_idx(e, "jl" if left else "jr")
            if left:
                return e.dma_start(out=d["halo_l"].rearrange("(k q) n -> k q n", k=4), in_=allr[jn, :, :, TOK - HALO:TOK])
            return e.dma_start(out=d["halo_r"].rearrange("(k q) n -> k q n", k=4), in_=allr[jn, :, :, 0:HALO])
        agk = [("xT_all", k) for k in range(4)]
        p.dma(None, None, reads=agk, writes=["halo_l"], fn=lambda e: halo(e, True))
        p.dma(None, None, reads=agk, writes=["halo_r"], fn=lambda e: halo(e, False))
        p.dma(xb[:, :, 0:HALO], d["halo_l"].rearrange("(c p) n -> p c n", p=128), reads=["halo_l"],
              writes=[("xbh", 0, k) for k in range(4)])
        p.dma(xb[:, :, HALO + TOK:TH], d["halo_r"].rearrange("(c p) n -> p c n", p=128), reads=["halo_r"],
              writes=[("xbh", 1, k) for k in range(4)])
        for t0 in range(0, TOK, 512):
            keys = sorted(set([(HALO + t0) // 512, (HALO + t0 + 511) // 512]))
            p.dma(xb[:, :, HALO + t0:HALO + t0 + 512], own[:, :, t0:t0 + 512], reads=["xT_own"],
                  writes=[("xb", k) for k in keys])
    else:
        xTv = d["xT"].rearrange("(c p) n -> p c n", p=128)
        for t0 in range(0, TH, 512):
            w = min(512, TH - t0)
            p.dma(xb[:, :, t0:t0 + w], xTv[:, :, t0:t0 + w], writes=[("xb", t0 // 512)], queue="pool")

    def xbk(a, b):
        ks = [("xb", t) for t in range(a // 512, (b - 1) // 512 + 1)]
        if a < HALO:
            ks += [("xbh", 0, k) for k in range(4)]
        if b > HALO + TOK:
            ks += [("xbh", 1, k) for k in range(4)]
        return ks
    p.dma(wv[:], d["w_qkv"].rearrange("(c p) n -> p c n", p=128)[:, :, 1280:1536], writes=["wv"], queue="pool")
    p.dma(masks[:], d["masks"], writes=["masks"], queue="pool")
    p.dma(cm0[:], d["cmats"][:, 0, :], writes=["cm"], queue="pool")
    p.dma(cm1[:], d["cmats"][:, 1, :], writes=["cm"], queue="pool")
    p.dma(cosT[:], d["cosT"], writes=["cos"])
    p.dma(sinT[:], d["sinT"], writes=["sin"])
    p.dma(sink[:], d["sink"], writes=["sink"])
    p.ts("dve", nsink[:], sink[:], -1.0, None, ALU.mult, None, ["sink"], ["nsink"])
    wqv = d["w_qkv"].rearrange("(c p) n -> p c n", p=128)
    def emit_v():
        for tt in range(TH // 128):
            pst, psk = ring.next()
            for k in range(8):
                p.mm(pst[:, 0:256], xb[:, k, tt * 128:(tt + 1) * 128], wv[:, k, :], k == 0, k == 7, xbk(tt * 128, (tt + 1) * 128) + ["wv"], [psk])
            p.copy("dve" if tt % 2 else "act", V[:, tt, :], pst[:, 0:256], [psk], [("V", tt)]) if tt % 2 else \
                p.act(V[:, tt, :], pst[:, 0:256], AF.Copy, [psk], [("V", tt)])
    it = 0
    for m in list(range(8)) + ['v', 8, 9]:
        if m == 'v':
            emit_v()
            continue
        wt = wq[m % 3]
        wk = ("wq", m % 3)
        p.dma(wt[:], wqv[:, :, m * 128:(m + 1) * 128], writes=[wk], queue="pool")
        isq = m < 8
        ntok = TOK if isq else TH
        off = HALO if isq else 0
        t0 = 0
        while t0 < ntok:
            w = min(512, ntok - t0)
            pst, psk = ring.next()
            for k in range(8):
                p.mm(pst[:, 0:w], wt[:, k, :], xb[:, k, off + t0: off + t0 + w], k == 0, k == 7, xbk(off + t0, off + t0 + w) + [wk], [psk])
            qr = qraw[it % 2]
            qk = ("qraw", it % 2)
            p.act(qr[:, 0:w], pst[:, 0:w], AF.Copy, [psk], [qk])
            ps2, ps2k = ring.next()
            p.mm(ps2[:, 0:w], pswap, qr[:, 0:w], True, True, ["cm", qk], [ps2k])
            a1, a2 = r1[it % 2], r2[it % 2]
            k1, k2 = ("r1", it % 2), ("r2", it % 2)
            p.tt("dve", a1[:, 0:w], cosT[:, off + t0: off + t0 + w], pst[:, 0:w], ALU.mult, [psk, "cos"], [k1])
            p.tt("dve", a2[:, 0:w], sinT[:, off + t0: off + t0 + w], ps2[:, 0:w], ALU.mult, [ps2k, "sin"], [k2])
            if isq:
                dst = qT[:, m, t0:t0 + w]
                dk = [("qT", m, t0 // 512)]
            else:
                dst = kT[:, m - 8, t0:t0 + w]
                dk = [("kT", m - 8, t0 // 512)]
            p.tt("dve", dst, a1[:, 0:w], a2[:, 0:w], ALU.add, [k1, k2], dk)
            it += 1
            t0 += w
    n = 0
    for grp in range(4):
        for h in range(8):
            g = h // 4
            po, pok = oring.next()
            for qi in range(4):
                qb = grp * 4 + qi
                mi = 0 if qb == 0 else (2 if qb == 15 else 1)
                pss, pssk = ring.next()
                kkeys = [("kT", g, t) for t in sorted(set([(qb * 128) // 512, (qb * 128 + 383) // 512]))]
                p.mm(pss[:, 0:384], qT[:, h, qb * 128:(qb + 1) * 128], kT[:, g, qb * 128: qb * 128 + 384], True, False,
                     [("qT", h, grp)] + kkeys, [pssk])
                p.mm(pss[:, 0:384], ident, masks[:, mi, :], False, True, ["cm", "masks"], [pssk])
                cl = cols[n % 2]
                ck = ("cols", n % 2)
                p.op("dve", lambda e, cl=cl, pss=pss: e.reduce_max(out=cl[:, 0:1], in_=pss[:, 0:384], axis=AX.X), [pssk], [ck])
                p.ts("dve", cl[:, 1:2], cl[:, 0:1], -ATT_SCALE, nsink[:, h:h + 1], ALU.mult, ALU.min, [ck, "nsink"], [ck])
                Pt = P[n % 2]
                pk = ("P", n % 2)
                p.act(Pt[:], pss[:, 0:384], AF.Exp, [pssk, ck], [pk, ck], scale=ATT_SCALE, bias=cl[:, 1:2],
                      accum_out=cl[:, 2:3])
                p.act(cl[:, 3:4], cl[:, 1:2], AF.Exp, [ck, "sink"], [ck], bias=sink[:, h:h + 1])
                p.tt("dve", cl[:, 4:5], cl[:, 2:3], cl[:, 3:4], ALU.add, [ck], [ck])
                p.op("dve", lambda e, cl=cl: e.reciprocal(out=cl[:, 5:6], in_=cl[:, 4:5]), [ck], [ck])
                Dt = D[n % 2]
                dk = ("D", n % 2)
                p.ts("dve", Dt[:], ident, cl[:, 5:6], None, ALU.mult, None, ["cm", ck], [dk])
                ppt, pptk = ring.next()
                for kb in range(3):
                    p.mm(ppt[:, kb * 128:(kb + 1) * 128], Pt[:, kb * 128:(kb + 1) * 128], Dt[:], True, True, [pk, dk], [pptk])
                PTt = PT[n % 2]
                ptk = ("PT", n % 2)
                p.act(PTt[:], ppt[:, 0:384], AF.Copy, [pptk], [ptk])
                for kb in range(3):
                    p.mm(po[:, qi * 128:(qi + 1) * 128], V[:, qb + kb, g * 128:(g + 1) * 128], PTt[:, kb * 128:(kb + 1) * 128],
                         kb == 0, kb == 2, [ptk, ("V", qb + kb)], [pok])
                n += 1
            p.copy("dve", qT[:, h, grp * 512:(grp + 1) * 512], po[:], [pok], [("qT", h, grp)])
            p.dma(d["oT"][h * 128:(h + 1) * 128, grp * 512:(grp + 1) * 512], qT[:, h, grp * 512:(grp + 1) * 512],
                  reads=[("qT", h, grp)], writes=[("o_out", h, grp)])


def attn_consts(j):
    inv = (10000.0 ** (-np.arange(0, 128, 2, dtype=np.float32) / 128)).astype(np.float32)
    pos = (j * TOK - HALO + np.arange(TH)).astype(np.float32)
    ang = pos[:, None] * inv[None, :]
    cos = np.cos(ang).astype(np.float32).T
    sin = np.sin(ang).astype(np.float32).T
    cosT = np.concatenate([cos, cos], axis=0)
    sinT = np.concatenate([-sin, sin], axis=0)
    qi = np.arange(128)[:, None]
    kj = np.arange(384)[None, :]
    rel = kj - 128 - qi
    base = np.where(np.abs(rel) <= 128, 0.0, -30000.0).astype(np.float32)
    first = base.copy(); first[:, 0:128] = -30000.0
    last = base.copy(); last[:, 256:384] = -30000.0
    masks = np.stack([first if j == 0 else base, base, last if j == 3 else base], axis=1)
    ident = np.eye(128, dtype=np.float32)
    swap = np.zeros((128, 128), np.float32)
    mm = np.arange(128)
    swap[(mm + 64) % 128, mm] = 1.0
    cmats = np.stack([ident, swap], axis=1)
    return {"cosT": np.ascontiguousarray(cosT), "sinT": np.ascontiguousarray(sinT),
            "masks": np.ascontiguousarray(masks), "cmats": np.ascontiguousarray(cmats)}


def halo_xT(x_b, j):
    out = np.zeros((1024, TH), x_b.dtype)
    lo, hi = j * TOK - HALO, (j + 1) * TOK + HALO
    slo, shi = max(lo, 0), min(hi, S_LEN)
    out[:, slo - lo: shi - lo] = x_b[slo:shi].T
    return out


ALPHA = 8 ** 0.25
LN_EPS = 1e-5
N_EXP = 32
CAP = 256
NSLOT = N_EXP * CAP
NROWS = NSLOT + 128


def build_mix(x_dtype=F32):
    nc = bass.Bass("TRN2", target_bir_lowering=False)
    d = {}
    d["xT"] = nc.dram_tensor("xT", [1024, TOK], x_dtype, kind="ExternalInput").ap()
    d["x_tok"] = nc.dram_tensor("x_tok", [TOK, 1024], F32, kind="ExternalInput").ap()
    d["oT"] = nc.dram_tensor("oT", [1024, TOK], BF16, kind="ExternalInput").ap()
    d["hgT"] = nc.dram_tensor("hgT", [1024, TOK], BF16, kind="ExternalInput").ap()
    d["w4"] = nc.dram_tensor("w4", [4, 1024, 1024], F32, kind="ExternalInput").ap()
    d["w_out"] = nc.dram_tensor("w_out", [1024, 1024], F32, kind="ExternalInput").ap()
    d["ln"] = nc.dram_tensor("ln", [128, 2, 1024], F32, kind="ExternalInput").ap()
    d["w_rt"] = nc.dram_tensor("w_rt", [1024, 36], F32, kind="ExternalInput").ap()
    d["b_rt"] = nc.dram_tensor("b_rt", [128, 36], F32, kind="ExternalInput").ap()
    d["cst"] = nc.dram_tensor("cst", [128, 3, 128], F32, kind="ExternalInput").ap()
    d["cst2"] = nc.dram_tensor("cst2", [128, 40], F32, kind="ExternalInput").ap()
    d["x1"] = nc.dram_tensor("x1", [TOK, 1024], F32, kind="ExternalOutput").ap()
    d["xdisp"] = nc.dram_tensor("xdisp", [NROWS, 1024], BF16, kind="ExternalOutput").ap()
    d["slots"] = nc.dram_tensor("slots", [TOK, 2], I32, kind="ExternalOutput").ap()
    d["gates"] = nc.dram_tensor("gates", [TOK, 2], F32, kind="ExternalOutput").ap()
    p = Prog(nc)
    fk = emit_mix(p, d)
    p.finish(fk)
    return nc


def emit_ln(p, y, out, lnt, which, stat, reads, writes, tag):
    st6, mv, rs = stat
    sk = ("lnstat", tag)
    for hh in range(2):
        p.op("dve", lambda e, hh=hh: e.bn_stats(out=st6[:, hh * 6:(hh + 1) * 6], in_=y[:, hh * 512:(hh + 1) * 512]),
             reads + [sk], [sk])
    p.op("dve", lambda e: e.bn_aggr(out=mv[:, 0:2], in_=st6[:, 0:12]), [sk], [sk])
    p.act(rs[:, 0:1], mv[:, 1:2], AF.Sqrt, [sk], [sk], bias=LN_EPS)
    p.op("dve", lambda e: e.reciprocal(out=rs[:, 1:2], in_=rs[:, 0:1]), [sk], [sk])
    p.ts("dve", out, y, mv[:, 0:1], rs[:, 1:2], ALU.subtract, ALU.mult, reads + [sk], writes)
    p.tt("pool", out, out, lnt[:, which, 0, :], ALU.mult, writes + ["ln"], writes)
    p.tt("pool", out, out, lnt[:, which, 1, :], ALU.add, writes + ["ln"], writes)


def emit_mix(p, d, pfx="m"):
    ot = [p.sb(pfx + "ot%d" % i, [128, 8, 512], BF16) for i in range(1)] * 2
    hg = [p.sb(pfx + "hg%d" % i, [128, 8, 512], BF16) for i in range(1)] * 2
    xb = [p.sb(pfx + "xb%d" % i, [128, 8, 512], BF16) for i in range(1)] * 2
    w4r = p.sb(pfx + "w4r", [128, 4, 8, 1024], BF16)
    wo = p.sb(pfx + "wo", [128, 8, 1024], BF16)
    mg = [p.sb(pfx + "mg%d" % i, [128, 8, 512], BF16) for i in range(2)]
    t1 = [p.sb(pfx + "t1_%d" % i, [128, 512], F32) for i in range(2)]
    t2 = [p.sb(pfx + "t2_%d" % i, [128, 512], F32) for i in range(2)]
    lnt = p.sb(pfx + "ln", [128, 1, 2, 1024], F32)
    wrt = p.sb(pfx + "wrt", [128, 8, 36], F32)
    brt = p.sb(pfx + "brt", [128, 36], F32)
    cst = p.sb(pfx + "cst", [128, 3, 128], F32)
    cstb = p.sb(pfx + "cstb", [128, 2, 128], BF16)
    cst2 = p.sb(pfx + "cst2", [128, 40], F32)
    zero = p.sb(pfx + "zero", [128, 1024], BF16)
    xt = [p.sb(pfx + "xt%d" % i, [128, 1024], F32) for i in range(2)]
    y = [p.sb(pfx + "y%d" % i, [128, 1024], F32) for i in range(2)]
    x1 = [p.sb(pfx + "x1_%d" % i, [128, 1024], F32) for i in range(2)]
    x1b = [p.sb(pfx + "x1b%d" % i, [128, 1024], BF16) for i in range(2)]
    x1T = [p.sb(pfx + "x1T%d" % i, [128, 8, 128], F32) for i in range(2)]
    st6 = p.sb(pfx + "st6", [128, 12], F32)
    mv = p.sb(pfx + "mv", [128, 2], F32)
    rs = p.sb(pfx + "rs", [128, 2], F32)
    rt = [p.sb(pfx + "rt%d" % i, [128, 64], F32) for i in range(2)]
    E = [p.sb(pfx + "E%d" % i, [128, 3, 32], F32) for i in range(2)]
    Eb = [p.sb(pfx + "Eb%d" % i, [128, 32], BF16) for i in range(2)]
    base = p.sb(pfx + "base", [128, 32], F32)
    sl = [p.sb(pfx + "sl%d" % i, [128, 2], I32) for i in range(2)]
    gt = [p.sb(pfx + "gt%d" % i, [128, 2], F32) for i in range(2)]
    i8 = [p.sb(pfx + "i8_%d" % i, [128, 8], U32) for i in range(2)]
    ring = PsumRing(p, 8, pfx + "ps")
    ident = cst[:, 0, :]
    iota32 = cst2[:, 0:32]
    iota4 = cst2[:, 32:36]
    trash = cst2[:, 36:37]

    p.dma(lnt[:, 0, :, :], d["ln"], writes=["ln"])
    p.dma(wrt[:], d["w_rt"].rearrange("(c p) n -> p c n", p=128), writes=["wrt"])
    p.dma(brt[:], d["b_rt"], writes=["brt"])
    p.dma(cst[:], d["cst"], writes=["cst"])
    p.dma(cst2[:], d["cst2"], writes=["cst2"])
    p.copy("dve", cstb[:, 0, :], cst[:, 1, :], ["cst"], ["cstb"])
    p.copy("dve", cstb[:, 1, :], cst[:, 2, :], ["cst"], ["cstb"])
    p.memset("pool", zero[:], 0.0, ["zero"])
    p.memset("pool", base[:], 0.0, ["base"])
    zk = []
    for r0 in range(0, NROWS, 1024):
        nr = min(1024, NROWS - r0)
        p.dma(d["xdisp"][r0:r0 + nr, :].rearrange("(a p) n -> p a n", p=128),
              zero[:].partition_broadcast(128) if False else zero[:, None, :].to_broadcast([128, nr // 128, 1024]),
              reads=["zero"], writes=[("xdz", r0)])
        zk.append(("xdz", r0))
    for q in range(4):
        for c0 in range(0, 1024, 512):
            p.dma(w4r[:, q, :, c0:c0 + 512], d["w4"][q].rearrange("(c p) n -> p c n", p=128)[:, :, c0:c0 + 512],
                  writes=[("w4r", q, c0)], queue="pool")
    w4k = [[("w4r", q, 0), ("w4r", q, 512)] for q in range(4)]
    for c0 in range(0, 1024, 512):
        p.dma(wo[:, :, c0:c0 + 512], d["w_out"].rearrange("(c p) n -> p c n", p=128)[:, :, c0:c0 + 512], writes=[("wo", c0)], queue="pool")
    oTv = d["oT"].rearrange("(c p) n -> p c n", p=128)
    hgv = d["hgT"].rearrange("(c p) n -> p c n", p=128) if "hgT" in d else None
    xTv = (d["xT_own"] if "xT_own" in d else d["xT"]).rearrange("(c p) n -> p c n", p=128)
    fin = []
    wn = 0
    tile_i = 0
    for T in range(TOK // 512):
        b = T % 2
        p.dma(ot[b][:], oTv[:, :, T * 512:(T + 1) * 512], writes=[("ot", 0)])
        if "hg_all" in d:
            if T == 0:
                hat = d["hg_all"][128:128 + 4096, :].rearrange("(t q) n -> t q n", t=4)
                p.dma(None, None, reads=["hg_all"], writes=["hg_mine"], queue="act",
                      fn=lambda e: e.dma_start(out=d["hg_mine"], in_=hat[p.core_idx(e, "j")]))
            p.dma(hg[b][:], d["hg_mine"].rearrange("(c p) n -> p c n", p=128)[:, :, T * 512:(T + 1) * 512],
                  reads=["hg_mine"], writes=[("hg", 0)])
            p.dma(xb[b][:], xTv[:, :, T * 512:(T + 1) * 512], reads=["xT_own"], writes=[("xb", 0)])
        else:
            p.dma(hg[b][:], hgv[:, :, T * 512:(T + 1) * 512], writes=[("hg", 0)])
            p.dma(xb[b][:], xTv[:, :, T * 512:(T + 1) * 512], writes=[("xb", 0)], queue="pool")
        for m in range(8):
            banks = [ring.next() for _ in range(4)]
            srcs = [ot[b], hg[b], xb[b], xb[b]]
            skeys = [("ot", 0), ("hg", 0), ("xb", 0), ("xb", 0)]
            for q in range(4):
                pst, psk = banks[q]
                for k in range(8):
                    p.mm(pst[:], w4r[:, q, k, m * 128:(m + 1) * 128], srcs[q][:, k, :], k == 0, k == 7,
                         w4k[q] + [skeys[q]], [psk])
            a1, a2 = t1[m % 2], t2[m % 2]
            k1, k2 = ("t1", m % 2), ("t2", m % 2)
            p.act(a1[:], banks[2][0][:], AF.Sigmoid, [banks[2][1]], [k1])
            p.act(a2[:], banks[3][0][:], AF.Sigmoid, [banks[3][1]], [k2])
            p.tt("dve", a1[:], a1[:], banks[0][0][:], ALU.mult, [k1, banks[0][1]], [k1])
            p.tt("dve", a2[:], a2[:], banks[1][0][:], ALU.mult, [k2, banks[1][1]], [k2])
            p.tt("pool", mg[b][:, m, :], a1[:], a2[:], ALU.add, [k1, k2], [("mg", b, m)])
        mgk = [("mg", b, m) for m in range(8)]
        def stage_a(s, tile_i):
            tb = tile_i % 2
            tok0 = T * 512 + s * 128
            x1k = ("x1", tb)
            p.dma(xt[tb][:], d["x_tok"][tok0:tok0 + 128, :], writes=[("xt", tb)])
            for hh in range(2):
                pst, psk = ring.next()
                for k in range(8):
                    p.mm(pst[:], mg[b][:, k, s * 128:(s + 1) * 128], wo[:, k, hh * 512:(hh + 1) * 512], k == 0, k == 7,
                         mgk + [("wo", hh * 512)], [psk])
                p.stt(y[tb][:, hh * 512:(hh + 1) * 512], xt[tb][:, hh * 512:(hh + 1) * 512], ALPHA, pst[:], ALU.mult, ALU.add,
                      [("xt", tb), psk], [("y", tb, hh)])
            x1k = ("x1", tb)
            emit_ln(p, y[tb][:], x1[tb][:], lnt, 0, (st6, mv, rs), [("y", tb, 0), ("y", tb, 1)], [x1k], "a")
            p.dma(d["x1"][tok0:tok0 + 128, :], x1[tb][:], reads=[x1k], writes=[("x1o", tile_i)])
            fin.append(("x1o", tile_i))
            p.act(x1b[tb][:], x1[tb][:], AF.Copy, [x1k], [("x1b", tb)])

        def stage_b(s, tile_i):
            tb = tile_i % 2
            tok0 = T * 512 + s * 128
            x1k = ("x1", tb)
            for hh in range(2):
                pst, psk = ring.next()
                for c in range(4):
                    k = hh * 4 + c
                    p.tr(pst[:, c * 128:(c + 1) * 128], x1[tb][:, k * 128:(k + 1) * 128], ident, [x1k, "cst"], [psk])
                p.copy("dve", x1T[tb][:, hh * 4:(hh + 1) * 4, :], pst[:].rearrange("p (c n) -> p c n", c=4), [psk],
                       [("x1T", tb, hh)])
            pl, plk = ring.next()
            for k in range(8):
                p.mm(pl[:, 0:36], x1T[tb][:, k, :], wrt[:, k, :], k == 0, k == 7, [("x1T", tb, 0), ("x1T", tb, 1), "wrt"], [plk])
            r = rt[tb]
            rk = ("rt", tb)
            p.tt("dve", r[:, 0:36], pl[:, 0:36], brt[:], ALU.add, [plk, "brt", rk], [rk])
            p.op("dve", lambda e, r=r: e.reduce_max(out=r[:, 36:37], in_=r[:, 0:4], axis=AX.X), [rk], [rk])
            p.ts("dve", r[:, 37:38], r[:, 36:37], -1.0, None, ALU.mult, None, [rk], [rk])
            p.act(r[:, 44:48], r[:, 0:4], AF.Exp, [rk], [rk], bias=r[:, 37:38], accum_out=r[:, 38:39])
            p.op("dve", lambda e, r=r: e.reciprocal(out=r[:, 39:40], in_=r[:, 38:39]), [rk], [rk])
            p.ts("dve", r[:, 40:44], r[:, 0:4], r[:, 36:37], None, ALU.is_equal, None, [rk], [rk])
            p.ts("dve", r[:, 48:56], r[:, 4:12], r[:, 40:41], None, ALU.mult, None, [rk], [rk])
            for g in range(1, 4):
                p.stt(r[:, 48:56], r[:, 4 + 8 * g:12 + 8 * g], r[:, 40 + g:41 + g], r[:, 48:56], ALU.mult, ALU.add, [rk], [rk])
            Et = E[tb]
            ek = ("E", tb)
            p.tt("dve", Et[:, 2, 0:4], r[:, 40:44], iota4, ALU.mult, [rk, "cst2", ek], [ek])
            p.op("dve", lambda e, r=r, Et=Et: e.reduce_sum(out=r[:, 58:59], in_=Et[:, 2, 0:4], axis=AX.X), [rk, ek], [rk])
            m8 = r[:, 48:56]
            i8t = i8[tb]
            p.op("dve", lambda e, r=r, Et=Et: e.max(out=Et[:, 2, 8:16], in_=r[:, 48:56]), [rk, ek], [ek])
            p.op("dve", lambda e, r=r, Et=Et, i8t=i8t: e.max_index(out=i8t[:], in_max=Et[:, 2, 8:16], in_values=r[:, 48:56]),
                 [rk, ek], [("i8", tb)])
            g = gt[tb]
            gk = ("gt", tb)
            p.tt("dve", r[:, 56:57], Et[:, 2, 8:9], Et[:, 2, 9:10], ALU.subtract, [ek, rk], [rk])
            p.act(r[:, 57:58], r[:, 56:57], AF.Sigmoid, [rk], [rk])
            p.tt("dve", g[:, 0:1], r[:, 57:58], r[:, 39:40], ALU.mult, [rk, gk], [gk])
            p.tt("dve", g[:, 1:2], r[:, 39:40], g[:, 0:1], ALU.subtract, [rk, gk], [gk])
            p.copy("dve", r[:, 59:61], i8t[:, 0:2], [("i8", tb), rk], [rk])
            p.stt(r[:, 59:61], r[:, 58:59].to_broadcast([128, 2]), 8.0, r[:, 59:61], ALU.mult, ALU.add, [rk], [rk])
            for kk in range(2):
                p.ts("dve", Et[:, kk, :], iota32, r[:, 59 + kk:60 + kk], None, ALU.is_equal, None, ["cst2", rk, ek], [ek])
            p.tt("dve", Eb[tb][:], Et[:, 0, :], Et[:, 1, :], ALU.add, [ek], [("Eb", tb)])
            pc, pck = ring.next()
            p.mm(pc[:, 0:32], cstb[:, 0, :], Eb[tb][:], True, True, ["cstb", ("Eb", tb)], [pck])
            p.mm(pc[:, 32:64], cstb[:, 1, :], Eb[tb][:], True, True, ["cstb", ("Eb", tb)], [pck])
            p.tt("dve", Et[:, 2, :], pc[:, 0:32], base[:], ALU.add, [pck, "base", ek], [ek])
            for kk in range(2):
                p.tt("dve", Et[:, kk, :], Et[:, kk, :], Et[:, 2, :], ALU.mult, [ek], [ek])
                p.op("dve", lambda e, r=r, Et=Et, kk=kk: e.reduce_sum(out=r[:, 61 + kk:62 + kk], in_=Et[:, kk, :], axis=AX.X),
                     [ek, rk], [rk])
            p.tt("dve", base[:], base[:], pc[:, 32:64], ALU.add, [pck, "base"], ["base"])
            for kk in range(2):
                p.ts("dve", r[:, 63:64], r[:, 61 + kk:62 + kk], float(CAP), None, ALU.is_lt, None, [rk], [rk])
                p.stt(r[:, 61 + kk:62 + kk], r[:, 59 + kk:60 + kk], float(CAP), r[:, 61 + kk:62 + kk], ALU.mult, ALU.add,
                      [rk], [rk])
                p.tt("dve", r[:, 61 + kk:62 + kk], r[:, 61 + kk:62 + kk], trash, ALU.subtract, [rk, "cst2"], [rk])
                p.tt("dve", r[:, 61 + kk:62 + kk], r[:, 61 + kk:62 + kk], r[:, 63:64], ALU.mult, [rk], [rk])
                p.tt("dve", r[:, 61 + kk:62 + kk], r[:, 61 + kk:62 + kk], trash, ALU.add, [rk, "cst2"], [rk])
                p.tt("dve", g[:, kk:kk + 1], g[:, kk:kk + 1], r[:, 63:64], ALU.mult, [rk, gk], [gk])
            slt = sl[tb]
            slk = ("sl", tb)
            p.copy("dve", slt[:], r[:, 61:63], [rk], [slk])
            p.dma(d["slots"][tok0:tok0 + 128, :], slt[:], reads=[slk], writes=[("slo", tile_i)])
            p.dma(d["gates"][tok0:tok0 + 128, :], g[:], reads=[gk], writes=[("gto", tile_i)])
            fin.extend([("slo", tile_i), ("gto", tile_i)])
            for kk in range(2):
                p.dma(None, None, reads=[("x1b", tb), slk] + zk, writes=[("xdo", tile_i, kk)], queue="pool",
                      fn=lambda e, tb=tb, kk=kk, slt=slt: e.indirect_dma_start(
                          out=d["xdisp"], out_offset=bass.IndirectOffsetOnAxis(ap=slt[:, kk:kk + 1], axis=0),
                          in_=x1b[tb][:, :], in_offset=None))
                fin.append(("xdo", tile_i, kk))

        stage_a(0, T * 4)
        for s in range(4):
            if s + 1 < 4:
                stage_a(s + 1, T * 4 + s + 1)
            stage_b(s, T * 4 + s)
    return fin


def mix_consts():
    ident = np.eye(128, dtype=np.float32)
    tp = np.arange(128)[:, None]
    t = np.arange(128)[None, :]
    lower = (tp < t).astype(np.float32)
    ones = np.ones((128, 128), np.float32)
    cst = np.stack([ident, lower, ones], axis=1)
    cst2 = np.zeros((128, 40), np.float32)
    cst2[:, 0:32] = np.arange(32, dtype=np.float32)[None, :]
    cst2[:, 32:36] = np.arange(4, dtype=np.float32)[None, :]
    cst2[:, 36] = NSLOT + np.arange(128)
    return {"cst": np.ascontiguousarray(cst), "cst2": cst2}


def build_moe():
    nc = bass.Bass("TRN2", target_bir_lowering=False)
    d = {}
    d["xdisp"] = nc.dram_tensor("xdisp", [NROWS, 1024], BF16, kind="ExternalInput").ap()
    d["x1"] = nc.dram_tensor("x1", [TOK, 1024], F32, kind="ExternalInput").ap()
    d["slots"] = nc.dram_tensor("slots", [TOK, 2], I32, kind="ExternalInput").ap()
    d["gates"] = nc.dram_tensor("gates", [TOK, 2], F32, kind="ExternalInput").ap()
    d["w_g"] = nc.dram_tensor("w_g", [N_EXP, 1024, 512], F32, kind="ExternalInput").ap()
    d["w_u"] = nc.dram_tensor("w_u", [N_EXP, 1024, 512], F32, kind="ExternalInput").ap()
    d["w_d"] = nc.dram_tensor("w_d", [N_EXP, 512, 1024], F32, kind="ExternalInput").ap()
    d["ln"] = nc.dram_tensor("ln", [128, 2, 1024], F32, kind="ExternalInput").ap()
    d["ident"] = nc.dram_tensor("ident", [128, 128], F32, kind="ExternalInput").ap()
    d["ydisp"] = nc.dram_tensor("ydisp", [NROWS, 1024], F32).ap()
    d["x2"] = nc.dram_tensor("x2", [TOK, 1024], F32, kind="ExternalOutput").ap()
    p = Prog(nc)
    fk = emit_moe(p, d)
    p.finish(fk)
    return nc


def emit_moe(p, d, pfx="e"):
    wg = [p.sb(pfx + "wg%d" % i, [128, 8, 512], BF16) for i in range(2)]
    wu = [p.sb(pfx + "wu%d" % i, [128, 8, 512], BF16) for i in range(2)]
    wd = [p.sb(pfx + "wd%d" % i, [128, 4, 1024], BF16) for i in range(2)]
    xe = [p.sb(pfx + "xe%d" % i, [128, 1024], BF16) for i in range(2)]
    xeT = [p.sb(pfx + "xeT%d" % i, [128, 8, CAP], BF16) for i in range(2)]
    hT = [p.sb(pfx + "hT%d" % i, [128, 4, CAP], BF16) for i in range(2)]
    sg = [p.sb(pfx + "sg%d" % i, [128, CAP], F32) for i in range(2)]
    yt = [p.sb(pfx + "yt%d" % i, [128, 1024], F32) for i in range(2)]
    ident = p.sb(pfx + "ident", [128, 128], BF16)
    lnt = p.sb(pfx + "ln", [128, 1, 2, 1024], F32)
    zero = p.sb(pfx + "zero", [128, 1024], F32)
    sl = [p.sb(pfx + "sl%d" % i, [128, 2], I32) for i in range(2)]
    gt = [p.sb(pfx + "gt%d" % i, [128, 2], F32) for i in range(2)]
    x1t = [p.sb(pfx + "x1t%d" % i, [128, 1024], F32) for i in range(2)]
    ya = [p.sb(pfx + "ya%d" % i, [128, 1024], F32) for i in range(2)]
    yb = [p.sb(pfx + "yb%d" % i, [128, 1024], F32) for i in range(2)]
    yo = [p.sb(pfx + "yo%d" % i, [128, 1024], F32) for i in range(2)]
    st6 = p.sb(pfx + "st6", [128, 12], F32)
    mv = p.sb(pfx + "mv", [128, 2], F32)
    rs = p.sb(pfx + "rs", [128, 2], F32)
    if "xT_next" in d:
        xTn = [p.sb(pfx + "xTn%d" % i, [128, 8, 512], BF16) for i in range(2)]
        identf = p.sb(pfx + "identf", [128, 128], F32)
        p.dma(identf[:], d["ident"], writes=["identf"])
    ring = PsumRing(p, 6, pfx + "ps")
    ptr = [p.ps(pfx + "ptr%d" % i, [128, 1024], BF16) for i in range(2)]
    for i in range(2):
        p.exclusive.add((pfx + "ptr", i))

    p.dma(ident[:], d["ident"], writes=["ident"], queue="pool")
    p.dma(lnt[:, 0, :, :], d["ln"], writes=["ln"])
    p.memset("pool", zero[:], 0.0, ["zero"])
    p.dma(d["ydisp"][NSLOT:NROWS, :], zero[:], reads=["zero"], writes=[("yd", -1, 0)])
    ydk = [("yd", -1, 0)]
    nb = 0
    nstg = 0
    stg = [p.sb(pfx + "stg%d" % i, [128, 4096], F32) for i in range(3)]
    for e in range(N_EXP):
        b = e % 2
        wk = ("w", b)
        for wi, (wsrc, wdst, wkey) in enumerate(((d["w_g"][e], wg[b], ("wg", b)), (d["w_u"][e], wu[b], ("wu", b)),
                                                 (d["w_d"][e], wd[b], ("wd", b)))):
            if wi < 2:
                p.dma(wdst[:], wsrc.rearrange("(c p) n -> p c n", p=128), writes=[wkey], queue="pool")
                continue
            sgi = nstg % 3
            nstg += 1
            nchunk = 4 if wi == 2 else 8
            p.dma(stg[sgi][:].rearrange("p (c n) -> p c n", c=nchunk), wsrc.rearrange("(c p) n -> p c n", p=128),
                  writes=[("stg", sgi)])
            dflat = wdst[:].rearrange("p c n -> p (c n)")
            if wi == 1:
                p.copy("pool", dflat, stg[sgi][:], [("stg", sgi)], [wkey])
            else:
                p.act(dflat, stg[sgi][:], AF.Copy, [("stg", sgi)], [wkey])
        for blk in range(CAP // 128):
            xb_ = xe[nb % 2]
            xk = ("xe", nb % 2)
            r0 = e * CAP + blk * 128
            p.dma(xb_[:], d["xdisp"][r0:r0 + 128, :], writes=[xk])
            pt = ptr[nb % 2]
            ptk = (pfx + "ptr", nb % 2)
            for k in range(8):
                p.tr(pt[:, k * 128:(k + 1) * 128], xb_[:, k * 128:(k + 1) * 128], ident[:], [xk, "ident"], [ptk])
            p.copy("dve" if blk else "act", xeT[b][:, :, blk * 128:(blk + 1) * 128], pt[:].rearrange("p (c n) -> p c n", c=8),
                   [ptk], [("xeT", b, blk)]) if blk else \
                p.act(xeT[b][:, :, blk * 128:(blk + 1) * 128], pt[:].rearrange("p (c n) -> p c n", c=8), AF.Copy,
                      [ptk], [("xeT", b, blk)])
            nb += 1
        xtk = [("xeT", b, blk) for blk in range(CAP // 128)]
        for m in range(4):
            pg, pgk = ring.next()
            pu, puk = ring.next()
            for k in range(8):
                p.mm(pg[:, 0:CAP], wg[b][:, k, m * 128:(m + 1) * 128], xeT[b][:, k, :], k == 0, k == 7, [("wg", b)] + xtk, [pgk])
            for k in range(8):
                p.mm(pu[:, 0:CAP], wu[b][:, k, m * 128:(m + 1) * 128], xeT[b][:, k, :], k == 0, k == 7, [("wu", b)] + xtk, [puk])
            s_ = sg[m % 2]
            sk = ("sg", m % 2)
            p.act(s_[:], pg[:, 0:CAP], AF.Silu, [pgk], [sk])
            p.tt("dve", hT[b][:, m, :], s_[:], pu[:, 0:CAP], ALU.mult, [sk, puk], [("hT", b, m)])
        htk = [("hT", b, m) for m in range(4)]
        for blk in range(CAP // 128):
            y_ = yt[blk % 2]
            yk = ("yt", blk % 2)
            for hh in range(2):
                py, pyk = ring.next()
                for k in range(4):
                    p.mm(py[:], hT[b][:, k, blk * 128:(blk + 1) * 128], wd[b][:, k, hh * 512:(hh + 1) * 512], k == 0, k == 3,
                         htk + [("wd", b)], [pyk])
                if hh == 0:
                    p.act(y_[:, 0:512], py[:], AF.Copy, [pyk], [yk])
                else:
                    p.copy("dve", y_[:, 512:1024], py[:], [pyk], [yk])
            r0 = e * CAP + blk * 128
            p.dma(d["ydisp"][r0:r0 + 128, :], y_[:], reads=[yk], writes=[("yd", e, blk)])
            ydk.append(("yd", e, blk))
    fin = []
    for t in range(TOK // 128):
        b = t % 2
        tok0 = t * 128
        p.dma(sl[b][:], d["slots"][tok0:tok0 + 128, :], writes=[("sl", b)])
        p.dma(gt[b][:], d["gates"][tok0:tok0 + 128, :], writes=[("gt", b)])
        p.dma(x1t[b][:], d["x1"][tok0:tok0 + 128, :], writes=[("x1t", b)])
        for kk, dst in enumerate((ya[b], yb[b])):
            p.dma(None, None, reads=[("sl", b)] + ydk, writes=[("yab", b, kk)], queue="pool",
                  fn=lambda e, dst=dst, b=b, kk=kk: e.indirect_dma_start(
                      out=dst[:, :], out_offset=None, in_=d["ydisp"],
                      in_offset=bass.IndirectOffsetOnAxis(ap=sl[b][:, kk:kk + 1], axis=0)))
        fk_ = ("f", b)
        p.ts("dve", ya[b][:], ya[b][:], gt[b][:, 0:1], None, ALU.mult, None, [("yab", b, 0), ("gt", b)], [("yab", b, 0)])
        p.stt(ya[b][:], yb[b][:], gt[b][:, 1:2], ya[b][:], ALU.mult, ALU.add, [("yab", b, 0), ("yab", b, 1), ("gt", b)],
              [("yab", b, 0)])
        p.stt(ya[b][:], x1t[b][:], ALPHA, ya[b][:], ALU.mult, ALU.add, [("yab", b, 0), ("x1t", b)], [("yab", b, 0)])
        emit_ln(p, ya[b][:], yo[b][:], lnt, 0, (st6, mv, rs), [("yab", b, 0)], [("yo", b)], "b")
        p.dma(d["x2"][tok0:tok0 + 128, :], yo[b][:], reads=[("yo", b)], writes=[("x2o", t)])
        fin.append(("x2o", t))
        if "xT_next" in d:
            xn = xTn[(t // 4) % 2]
            xnk = ("xTn", (t // 4) % 2)
            for hh in range(2):
                pst, psk = ring.next()
                for c in range(4):
                    k = hh * 4 + c
                    p.tr(pst[:, c * 128:(c + 1) * 128], yo[b][:, k * 128:(k + 1) * 128], identf[:], [("yo", b), "identf"], [psk])
                p.copy("dve" if hh else "pool", xn[:, hh * 4:(hh + 1) * 4, (t % 4) * 128:(t % 4 + 1) * 128],
                       pst[:].rearrange("p (c n) -> p c n", c=4), [psk], [xnk]) if hh else \
                    p.act(xn[:, hh * 4:(hh + 1) * 4, (t % 4) * 128:(t % 4 + 1) * 128],
                          pst[:].rearrange("p (c n) -> p c n", c=4), AF.Copy, [psk], [xnk])
            if t % 4 == 3:
                T4 = t // 4
                p.dma(d["xT_next"].rearrange("(c p) n -> p c n", p=128)[:, :, T4 * 512:(T4 + 1) * 512], xn[:],
                      reads=[xnk], writes=[("xTn_out", T4)])
    return fin


_PROGS = {}


def _prog(name, builder):
    if name not in _PROGS:
        _PROGS[name] = builder()
    return _PROGS[name]


def _run(nc, in_maps):
    res = run_bass_kernel_spmd(nc, in_maps, core_ids=list(range(8)))
    return res.results


def kernel_unfused(x, w_in, w_sink, w_conv, b_conv, w_rec_gate, b_rec_gate, w_in_gate, b_in_gate, lru_lambda,
                   w_attn_o, w_rnn_o, w_out, ln_g, ln_b, w_router_group, b_router_group, w_router_expert, b_router_expert,
                   w_exp_gate, w_exp_up, w_exp_down):
    f = lambda a: np.asarray(a, dtype=np.float32)
    x = f(x)
    w_in, w_sink, w_conv, b_conv = f(w_in), f(w_sink), f(w_conv), f(b_conv)
    w_rec_gate, b_rec_gate, w_in_gate, b_in_gate, lru_lambda = f(w_rec_gate), f(b_rec_gate), f(w_in_gate), f(b_in_gate), f(lru_lambda)
    w_attn_o, w_rnn_o, w_out, ln_g, ln_b = f(w_attn_o), f(w_rnn_o), f(w_out), f(ln_g), f(ln_b)
    w_router_group, b_router_group = f(w_router_group), f(b_router_group)
    w_router_expert, b_router_expert = f(w_router_expert), f(b_router_expert)
    w_exp_gate, w_exp_up, w_exp_down = f(w_exp_gate), f(w_exp_up), f(w_exp_down)
    depth = w_in.shape[0]
    nc_r = _prog("rnn", build_rnn)
    nc_a = _prog("attn", build_attn)
    nc_m = _prog("mix", build_mix)
    nc_e = _prog("moe", build_moe)
    aconst = [attn_consts(j) for j in range(4)]
    mconst = mix_consts()
    ident = np.eye(128, dtype=np.float32)
    cores = [(c // 4, c % 4) for c in range(8)]
    for l in range(depth):
        xTs = [np.ascontiguousarray(x[b].T) for b in range(2)]
        maps = []
        for (b, j) in cores:
            m = pack_rnn_inputs(l, j, w_in, w_conv, b_conv, w_rec_gate, b_rec_gate, w_in_gate, b_in_gate, lru_lambda)
            m["xT"] = xTs[b]
            maps.append(m)
        res = _run(nc_r, maps)
        hgT = [np.concatenate([np.asarray(res[b * 4 + j]["hgT"]) for j in range(4)], axis=0) for b in range(2)]
        w_qkv = np.ascontiguousarray(w_in[l][:, 0:1536])
        sink = np.ascontiguousarray(np.broadcast_to(w_sink[l][None, :], (128, 8)))
        maps = []
        for (b, j) in cores:
            m = dict(aconst[j])
            m["xT"] = halo_xT(x[b], j)
            m["w_qkv"] = w_qkv
            m["sink"] = sink
            maps.append(m)
        res = _run(nc_a, maps)
        oT = [np.asarray(res[c]["oT"]) for c in range(8)]
        w4 = np.ascontiguousarray(np.stack([w_attn_o[l], w_rnn_o[l], w_in[l][:, 3584:4608], w_in[l][:, 4608:5632]]))
        ln1 = np.ascontiguousarray(np.broadcast_to(np.stack([ln_g[l, 0], ln_b[l, 0]])[None], (128, 2, 1024)))
        w_rt = np.ascontiguousarray(np.concatenate([w_router_group[l], w_router_expert[l]], axis=1))
        b_rt = np.ascontiguousarray(np.broadcast_to(np.concatenate([b_router_group[l], b_router_expert[l]])[None], (128, 36)))
        wo = np.ascontiguousarray(w_out[l])
        maps = []
        for c, (b, j) in enumerate(cores):
            m = dict(mconst)
            m["xT"] = np.ascontiguousarray(xTs[b][:, j * TOK:(j + 1) * TOK])
            m["x_tok"] = np.ascontiguousarray(x[b][j * TOK:(j + 1) * TOK])
            m["oT"] = oT[c]
            m["hgT"] = np.ascontiguousarray(hgT[b][:, j * TOK:(j + 1) * TOK])
            m["w4"] = w4
            m["w_out"] = wo
            m["ln"] = ln1
            m["w_rt"] = w_rt
            m["b_rt"] = b_rt
            maps.append(m)
        res = _run(nc_m, maps)
        ln2 = np.ascontiguousarray(np.broadcast_to(np.stack([ln_g[l, 1], ln_b[l, 1]])[None], (128, 2, 1024)))
        wg_, wu_, wd_ = np.ascontiguousarray(w_exp_gate[l]), np.ascontiguousarray(w_exp_up[l]), np.ascontiguousarray(w_exp_down[l])
        maps = []
        for c in range(8):
            maps.append({"xdisp": np.asarray(res[c]["xdisp"]), "x1": np.asarray(res[c]["x1"]),
                         "slots": np.asarray(res[c]["slots"]), "gates": np.asarray(res[c]["gates"]),
                         "w_g": wg_, "w_u": wu_, "w_d": wd_, "ln": ln2, "ident": ident})
        res = _run(nc_e, maps)
        x = np.stack([np.concatenate([np.asarray(res[b * 4 + j]["x2"]) for j in range(4)], axis=0) for b in range(2)])
    return np.ascontiguousarray(x.astype(np.float32))


GROUPS4 = [[0, 1, 2, 3], [4, 5, 6, 7]]


def build_fused(depth=4):
    nc = bass.Bass("TRN2", target_bir_lowering=False)
    L = depth

    def ext(name, shape, dt=F32):
        return nc.dram_tensor(name, list(shape), dt, kind="ExternalInput").ap()

    def internal(name, shape, dt):
        return nc.dram_tensor(name, list(shape), dt).ap()
    I = {}
    I["x_tok0"] = ext("x_tok0", [TOK, 1024])
    I["xT0"] = ext("xT0", [1024, TOK])
    I["w_r"] = ext("w_r", [L, 1024, 512])
    I["w_gt"] = ext("w_gt", [L, 128, 8, 128])
    I["small"] = ext("small", [L, 128, 2, NSM])
    I["w_qkv"] = ext("w_qkv", [L, 1024, 1536])
    I["cosT"] = ext("cosT", [128, TH])
    I["sinT"] = ext("sinT", [128, TH])
    I["masks"] = ext("masks", [128, 3, 384])
    I["cmats"] = ext("cmats", [128, 2, 128])
    I["sink"] = ext("sink", [L, 128, 8])
    I["w4"] = ext("w4", [L, 4, 1024, 1024])
    I["w_out"] = ext("w_out", [L, 1024, 1024])
    I["ln1"] = ext("ln1", [L, 128, 2, 1024])
    I["ln2"] = ext("ln2", [L, 128, 2, 1024])
    I["w_rt"] = ext("w_rt", [L, 1024, 36])
    I["b_rt"] = ext("b_rt", [L, 128, 36])
    I["cst"] = ext("cst", [128, 3, 128])
    I["cst2"] = ext("cst2", [128, 40])
    I["w_eg"] = ext("w_eg", [L, N_EXP, 1024, 512])
    I["w_eu"] = ext("w_eu", [L, N_EXP, 1024, 512])
    I["w_ed"] = ext("w_ed", [L, N_EXP, 512, 1024])
    I["ident"] = ext("ident", [128, 128])
    out = nc.dram_tensor("out", [TOK, 1024], F32, kind="ExternalOutput").ap()
    xT_own = [internal("xT_own%d" % i, [1024, TOK], BF16) for i in range(2)]
    xT_all = [internal("xT_all%d" % i, [128 + 4096, TOK], BF16) for i in range(2)]
    x_tok_i = [internal("x_tok_i%d" % i, [TOK, 1024], F32) for i in range(2)]
    hg_own = [internal("hg_own%d" % i, [4, 256, TOK], BF16) for i in range(2)]
    hg_all = [internal("hg_all%d" % i, [128 + 4096, TOK], BF16) for i in range(2)]
    halo_l = internal("halo_l", [1024, HALO], BF16)
    halo_r = internal("halo_r", [1024, HALO], BF16)
    hg_mine = internal("hg_mine", [1024, TOK], BF16)
    oT = internal("oT_i", [1024, TOK], BF16)
    x1 = internal("x1_i", [TOK, 1024], F32)
    xdisp = internal("xdisp_i", [NROWS, 1024], BF16)
    slots = internal("slots_i", [TOK, 2], I32)
    gates = internal("gates_i", [TOK, 2], F32)
    ydisp = internal("ydisp_i", [NROWS, 1024], F32)

    p = Prog(nc)
    p.begin_phase()
    st = [p.sb("pro%d" % i, [128, 8, 512], BF16) for i in range(2)]
    src = I["xT0"].rearrange("(c p) n -> p c n", p=128)
    dst = xT_own[0].rearrange("(c p) n -> p c n", p=128)
    for t in range(TOK // 512):
        p.dma(st[t % 2][:], src[:, :, t * 512:(t + 1) * 512], writes=[("pro", t % 2)], queue="pool")
        p.dma(dst[:, :, t * 512:(t + 1) * 512], st[t % 2][:], reads=[("pro", t % 2)], writes=[("xT_own_w", t)])
    for k in range(4):
        p.coll("AllGather", GROUPS4, xT_own[0][k * 256:(k + 1) * 256, :], xT_all[0][128 + k * 1024:128 + (k + 1) * 1024, :],
               reads=[("xT_own_w", t) for t in range(TOK // 512)], writes=[("xT_all", k)])
    p.end_phase()
    for l in range(L):
        par = l % 2
        last = (l == L - 1)
        p.begin_phase()
        if l > 0:
            for k in range(4):
                p.coll("AllGather", GROUPS4, xT_own[par][k * 256:(k + 1) * 256, :],
                       xT_all[par][128 + k * 1024:128 + (k + 1) * 1024, :], reads=[], writes=[("xT_all", k)])
        emit_attn(p, {"xT_own": xT_own[par], "xT_all": xT_all[par], "w_qkv": I["w_qkv"][l], "cosT": I["cosT"],
                      "sinT": I["sinT"], "masks": I["masks"], "cmats": I["cmats"], "sink": I["sink"][l], "oT": oT,
                      "halo_l": halo_l, "halo_r": halo_r},
                  pfx="a%d" % l)
        p.end_phase()
        p.begin_phase()
        emit_rnn(p, xT_all[par], I["w_r"][l], I["w_gt"][l], I["small"][l], hg_own[par], pfx="r%d" % l)
        for t in range(4):
            p.coll("AllGather", GROUPS4, hg_own[par][t], hg_all[par][128 + t * 1024:128 + (t + 1) * 1024, :],
                   reads=[("hg_out", b, t) for b in range(2)], writes=[("hg_all", t)])
        p.end_phase()
        p.begin_phase()
        emit_mix(p, {"xT_own": xT_own[par], "x_tok": (I["x_tok0"] if l == 0 else x_tok_i[par]), "oT": oT,
                     "hg_all": hg_all[par], "hg_mine": hg_mine, "w4": I["w4"][l], "w_out": I["w_out"][l], "ln": I["ln1"][l],
                     "w_rt": I["w_rt"][l], "b_rt": I["b_rt"][l], "cst": I["cst"], "cst2": I["cst2"],
                     "x1": x1, "xdisp": xdisp, "slots": slots, "gates": gates}, pfx="m%d" % l)
        p.end_phase()
        p.begin_phase()
        dd = {"xdisp": xdisp, "x1": x1, "slots": slots, "gates": gates, "w_g": I["w_eg"][l], "w_u": I["w_eu"][l],
              "w_d": I["w_ed"][l], "ln": I["ln2"][l], "ident": I["ident"], "ydisp": ydisp,
              "x2": (out if last else x_tok_i[1 - par])}
        if not last:
            dd["xT_next"] = xT_own[1 - par]
        emit_moe(p, dd, pfx="e%d" % l)
        p.end_phase()
    p.finish([])
    return nc


def fused_inputs(depth, x, w_in, w_sink, w_conv, b_conv, w_rec_gate, b_rec_gate, w_in_gate, b_in_gate, lru_lambda,
                 w_attn_o, w_rnn_o, w_out, ln_g, ln_b, w_router_group, b_router_group, w_router_expert, b_router_expert,
                 w_exp_gate, w_exp_up, w_exp_down):
    L = depth
    ca = np.ascontiguousarray
    shared = {}
    shared["w_qkv"] = ca(w_in[:L, :, 0:1536])
    shared["sink"] = ca(np.broadcast_to(w_sink[:L, None, :], (L, 128, 8)))
    shared["w4"] = ca(np.stack([np.stack([w_attn_o[l], w_rnn_o[l], w_in[l][:, 3584:4608], w_in[l][:, 4608:5632]]) for l in range(L)]))
    shared["w_out"] = ca(w_out[:L])
    shared["ln1"] = ca(np.broadcast_to(np.stack([ln_g[:L, 0], ln_b[:L, 0]], axis=1)[:, None], (L, 128, 2, 1024)))
    shared["ln2"] = ca(np.broadcast_to(np.stack([ln_g[:L, 1], ln_b[:L, 1]], axis=1)[:, None], (L, 128, 2, 1024)))
    shared["w_rt"] = ca(np.concatenate([w_router_group[:L], w_router_expert[:L]], axis=2))
    shared["b_rt"] = ca(np.broadcast_to(np.concatenate([b_router_group[:L], b_router_expert[:L]], axis=1)[:, None, :], (L, 128, 36)))
    shared["w_eg"] = ca(w_exp_gate[:L])
    shared["w_eu"] = ca(w_exp_up[:L])
    shared["w_ed"] = ca(w_exp_down[:L])
    shared["ident"] = np.eye(128, dtype=np.float32)
    shared.update(mix_consts())
    rn = []
    for j in range(4):
        packs = [pack_rnn_inputs(l, j, w_in, w_conv, b_conv, w_rec_gate, b_rec_gate, w_in_gate, b_in_gate, lru_lambda)
                 for l in range(L)]
        rn.append({"w_r": ca(np.stack([q["w_r"] for q in packs])), "w_gt": ca(np.stack([q["w_g"] for q in packs])),
                   "small": ca(np.stack([q["small"] for q in packs]))})
    maps = []
    for c in range(8):
        b, j = c // 4, c % 4
        m = dict(shared)
        m.update(rn[j])
        m.update(attn_consts(j))
        xs = x[b][j * TOK:(j + 1) * TOK]
        m["x_tok0"] = ca(xs)
        m["xT0"] = ca(xs.T)
        maps.append(m)
    return maps


def kernel_fused(depth, **inp):
    key = "fused%d" % depth
    nc = _prog(key, lambda: build_fused(depth))
    names = ["x", "w_in", "w_sink", "w_conv", "b_conv", "w_rec_gate", "b_rec_gate", "w_in_gate", "b_in_gate", "lru_lambda",
             "w_attn_o", "w_rnn_o", "w_out", "ln_g", "ln_b", "w_router_group", "b_router_group", "w_router_expert",
             "b_router_expert", "w_exp_gate", "w_exp_up", "w_exp_down"]
    args = [np.asarray(inp[n], dtype=np.float32) for n in names]
    maps = fused_inputs(depth, *args)
    res = _run(nc, maps)
    x = np.stack([np.concatenate([np.asarray(res[b * 4 + j]["out"]) for j in range(4)], axis=0) for b in range(2)])
    return np.ascontiguousarray(x.astype(np.float32))


def kernel(**inputs):
    return kernel_fused(4, **inputs)
```

```python
import contextlib
import numpy as np
import concourse.bass as bass
import concourse.mybir as mybir
from concourse.bass_utils import run_bass_kernel_spmd

F32 = mybir.dt.float32
BF16 = mybir.dt.bfloat16
I32 = mybir.dt.int32
U32 = mybir.dt.uint32
AF = mybir.ActivationFunctionType
ALU = mybir.AluOpType
AX = mybir.AxisListType


class Prog:
    ENG = ("pe", "dve", "act", "pool", "sp")

    def __init__(self, nc, n_slots=8):
        self.nc = nc
        self.es = contextlib.ExitStack()
        self.q = {e: [] for e in self.ENG}
        self.cnt = {e: 0 for e in self.ENG}
        self.sem = {e: self.es.enter_context(nc.semaphore("s_" + e)) for e in self.ENG}
        self.n_slots = n_slots
        self.pool_slots = 2
        self.dq = ("sp", "act", "pool")
        self.dsem = {q: [self.es.enter_context(nc.semaphore("d_%s%d" % (q, i))) for i in range(n_slots)]
                     for q in self.dq}
        self.dn = {q: 0 for q in self.dq}
        self.sems = {}
        for e in self.ENG:
            self.sems[("c", e)] = self.sem[e]
        for q in self.dq:
            for i in range(n_slots):
                self.sems[("d", q, i)] = self.dsem[q][i]
        self.lastw = {}
        self.readers = {}
        self.waited = {e: {} for e in self.ENG}
        self.n_ops = 0
        self.exclusive = set()
        self.csem = []
        self.ph = None
        self.latest = {}
        self._cidx = {}

    def sb(self, name, shape, dtype):
        st = self.ph if self.ph is not None else self.es
        return st.enter_context(self.nc.sbuf_tensor(name, list(shape), dtype))

    def ps(self, name, shape, dtype):
        st = self.ph if self.ph is not None else self.es
        return st.enter_context(self.nc.psum_tensor(name, list(shape), dtype))

    def core_idx(self, e, which):
        key = id(e)
        if key not in self._cidx:
            pid = e.partition_id()
            vals = {}
            for name, off in (("j", 0), ("jl", 3), ("jr", 5)):
                vals[name] = e.snap((pid + off) % 4, min_val=0, max_val=3)
            self._cidx[key] = vals
        return self._cidx[key][which]

    def begin_phase(self):
        assert self.ph is None
        self.ph = contextlib.ExitStack()
        self.exclusive = set()

    def _barrier(self):
        for e in self.ENG:
            waits = []
            for sk, v in self.latest.items():
                if sk == ("c", e):
                    continue
                if self.waited[e].get(sk, 0) < v:
                    self.waited[e][sk] = v
                    waits.append((sk, v))
            if waits:
                self.q[e].append((waits, None, None, 0))
        self.lastw = {}
        self.readers = {}

    def _emit_block(self):
        nc = self.nc
        engobj = {"pe": "tensor", "dve": "vector", "act": "scalar", "pool": "gpsimd", "sp": "sync"}
        with nc.Block() as block:
            for e in self.ENG:
                items = self.q[e]

                def body(eng, items=items):
                    for waits, fn, sk, amt in items:
                        for wsk, v in waits:
                            eng.wait_ge(self.sems[wsk], v)
                        if fn is not None:
                            ins = fn(eng)
                            if amt is None:
                                ins.then_inc(self.sems[sk])
                            else:
                                ins.then_inc(self.sems[sk], amt)
                getattr(block, engobj[e])(body)
        self.q = {e: [] for e in self.ENG}

    def end_phase(self):
        self._barrier()
        self._emit_block()
        self.ph.close()
        self.ph = None

    def _deps(self, eng, reads, writes, is_dma):
        need = {}

        def add(d):
            for sk, v in d.items():
                if need.get(sk, 0) < v:
                    need[sk] = v
        for k in reads:
            if k in self.lastw:
                add(self.lastw[k])
        for k in writes:
            if k in self.lastw:
                add(self.lastw[k])
            if k in self.readers:
                add(self.readers[k])
        out = []
        for sk, v in need.items():
            if (not is_dma) and eng == "pe" and sk == ("c", "pe"):
                continue
            if self.waited[eng].get(sk, 0) >= v:
                continue
            self.waited[eng][sk] = v
            out.append((sk, v))
        return out

    def _mark(self, reads, writes, tok):
        sk, v = tok
        if self.latest.get(sk, 0) < v:
            self.latest[sk] = v
        for k in reads:
            r = self.readers.setdefault(k, {})
            if r.get(sk, 0) < v:
                r[sk] = v
        for k in writes:
            self.lastw[k] = {sk: v}
            self.readers[k] = {}

    def _excl(self, reads, writes):
        ex = [k for k in reads if k in self.exclusive]
        if ex:
            writes = list(writes) + [k for k in ex if k not in writes]
        return reads, writes

    def op(self, eng, fn, reads=(), writes=()):
        reads, writes = self._excl(reads, writes)
        waits = self._deps(eng, reads, writes, False)
        self.cnt[eng] += 1
        tok = (("c", eng), self.cnt[eng])
        self.q[eng].append((waits, fn, tok[0], 1))
        self._mark(reads, writes, tok)
        self.n_ops += 1

    def dma(self, out, in_, reads=(), writes=(), queue="sp", fn=None):
        n = self.dn[queue]
        self.dn[queue] += 1
        ns = self.n_slots if queue != "pool" else min(self.n_slots, self.pool_slots)
        slot = n % ns
        sk = ("d", queue, slot)
        waits = self._deps(queue, reads, writes, True)
        prev = 16 * (n // ns)
        if prev > 0 and self.waited[queue].get(sk, 0) < prev:
            self.waited[queue][sk] = prev
            waits.append((sk, prev))
        if fn is None:
            def fn(e, out=out, in_=in_):
                return e.dma_start(out=out, in_=in_)
        tok = (sk, prev + 16)
        self.q[queue].append((waits, fn, sk, 16))
        self._mark(reads, writes, tok)
        self.n_ops += 1

    def mm(self, out, lhsT, rhs, start, stop, reads, writes):
        self.op("pe", lambda e: e.matmul(out, lhsT=lhsT, rhs=rhs, start=start, stop=stop), reads, writes)

    def tr(self, out, in_, ident, reads, writes):
        self.op("pe", lambda e: e.transpose(out=out, in_=in_, identity=ident), reads, writes)

    def act(self, out, in_, func, reads, writes, scale=1.0, bias=None, accum_out=None):
        kw = {}
        if bias is not None:
            kw["bias"] = bias
        if accum_out is not None:
            kw["accum_out"] = accum_out
        self.op("act", lambda e: e.activation(out=out, in_=in_, func=func, scale=scale, **kw), reads, writes)

    def ts(self, eng, out, in0, s1, s2, op0, op1, reads, writes, accum_out=None):
        kw = {}
        if accum_out is not None:
            kw["accum_out"] = accum_out
        if op1 is None:
            self.op(eng, lambda e: e.tensor_scalar(out=out, in0=in0, scalar1=s1, scalar2=None, op0=op0, **kw), reads, writes)
        else:
            self.op(eng, lambda e: e.tensor_scalar(out=out, in0=in0, scalar1=s1, scalar2=s2, op0=op0, op1=op1, **kw),
                    reads, writes)

    def tt(self, eng, out, in0, in1, op, reads, writes):
        self.op(eng, lambda e: e.tensor_tensor(out=out, in0=in0, in1=in1, op=op), reads, writes)

    def stt(self, out, in0, scalar, in1, op0, op1, reads, writes):
        self.op("dve", lambda e: e.scalar_tensor_tensor(out=out, in0=in0, scalar=scalar, in1=in1, op0=op0, op1=op1),
                reads, writes)

    def scan(self, out, d0, d1, init, reads, writes):
        self.op("dve", lambda e: e.tensor_tensor_scan(out=out, data0=d0, data1=d1, initial=init, op0=ALU.mult, op1=ALU.add),
                reads, writes)

    def copy(self, eng, out, in_, reads, writes):
        self.op(eng, lambda e: e.tensor_copy(out=out, in_=in_), reads, writes)

    def memset(self, eng, ap, val, writes):
        self.op(eng, lambda e: e.memset(ap, val), (), writes)

    def coll(self, kind, groups, in_ap, out_ap, reads=(), writes=()):
        idx = len(self.csem)
        sem = self.es.enter_context(self.nc.semaphore("cc%d" % idx))
        self.csem.append(sem)
        sk = ("k", idx)
        self.sems[sk] = sem
        waits = self._deps("pool", reads, writes, True)

        def fn(e):
            return e.collective_compute(kind, ALU.bypass, replica_groups=groups, ins=[in_ap], outs=[out_ap])
        self.q["pool"].append((waits, fn, sk, None))
        self._mark(reads, writes, (sk, 1))
        self.n_ops += 1

    def finish(self, final_keys):
        self._barrier()
        self._emit_block()
        if self.ph is not None:
            self.ph.close()
            self.ph = None
        self.es.close()


def _rev(ap2d, n):
    apl = [list(s) for s in ap2d.ap]
    assert len(apl) == 2 and apl[1][1] == n and apl[1][0] == 1, apl
    from concourse.ap import AP
    return AP(ap2d.tensor, ap2d.offset + (n - 1), [apl[0], [-1, n]])


class PsumRing:
    def __init__(self, p, n=8, name="ps"):
        self.p = p
        self.t = [p.ps("%s%d" % (name, i), [128, 512], F32) for i in range(n)]
        for i in range(n):
            p.exclusive.add((name, i))
        self.i = 0
        self.n = n
        self.name = name

    def next(self):
        i = self.i
        self.i = (self.i + 1) % self.n
        return self.t[i], (self.name, i)


S_LEN = 8192
NSM = 11
GELU_C = 1.5957691216057308


def build_rnn(x_dtype=F32):
    nc = bass.Bass("TRN2", target_bir_lowering=False)
    xT = nc.dram_tensor("xT", [1024, S_LEN], x_dtype, kind="ExternalInput").ap()
    w_r = nc.dram_tensor("w_r", [1024, 512], F32, kind="ExternalInput").ap()
    w_g = nc.dram_tensor("w_g", [128, 8, 128], F32, kind="ExternalInput").ap()
    small = nc.dram_tensor("small", [128, 2, NSM], F32, kind="ExternalInput").ap()
    hgT = nc.dram_tensor("hgT", [256, S_LEN], BF16, kind="ExternalOutput").ap()
    p = Prog(nc)
    emit_rnn(p, xT, w_r, w_g, small, hgT)
    p.finish([("hg_out", b, c) for b in range(2) for c in range(4)])
    return nc


def emit_rnn(p, xT, w_r, w_g, small, hgT, pfx="r"):
    T = S_LEN
    CH = 2048
    NCH = T // CH
    wb = p.sb(pfx + "wb", [128, 8, 512], BF16)
    wg = p.sb(pfx + "wg", [128, 8, 128], BF16)
    sm = p.sb(pfx + "sm", [128, 2, NSM], F32)
    cl = p.sb(pfx + "cl", [128, 2, 2, 2], F32)
    zt = p.sb(pfx + "zt", [128, 4], F32)
    pt = p.sb(pfx + "pt", [128, 4], F32)
    xb = [p.sb(pfx + "xb%d" % i, [128, 8, 512], BF16) for i in range(2)]
    xr_full = p.sb(pfx + "xrf", [128, T + 4], F32)
    gy = p.sb(pfx + "gy", [128, T], BF16)
    xc = p.sb(pfx + "xc", [128, T], F32)
    xcb = p.sb(pfx + "xcb", [128, T], BF16)
    g1 = [p.sb(pfx + "g1_%d" % i, [128, 512], F32) for i in range(2)]
    g2 = [p.sb(pfx + "g2_%d" % i, [128, 512], F32) for i in range(2)]
    rt = p.sb(pfx + "rt", [128, CH], F32)
    it = p.sb(pfx + "it", [128, CH], F32)
    at = p.sb(pfx + "at", [128, CH], F32)
    hb = [p.sb(pfx + "hb%d" % i, [128, CH], F32) for i in range(2)]
    hgb = [p.sb(pfx + "hgb%d" % i, [128, CH], BF16) for i in range(2)]
    ring = PsumRing(p, 8, pfx + "ps")

    p.dma(wb[:], w_r.rearrange("(c p) n -> p c n", p=128), writes=["wb"], queue="pool")
    p.dma(wg[:], w_g, writes=["wg"], queue="pool")
    p.dma(sm[:], small, writes=["sm"])
    for b in range(2):
        for d in range(2):
            j = b * 2 + d
            p.act(zt[:, j:j + 1], sm[:, b, 5 + 3 * d + 2: 5 + 3 * d + 3], AF.Exp, ["sm"], ["zt"], scale=-1.0)
    p.ts("dve", pt[:], zt[:], -1.0 / 6, 1.0 / 5, ALU.mult, ALU.add, ["zt"], ["pt"])
    for cst in (-1.0 / 4, 1.0 / 3, -1.0 / 2, 1.0):
        p.tt("dve", pt[:], pt[:], zt[:], ALU.mult, ["pt", "zt"], ["pt"])
        p.ts("dve", pt[:], pt[:], cst, None, ALU.add, None, ["pt"], ["pt"])
    p.tt("dve", pt[:], pt[:], zt[:], ALU.mult, ["pt", "zt"], ["pt"])
    for b in range(2):
        for d in range(2):
            j = b * 2 + d
            p.ts("dve", cl[:, b, d, 0:1], pt[:, j:j + 1], -8.0, None, ALU.mult, None, ["pt"], ["cl"])
            p.ts("dve", cl[:, b, d, 1:2], pt[:, j:j + 1], -16.0, None, ALU.mult, None, ["pt"], ["cl"])
    chunked = (xT.shape[0] == 4096 + 128)
    if chunked:
        xTv = xT[128:128 + 4096, :].rearrange("(k r c p) n -> p r k c n", k=4, r=4, c=2, p=128)
    else:
        xTv = xT.rearrange("(c p) n -> p c n", p=128)

    def xrk(c):
        return [("xr", t) for t in range(4 * c, 4 * c + 4)]

    for blk in range(2):
        p.memset("pool", xr_full[:, 0:2], 0.0, [("xr", -1)])
        p.memset("pool", xr_full[:, T + 2:T + 4], 0.0, [("xr", 16)])
        for t in range(T // 512):
            xbt = xb[t % 2]
            xbk = ("xb", t % 2)
            if chunked:
                for k in range(4):
                    p.dma(xbt[:, 2 * k:2 * k + 2, :], xTv[:, t // 4, k, :, (t % 4) * 512:(t % 4 + 1) * 512],
                          reads=["xT_all"], writes=[xbk])
            else:
                p.dma(xbt[:], xTv[:, :, t * 512:(t + 1) * 512], writes=[xbk], queue="pool")
            for m in range(2):
                pst, psk = ring.next()
                col = (0 if m == 0 else 256) + blk * 128
                for k in range(8):
                    p.mm(pst[:], wb[:, k, col:col + 128], xbt[:, k, :], k == 0, k == 7, ["wb", xbk], [psk])
                if m == 0:
                    p.act(xr_full[:, 2 + t * 512: 2 + (t + 1) * 512], pst[:], AF.Copy, [psk], [("xr", t)])
                else:
                    a1, a2 = g1[t % 2], g2[t % 2]
                    k1, k2 = ("g1", t % 2), ("g2", t % 2)
                    p.act(a1[:], pst[:], AF.Square, [psk], [k1])
                    p.ts("dve", a1[:], a1[:], 0.044715, 1.0, ALU.mult, ALU.add, [k1], [k1])
                    p.tt("dve", a1[:], a1[:], pst[:], ALU.mult, [k1, psk], [k1])
                    p.act(a2[:], a1[:], AF.Sigmoid, [k1], [k2], scale=GELU_C)
                    p.tt("dve", gy[:, t * 512:(t + 1) * 512], a2[:], pst[:], ALU.mult, [k2, psk], [("gy", t)])
        for c in range(NCH):
            o = c * CH
            rk = [("xr", t) for t in range(4 * c - 1, 4 * c + 5)]
            p.ts("dve", xc[:, o:o + CH], xr_full[:, o:o + CH], sm[:, blk, 0:1], sm[:, blk, 4:5], ALU.mult, ALU.add,
                 rk + ["sm"], [("xc", c)])
            for tap in range(1, 4):
                p.stt(xc[:, o:o + CH], xr_full[:, o + tap:o + tap + CH], sm[:, blk, tap:tap + 1], xc[:, o:o + CH],
                      ALU.mult, ALU.add, rk + ["sm", ("xc", c)], [("xc", c)])
            p.copy("pool", xcb[:, o:o + CH], xc[:, o:o + CH], [("xc", c)], [("xcb", c)])
        hf = xr_full
        for d in range(2):
            order = list(range(NCH)) if d == 0 else list(range(NCH - 1, -1, -1))
            prev = None
            for ci, c in enumerate(order):
                o = c * CH
                for s in range(CH // 512):
                    for g in range(2):
                        pst, psk = ring.next()
                        p.mm(pst[:], wg[:, blk * 4 + d * 2 + g, :], xcb[:, o + s * 512: o + (s + 1) * 512], True, True,
                             ["wg", ("xcb", c)], [psk])
                        dst = rt if g == 0 else it
                        p.act(dst[:, s * 512:(s + 1) * 512], pst[:], AF.Sigmoid, [psk, "sm"],
                              [("rt" if g == 0 else "it", s)], bias=sm[:, blk, 5 + 3 * d + g: 5 + 3 * d + g + 1])
                rtk = [("rt", s) for s in range(4)]
                itk = [("it", s) for s in range(4)]
                p.act(at[:], rt[:], AF.Exp, rtk + ["cl"], ["at"], scale=cl[:, blk, d, 0:1])
                p.act(rt[:], rt[:], AF.Exp, rtk + ["cl"], rtk, scale=cl[:, blk, d, 1:2])
                p.act(rt[:], rt[:], AF.Sqrt, rtk, rtk, scale=-1.0, bias=1.0)
                p.tt("pool", it[:], it[:], xc[:, o:o + CH], ALU.mult, itk + [("xc", c)], itk)
                p.tt("pool", it[:], it[:], rt[:], ALU.mult, itk + rtk, itk)
                if d == 0:
                    init = 0.0 if ci == 0 else hf[:, 2 + o - 1: 2 + o]
                    p.scan(hf[:, 2 + o: 2 + o + CH], at[:], it[:], init, ["at"] + itk + [("xr", 4 * c - 1)], xrk(c))
                else:
                    hbt = hb[ci % 2]
                    init = 0.0 if ci == 0 else prev[:, 0:1]
                    p.scan(_rev(hbt[:], CH), _rev(at[:], CH), _rev(it[:], CH), init,
                           ["at"] + itk + [("hb", (ci + 1) % 2)], [("hb", ci % 2)])
                    prev = hbt
                    hg = hgb[ci % 2]
                    p.tt("pool", at[:], hbt[:], hf[:, 2 + o: 2 + o + CH], ALU.add, [("hb", ci % 2), "at"] + xrk(c), ["at"])
                    p.tt("dve", hg[:], at[:], gy[:, o:o + CH], ALU.mult, ["at"] + [("gy", t) for t in range(4 * c, 4 * c + 4)],
                         [("hgb", ci % 2)])
                    hdst = hgT[c, blk * 128:(blk + 1) * 128, :] if len(hgT.shape) == 3 else hgT[blk * 128:(blk + 1) * 128, o:o + CH]
                    p.dma(hdst, hg[:], reads=[("hgb", ci % 2)], writes=[("hg_out", blk, c)])


def pack_rnn_inputs(l, j, w_in, w_conv, b_conv, w_rec_gate, b_rec_gate, w_in_gate, b_in_gate, lru_lambda):
    XR0 = 1024 + 512
    YR0 = XR0 + 1024
    c0 = 2 * j * 128
    w_r = np.concatenate([w_in[l][:, XR0 + c0: XR0 + c0 + 256], w_in[l][:, YR0 + c0: YR0 + c0 + 256]], axis=1)
    w_g = np.empty((128, 8, 128), np.float32)
    small = np.empty((128, 2, NSM), np.float32)
    for b in range(2):
        cb = 2 * j + b
        sl = slice(cb * 128, (cb + 1) * 128)
        small[:, b, 0:4] = w_conv[l][:, sl].T
        small[:, b, 4] = b_conv[l][sl]
        for d in range(2):
            w_g[:, b * 4 + d * 2 + 0, :] = w_rec_gate[l, d, cb]
            w_g[:, b * 4 + d * 2 + 1, :] = w_in_gate[l, d, cb]
            small[:, b, 5 + 3 * d + 0] = b_rec_gate[l, d][sl]
            small[:, b, 5 + 3 * d + 1] = b_in_gate[l, d][sl]
            small[:, b, 5 + 3 * d + 2] = lru_lambda[l, d][sl]
    return {"w_r": np.ascontiguousarray(w_r), "w_g": w_g, "small": small}


TOK = 2048
HALO = 128
TH = TOK + 2 * HALO
ATT_SCALE = 128 ** -0.5


def build_attn(x_dtype=F32):
    nc = bass.Bass("TRN2", target_bir_lowering=False)
    d = {}
    d["xT"] = nc.dram_tensor("xT", [1024, TH], x_dtype, kind="ExternalInput").ap()
    d["w_qkv"] = nc.dram_tensor("w_qkv", [1024, 1536], F32, kind="ExternalInput").ap()
    d["cosT"] = nc.dram_tensor("cosT", [128, TH], F32, kind="ExternalInput").ap()
    d["sinT"] = nc.dram_tensor("sinT", [128, TH], F32, kind="ExternalInput").ap()
    d["masks"] = nc.dram_tensor("masks", [128, 3, 384], F32, kind="ExternalInput").ap()
    d["cmats"] = nc.dram_tensor("cmats", [128, 2, 128], F32, kind="ExternalInput").ap()
    d["sink"] = nc.dram_tensor("sink", [128, 8], F32, kind="ExternalInput").ap()
    d["oT"] = nc.dram_tensor("oT", [1024, TOK], BF16, kind="ExternalOutput").ap()
    p = Prog(nc)
    emit_attn(p, d)
    p.finish([("o_out", h, g) for h in range(8) for g in range(4)])
    return nc


def emit_attn(p, d, pfx="a"):
    xb = p.sb(pfx + "xb", [128, 8, TH], BF16)
    wq = [p.sb(pfx + "wq%d" % i, [128, 8, 128], BF16) for i in range(3)]
    wv = p.sb(pfx + "wv", [128, 8, 256], BF16)
    cosT = p.sb(pfx + "cos", [128, TH], F32)
    sinT = p.sb(pfx + "sin", [128, TH], F32)
    masks = p.sb(pfx + "masks", [128, 3, 384], BF16)
    cm0 = p.sb(pfx + "cm0", [128, 128], BF16)
    cm1 = p.sb(pfx + "cm1", [128, 128], BF16)
    sink = p.sb(pfx + "sink", [128, 8], F32)
    nsink = p.sb(pfx + "nsink", [128, 8], F32)
    qT = p.sb(pfx + "qT", [128, 8, TOK], BF16)
    kT = p.sb(pfx + "kT", [128, 2, TH], BF16)
    V = p.sb(pfx + "V", [128, TH // 128, 256], BF16)
    qraw = [p.sb(pfx + "qraw%d" % i, [128, 512], BF16) for i in range(2)]
    r1 = [p.sb(pfx + "r1_%d" % i, [128, 512], F32) for i in range(2)]
    r2 = [p.sb(pfx + "r2_%d" % i, [128, 512], F32) for i in range(2)]
    P = [p.sb(pfx + "P%d" % i, [128, 384], BF16) for i in range(2)]
    PT = [p.sb(pfx + "PT%d" % i, [128, 384], BF16) for i in range(2)]
    D = [p.sb(pfx + "D%d" % i, [128, 128], BF16) for i in range(2)]
    cols = [p.sb(pfx + "cols%d" % i, [128, 8], F32) for i in range(2)]
    ring = PsumRing(p, 6, pfx + "ps")
    oring = PsumRing(p, 2, pfx + "po")
    ident = cm0[:]
    pswap = cm1[:]

    if "xT_own" in d:
        own = d["xT_own"].rearrange("(c p) n -> p c n", p=128)
        allv = d["xT_all"][128:128 + 4096, :].rearrange("(k r c p) n -> p r k c n", k=4, r=4, c=2, p=128)
        nc_ = p.nc

        allr = d["xT_all"][128:128 + 4096, :].rearrange("(k r q) n -> r k q n", k=4, r=4, q=256)

        def halo(e, left):
            jn = p.core_idx(e, "jl" if left else "jr")
            if left:
                return e.dma_start(out=d["halo_l"].rearrange("(k q) n -> k q n", k=4), in_=allr[jn, :, :, TOK - HALO:TOK])
            return e.dma_start(out=d["halo_r"].rearrange("(k q) n -> k q n", k=4), in_=allr[jn, :, :, 0:HALO])
        agk = [("xT_all", k) for k in range(4)]
        p.dma(None, None, reads=agk, writes=["halo_l"], fn=lambda e: halo(e, True))
        p.dma(None, None, reads=agk, writes=["halo_r"], fn=lambda e: halo(e, False))
        p.dma(xb[:, :, 0:HALO], d["halo_l"].rearrange("(c p) n -> p c n", p=128), reads=["halo_l"],
              writes=[("xbh", 0, k) for k in range(4)])
        p.dma(xb[:, :, HALO + TOK:TH], d["halo_r"].rearrange("(c p) n -> p c n", p=128), reads=["halo_r"],
              writes=[("xbh", 1, k) for k in range(4)])
        for t0 in range(0, TOK, 512):
            keys = sorted(set([(HALO + t0) // 512, (HALO + t0 + 511) // 512]))
            p.dma(xb[:, :, HALO + t0:HALO + t0 + 512], own[:, :, t0:t0 + 512], reads=["xT_own"],
                  writes=[("xb", k) for k in keys])
    else:
        xTv = d["xT"].rearrange("(c p) n -> p c n", p=128)
        for t0 in range(0, TH, 512):
            w = min(512, TH - t0)
            p.dma(xb[:, :, t0:t0 + w], xTv[:, :, t0:t0 + w], writes=[("xb", t0 // 512)], queue="pool")

    def xbk(a, b):
        ks = [("xb", t) for t in range(a // 512, (b - 1) // 512 + 1)]
        if a < HALO:
            ks += [("xbh", 0, k) for k in range(4)]
        if b > HALO + TOK:
            ks += [("xbh", 1, k) for k in range(4)]
        return ks
    p.dma(wv[:], d["w_qkv"].rearrange("(c p) n -> p c n", p=128)[:, :, 1280:1536], writes=["wv"], queue="pool")
    p.dma(masks[:], d["masks"], writes=["masks"], queue="pool")
    p.dma(cm0[:], d["cmats"][:, 0, :], writes=["cm"], queue="pool")
    p.dma(cm1[:], d["cmats"][:, 1, :], writes=["cm"], queue="pool")
    p.dma(cosT[:], d["cosT"], writes=["cos"])
    p.dma(sinT[:], d["sinT"], writes=["sin"])
    p.dma(sink[:], d["sink"], writes=["sink"])
    p.ts("dve", nsink[:], sink[:], -1.0, None, ALU.mult, None, ["sink"], ["nsink"])
    wqv = d["w_qkv"].rearrange("(c p) n -> p c n", p=128)
    def emit_v():
        for tt in range(TH // 128):
            pst, psk = ring.next()
            for k in range(8):
                p.mm(pst[:, 0:256], xb[:, k, tt * 128:(tt + 1) * 128], wv[:, k, :], k == 0, k == 7, xbk(tt * 128, (tt + 1) * 128) + ["wv"], [psk])
            p.copy("dve" if tt % 2 else "act", V[:, tt, :], pst[:, 0:256], [psk], [("V", tt)]) if tt % 2 else \
                p.act(V[:, tt, :], pst[:, 0:256], AF.Copy, [psk], [("V", tt)])
    it = 0
    for m in list(range(8)) + ['v', 8, 9]:
        if m == 'v':
            emit_v()
            continue
        wt = wq[m % 3]
        wk = ("wq", m % 3)
        p.dma(wt[:], wqv[:, :, m * 128:(m + 1) * 128], writes=[wk], queue="pool")
        isq = m < 8
        ntok = TOK if isq else TH
        off = HALO if isq else 0
        t0 = 0
        while t0 < ntok:
            w = min(512, ntok - t0)
            pst, psk = ring.next()
            for k in range(8):
                p.mm(pst[:, 0:w], wt[:, k, :], xb[:, k, off + t0: off + t0 + w], k == 0, k == 7, xbk(off + t0, off + t0 + w) + [wk], [psk])
            qr = qraw[it % 2]
            qk = ("qraw", it % 2)
            p.act(qr[:, 0:w], pst[:, 0:w], AF.Copy, [psk], [qk])
            ps2, ps2k = ring.next()
            p.mm(ps2[:, 0:w], pswap, qr[:, 0:w], True, True, ["cm", qk], [ps2k])
            a1, a2 = r1[it % 2], r2[it % 2]
            k1, k2 = ("r1", it % 2), ("r2", it % 2)
            p.tt("dve", a1[:, 0:w], cosT[:, off + t0: off + t0 + w], pst[:, 0:w], ALU.mult, [psk, "cos"], [k1])
            p.tt("dve", a2[:, 0:w], sinT[:, off + t0: off + t0 + w], ps2[:, 0:w], ALU.mult, [ps2k, "sin"], [k2])
            if isq:
                dst = qT[:, m, t0:t0 + w]
                dk = [("qT", m, t0 // 512)]
            else:
                dst = kT[:, m - 8, t0:t0 + w]
                dk = [("kT", m - 8, t0 // 512)]
            p.tt("dve", dst, a1[:, 0:w], a2[:, 0:w], ALU.add, [k1, k2], dk)
            it += 1
            t0 += w
    its = [(grp, h, qi) for grp in range(4) for h in range(8) for qi in range(4)]
    NB = 4
    colsN = cols + [p.sb(pfx + "colsx%d" % i, [128, 8], F32) for i in range(NB - 2)]
    PN = P + [p.sb(pfx + "Px%d" % i, [128, 384], BF16) for i in range(NB - 2)]
    state = {}

    def stage_a(n):
        grp, h, qi = its[n]
        g = h // 4
        qb = grp * 4 + qi
        if qi == 0:
            state[(grp, h)] = oring.next()
        mi = 0 if qb == 0 else (2 if qb == 15 else 1)
        pss, pssk = ring.next()
        kkeys = [("kT", g, t) for t in sorted(set([(qb * 128) // 512, (qb * 128 + 383) // 512]))]
        p.mm(pss[:, 0:384], qT[:, h, qb * 128:(qb + 1) * 128], kT[:, g, qb * 128: qb * 128 + 384], True, False,
             [("qT", h, grp)] + kkeys, [pssk])
        p.mm(pss[:, 0:384], ident, masks[:, mi, :], False, True, ["cm", "masks"], [pssk])
        cl = colsN[n % NB]
        ck = ("cols", n % NB)
        p.op("dve", lambda e, cl=cl, pss=pss: e.reduce_max(out=cl[:, 0:1], in_=pss[:, 0:384], axis=AX.X), [pssk], [ck])
        p.ts("dve", cl[:, 1:2], cl[:, 0:1], -ATT_SCALE, nsink[:, h:h + 1], ALU.mult, ALU.min, [ck, "nsink"], [ck])
        Pt = PN[n % NB]
        pk = ("P", n % NB)
        p.act(Pt[:], pss[:, 0:384], AF.Exp, [pssk, ck], [pk, ck], scale=ATT_SCALE, bias=cl[:, 1:2], accum_out=cl[:, 2:3])
        p.act(cl[:, 3:4], cl[:, 1:2], AF.Exp, [ck, "sink"], [ck], bias=sink[:, h:h + 1])

    def stage_b(n):
        grp, h, qi = its[n]
        g = h // 4
        qb = grp * 4 + qi
        po, pok = state[(grp, h)]
        cl = colsN[n % NB]
        ck = ("cols", n % NB)
        Pt = PN[n % NB]
        pk = ("P", n % NB)
        p.tt("dve", cl[:, 4:5], cl[:, 2:3], cl[:, 3:4], ALU.add, [ck], [ck])
        p.op("dve", lambda e, cl=cl: e.reciprocal(out=cl[:, 5:6], in_=cl[:, 4:5]), [ck], [ck])
        Dt = D[n % 2]
        dk = ("D", n % 2)
        p.ts("dve", Dt[:], ident, cl[:, 5:6], None, ALU.mult, None, ["cm", ck], [dk])
        ppt, pptk = ring.next()
        for kb in range(3):
            p.mm(ppt[:, kb * 128:(kb + 1) * 128], Pt[:, kb * 128:(kb + 1) * 128], Dt[:], True, True, [pk, dk], [pptk])
        PTt = PT[n % 2]
        ptk = ("PT", n % 2)
        p.act(PTt[:], ppt[:, 0:384], AF.Copy, [pptk], [ptk])
        for kb in range(3):
            p.mm(po[:, qi * 128:(qi + 1) * 128], V[:, qb + kb, g * 128:(g + 1) * 128], PTt[:, kb * 128:(kb + 1) * 128],
                 kb == 0, kb == 2, [ptk, ("V", qb + kb)], [pok])
        if qi == 3:
            p.copy("dve", qT[:, h, grp * 512:(grp + 1) * 512], po[:], [pok], [("qT", h, grp)])
            p.dma(d["oT"][h * 128:(h + 1) * 128, grp * 512:(grp + 1) * 512], qT[:, h, grp * 512:(grp + 1) * 512],
                  reads=[("qT", h, grp)], writes=[("o_out", h, grp)])

    SKEW = 2
    for n in range(len(its) + SKEW):
        if n < len(its):
            stage_a(n)
        if n - SKEW >= 0:
            stage_b(n - SKEW)


def attn_consts(j):
    inv = (10000.0 ** (-np.arange(0, 128, 2, dtype=np.float32) / 128)).astype(np.float32)
    pos = (j * TOK - HALO + np.arange(TH)).astype(np.float32)
    ang = pos[:, None] * inv[None, :]
    cos = np.cos(ang).astype(np.float32).T
    sin = np.sin(ang).astype(np.float32).T
    cosT = np.concatenate([cos, cos], axis=0)
    sinT = np.concatenate([-sin, sin], axis=0)
    qi = np.arange(128)[:, None]
    kj = np.arange(384)[None, :]
    rel = kj - 128 - qi
    base = np.where(np.abs(rel) <= 128, 0.0, -30000.0).astype(np.float32)
    first = base.copy(); first[:, 0:128] = -30000.0
    last = base.copy(); last[:, 256:384] = -30000.0
    masks = np.stack([first if j == 0 else base, base, last if j == 3 else base], axis=1)
    ident = np.eye(128, dtype=np.float32)
    swap = np.zeros((128, 128), np.float32)
    mm = np.arange(128)
    swap[(mm + 64) % 128, mm] = 1.0
    cmats = np.stack([ident, swap], axis=1)
    return {"cosT": np.ascontiguousarray(cosT), "sinT": np.ascontiguousarray(sinT),
            "masks": np.ascontiguousarray(masks), "cmats": np.ascontiguousarray(cmats)}


def halo_xT(x_b, j):
    out = np.zeros((1024, TH), x_b.dtype)
    lo, hi = j * TOK - HALO, (j + 1) * TOK + HALO
    slo, shi = max(lo, 0), min(hi, S_LEN)
    out[:, slo - lo: shi - lo] = x_b[slo:shi].T
    return out


ALPHA = 8 ** 0.25
LN_EPS = 1e-5
N_EXP = 32
CAP = 256
NSLOT = N_EXP * CAP
NROWS = NSLOT + 128


def build_mix(x_dtype=F32):
    nc = bass.Bass("TRN2", target_bir_lowering=False)
    d = {}
    d["xT"] = nc.dram_tensor("xT", [1024, TOK], x_dtype, kind="ExternalInput").ap()
    d["x_tok"] = nc.dram_tensor("x_tok", [TOK, 1024], F32, kind="ExternalInput").ap()
    d["oT"] = nc.dram_tensor("oT", [1024, TOK], BF16, kind="ExternalInput").ap()
    d["hgT"] = nc.dram_tensor("hgT", [1024, TOK], BF16, kind="ExternalInput").ap()
    d["w4"] = nc.dram_tensor("w4", [4, 1024, 1024], F32, kind="ExternalInput").ap()
    d["w_out"] = nc.dram_tensor("w_out", [1024, 1024], F32, kind="ExternalInput").ap()
    d["ln"] = nc.dram_tensor("ln", [128, 2, 1024], F32, kind="ExternalInput").ap()
    d["w_rt"] = nc.dram_tensor("w_rt", [1024, 36], F32, kind="ExternalInput").ap()
    d["b_rt"] = nc.dram_tensor("b_rt", [128, 36], F32, kind="ExternalInput").ap()
    d["cst"] = nc.dram_tensor("cst", [128, 3, 128], F32, kind="ExternalInput").ap()
    d["cst2"] = nc.dram_tensor("cst2", [128, 40], F32, kind="ExternalInput").ap()
    d["x1"] = nc.dram_tensor("x1", [TOK, 1024], F32, kind="ExternalOutput").ap()
    d["xdisp"] = nc.dram_tensor("xdisp", [NROWS, 1024], BF16, kind="ExternalOutput").ap()
    d["slots"] = nc.dram_tensor("slots", [TOK, 2], I32, kind="ExternalOutput").ap()
    d["gates"] = nc.dram_tensor("gates", [TOK, 2], F32, kind="ExternalOutput").ap()
    p = Prog(nc)
    fk = emit_mix(p, d)
    p.finish(fk)
    return nc


def emit_ln(p, y, out, lnt, which, stat, reads, writes, tag):
    st6, mv, rs = stat
    sk = ("lnstat", tag)
    for hh in range(2):
        p.op("dve", lambda e, hh=hh: e.bn_stats(out=st6[:, hh * 6:(hh + 1) * 6], in_=y[:, hh * 512:(hh + 1) * 512]),
             reads + [sk], [sk])
    p.op("dve", lambda e: e.bn_aggr(out=mv[:, 0:2], in_=st6[:, 0:12]), [sk], [sk])
    p.act(rs[:, 0:1], mv[:, 1:2], AF.Sqrt, [sk], [sk], bias=LN_EPS)
    p.op("dve", lambda e: e.reciprocal(out=rs[:, 1:2], in_=rs[:, 0:1]), [sk], [sk])
    p.ts("dve", out, y, mv[:, 0:1], rs[:, 1:2], ALU.subtract, ALU.mult, reads + [sk], writes)
    p.tt("pool", out, out, lnt[:, which, 0, :], ALU.mult, writes + ["ln"], writes)
    p.tt("pool", out, out, lnt[:, which, 1, :], ALU.add, writes + ["ln"], writes)


def emit_mix(p, d, pfx="m"):
    ot = [p.sb(pfx + "ot%d" % i, [128, 8, 512], BF16) for i in range(1)] * 2
    hg = [p.sb(pfx + "hg%d" % i, [128, 8, 512], BF16) for i in range(1)] * 2
    xb = [p.sb(pfx + "xb%d" % i, [128, 8, 512], BF16) for i in range(1)] * 2
    w4r = p.sb(pfx + "w4r", [128, 4, 8, 1024], BF16)
    wo = p.sb(pfx + "wo", [128, 8, 1024], BF16)
    mg = [p.sb(pfx + "mg%d" % i, [128, 8, 512], BF16) for i in range(2)]
    t1 = [p.sb(pfx + "t1_%d" % i, [128, 512], F32) for i in range(2)]
    t2 = [p.sb(pfx + "t2_%d" % i, [128, 512], F32) for i in range(2)]
    lnt = p.sb(pfx + "ln", [128, 1, 2, 1024], F32)
    wrt = p.sb(pfx + "wrt", [128, 8, 36], F32)
    brt = p.sb(pfx + "brt", [128, 36], F32)
    cst = p.sb(pfx + "cst", [128, 3, 128], F32)
    cstb = p.sb(pfx + "cstb", [128, 2, 128], BF16)
    cst2 = p.sb(pfx + "cst2", [128, 40], F32)
    zero = p.sb(pfx + "zero", [128, 1024], BF16)
    xt = [p.sb(pfx + "xt%d" % i, [128, 1024], F32) for i in range(2)]
    y = [p.sb(pfx + "y%d" % i, [128, 1024], F32) for i in range(2)]
    x1 = [p.sb(pfx + "x1_%d" % i, [128, 1024], F32) for i in range(2)]
    x1b = [p.sb(pfx + "x1b%d" % i, [128, 1024], BF16) for i in range(2)]
    x1T = [p.sb(pfx + "x1T%d" % i, [128, 8, 128], F32) for i in range(2)]
    st6 = p.sb(pfx + "st6", [128, 12], F32)
    mv = p.sb(pfx + "mv", [128, 2], F32)
    rs = p.sb(pfx + "rs", [128, 2], F32)
    rt = [p.sb(pfx + "rt%d" % i, [128, 64], F32) for i in range(2)]
    E = [p.sb(pfx + "E%d" % i, [128, 3, 32], F32) for i in range(2)]
    Eb = [p.sb(pfx + "Eb%d" % i, [128, 32], BF16) for i in range(2)]
    base = p.sb(pfx + "base", [128, 32], F32)
    sl = [p.sb(pfx + "sl%d" % i, [128, 2], I32) for i in range(2)]
    gt = [p.sb(pfx + "gt%d" % i, [128, 2], F32) for i in range(2)]
    i8 = [p.sb(pfx + "i8_%d" % i, [128, 8], U32) for i in range(2)]
    ring = PsumRing(p, 8, pfx + "ps")
    ident = cst[:, 0, :]
    iota32 = cst2[:, 0:32]
    iota4 = cst2[:, 32:36]
    trash = cst2[:, 36:37]

    p.dma(lnt[:, 0, :, :], d["ln"], writes=["ln"])
    p.dma(wrt[:], d["w_rt"].rearrange("(c p) n -> p c n", p=128), writes=["wrt"])
    p.dma(brt[:], d["b_rt"], writes=["brt"])
    p.dma(cst[:], d["cst"], writes=["cst"])
    p.dma(cst2[:], d["cst2"], writes=["cst2"])
    p.copy("dve", cstb[:, 0, :], cst[:, 1, :], ["cst"], ["cstb"])
    p.copy("dve", cstb[:, 1, :], cst[:, 2, :], ["cst"], ["cstb"])
    p.memset("pool", zero[:], 0.0, ["zero"])
    p.memset("pool", base[:], 0.0, ["base"])
    zk = []
    for r0 in range(0, NROWS, 1024):
        nr = min(1024, NROWS - r0)
        p.dma(d["xdisp"][r0:r0 + nr, :].rearrange("(a p) n -> p a n", p=128),
              zero[:].partition_broadcast(128) if False else zero[:, None, :].to_broadcast([128, nr // 128, 1024]),
              reads=["zero"], writes=[("xdz", r0)])
        zk.append(("xdz", r0))
    for q in range(4):
        for c0 in range(0, 1024, 512):
            p.dma(w4r[:, q, :, c0:c0 + 512], d["w4"][q].rearrange("(c p) n -> p c n", p=128)[:, :, c0:c0 + 512],
                  writes=[("w4r", q, c0)], queue="pool")
    w4k = [[("w4r", q, 0), ("w4r", q, 512)] for q in range(4)]
    for c0 in range(0, 1024, 512):
        p.dma(wo[:, :, c0:c0 + 512], d["w_out"].rearrange("(c p) n -> p c n", p=128)[:, :, c0:c0 + 512], writes=[("wo", c0)], queue="pool")
    oTv = d["oT"].rearrange("(c p) n -> p c n", p=128)
    hgv = d["hgT"].rearrange("(c p) n -> p c n", p=128) if "hgT" in d else None
    xTv = (d["xT_own"] if "xT_own" in d else d["xT"]).rearrange("(c p) n -> p c n", p=128)
    fin = []
    wn = 0
    tile_i = 0
    for T in range(TOK // 512):
        b = T % 2
        p.dma(ot[b][:], oTv[:, :, T * 512:(T + 1) * 512], writes=[("ot", 0)])
        if "hg_all" in d:
            if T == 0:
                hat = d["hg_all"][128:128 + 4096, :].rearrange("(t q) n -> t q n", t=4)
                p.dma(None, None, reads=["hg_all"], writes=["hg_mine"], queue="act",
                      fn=lambda e: e.dma_start(out=d["hg_mine"], in_=hat[p.core_idx(e, "j")]))
            p.dma(hg[b][:], d["hg_mine"].rearrange("(c p) n -> p c n", p=128)[:, :, T * 512:(T + 1) * 512],
                  reads=["hg_mine"], writes=[("hg", 0)])
            p.dma(xb[b][:], xTv[:, :, T * 512:(T + 1) * 512], reads=["xT_own"], writes=[("xb", 0)])
        else:
            p.dma(hg[b][:], hgv[:, :, T * 512:(T + 1) * 512], writes=[("hg", 0)])
            p.dma(xb[b][:], xTv[:, :, T * 512:(T + 1) * 512], writes=[("xb", 0)], queue="pool")
        for m in range(8):
            banks = [ring.next() for _ in range(4)]
            srcs = [ot[b], hg[b], xb[b], xb[b]]
            skeys = [("ot", 0), ("hg", 0), ("xb", 0), ("xb", 0)]
            for q in range(4):
                pst, psk = banks[q]
                for k in range(8):
                    p.mm(pst[:], w4r[:, q, k, m * 128:(m + 1) * 128], srcs[q][:, k, :], k == 0, k == 7,
                         w4k[q] + [skeys[q]], [psk])
            a1, a2 = t1[m % 2], t2[m % 2]
            k1, k2 = ("t1", m % 2), ("t2", m % 2)
            p.act(a1[:], banks[2][0][:], AF.Sigmoid, [banks[2][1]], [k1])
            p.act(a2[:], banks[3][0][:], AF.Sigmoid, [banks[3][1]], [k2])
            p.tt("dve", a1[:], a1[:], banks[0][0][:], ALU.mult, [k1, banks[0][1]], [k1])
            p.tt("dve", a2[:], a2[:], banks[1][0][:], ALU.mult, [k2, banks[1][1]], [k2])
            p.tt("pool", mg[b][:, m, :], a1[:], a2[:], ALU.add, [k1, k2], [("mg", b, m)])
        mgk = [("mg", b, m) for m in range(8)]
        def stage_a(s, tile_i):
            tb = tile_i % 2
            tok0 = T * 512 + s * 128
            x1k = ("x1", tb)
            p.dma(xt[tb][:], d["x_tok"][tok0:tok0 + 128, :], writes=[("xt", tb)])
            for hh in range(2):
                pst, psk = ring.next()
                for k in range(8):
                    p.mm(pst[:], mg[b][:, k, s * 128:(s + 1) * 128], wo[:, k, hh * 512:(hh + 1) * 512], k == 0, k == 7,
                         mgk + [("wo", hh * 512)], [psk])
                p.stt(y[tb][:, hh * 512:(hh + 1) * 512], xt[tb][:, hh * 512:(hh + 1) * 512], ALPHA, pst[:], ALU.mult, ALU.add,
                      [("xt", tb), psk], [("y", tb, hh)])
            x1k = ("x1", tb)
            emit_ln(p, y[tb][:], x1[tb][:], lnt, 0, (st6, mv, rs), [("y", tb, 0), ("y", tb, 1)], [x1k], "a")
            p.dma(d["x1"][tok0:tok0 + 128, :], x1[tb][:], reads=[x1k], writes=[("x1o", tile_i)])
            fin.append(("x1o", tile_i))
            p.act(x1b[tb][:], x1[tb][:], AF.Copy, [x1k], [("x1b", tb)])

        def stage_b(s, tile_i):
            tb = tile_i % 2
            tok0 = T * 512 + s * 128
            x1k = ("x1", tb)
            for hh in range(2):
                pst, psk = ring.next()
                for c in range(4):
                    k = hh * 4 + c
                    p.tr(pst[:, c * 128:(c + 1) * 128], x1[tb][:, k * 128:(k + 1) * 128], ident, [x1k, "cst"], [psk])
                p.copy("dve", x1T[tb][:, hh * 4:(hh + 1) * 4, :], pst[:].rearrange("p (c n) -> p c n", c=4), [psk],
                       [("x1T", tb, hh)])
            pl, plk = ring.next()
            for k in range(8):
                p.mm(pl[:, 0:36], x1T[tb][:, k, :], wrt[:, k, :], k == 0, k == 7, [("x1T", tb, 0), ("x1T", tb, 1), "wrt"], [plk])
            r = rt[tb]
            rk = ("rt", tb)
            p.tt("dve", r[:, 0:36], pl[:, 0:36], brt[:], ALU.add, [plk, "brt", rk], [rk])
            p.op("dve", lambda e, r=r: e.reduce_max(out=r[:, 36:37], in_=r[:, 0:4], axis=AX.X), [rk], [rk])
            p.ts("dve", r[:, 37:38], r[:, 36:37], -1.0, None, ALU.mult, None, [rk], [rk])
            p.act(r[:, 44:48], r[:, 0:4], AF.Exp, [rk], [rk], bias=r[:, 37:38], accum_out=r[:, 38:39])
            p.op("dve", lambda e, r=r: e.reciprocal(out=r[:, 39:40], in_=r[:, 38:39]), [rk], [rk])
            p.ts("dve", r[:, 40:44], r[:, 0:4], r[:, 36:37], None, ALU.is_equal, None, [rk], [rk])
            p.ts("dve", r[:, 48:56], r[:, 4:12], r[:, 40:41], None, ALU.mult, None, [rk], [rk])
            for g in range(1, 4):
                p.stt(r[:, 48:56], r[:, 4 + 8 * g:12 + 8 * g], r[:, 40 + g:41 + g], r[:, 48:56], ALU.mult, ALU.add, [rk], [rk])
            Et = E[tb]
            ek = ("E", tb)
            p.tt("dve", Et[:, 2, 0:4], r[:, 40:44], iota4, ALU.mult, [rk, "cst2", ek], [ek])
            p.op("dve", lambda e, r=r, Et=Et: e.reduce_sum(out=r[:, 58:59], in_=Et[:, 2, 0:4], axis=AX.X), [rk, ek], [rk])
            m8 = r[:, 48:56]
            i8t = i8[tb]
            p.op("dve", lambda e, r=r, Et=Et: e.max(out=Et[:, 2, 8:16], in_=r[:, 48:56]), [rk, ek], [ek])
            p.op("dve", lambda e, r=r, Et=Et, i8t=i8t: e.max_index(out=i8t[:], in_max=Et[:, 2, 8:16], in_values=r[:, 48:56]),
                 [rk, ek], [("i8", tb)])
            g = gt[tb]
            gk = ("gt", tb)
            p.tt("dve", r[:, 56:57], Et[:, 2, 8:9], Et[:, 2, 9:10], ALU.subtract, [ek, rk], [rk])
            p.act(r[:, 57:58], r[:, 56:57], AF.Sigmoid, [rk], [rk])
            p.tt("dve", g[:, 0:1], r[:, 57:58], r[:, 39:40], ALU.mult, [rk, gk], [gk])
            p.tt("dve", g[:, 1:2], r[:, 39:40], g[:, 0:1], ALU.subtract, [rk, gk], [gk])
            p.copy("dve", r[:, 59:61], i8t[:, 0:2], [("i8", tb), rk], [rk])
            p.stt(r[:, 59:61], r[:, 58:59].to_broadcast([128, 2]), 8.0, r[:, 59:61], ALU.mult, ALU.add, [rk], [rk])
            for kk in range(2):
                p.ts("dve", Et[:, kk, :], iota32, r[:, 59 + kk:60 + kk], None, ALU.is_equal, None, ["cst2", rk, ek], [ek])
            p.tt("dve", Eb[tb][:], Et[:, 0, :], Et[:, 1, :], ALU.add, [ek], [("Eb", tb)])
            pc, pck = ring.next()
            p.mm(pc[:, 0:32], cstb[:, 0, :], Eb[tb][:], True, True, ["cstb", ("Eb", tb)], [pck])
            p.mm(pc[:, 32:64], cstb[:, 1, :], Eb[tb][:], True, True, ["cstb", ("Eb", tb)], [pck])
            p.tt("dve", Et[:, 2, :], pc[:, 0:32], base[:], ALU.add, [pck, "base", ek], [ek])
            for kk in range(2):
                p.tt("dve", Et[:, kk, :], Et[:, kk, :], Et[:, 2, :], ALU.mult, [ek], [ek])
                p.op("dve", lambda e, r=r, Et=Et, kk=kk: e.reduce_sum(out=r[:, 61 + kk:62 + kk], in_=Et[:, kk, :], axis=AX.X),
                     [ek, rk], [rk])
            p.tt("dve", base[:], base[:], pc[:, 32:64], ALU.add, [pck, "base"], ["base"])
            for kk in range(2):
                p.ts("dve", r[:, 63:64], r[:, 61 + kk:62 + kk], float(CAP), None, ALU.is_lt, None, [rk], [rk])
                p.stt(r[:, 61 + kk:62 + kk], r[:, 59 + kk:60 + kk], float(CAP), r[:, 61 + kk:62 + kk], ALU.mult, ALU.add,
                      [rk], [rk])
                p.tt("dve", r[:, 61 + kk:62 + kk], r[:, 61 + kk:62 + kk], trash, ALU.subtract, [rk, "cst2"], [rk])
                p.tt("dve", r[:, 61 + kk:62 + kk], r[:, 61 + kk:62 + kk], r[:, 63:64], ALU.mult, [rk], [rk])
                p.tt("dve", r[:, 61 + kk:62 + kk], r[:, 61 + kk:62 + kk], trash, ALU.add, [rk, "cst2"], [rk])
                p.tt("dve", g[:, kk:kk + 1], g[:, kk:kk + 1], r[:, 63:64], ALU.mult, [rk, gk], [gk])
            slt = sl[tb]
            slk = ("sl", tb)
            p.copy("dve", slt[:], r[:, 61:63], [rk], [slk])
            p.dma(d["slots"][tok0:tok0 + 128, :], slt[:], reads=[slk], writes=[("slo", tile_i)])
            p.dma(d["gates"][tok0:tok0 + 128, :], g[:], reads=[gk], writes=[("gto", tile_i)])
            fin.extend([("slo", tile_i), ("gto", tile_i)])
            for kk in range(2):
                p.dma(None, None, reads=[("x1b", tb), slk] + zk, writes=[("xdo", tile_i, kk)], queue="pool",
                      fn=lambda e, tb=tb, kk=kk, slt=slt: e.indirect_dma_start(
                          out=d["xdisp"], out_offset=bass.IndirectOffsetOnAxis(ap=slt[:, kk:kk + 1], axis=0),
                          in_=x1b[tb][:, :], in_offset=None))
                fin.append(("xdo", tile_i, kk))

        stage_a(0, T * 4)
        for s in range(4):
            if s + 1 < 4:
                stage_a(s + 1, T * 4 + s + 1)
            stage_b(s, T * 4 + s)
    return fin


def mix_consts():
    ident = np.eye(128, dtype=np.float32)
    tp = np.arange(128)[:, None]
    t = np.arange(128)[None, :]
    lower = (tp < t).astype(np.float32)
    ones = np.ones((128, 128), np.float32)
    cst = np.stack([ident, lower, ones], axis=1)
    cst2 = np.zeros((128, 40), np.float32)
    cst2[:, 0:32] = np.arange(32, dtype=np.float32)[None, :]
    cst2[:, 32:36] = np.arange(4, dtype=np.float32)[None, :]
    cst2[:, 36] = NSLOT + np.arange(128)
    return {"cst": np.ascontiguousarray(cst), "cst2": cst2}


def build_moe():
    nc = bass.Bass("TRN2", target_bir_lowering=False)
    d = {}
    d["xdisp"] = nc.dram_tensor("xdisp", [NROWS, 1024], BF16, kind="ExternalInput").ap()
    d["x1"] = nc.dram_tensor("x1", [TOK, 1024], F32, kind="ExternalInput").ap()
    d["slots"] = nc.dram_tensor("slots", [TOK, 2], I32, kind="ExternalInput").ap()
    d["gates"] = nc.dram_tensor("gates", [TOK, 2], F32, kind="ExternalInput").ap()
    d["w_g"] = nc.dram_tensor("w_g", [N_EXP, 1024, 512], F32, kind="ExternalInput").ap()
    d["w_u"] = nc.dram_tensor("w_u", [N_EXP, 1024, 512], F32, kind="ExternalInput").ap()
    d["w_d"] = nc.dram_tensor("w_d", [N_EXP, 512, 1024], F32, kind="ExternalInput").ap()
    d["ln"] = nc.dram_tensor("ln", [128, 2, 1024], F32, kind="ExternalInput").ap()
    d["ident"] = nc.dram_tensor("ident", [128, 128], F32, kind="ExternalInput").ap()
    d["ydisp"] = nc.dram_tensor("ydisp", [NROWS, 1024], F32).ap()
    d["x2"] = nc.dram_tensor("x2", [TOK, 1024], F32, kind="ExternalOutput").ap()
    p = Prog(nc)
    fk = emit_moe(p, d)
    p.finish(fk)
    return nc


def emit_moe(p, d, pfx="e"):
    wg = [p.sb(pfx + "wg%d" % i, [128, 8, 512], BF16) for i in range(2)]
    wu = [p.sb(pfx + "wu%d" % i, [128, 8, 512], BF16) for i in range(2)]
    wd = [p.sb(pfx + "wd%d" % i, [128, 4, 1024], BF16) for i in range(2)]
    xe = [p.sb(pfx + "xe%d" % i, [128, 1024], BF16) for i in range(2)]
    xeT = [p.sb(pfx + "xeT%d" % i, [128, 8, CAP], BF16) for i in range(2)]
    hT = [p.sb(pfx + "hT%d" % i, [128, 4, CAP], BF16) for i in range(2)]
    sg = [p.sb(pfx + "sg%d" % i, [128, CAP], F32) for i in range(2)]
    yt = [p.sb(pfx + "yt%d" % i, [128, 1024], F32) for i in range(2)]
    ident = p.sb(pfx + "ident", [128, 128], BF16)
    lnt = p.sb(pfx + "ln", [128, 1, 2, 1024], F32)
    zero = p.sb(pfx + "zero", [128, 1024], F32)
    sl = [p.sb(pfx + "sl%d" % i, [128, 2], I32) for i in range(2)]
    gt = [p.sb(pfx + "gt%d" % i, [128, 2], F32) for i in range(2)]
    x1t = [p.sb(pfx + "x1t%d" % i, [128, 1024], F32) for i in range(2)]
    ya = [p.sb(pfx + "ya%d" % i, [128, 1024], F32) for i in range(2)]
    yb = [p.sb(pfx + "yb%d" % i, [128, 1024], F32) for i in range(2)]
    yo = [p.sb(pfx + "yo%d" % i, [128, 1024], F32) for i in range(2)]
    st6 = p.sb(pfx + "st6", [128, 12], F32)
    mv = p.sb(pfx + "mv", [128, 2], F32)
    rs = p.sb(pfx + "rs", [128, 2], F32)
    if "xT_next" in d:
        xTn = [p.sb(pfx + "xTn%d" % i, [128, 8, 512], BF16) for i in range(2)]
        identf = p.sb(pfx + "identf", [128, 128], F32)
        p.dma(identf[:], d["ident"], writes=["identf"])
    ring = PsumRing(p, 6, pfx + "ps")
    ptr = [p.ps(pfx + "ptr%d" % i, [128, 1024], BF16) for i in range(2)]
    for i in range(2):
        p.exclusive.add((pfx + "ptr", i))

    p.dma(ident[:], d["ident"], writes=["ident"], queue="pool")
    p.dma(lnt[:, 0, :, :], d["ln"], writes=["ln"])
    p.memset("pool", zero[:], 0.0, ["zero"])
    p.dma(d["ydisp"][NSLOT:NROWS, :], zero[:], reads=["zero"], writes=[("yd", -1, 0)])
    ydk = [("yd", -1, 0)]
    nb = 0
    nstg = 0
    stg = [p.sb(pfx + "stg%d" % i, [128, 4096], F32) for i in range(3)]
    for e in range(N_EXP):
        b = e % 2
        wk = ("w", b)
        for wi, (wsrc, wdst, wkey) in enumerate(((d["w_g"][e], wg[b], ("wg", b)), (d["w_u"][e], wu[b], ("wu", b)),
                                                 (d["w_d"][e], wd[b], ("wd", b)))):
            if wi < 2:
                p.dma(wdst[:], wsrc.rearrange("(c p) n -> p c n", p=128), writes=[wkey], queue="pool")
                continue
            sgi = nstg % 3
            nstg += 1
            nchunk = 4 if wi == 2 else 8
            p.dma(stg[sgi][:].rearrange("p (c n) -> p c n", c=nchunk), wsrc.rearrange("(c p) n -> p c n", p=128),
                  writes=[("stg", sgi)])
            dflat = wdst[:].rearrange("p c n -> p (c n)")
            if wi == 1:
                p.copy("pool", dflat, stg[sgi][:], [("stg", sgi)], [wkey])
            else:
                p.act(dflat, stg[sgi][:], AF.Copy, [("stg", sgi)], [wkey])
        for blk in range(CAP // 128):
            xb_ = xe[nb % 2]
            xk = ("xe", nb % 2)
            r0 = e * CAP + blk * 128
            p.dma(xb_[:], d["xdisp"][r0:r0 + 128, :], writes=[xk])
            pt = ptr[nb % 2]
            ptk = (pfx + "ptr", nb % 2)
            for k in range(8):
                p.tr(pt[:, k * 128:(k + 1) * 128], xb_[:, k * 128:(k + 1) * 128], ident[:], [xk, "ident"], [ptk])
            p.copy("dve" if blk else "act", xeT[b][:, :, blk * 128:(blk + 1) * 128], pt[:].rearrange("p (c n) -> p c n", c=8),
                   [ptk], [("xeT", b, blk)]) if blk else \
                p.act(xeT[b][:, :, blk * 128:(blk + 1) * 128], pt[:].rearrange("p (c n) -> p c n", c=8), AF.Copy,
                      [ptk], [("xeT", b, blk)])
            nb += 1
        xtk = [("xeT", b, blk) for blk in range(CAP // 128)]
        for m in range(4):
            pg, pgk = ring.next()
            pu, puk = ring.next()
            for k in range(8):
                p.mm(pg[:, 0:CAP], wg[b][:, k, m * 128:(m + 1) * 128], xeT[b][:, k, :], k == 0, k == 7, [("wg", b)] + xtk, [pgk])
            for k in range(8):
                p.mm(pu[:, 0:CAP], wu[b][:, k, m * 128:(m + 1) * 128], xeT[b][:, k, :], k == 0, k == 7, [("wu", b)] + xtk, [puk])
            s_ = sg[m % 2]
            sk = ("sg", m % 2)
            p.act(s_[:], pg[:, 0:CAP], AF.Silu, [pgk], [sk])
            p.tt("dve", hT[b][:, m, :], s_[:], pu[:, 0:CAP], ALU.mult, [sk, puk], [("hT", b, m)])
        htk = [("hT", b, m) for m in range(4)]
        for blk in range(CAP // 128):
            y_ = yt[blk % 2]
            yk = ("yt", blk % 2)
            for hh in range(2):
                py, pyk = ring.next()
                for k in range(4):
                    p.mm(py[:], hT[b][:, k, blk * 128:(blk + 1) * 128], wd[b][:, k, hh * 512:(hh + 1) * 512], k == 0, k == 3,
                         htk + [("wd", b)], [pyk])
                if hh == 0:
                    p.act(y_[:, 0:512], py[:], AF.Copy, [pyk], [yk])
                else:
                    p.copy("dve", y_[:, 512:1024], py[:], [pyk], [yk])
            r0 = e * CAP + blk * 128
            p.dma(d["ydisp"][r0:r0 + 128, :], y_[:], reads=[yk], writes=[("yd", e, blk)])
            ydk.append(("yd", e, blk))
    fin = []

    def cload(t):
        b = t % 2
        tok0 = t * 128
        p.dma(sl[b][:], d["slots"][tok0:tok0 + 128, :], writes=[("sl", b)])
        p.dma(gt[b][:], d["gates"][tok0:tok0 + 128, :], writes=[("gt", b)])
        p.dma(x1t[b][:], d["x1"][tok0:tok0 + 128, :], writes=[("x1t", b)])
        for kk, dst in enumerate((ya[b], yb[b])):
            p.dma(None, None, reads=[("sl", b)] + ydk, writes=[("yab", b, kk)], queue="pool",
                  fn=lambda e, dst=dst, b=b, kk=kk: e.indirect_dma_start(
                      out=dst[:, :], out_offset=None, in_=d["ydisp"],
                      in_offset=bass.IndirectOffsetOnAxis(ap=sl[b][:, kk:kk + 1], axis=0)))
    cload(0)
    for t in range(TOK // 128):
        b = t % 2
        tok0 = t * 128
        if t + 1 < TOK // 128:
            cload(t + 1)
        fk_ = ("f", b)
        p.ts("dve", ya[b][:], ya[b][:], gt[b][:, 0:1], None, ALU.mult, None, [("yab", b, 0), ("gt", b)], [("yab", b, 0)])
        p.stt(ya[b][:], yb[b][:], gt[b][:, 1:2], ya[b][:], ALU.mult, ALU.add, [("yab", b, 0), ("yab", b, 1), ("gt", b)],
              [("yab", b, 0)])
        p.stt(ya[b][:], x1t[b][:], ALPHA, ya[b][:], ALU.mult, ALU.add, [("yab", b, 0), ("x1t", b)], [("yab", b, 0)])
        emit_ln(p, ya[b][:], yo[b][:], lnt, 0, (st6, mv, rs), [("yab", b, 0)], [("yo", b)], "b")
        p.dma(d["x2"][tok0:tok0 + 128, :], yo[b][:], reads=[("yo", b)], writes=[("x2o", t)])
        fin.append(("x2o", t))
        if "xT_next" in d:
            xn = xTn[(t // 4) % 2]
            xnk = ("xTn", (t // 4) % 2)
            for hh in range(2):
                pst, psk = ring.next()
                for c in range(4):
                    k = hh * 4 + c
                    p.tr(pst[:, c * 128:(c + 1) * 128], yo[b][:, k * 128:(k + 1) * 128], identf[:], [("yo", b), "identf"], [psk])
                p.copy("dve" if hh else "pool", xn[:, hh * 4:(hh + 1) * 4, (t % 4) * 128:(t % 4 + 1) * 128],
                       pst[:].rearrange("p (c n) -> p c n", c=4), [psk], [xnk]) if hh else \
                    p.act(xn[:, hh * 4:(hh + 1) * 4, (t % 4) * 128:(t % 4 + 1) * 128],
                          pst[:].rearrange("p (c n) -> p c n", c=4), AF.Copy, [psk], [xnk])
            if t % 4 == 3:
                T4 = t // 4
                p.dma(d["xT_next"].rearrange("(c p) n -> p c n", p=128)[:, :, T4 * 512:(T4 + 1) * 512], xn[:],
                      reads=[xnk], writes=[("xTn_out", T4)])
    return fin


_PROGS = {}


def _prog(name, builder):
    if name not in _PROGS:
        _PROGS[name] = builder()
    return _PROGS[name]


def _run(nc, in_maps):
    res = run_bass_kernel_spmd(nc, in_maps, core_ids=list(range(8)))
    return res.results


def kernel_unfused(x, w_in, w_sink, w_conv, b_conv, w_rec_gate, b_rec_gate, w_in_gate, b_in_gate, lru_lambda,
                   w_attn_o, w_rnn_o, w_out, ln_g, ln_b, w_router_group, b_router_group, w_router_expert, b_router_expert,
                   w_exp_gate, w_exp_up, w_exp_down):
    f = lambda a: np.asarray(a, dtype=np.float32)
    x = f(x)
    w_in, w_sink, w_conv, b_conv = f(w_in), f(w_sink), f(w_conv), f(b_conv)
    w_rec_gate, b_rec_gate, w_in_gate, b_in_gate, lru_lambda = f(w_rec_gate), f(b_rec_gate), f(w_in_gate), f(b_in_gate), f(lru_lambda)
    w_attn_o, w_rnn_o, w_out, ln_g, ln_b = f(w_attn_o), f(w_rnn_o), f(w_out), f(ln_g), f(ln_b)
    w_router_group, b_router_group = f(w_router_group), f(b_router_group)
    w_router_expert, b_router_expert = f(w_router_expert), f(b_router_expert)
    w_exp_gate, w_exp_up, w_exp_down = f(w_exp_gate), f(w_exp_up), f(w_exp_down)
    depth = w_in.shape[0]
    nc_r = _prog("rnn", build_rnn)
    nc_a = _prog("attn", build_attn)
    nc_m = _prog("mix", build_mix)
    nc_e = _prog("moe", build_moe)
    aconst = [attn_consts(j) for j in range(4)]
    mconst = mix_consts()
    ident = np.eye(128, dtype=np.float32)
    cores = [(c // 4, c % 4) for c in range(8)]
    for l in range(depth):
        xTs = [np.ascontiguousarray(x[b].T) for b in range(2)]
        maps = []
        for (b, j) in cores:
            m = pack_rnn_inputs(l, j, w_in, w_conv, b_conv, w_rec_gate, b_rec_gate, w_in_gate, b_in_gate, lru_lambda)
            m["xT"] = xTs[b]
            maps.append(m)
        res = _run(nc_r, maps)
        hgT = [np.concatenate([np.asarray(res[b * 4 + j]["hgT"]) for j in range(4)], axis=0) for b in range(2)]
        w_qkv = np.ascontiguousarray(w_in[l][:, 0:1536])
        sink = np.ascontiguousarray(np.broadcast_to(w_sink[l][None, :], (128, 8)))
        maps = []
        for (b, j) in cores:
            m = dict(aconst[j])
            m["xT"] = halo_xT(x[b], j)
            m["w_qkv"] = w_qkv
            m["sink"] = sink
            maps.append(m)
        res = _run(nc_a, maps)
        oT = [np.asarray(res[c]["oT"]) for c in range(8)]
        w4 = np.ascontiguousarray(np.stack([w_attn_o[l], w_rnn_o[l], w_in[l][:, 3584:4608], w_in[l][:, 4608:5632]]))
        ln1 = np.ascontiguousarray(np.broadcast_to(np.stack([ln_g[l, 0], ln_b[l, 0]])[None], (128, 2, 1024)))
        w_rt = np.ascontiguousarray(np.concatenate([w_router_group[l], w_router_expert[l]], axis=1))
        b_rt = np.ascontiguousarray(np.broadcast_to(np.concatenate([b_router_group[l], b_router_expert[l]])[None], (128, 36)))
        wo = np.ascontiguousarray(w_out[l])
        maps = []
        for c, (b, j) in enumerate(cores):
            m = dict(mconst)
            m["xT"] = np.ascontiguousarray(xTs[b][:, j * TOK:(j + 1) * TOK])
            m["x_tok"] = np.ascontiguousarray(x[b][j * TOK:(j + 1) * TOK])
            m["oT"] = oT[c]
            m["hgT"] = np.ascontiguousarray(hgT[b][:, j * TOK:(j + 1) * TOK])
            m["w4"] = w4
            m["w_out"] = wo
            m["ln"] = ln1
            m["w_rt"] = w_rt
            m["b_rt"] = b_rt
            maps.append(m)
        res = _run(nc_m, maps)
        ln2 = np.ascontiguousarray(np.broadcast_to(np.stack([ln_g[l, 1], ln_b[l, 1]])[None], (128, 2, 1024)))
        wg_, wu_, wd_ = np.ascontiguousarray(w_exp_gate[l]), np.ascontiguousarray(w_exp_up[l]), np.ascontiguousarray(w_exp_down[l])
        maps = []
        for c in range(8):
            maps.append({"xdisp": np.asarray(res[c]["xdisp"]), "x1": np.asarray(res[c]["x1"]),
                         "slots": np.asarray(res[c]["slots"]), "gates": np.asarray(res[c]["gates"]),
                         "w_g": wg_, "w_u": wu_, "w_d": wd_, "ln": ln2, "ident": ident})
        res = _run(nc_e, maps)
        x = np.stack([np.concatenate([np.asarray(res[b * 4 + j]["x2"]) for j in range(4)], axis=0) for b in range(2)])
    return np.ascontiguousarray(x.astype(np.float32))


GROUPS4 = [[0, 1, 2, 3], [4, 5, 6, 7]]


def build_fused(depth=4):
    nc = bass.Bass("TRN2", target_bir_lowering=False)
    L = depth

    def ext(name, shape, dt=F32):
        return nc.dram_tensor(name, list(shape), dt, kind="ExternalInput").ap()

    def internal(name, shape, dt):
        return nc.dram_tensor(name, list(shape), dt).ap()
    I = {}
    I["x_tok0"] = ext("x_tok0", [TOK, 1024])
    I["xT0"] = ext("xT0", [1024, TOK])
    I["w_r"] = ext("w_r", [L, 1024, 512])
    I["w_gt"] = ext("w_gt", [L, 128, 8, 128])
    I["small"] = ext("small", [L, 128, 2, NSM])
    I["w_qkv"] = ext("w_qkv", [L, 1024, 1536])
    I["cosT"] = ext("cosT", [128, TH])
    I["sinT"] = ext("sinT", [128, TH])
    I["masks"] = ext("masks", [128, 3, 384])
    I["cmats"] = ext("cmats", [128, 2, 128])
    I["sink"] = ext("sink", [L, 128, 8])
    I["w4"] = ext("w4", [L, 4, 1024, 1024])
    I["w_out"] = ext("w_out", [L, 1024, 1024])
    I["ln1"] = ext("ln1", [L, 128, 2, 1024])
    I["ln2"] = ext("ln2", [L, 128, 2, 1024])
    I["w_rt"] = ext("w_rt", [L, 1024, 36])
    I["b_rt"] = ext("b_rt", [L, 128, 36])
    I["cst"] = ext("cst", [128, 3, 128])
    I["cst2"] = ext("cst2", [128, 40])
    I["w_eg"] = ext("w_eg", [L, N_EXP, 1024, 512])
    I["w_eu"] = ext("w_eu", [L, N_EXP, 1024, 512])
    I["w_ed"] = ext("w_ed", [L, N_EXP, 512, 1024])
    I["ident"] = ext("ident", [128, 128])
    out = nc.dram_tensor("out", [TOK, 1024], F32, kind="ExternalOutput").ap()
    xT_own = [internal("xT_own%d" % i, [1024, TOK], BF16) for i in range(2)]
    xT_all = [internal("xT_all%d" % i, [128 + 4096, TOK], BF16) for i in range(2)]
    x_tok_i = [internal("x_tok_i%d" % i, [TOK, 1024], F32) for i in range(2)]
    hg_own = [internal("hg_own%d" % i, [4, 256, TOK], BF16) for i in range(2)]
    hg_all = [internal("hg_all%d" % i, [128 + 4096, TOK], BF16) for i in range(2)]
    halo_l = internal("halo_l", [1024, HALO], BF16)
    halo_r = internal("halo_r", [1024, HALO], BF16)
    hg_mine = internal("hg_mine", [1024, TOK], BF16)
    oT = internal("oT_i", [1024, TOK], BF16)
    x1 = internal("x1_i", [TOK, 1024], F32)
    xdisp = internal("xdisp_i", [NROWS, 1024], BF16)
    slots = internal("slots_i", [TOK, 2], I32)
    gates = internal("gates_i", [TOK, 2], F32)
    ydisp = internal("ydisp_i", [NROWS, 1024], F32)

    p = Prog(nc)
    p.begin_phase()
    st = [p.sb("pro%d" % i, [128, 8, 512], BF16) for i in range(2)]
    src = I["xT0"].rearrange("(c p) n -> p c n", p=128)
    dst = xT_own[0].rearrange("(c p) n -> p c n", p=128)
    for t in range(TOK // 512):
        p.dma(st[t % 2][:], src[:, :, t * 512:(t + 1) * 512], writes=[("pro", t % 2)], queue="pool")
        p.dma(dst[:, :, t * 512:(t + 1) * 512], st[t % 2][:], reads=[("pro", t % 2)], writes=[("xT_own_w", t)])
    for k in range(4):
        p.coll("AllGather", GROUPS4, xT_own[0][k * 256:(k + 1) * 256, :], xT_all[0][128 + k * 1024:128 + (k + 1) * 1024, :],
               reads=[("xT_own_w", t) for t in range(TOK // 512)], writes=[("xT_all", k)])
    p.end_phase()
    for l in range(L):
        par = l % 2
        last = (l == L - 1)
        p.begin_phase()
        if l > 0:
            for k in range(4):
                p.coll("AllGather", GROUPS4, xT_own[par][k * 256:(k + 1) * 256, :],
                       xT_all[par][128 + k * 1024:128 + (k + 1) * 1024, :], reads=[], writes=[("xT_all", k)])
        emit_attn(p, {"xT_own": xT_own[par], "xT_all": xT_all[par], "w_qkv": I["w_qkv"][l], "cosT": I["cosT"],
                      "sinT": I["sinT"], "masks": I["masks"], "cmats": I["cmats"], "sink": I["sink"][l], "oT": oT,
                      "halo_l": halo_l, "halo_r": halo_r},
                  pfx="a%d" % l)
        p.end_phase()
        p.begin_phase()
        emit_rnn(p, xT_all[par], I["w_r"][l], I["w_gt"][l], I["small"][l], hg_own[par], pfx="r%d" % l)
        for t in range(4):
            p.coll("AllGather", GROUPS4, hg_own[par][t], hg_all[par][128 + t * 1024:128 + (t + 1) * 1024, :],
                   reads=[("hg_out", b, t) for b in range(2)], writes=[("hg_all", t)])
        p.end_phase()
        p.begin_phase()
        emit_mix(p, {"xT_own": xT_own[par], "x_tok": (I["x_tok0"] if l == 0 else x_tok_i[par]), "oT": oT,
                     "hg_all": hg_all[par], "hg_mine": hg_mine, "w4": I["w4"][l], "w_out": I["w_out"][l], "ln": I["ln1"][l],
                     "w_rt": I["w_rt"][l], "b_rt": I["b_rt"][l], "cst": I["cst"], "cst2": I["cst2"],
                     "x1": x1, "xdisp": xdisp, "slots": slots, "gates": gates}, pfx="m%d" % l)
        p.end_phase()
        p.begin_phase()
        dd = {"xdisp": xdisp, "x1": x1, "slots": slots, "gates": gates, "w_g": I["w_eg"][l], "w_u": I["w_eu"][l],
              "w_d": I["w_ed"][l], "ln": I["ln2"][l], "ident": I["ident"], "ydisp": ydisp,
              "x2": (out if last else x_tok_i[1 - par])}
        if not last:
            dd["xT_next"] = xT_own[1 - par]
        emit_moe(p, dd, pfx="e%d" % l)
        p.end_phase()
    p.finish([])
    return nc


def fused_inputs(depth, x, w_in, w_sink, w_conv, b_conv, w_rec_gate, b_rec_gate, w_in_gate, b_in_gate, lru_lambda,
                 w_attn_o, w_rnn_o, w_out, ln_g, ln_b, w_router_group, b_router_group, w_router_expert, b_router_expert,
                 w_exp_gate, w_exp_up, w_exp_down):
    L = depth
    ca = np.ascontiguousarray
    shared = {}
    shared["w_qkv"] = ca(w_in[:L, :, 0:1536])
    shared["sink"] = ca(np.broadcast_to(w_sink[:L, None, :], (L, 128, 8)))
    shared["w4"] = ca(np.stack([np.stack([w_attn_o[l], w_rnn_o[l], w_in[l][:, 3584:4608], w_in[l][:, 4608:5632]]) for l in range(L)]))
    shared["w_out"] = ca(w_out[:L])
    shared["ln1"] = ca(np.broadcast_to(np.stack([ln_g[:L, 0], ln_b[:L, 0]], axis=1)[:, None], (L, 128, 2, 1024)))
    shared["ln2"] = ca(np.broadcast_to(np.stack([ln_g[:L, 1], ln_b[:L, 1]], axis=1)[:, None], (L, 128, 2, 1024)))
    shared["w_rt"] = ca(np.concatenate([w_router_group[:L], w_router_expert[:L]], axis=2))
    shared["b_rt"] = ca(np.broadcast_to(np.concatenate([b_router_group[:L], b_router_expert[:L]], axis=1)[:, None, :], (L, 128, 36)))
    shared["w_eg"] = ca(w_exp_gate[:L])
    shared["w_eu"] = ca(w_exp_up[:L])
    shared["w_ed"] = ca(w_exp_down[:L])
    shared["ident"] = np.eye(128, dtype=np.float32)
    shared.update(mix_consts())
    rn = []
    for j in range(4):
        packs = [pack_rnn_inputs(l, j, w_in, w_conv, b_conv, w_rec_gate, b_rec_gate, w_in_gate, b_in_gate, lru_lambda)
                 for l in range(L)]
        rn.append({"w_r": ca(np.stack([q["w_r"] for q in packs])), "w_gt": ca(np.stack([q["w_g"] for q in packs])),
                   "small": ca(np.stack([q["small"] for q in packs]))})
    maps = []
    for c in range(8):
        b, j = c // 4, c % 4
        m = dict(shared)
        m.update(rn[j])
        m.update(attn_consts(j))
        xs = x[b][j * TOK:(j + 1) * TOK]
        m["x_tok0"] = ca(xs)
        m["xT0"] = ca(xs.T)
        maps.append(m)
    return maps


def kernel_fused(depth, **inp):
    key = "fused%d" % depth
    nc = _prog(key, lambda: build_fused(depth))
    names = ["x", "w_in", "w_sink", "w_conv", "b_conv", "w_rec_gate", "b_rec_gate", "w_in_gate", "b_in_gate", "lru_lambda",
             "w_attn_o", "w_rnn_o", "w_out", "ln_g", "ln_b", "w_router_group", "b_router_group", "w_router_expert",
             "b_router_expert", "w_exp_gate", "w_exp_up", "w_exp_down"]
    args = [np.asarray(inp[n], dtype=np.float32) for n in names]
    maps = fused_inputs(depth, *args)
    res = _run(nc, maps)
    x = np.stack([np.concatenate([np.asarray(res[b * 4 + j]["out"]) for j in range(4)], axis=0) for b in range(2)])
    return np.ascontiguousarray(x.astype(np.float32))


def kernel(**inputs):
    return kernel_fused(4, **inputs)
```

```python
import contextlib
import numpy as np
import concourse.bass as bass
import concourse.mybir as mybir
from concourse.bass_utils import run_bass_kernel_spmd

F32 = mybir.dt.float32
BF16 = mybir.dt.bfloat16
I32 = mybir.dt.int32
U32 = mybir.dt.uint32
AF = mybir.ActivationFunctionType
ALU = mybir.AluOpType
AX = mybir.AxisListType


class Prog:
    ENG = ("pe", "dve", "act", "pool", "sp")

    def __init__(self, nc, n_slots=8):
        self.nc = nc
        self.es = contextlib.ExitStack()
        self.q = {e: [] for e in self.ENG}
        self.cnt = {e: 0 for e in self.ENG}
        self.sem = {e: self.es.enter_context(nc.semaphore("s_" + e)) for e in self.ENG}
        self.n_slots = n_slots
        self.pool_slots = 2
        self.dq = ("sp", "act", "pool")
        self.dsem = {q: [self.es.enter_context(nc.semaphore("d_%s%d" % (q, i))) for i in range(n_slots)]
                     for q in self.dq}
        self.dn = {q: 0 for q in self.dq}
        self.sems = {}
        for e in self.ENG:
            self.sems[("c", e)] = self.sem[e]
        for q in self.dq:
            for i in range(n_slots):
                self.sems[("d", q, i)] = self.dsem[q][i]
        self.lastw = {}
        self.readers = {}
        self.waited = {e: {} for e in self.ENG}
        self.n_ops = 0
        self.exclusive = set()
        self.csem = []
        self.ph = None
        self.latest = {}
        self._cidx = {}

    def sb(self, name, shape, dtype):
        st = self.ph if self.ph is not None else self.es
        return st.enter_context(self.nc.sbuf_tensor(name, list(shape), dtype))

    def ps(self, name, shape, dtype):
        st = self.ph if self.ph is not None else self.es
        return st.enter_context(self.nc.psum_tensor(name, list(shape), dtype))

    def core_idx(self, e, which):
        key = id(e)
        if key not in self._cidx:
            pid = e.partition_id()
            vals = {}
            for name, off in (("j", 0), ("jl", 3), ("jr", 5)):
                vals[name] = e.snap((pid + off) % 4, min_val=0, max_val=3)
            self._cidx[key] = vals
        return self._cidx[key][which]

    def begin_phase(self):
        assert self.ph is None
        self.ph = contextlib.ExitStack()
        self.exclusive = set()

    def _barrier(self):
        for e in self.ENG:
            waits = []
            for sk, v in self.latest.items():
                if sk == ("c", e):
                    continue
                if self.waited[e].get(sk, 0) < v:
                    self.waited[e][sk] = v
                    waits.append((sk, v))
            if waits:
                self.q[e].append((waits, None, None, 0))
        self.lastw = {}
        self.readers = {}

    def _emit_block(self):
        nc = self.nc
        engobj = {"pe": "tensor", "dve": "vector", "act": "scalar", "pool": "gpsimd", "sp": "sync"}
        with nc.Block() as block:
            for e in self.ENG:
                items = self.q[e]

                def body(eng, items=items):
                    for waits, fn, sk, amt in items:
                        for wsk, v in waits:
                            eng.wait_ge(self.sems[wsk], v)
                        if fn is not None:
                            ins = fn(eng)
                            if amt is None:
                                ins.then_inc(self.sems[sk])
                            else:
                                ins.then_inc(self.sems[sk], amt)
                getattr(block, engobj[e])(body)
        self.q = {e: [] for e in self.ENG}

    def end_phase(self):
        self._barrier()
        self._emit_block()
        self.ph.close()
        self.ph = None

    def _deps(self, eng, reads, writes, is_dma):
        need = {}

        def add(d):
            for sk, v in d.items():
                if need.get(sk, 0) < v:
                    need[sk] = v
        for k in reads:
            if k in self.lastw:
                add(self.lastw[k])
        for k in writes:
            if k in self.lastw:
                add(self.lastw[k])
            if k in self.readers:
                add(self.readers[k])
        out = []
        for sk, v in need.items():
            if (not is_dma) and eng == "pe" and sk == ("c", "pe"):
                continue
            if self.waited[eng].get(sk, 0) >= v:
                continue
            self.waited[eng][sk] = v
            out.append((sk, v))
        return out

    def _mark(self, reads, writes, tok):
        sk, v = tok
        if self.latest.get(sk, 0) < v:
            self.latest[sk] = v
        for k in reads:
            r = self.readers.setdefault(k, {})
            if r.get(sk, 0) < v:
                r[sk] = v
        for k in writes:
            self.lastw[k] = {sk: v}
            self.readers[k] = {}

    def _excl(self, reads, writes):
        ex = [k for k in reads if k in self.exclusive]
        if ex:
            writes = list(writes) + [k for k in ex if k not in writes]
        return reads, writes

    def op(self, eng, fn, reads=(), writes=()):
        reads, writes = self._excl(reads, writes)
        waits = self._deps(eng, reads, writes, False)
        self.cnt[eng] += 1
        tok = (("c", eng), self.cnt[eng])
        self.q[eng].append((waits, fn, tok[0], 1))
        self._mark(reads, writes, tok)
        self.n_ops += 1

    def dma(self, out, in_, reads=(), writes=(), queue="sp", fn=None):
        n = self.dn[queue]
        self.dn[queue] += 1
        ns = self.n_slots if queue != "pool" else min(self.n_slots, self.pool_slots)
        slot = n % ns
        sk = ("d", queue, slot)
        waits = self._deps(queue, reads, writes, True)
        prev = 16 * (n // ns)
        if prev > 0 and self.waited[queue].get(sk, 0) < prev:
            self.waited[queue][sk] = prev
            waits.append((sk, prev))
        if fn is None:
            def fn(e, out=out, in_=in_):
                return e.dma_start(out=out, in_=in_)
        tok = (sk, prev + 16)
        self.q[queue].append((waits, fn, sk, 16))
        self._mark(reads, writes, tok)
        self.n_ops += 1

    def mm(self, out, lhsT, rhs, start, stop, reads, writes):
        self.op("pe", lambda e: e.matmul(out, lhsT=lhsT, rhs=rhs, start=start, stop=stop), reads, writes)

    def tr(self, out, in_, ident, reads, writes):
        self.op("pe", lambda e: e.transpose(out=out, in_=in_, identity=ident), reads, writes)

    def act(self, out, in_, func, reads, writes, scale=1.0, bias=None, accum_out=None):
        kw = {}
        if bias is not None:
            kw["bias"] = bias
        if accum_out is not None:
            kw["accum_out"] = accum_out
        self.op("act", lambda e: e.activation(out=out, in_=in_, func=func, scale=scale, **kw), reads, writes)

    def ts(self, eng, out, in0, s1, s2, op0, op1, reads, writes, accum_out=None):
        kw = {}
        if accum_out is not None:
            kw["accum_out"] = accum_out
        if op1 is None:
            self.op(eng, lambda e: e.tensor_scalar(out=out, in0=in0, scalar1=s1, scalar2=None, op0=op0, **kw), reads, writes)
        else:
            self.op(eng, lambda e: e.tensor_scalar(out=out, in0=in0, scalar1=s1, scalar2=s2, op0=op0, op1=op1, **kw),
                    reads, writes)

    def tt(self, eng, out, in0, in1, op, reads, writes):
        self.op(eng, lambda e: e.tensor_tensor(out=out, in0=in0, in1=in1, op=op), reads, writes)

    def stt(self, out, in0, scalar, in1, op0, op1, reads, writes):
        self.op("dve", lambda e: e.scalar_tensor_tensor(out=out, in0=in0, scalar=scalar, in1=in1, op0=op0, op1=op1),
                reads, writes)

    def scan(self, out, d0, d1, init, reads, writes):
        self.op("dve", lambda e: e.tensor_tensor_scan(out=out, data0=d0, data1=d1, initial=init, op0=ALU.mult, op1=ALU.add),
                reads, writes)

    def copy(self, eng, out, in_, reads, writes):
        self.op(eng, lambda e: e.tensor_copy(out=out, in_=in_), reads, writes)

    def memset(self, eng, ap, val, writes):
        self.op(eng, lambda e: e.memset(ap, val), (), writes)

    def coll(self, kind, groups, in_ap, out_ap, reads=(), writes=()):
        idx = len(self.csem)
        sem = self.es.enter_context(self.nc.semaphore("cc%d" % idx))
        self.csem.append(sem)
        sk = ("k", idx)
        self.sems[sk] = sem
        waits = self._deps("pool", reads, writes, True)

        def fn(e):
            return e.collective_compute(kind, ALU.bypass, replica_groups=groups, ins=[in_ap], outs=[out_ap])
        self.q["pool"].append((waits, fn, sk, None))
        self._mark(reads, writes, (sk, 1))
        self.n_ops += 1

    def finish(self, final_keys):
        self._barrier()
        self._emit_block()
        if self.ph is not None:
            self.ph.close()
            self.ph = None
        self.es.close()


def _rev(ap2d, n):
    apl = [list(s) for s in ap2d.ap]
    assert len(apl) == 2 and apl[1][1] == n and apl[1][0] == 1, apl
    from concourse.ap import AP
    return AP(ap2d.tensor, ap2d.offset + (n - 1), [apl[0], [-1, n]])


class PsumRing:
    def __init__(self, p, n=8, name="ps"):
        self.p = p
        self.t = [p.ps("%s%d" % (name, i), [128, 512], F32) for i in range(n)]
        for i in range(n):
            p.exclusive.add((name, i))
        self.i = 0
        self.n = n
        self.name = name

    def next(self):
        i = self.i
        self.i = (self.i + 1) % self.n
        return self.t[i], (self.name, i)


S_LEN = 8192
NSM = 11
GELU_C = 1.5957691216057308


def build_rnn(x_dtype=F32):
    nc = bass.Bass("TRN2", target_bir_lowering=False)
    xT = nc.dram_tensor("xT", [1024, S_LEN], x_dtype, kind="ExternalInput").ap()
    w_r = nc.dram_tensor("w_r", [1024, 512], F32, kind="ExternalInput").ap()
    w_g = nc.dram_tensor("w_g", [128, 8, 128], F32, kind="ExternalInput").ap()
    small = nc.dram_tensor("small", [128, 2, NSM], F32, kind="ExternalInput").ap()
    hgT = nc.dram_tensor("hgT", [256, S_LEN], BF16, kind="ExternalOutput").ap()
    p = Prog(nc)
    emit_rnn(p, xT, w_r, w_g, small, hgT)
    p.finish([("hg_out", b, c) for b in range(2) for c in range(4)])
    return nc


def emit_rnn(p, xT, w_r, w_g, small, hgT, pfx="r"):
    T = S_LEN
    CH = 2048
    NCH = T // CH
    wb = p.sb(pfx + "wb", [128, 8, 512], BF16)
    wg = p.sb(pfx + "wg", [128, 8, 128], BF16)
    sm = p.sb(pfx + "sm", [128, 2, NSM], F32)
    cl = p.sb(pfx + "cl", [128, 2, 2, 2], F32)
    zt = p.sb(pfx + "zt", [128, 4], F32)
    pt = p.sb(pfx + "pt", [128, 4], F32)
    xb = [p.sb(pfx + "xb%d" % i, [128, 8, 512], BF16) for i in range(2)]
    xr_full = p.sb(pfx + "xrf", [128, T + 4], F32)
    gy = p.sb(pfx + "gy", [128, T], BF16)
    xc = p.sb(pfx + "xc", [128, T], F32)
    xcb = p.sb(pfx + "xcb", [128, T], BF16)
    g1 = [p.sb(pfx + "g1_%d" % i, [128, 512], F32) for i in range(2)]
    g2 = [p.sb(pfx + "g2_%d" % i, [128, 512], F32) for i in range(2)]
    rtL = [p.sb(pfx + "rt%d" % i, [128, CH], F32) for i in range(2)]
    itL = [p.sb(pfx + "it%d" % i, [128, CH], F32) for i in range(2)]
    atL = [p.sb(pfx + "at%d" % i, [128, CH], F32) for i in range(2)]
    ccn = [0]
    hb = [p.sb(pfx + "hb%d" % i, [128, CH], F32) for i in range(2)]
    hgb = [p.sb(pfx + "hgb%d" % i, [128, CH], BF16) for i in range(2)]
    ring = PsumRing(p, 8, pfx + "ps")

    p.dma(wb[:], w_r.rearrange("(c p) n -> p c n", p=128), writes=["wb"], queue="pool")
    p.dma(wg[:], w_g, writes=["wg"], queue="pool")
    p.dma(sm[:], small, writes=["sm"])
    for b in range(2):
        for d in range(2):
            j = b * 2 + d
            p.act(zt[:, j:j + 1], sm[:, b, 5 + 3 * d + 2: 5 + 3 * d + 3], AF.Exp, ["sm"], ["zt"], scale=-1.0)
    p.ts("dve", pt[:], zt[:], -1.0 / 6, 1.0 / 5, ALU.mult, ALU.add, ["zt"], ["pt"])
    for cst in (-1.0 / 4, 1.0 / 3, -1.0 / 2, 1.0):
        p.tt("dve", pt[:], pt[:], zt[:], ALU.mult, ["pt", "zt"], ["pt"])
        p.ts("dve", pt[:], pt[:], cst, None, ALU.add, None, ["pt"], ["pt"])
    p.tt("dve", pt[:], pt[:], zt[:], ALU.mult, ["pt", "zt"], ["pt"])
    for b in range(2):
        for d in range(2):
            j = b * 2 + d
            p.ts("dve", cl[:, b, d, 0:1], pt[:, j:j + 1], -8.0, None, ALU.mult, None, ["pt"], ["cl"])
            p.ts("dve", cl[:, b, d, 1:2], pt[:, j:j + 1], -16.0, None, ALU.mult, None, ["pt"], ["cl"])
    chunked = (xT.shape[0] == 4096 + 128)
    if chunked:
        xTv = xT[128:128 + 4096, :].rearrange("(k r c p) n -> p r k c n", k=4, r=4, c=2, p=128)
    else:
        xTv = xT.rearrange("(c p) n -> p c n", p=128)

    def xrk(c):
        return [("xr", t) for t in range(4 * c, 4 * c + 4)]

    for blk in range(2):
        p.memset("pool", xr_full[:, 0:2], 0.0, [("xr", -1)])
        p.memset("pool", xr_full[:, T + 2:T + 4], 0.0, [("xr", 16)])
        for t in range(T // 512):
            xbt = xb[t % 2]
            xbk = ("xb", t % 2)
            if chunked:
                for k in range(4):
                    p.dma(xbt[:, 2 * k:2 * k + 2, :], xTv[:, t // 4, k, :, (t % 4) * 512:(t % 4 + 1) * 512],
                          reads=["xT_all"], writes=[xbk])
            else:
                p.dma(xbt[:], xTv[:, :, t * 512:(t + 1) * 512], writes=[xbk], queue="pool")
            for m in range(2):
                pst, psk = ring.next()
                col = (0 if m == 0 else 256) + blk * 128
                for k in range(8):
                    p.mm(pst[:], wb[:, k, col:col + 128], xbt[:, k, :], k == 0, k == 7, ["wb", xbk], [psk])
                if m == 0:
                    p.act(xr_full[:, 2 + t * 512: 2 + (t + 1) * 512], pst[:], AF.Copy, [psk], [("xr", t)])
                else:
                    a1, a2 = g1[t % 2], g2[t % 2]
                    k1, k2 = ("g1", t % 2), ("g2", t % 2)
                    p.act(a1[:], pst[:], AF.Square, [psk], [k1])
                    p.ts("dve", a1[:], a1[:], 0.044715, 1.0, ALU.mult, ALU.add, [k1], [k1])
                    p.tt("dve", a1[:], a1[:], pst[:], ALU.mult, [k1, psk], [k1])
                    p.act(a2[:], a1[:], AF.Sigmoid, [k1], [k2], scale=GELU_C)
                    p.tt("dve", gy[:, t * 512:(t + 1) * 512], a2[:], pst[:], ALU.mult, [k2, psk], [("gy", t)])
        for c in range(NCH):
            o = c * CH
            rk = [("xr", t) for t in range(4 * c - 1, 4 * c + 5)]
            p.ts("dve", xc[:, o:o + CH], xr_full[:, o:o + CH], sm[:, blk, 0:1], sm[:, blk, 4:5], ALU.mult, ALU.add,
                 rk + ["sm"], [("xc", c)])
            for tap in range(1, 4):
                p.stt(xc[:, o:o + CH], xr_full[:, o + tap:o + tap + CH], sm[:, blk, tap:tap + 1], xc[:, o:o + CH],
                      ALU.mult, ALU.add, rk + ["sm", ("xc", c)], [("xc", c)])
            p.copy("pool", xcb[:, o:o + CH], xc[:, o:o + CH], [("xc", c)], [("xcb", c)])
        hf = xr_full
        for d in range(2):
            order = list(range(NCH)) if d == 0 else list(range(NCH - 1, -1, -1))
            prev = None
            for ci, c in enumerate(order):
                o = c * CH
                cp = ccn[0] % 2
                ccn[0] += 1
                rt, it, at = rtL[cp], itL[cp], atL[cp]
                atk = ("at", cp)
                for s in range(CH // 512):
                    for g in range(2):
                        pst, psk = ring.next()
                        p.mm(pst[:], wg[:, blk * 4 + d * 2 + g, :], xcb[:, o + s * 512: o + (s + 1) * 512], True, True,
                             ["wg", ("xcb", c)], [psk])
                        dst = rt if g == 0 else it
                        p.act(dst[:, s * 512:(s + 1) * 512], pst[:], AF.Sigmoid, [psk, "sm"],
                              [("rt" if g == 0 else "it", cp, s)], bias=sm[:, blk, 5 + 3 * d + g: 5 + 3 * d + g + 1])
                rtk = [("rt", cp, s) for s in range(4)]
                itk = [("it", cp, s) for s in range(4)]
                p.act(at[:], rt[:], AF.Exp, rtk + ["cl"], [atk], scale=cl[:, blk, d, 0:1])
                p.act(rt[:], rt[:], AF.Exp, rtk + ["cl"], rtk, scale=cl[:, blk, d, 1:2])
                p.act(rt[:], rt[:], AF.Sqrt, rtk, rtk, scale=-1.0, bias=1.0)
                p.tt("pool", it[:], it[:], xc[:, o:o + CH], ALU.mult, itk + [("xc", c)], itk)
                p.tt("pool", it[:], it[:], rt[:], ALU.mult, itk + rtk, itk)
                if d == 0:
                    init = 0.0 if ci == 0 else hf[:, 2 + o - 1: 2 + o]
                    p.scan(hf[:, 2 + o: 2 + o + CH], at[:], it[:], init, [atk] + itk + [("xr", 4 * c - 1)], xrk(c))
                else:
                    hbt = hb[ci % 2]
                    init = 0.0 if ci == 0 else prev[:, 0:1]
                    p.scan(_rev(hbt[:], CH), _rev(at[:], CH), _rev(it[:], CH), init,
                           [atk] + itk + [("hb", (ci + 1) % 2)], [("hb", ci % 2)])
                    prev = hbt
                    hg = hgb[ci % 2]
                    p.tt("pool", at[:], hbt[:], hf[:, 2 + o: 2 + o + CH], ALU.add, [("hb", ci % 2), atk] + xrk(c), [atk])
                    p.tt("dve", hg[:], at[:], gy[:, o:o + CH], ALU.mult, [atk] + [("gy", t) for t in range(4 * c, 4 * c + 4)],
                         [("hgb", ci % 2)])
                    hdst = hgT[c, blk * 128:(blk + 1) * 128, :] if len(hgT.shape) == 3 else hgT[blk * 128:(blk + 1) * 128, o:o + CH]
                    p.dma(hdst, hg[:], reads=[("hgb", ci % 2)], writes=[("hg_out", blk, c)])


def pack_rnn_inputs(l, j, w_in, w_conv, b_conv, w_rec_gate, b_rec_gate, w_in_gate, b_in_gate, lru_lambda):
    XR0 = 1024 + 512
    YR0 = XR0 + 1024
    c0 = 2 * j * 128
    w_r = np.concatenate([w_in[l][:, XR0 + c0: XR0 + c0 + 256], w_in[l][:, YR0 + c0: YR0 + c0 + 256]], axis=1)
    w_g = np.empty((128, 8, 128), np.float32)
    small = np.empty((128, 2, NSM), np.float32)
    for b in range(2):
        cb = 2 * j + b
        sl = slice(cb * 128, (cb + 1) * 128)
        small[:, b, 0:4] = w_conv[l][:, sl].T
        small[:, b, 4] = b_conv[l][sl]
        for d in range(2):
            w_g[:, b * 4 + d * 2 + 0, :] = w_rec_gate[l, d, cb]
            w_g[:, b * 4 + d * 2 + 1, :] = w_in_gate[l, d, cb]
            small[:, b, 5 + 3 * d + 0] = b_rec_gate[l, d][sl]
            small[:, b, 5 + 3 * d + 1] = b_in_gate[l, d][sl]
            small[:, b, 5 + 3 * d + 2] = lru_lambda[l, d][sl]
    return {"w_r": np.ascontiguousarray(w_r), "w_g": w_g, "small": small}


TOK = 2048
HALO = 128
TH = TOK + 2 * HALO
ATT_SCALE = 128 ** -0.5


def build_attn(x_dtype=F32):
    nc = bass.Bass("TRN2", target_bir_lowering=False)
    d = {}
    d["xT"] = nc.dram_tensor("xT", [1024, TH], x_dtype, kind="ExternalInput").ap()
    d["w_qkv"] = nc.dram_tensor("w_qkv", [1024, 1536], F32, kind="ExternalInput").ap()
    d["cosT"] = nc.dram_tensor("cosT", [128, TH], F32, kind="ExternalInput").ap()
    d["sinT"] = nc.dram_tensor("sinT", [128, TH], F32, kind="ExternalInput").ap()
    d["masks"] = nc.dram_tensor("masks", [128, 3, 384], F32, kind="ExternalInput").ap()
    d["cmats"] = nc.dram_tensor("cmats", [128, 2, 128], F32, kind="ExternalInput").ap()
    d["sink"] = nc.dram_tensor("sink", [128, 8], F32, kind="ExternalInput").ap()
    d["oT"] = nc.dram_tensor("oT", [1024, TOK], BF16, kind="ExternalOutput").ap()
    p = Prog(nc)
    emit_attn(p, d)
    p.finish([("o_out", h, g) for h in range(8) for g in range(4)])
    return nc


def emit_attn(p, d, pfx="a"):
    xb = p.sb(pfx + "xb", [128, 8, TH], BF16)
    wq = [p.sb(pfx + "wq%d" % i, [128, 8, 128], BF16) for i in range(3)]
    wv = p.sb(pfx + "wv", [128, 8, 256], BF16)
    cosT = p.sb(pfx + "cos", [128, TH], F32)
    sinT = p.sb(pfx + "sin", [128, TH], F32)
    masks = p.sb(pfx + "masks", [128, 3, 384], BF16)
    cm0 = p.sb(pfx + "cm0", [128, 128], BF16)
    cm1 = p.sb(pfx + "cm1", [128, 128], BF16)
    sink = p.sb(pfx + "sink", [128, 8], F32)
    nsink = p.sb(pfx + "nsink", [128, 8], F32)
    qT = p.sb(pfx + "qT", [128, 8, TOK], BF16)
    kT = p.sb(pfx + "kT", [128, 2, TH], BF16)
    V = p.sb(pfx + "V", [128, TH // 128, 256], BF16)
    qraw = [p.sb(pfx + "qraw%d" % i, [128, 512], BF16) for i in range(2)]
    r1 = [p.sb(pfx + "r1_%d" % i, [128, 512], F32) for i in range(2)]
    r2 = [p.sb(pfx + "r2_%d" % i, [128, 512], F32) for i in range(2)]
    P = [p.sb(pfx + "P%d" % i, [128, 384], BF16) for i in range(2)]
    PT = [p.sb(pfx + "PT%d" % i, [128, 384], BF16) for i in range(2)]
    D = [p.sb(pfx + "D%d" % i, [128, 128], BF16) for i in range(2)]
    cols = [p.sb(pfx + "cols%d" % i, [128, 8], F32) for i in range(2)]
    ring = PsumRing(p, 6, pfx + "ps")
    oring = PsumRing(p, 2, pfx + "po")
    ident = cm0[:]
    pswap = cm1[:]

    if "xT_own" in d:
        own = d["xT_own"].rearrange("(c p) n -> p c n", p=128)
        allv = d["xT_all"][128:128 + 4096, :].rearrange("(k r c p) n -> p r k c n", k=4, r=4, c=2, p=128)
        nc_ = p.nc

        allr = d["xT_all"][128:128 + 4096, :].rearrange("(k r q) n -> r k q n", k=4, r=4, q=256)

        def halo(e, left):
            jn = p.core_idx(e, "jl" if left else "jr")
            if left:
                return e.dma_start(out=d["halo_l"].rearrange("(k q) n -> k q n", k=4), in_=allr[jn, :, :, TOK - HALO:TOK])
            return e.dma_start(out=d["halo_r"].rearrange("(k q) n -> k q n", k=4), in_=allr[jn, :, :, 0:HALO])
        agk = [("xT_all", k) for k in range(4)]
        p.dma(None, None, reads=agk, writes=["halo_l"], fn=lambda e: halo(e, True))
        p.dma(None, None, reads=agk, writes=["halo_r"], fn=lambda e: halo(e, False))
        p.dma(xb[:, :, 0:HALO], d["halo_l"].rearrange("(c p) n -> p c n", p=128), reads=["halo_l"],
              writes=[("xbh", 0, k) for k in range(4)])
        p.dma(xb[:, :, HALO + TOK:TH], d["halo_r"].rearrange("(c p) n -> p c n", p=128), reads=["halo_r"],
              writes=[("xbh", 1, k) for k in range(4)])
        for t0 in range(0, TOK, 512):
            keys = sorted(set([(HALO + t0) // 512, (HALO + t0 + 511) // 512]))
            p.dma(xb[:, :, HALO + t0:HALO + t0 + 512], own[:, :, t0:t0 + 512], reads=["xT_own"],
                  writes=[("xb", k) for k in keys])
    else:
        xTv = d["xT"].rearrange("(c p) n -> p c n", p=128)
        for t0 in range(0, TH, 512):
            w = min(512, TH - t0)
            p.dma(xb[:, :, t0:t0 + w], xTv[:, :, t0:t0 + w], writes=[("xb", t0 // 512)], queue="pool")

    def xbk(a, b):
        ks = [("xb", t) for t in range(a // 512, (b - 1) // 512 + 1)]
        if a < HALO:
            ks += [("xbh", 0, k) for k in range(4)]
        if b > HALO + TOK:
            ks += [("xbh", 1, k) for k in range(4)]
        return ks
    p.dma(wv[:], d["w_qkv"].rearrange("(c p) n -> p c n", p=128)[:, :, 1280:1536], writes=["wv"], queue="pool")
    p.dma(masks[:], d["masks"], writes=["masks"], queue="pool")
    p.dma(cm0[:], d["cmats"][:, 0, :], writes=["cm"], queue="pool")
    p.dma(cm1[:], d["cmats"][:, 1, :], writes=["cm"], queue="pool")
    p.dma(cosT[:], d["cosT"], writes=["cos"])
    p.dma(sinT[:], d["sinT"], writes=["sin"])
    p.dma(sink[:], d["sink"], writes=["sink"])
    p.ts("dve", nsink[:], sink[:], -1.0, None, ALU.mult, None, ["sink"], ["nsink"])
    wqv = d["w_qkv"].rearrange("(c p) n -> p c n", p=128)
    def emit_v():
        for tt in range(TH // 128):
            pst, psk = ring.next()
            for k in range(8):
                p.mm(pst[:, 0:256], xb[:, k, tt * 128:(tt + 1) * 128], wv[:, k, :], k == 0, k == 7, xbk(tt * 128, (tt + 1) * 128) + ["wv"], [psk])
            p.copy("dve" if tt % 2 else "act", V[:, tt, :], pst[:, 0:256], [psk], [("V", tt)]) if tt % 2 else \
                p.act(V[:, tt, :], pst[:, 0:256], AF.Copy, [psk], [("V", tt)])
    it = 0
    for m in list(range(8)) + ['v', 8, 9]:
        if m == 'v':
            emit_v()
            continue
        wt = wq[m % 3]
        wk = ("wq", m % 3)
        p.dma(wt[:], wqv[:, :, m * 128:(m + 1) * 128], writes=[wk], queue="pool")
        isq = m < 8
        ntok = TOK if isq else TH
        off = HALO if isq else 0
        t0 = 0
        while t0 < ntok:
            w = min(512, ntok - t0)
            pst, psk = ring.next()
            for k in range(8):
                p.mm(pst[:, 0:w], wt[:, k, :], xb[:, k, off + t0: off + t0 + w], k == 0, k == 7, xbk(off + t0, off + t0 + w) + [wk], [psk])
            qr = qraw[it % 2]
            qk = ("qraw", it % 2)
            p.act(qr[:, 0:w], pst[:, 0:w], AF.Copy, [psk], [qk])
            ps2, ps2k = ring.next()
            p.mm(ps2[:, 0:w], pswap, qr[:, 0:w], True, True, ["cm", qk], [ps2k])
            a1, a2 = r1[it % 2], r2[it % 2]
            k1, k2 = ("r1", it % 2), ("r2", it % 2)
            p.tt("dve", a1[:, 0:w], cosT[:, off + t0: off + t0 + w], pst[:, 0:w], ALU.mult, [psk, "cos"], [k1])
            p.tt("dve", a2[:, 0:w], sinT[:, off + t0: off + t0 + w], ps2[:, 0:w], ALU.mult, [ps2k, "sin"], [k2])
            if isq:
                dst = qT[:, m, t0:t0 + w]
                dk = [("qT", m, t0 // 512)]
            else:
                dst = kT[:, m - 8, t0:t0 + w]
                dk = [("kT", m - 8, t0 // 512)]
            p.tt("dve", dst, a1[:, 0:w], a2[:, 0:w], ALU.add, [k1, k2], dk)
            it += 1
            t0 += w
    its = [(grp, h, qi) for grp in range(4) for h in range(8) for qi in range(4)]
    NB = 4
    colsN = cols + [p.sb(pfx + "colsx%d" % i, [128, 8], F32) for i in range(NB - 2)]
    PN = P + [p.sb(pfx + "Px%d" % i, [128, 384], BF16) for i in range(NB - 2)]
    state = {}

    def stage_a(n):
        grp, h, qi = its[n]
        g = h // 4
        qb = grp * 4 + qi
        if qi == 0:
            state[(grp, h)] = oring.next()
        mi = 0 if qb == 0 else (2 if qb == 15 else 1)
        pss, pssk = ring.next()
        kkeys = [("kT", g, t) for t in sorted(set([(qb * 128) // 512, (qb * 128 + 383) // 512]))]
        p.mm(pss[:, 0:384], qT[:, h, qb * 128:(qb + 1) * 128], kT[:, g, qb * 128: qb * 128 + 384], True, False,
             [("qT", h, grp)] + kkeys, [pssk])
        p.mm(pss[:, 0:384], ident, masks[:, mi, :], False, True, ["cm", "masks"], [pssk])
        cl = colsN[n % NB]
        ck = ("cols", n % NB)
        p.op("dve", lambda e, cl=cl, pss=pss: e.reduce_max(out=cl[:, 0:1], in_=pss[:, 0:384], axis=AX.X), [pssk], [ck])
        p.ts("dve", cl[:, 1:2], cl[:, 0:1], -ATT_SCALE, nsink[:, h:h + 1], ALU.mult, ALU.min, [ck, "nsink"], [ck])
        Pt = PN[n % NB]
        pk = ("P", n % NB)
        p.act(Pt[:], pss[:, 0:384], AF.Exp, [pssk, ck], [pk, ck], scale=ATT_SCALE, bias=cl[:, 1:2], accum_out=cl[:, 2:3])
        p.act(cl[:, 3:4], cl[:, 1:2], AF.Exp, [ck, "sink"], [ck], bias=sink[:, h:h + 1])

    def stage_b(n):
        grp, h, qi = its[n]
        g = h // 4
        qb = grp * 4 + qi
        po, pok = state[(grp, h)]
        cl = colsN[n % NB]
        ck = ("cols", n % NB)
        Pt = PN[n % NB]
        pk = ("P", n % NB)
        p.tt("dve", cl[:, 4:5], cl[:, 2:3], cl[:, 3:4], ALU.add, [ck], [ck])
        p.op("dve", lambda e, cl=cl: e.reciprocal(out=cl[:, 5:6], in_=cl[:, 4:5]), [ck], [ck])
        Dt = D[n % 2]
        dk = ("D", n % 2)
        p.ts("dve", Dt[:], ident, cl[:, 5:6], None, ALU.mult, None, ["cm", ck], [dk])
        ppt, pptk = ring.next()
        for kb in range(3):
            p.mm(ppt[:, kb * 128:(kb + 1) * 128], Pt[:, kb * 128:(kb + 1) * 128], Dt[:], True, True, [pk, dk], [pptk])
        PTt = PT[n % 2]
        ptk = ("PT", n % 2)
        p.act(PTt[:], ppt[:, 0:384], AF.Copy, [pptk], [ptk])
        for kb in range(3):
            p.mm(po[:, qi * 128:(qi + 1) * 128], V[:, qb + kb, g * 128:(g + 1) * 128], PTt[:, kb * 128:(kb + 1) * 128],
                 kb == 0, kb == 2, [ptk, ("V", qb + kb)], [pok])
        if qi == 3:
            p.copy("dve", qT[:, h, grp * 512:(grp + 1) * 512], po[:], [pok], [("qT", h, grp)])
            p.dma(d["oT"][h * 128:(h + 1) * 128, grp * 512:(grp + 1) * 512], qT[:, h, grp * 512:(grp + 1) * 512],
                  reads=[("qT", h, grp)], writes=[("o_out", h, grp)])

    SKEW = 2
    for n in range(len(its) + SKEW):
        if n < len(its):
            stage_a(n)
        if n - SKEW >= 0:
            stage_b(n - SKEW)


def attn_consts(j):
    inv = (10000.0 ** (-np.arange(0, 128, 2, dtype=np.float32) / 128)).astype(np.float32)
    pos = (j * TOK - HALO + np.arange(TH)).astype(np.float32)
    ang = pos[:, None] * inv[None, :]
    cos = np.cos(ang).astype(np.float32).T
    sin = np.sin(ang).astype(np.float32).T
    cosT = np.concatenate([cos, cos], axis=0)
    sinT = np.concatenate([-sin, sin], axis=0)
    qi = np.arange(128)[:, None]
    kj = np.arange(384)[None, :]
    rel = kj - 128 - qi
    base = np.where(np.abs(rel) <= 128, 0.0, -30000.0).astype(np.float32)
    first = base.copy(); first[:, 0:128] = -30000.0
    last = base.copy(); last[:, 256:384] = -30000.0
    masks = np.stack([first if j == 0 else base, base, last if j == 3 else base], axis=1)
    ident = np.eye(128, dtype=np.float32)
    swap = np.zeros((128, 128), np.float32)
    mm = np.arange(128)
    swap[(mm + 64) % 128, mm] = 1.0
    cmats = np.stack([ident, swap], axis=1)
    return {"cosT": np.ascontiguousarray(cosT), "sinT": np.ascontiguousarray(sinT),
            "masks": np.ascontiguousarray(masks), "cmats": np.ascontiguousarray(cmats)}


def halo_xT(x_b, j):
    out = np.zeros((1024, TH), x_b.dtype)
    lo, hi = j * TOK - HALO, (j + 1) * TOK + HALO
    slo, shi = max(lo, 0), min(hi, S_LEN)
    out[:, slo - lo: shi - lo] = x_b[slo:shi].T
    return out


ALPHA = 8 ** 0.25
LN_EPS = 1e-5
N_EXP = 32
CAP = 256
NSLOT = N_EXP * CAP
NROWS = NSLOT + 128


def build_mix(x_dtype=F32):
    nc = bass.Bass("TRN2", target_bir_lowering=False)
    d = {}
    d["xT"] = nc.dram_tensor("xT", [1024, TOK], x_dtype, kind="ExternalInput").ap()
    d["x_tok"] = nc.dram_tensor("x_tok", [TOK, 1024], F32, kind="ExternalInput").ap()
    d["oT"] = nc.dram_tensor("oT", [1024, TOK], BF16, kind="ExternalInput").ap()
    d["hgT"] = nc.dram_tensor("hgT", [1024, TOK], BF16, kind="ExternalInput").ap()
    d["w4"] = nc.dram_tensor("w4", [4, 1024, 1024], F32, kind="ExternalInput").ap()
    d["w_out"] = nc.dram_tensor("w_out", [1024, 1024], F32, kind="ExternalInput").ap()
    d["ln"] = nc.dram_tensor("ln", [128, 2, 1024], F32, kind="ExternalInput").ap()
    d["w_rt"] = nc.dram_tensor("w_rt", [1024, 36], F32, kind="ExternalInput").ap()
    d["b_rt"] = nc.dram_tensor("b_rt", [128, 36], F32, kind="ExternalInput").ap()
    d["cst"] = nc.dram_tensor("cst", [128, 3, 128], F32, kind="ExternalInput").ap()
    d["cst2"] = nc.dram_tensor("cst2", [128, 40], F32, kind="ExternalInput").ap()
    d["x1"] = nc.dram_tensor("x1", [TOK, 1024], F32, kind="ExternalOutput").ap()
    d["xdisp"] = nc.dram_tensor("xdisp", [NROWS, 1024], BF16, kind="ExternalOutput").ap()
    d["slots"] = nc.dram_tensor("slots", [TOK, 2], I32, kind="ExternalOutput").ap()
    d["gates"] = nc.dram_tensor("gates", [TOK, 2], F32, kind="ExternalOutput").ap()
    p = Prog(nc)
    fk = emit_mix(p, d)
    p.finish(fk)
    return nc


def emit_ln(p, y, out, lnt, which, stat, reads, writes, tag):
    st6, mv, rs = stat
    sk = ("lnstat", tag)
    for hh in range(2):
        p.op("dve", lambda e, hh=hh: e.bn_stats(out=st6[:, hh * 6:(hh + 1) * 6], in_=y[:, hh * 512:(hh + 1) * 512]),
             reads + [sk], [sk])
    p.op("dve", lambda e: e.bn_aggr(out=mv[:, 0:2], in_=st6[:, 0:12]), [sk], [sk])
    p.act(rs[:, 0:1], mv[:, 1:2], AF.Sqrt, [sk], [sk], bias=LN_EPS)
    p.op("dve", lambda e: e.reciprocal(out=rs[:, 1:2], in_=rs[:, 0:1]), [sk], [sk])
    p.ts("dve", out, y, mv[:, 0:1], rs[:, 1:2], ALU.subtract, ALU.mult, reads + [sk], writes)
    p.tt("pool", out, out, lnt[:, which, 0, :], ALU.mult, writes + ["ln"], writes)
    p.tt("pool", out, out, lnt[:, which, 1, :], ALU.add, writes + ["ln"], writes)


def emit_mix(p, d, pfx="m"):
    ot = [p.sb(pfx + "ot%d" % i, [128, 8, 512], BF16) for i in range(1)] * 2
    hg = [p.sb(pfx + "hg%d" % i, [128, 8, 512], BF16) for i in range(1)] * 2
    xb = [p.sb(pfx + "xb%d" % i, [128, 8, 512], BF16) for i in range(1)] * 2
    w4r = p.sb(pfx + "w4r", [128, 4, 8, 1024], BF16)
    wo = p.sb(pfx + "wo", [128, 8, 1024], BF16)
    mg = [p.sb(pfx + "mg%d" % i, [128, 8, 512], BF16) for i in range(2)]
    t1 = [p.sb(pfx + "t1_%d" % i, [128, 512], F32) for i in range(2)]
    t2 = [p.sb(pfx + "t2_%d" % i, [128, 512], F32) for i in range(2)]
    lnt = p.sb(pfx + "ln", [128, 1, 2, 1024], F32)
    wrt = p.sb(pfx + "wrt", [128, 8, 36], F32)
    brt = p.sb(pfx + "brt", [128, 36], F32)
    cst = p.sb(pfx + "cst", [128, 3, 128], F32)
    cstb = p.sb(pfx + "cstb", [128, 2, 128], BF16)
    cst2 = p.sb(pfx + "cst2", [128, 40], F32)
    zero = p.sb(pfx + "zero", [128, 1024], BF16)
    xt = [p.sb(pfx + "xt%d" % i, [128, 1024], F32) for i in range(2)]
    y = [p.sb(pfx + "y%d" % i, [128, 1024], F32) for i in range(2)]
    x1 = [p.sb(pfx + "x1_%d" % i, [128, 1024], F32) for i in range(2)]
    x1b = [p.sb(pfx + "x1b%d" % i, [128, 1024], BF16) for i in range(2)]
    x1T = [p.sb(pfx + "x1T%d" % i, [128, 8, 128], F32) for i in range(2)]
    st6 = p.sb(pfx + "st6", [128, 12], F32)
    mv = p.sb(pfx + "mv", [128, 2], F32)
    rs = p.sb(pfx + "rs", [128, 2], F32)
    rt = [p.sb(pfx + "rt%d" % i, [128, 64], F32) for i in range(2)]
    E = [p.sb(pfx + "E%d" % i, [128, 3, 32], F32) for i in range(2)]
    Eb = [p.sb(pfx + "Eb%d" % i, [128, 32], BF16) for i in range(2)]
    base = p.sb(pfx + "base", [128, 32], F32)
    sl = [p.sb(pfx + "sl%d" % i, [128, 2], I32) for i in range(2)]
    gt = [p.sb(pfx + "gt%d" % i, [128, 2], F32) for i in range(2)]
    i8 = [p.sb(pfx + "i8_%d" % i, [128, 8], U32) for i in range(2)]
    ring = PsumRing(p, 8, pfx + "ps")
    ident = cst[:, 0, :]
    iota32 = cst2[:, 0:32]
    iota4 = cst2[:, 32:36]
    trash = cst2[:, 36:37]

    p.dma(lnt[:, 0, :, :], d["ln"], writes=["ln"])
    p.dma(wrt[:], d["w_rt"].rearrange("(c p) n -> p c n", p=128), writes=["wrt"])
    p.dma(brt[:], d["b_rt"], writes=["brt"])
    p.dma(cst[:], d["cst"], writes=["cst"])
    p.dma(cst2[:], d["cst2"], writes=["cst2"])
    p.copy("dve", cstb[:, 0, :], cst[:, 1, :], ["cst"], ["cstb"])
    p.copy("dve", cstb[:, 1, :], cst[:, 2, :], ["cst"], ["cstb"])
    p.memset("pool", zero[:], 0.0, ["zero"])
    p.memset("pool", base[:], 0.0, ["base"])
    zk = []
    for r0 in range(0, NROWS, 1024):
        nr = min(1024, NROWS - r0)
        p.dma(d["xdisp"][r0:r0 + nr, :].rearrange("(a p) n -> p a n", p=128),
              zero[:].partition_broadcast(128) if False else zero[:, None, :].to_broadcast([128, nr // 128, 1024]),
              reads=["zero"], writes=[("xdz", r0)])
        zk.append(("xdz", r0))
    for q in range(4):
        for c0 in range(0, 1024, 512):
            p.dma(w4r[:, q, :, c0:c0 + 512], d["w4"][q].rearrange("(c p) n -> p c n", p=128)[:, :, c0:c0 + 512],
                  writes=[("w4r", q, c0)], queue="pool")
    w4k = [[("w4r", q, 0), ("w4r", q, 512)] for q in range(4)]
    for c0 in range(0, 1024, 512):
        p.dma(wo[:, :, c0:c0 + 512], d["w_out"].rearrange("(c p) n -> p c n", p=128)[:, :, c0:c0 + 512], writes=[("wo", c0)], queue="pool")
    oTv = d["oT"].rearrange("(c p) n -> p c n", p=128)
    hgv = d["hgT"].rearrange("(c p) n -> p c n", p=128) if "hgT" in d else None
    xTv = (d["xT_own"] if "xT_own" in d else d["xT"]).rearrange("(c p) n -> p c n", p=128)
    fin = []
    wn = 0
    tile_i = 0
    for T in range(TOK // 512):
        b = T % 2
        p.dma(ot[b][:], oTv[:, :, T * 512:(T + 1) * 512], writes=[("ot", 0)])
        if "hg_all" in d:
            if T == 0:
                hat = d["hg_all"][128:128 + 4096, :].rearrange("(t q) n -> t q n", t=4)
                p.dma(None, None, reads=["hg_all"], writes=["hg_mine"], queue="act",
                      fn=lambda e: e.dma_start(out=d["hg_mine"], in_=hat[p.core_idx(e, "j")]))
            p.dma(hg[b][:], d["hg_mine"].rearrange("(c p) n -> p c n", p=128)[:, :, T * 512:(T + 1) * 512],
                  reads=["hg_mine"], writes=[("hg", 0)])
            p.dma(xb[b][:], xTv[:, :, T * 512:(T + 1) * 512], reads=["xT_own"], writes=[("xb", 0)])
        else:
            p.dma(hg[b][:], hgv[:, :, T * 512:(T + 1) * 512], writes=[("hg", 0)])
            p.dma(xb[b][:], xTv[:, :, T * 512:(T + 1) * 512], writes=[("xb", 0)], queue="pool")
        for m in range(8):
            banks = [ring.next() for _ in range(4)]
            srcs = [ot[b], hg[b], xb[b], xb[b]]
            skeys = [("ot", 0), ("hg", 0), ("xb", 0), ("xb", 0)]
            for q in range(4):
                pst, psk = banks[q]
                for k in range(8):
                    p.mm(pst[:], w4r[:, q, k, m * 128:(m + 1) * 128], srcs[q][:, k, :], k == 0, k == 7,
                         w4k[q] + [skeys[q]], [psk])
            a1, a2 = t1[m % 2], t2[m % 2]
            k1, k2 = ("t1", m % 2), ("t2", m % 2)
            p.act(a1[:], banks[2][0][:], AF.Sigmoid, [banks[2][1]], [k1])
            p.act(a2[:], banks[3][0][:], AF.Sigmoid, [banks[3][1]], [k2])
            p.tt("dve", a1[:], a1[:], banks[0][0][:], ALU.mult, [k1, banks[0][1]], [k1])
            p.tt("dve", a2[:], a2[:], banks[1][0][:], ALU.mult, [k2, banks[1][1]], [k2])
            p.tt("pool", mg[b][:, m, :], a1[:], a2[:], ALU.add, [k1, k2], [("mg", b, m)])
        mgk = [("mg", b, m) for m in range(8)]
        def stage_a(s, tile_i):
            tb = tile_i % 2
            tok0 = T * 512 + s * 128
            x1k = ("x1", tb)
            p.dma(xt[tb][:], d["x_tok"][tok0:tok0 + 128, :], writes=[("xt", tb)])
            for hh in range(2):
                pst, psk = ring.next()
                for k in range(8):
                    p.mm(pst[:], mg[b][:, k, s * 128:(s + 1) * 128], wo[:, k, hh * 512:(hh + 1) * 512], k == 0, k == 7,
                         mgk + [("wo", hh * 512)], [psk])
                p.stt(y[tb][:, hh * 512:(hh + 1) * 512], xt[tb][:, hh * 512:(hh + 1) * 512], ALPHA, pst[:], ALU.mult, ALU.add,
                      [("xt", tb), psk], [("y", tb, hh)])
            x1k = ("x1", tb)
            emit_ln(p, y[tb][:], x1[tb][:], lnt, 0, (st6, mv, rs), [("y", tb, 0), ("y", tb, 1)], [x1k], "a")
            p.dma(d["x1"][tok0:tok0 + 128, :], x1[tb][:], reads=[x1k], writes=[("x1o", tile_i)])
            fin.append(("x1o", tile_i))
            p.act(x1b[tb][:], x1[tb][:], AF.Copy, [x1k], [("x1b", tb)])

        def stage_b(s, tile_i):
            tb = tile_i % 2
            tok0 = T * 512 + s * 128
            x1k = ("x1", tb)
            for hh in range(2):
                pst, psk = ring.next()
                for c in range(4):
                    k = hh * 4 + c
                    p.tr(pst[:, c * 128:(c + 1) * 128], x1[tb][:, k * 128:(k + 1) * 128], ident, [x1k, "cst"], [psk])
                p.copy("dve", x1T[tb][:, hh * 4:(hh + 1) * 4, :], pst[:].rearrange("p (c n) -> p c n", c=4), [psk],
                       [("x1T", tb, hh)])
            pl, plk = ring.next()
            for k in range(8):
                p.mm(pl[:, 0:36], x1T[tb][:, k, :], wrt[:, k, :], k == 0, k == 7, [("x1T", tb, 0), ("x1T", tb, 1), "wrt"], [plk])
            r = rt[tb]
            rk = ("rt", tb)
            p.tt("dve", r[:, 0:36], pl[:, 0:36], brt[:], ALU.add, [plk, "brt", rk], [rk])
            p.op("dve", lambda e, r=r: e.reduce_max(out=r[:, 36:37], in_=r[:, 0:4], axis=AX.X), [rk], [rk])
            p.ts("dve", r[:, 37:38], r[:, 36:37], -1.0, None, ALU.mult, None, [rk], [rk])
            p.act(r[:, 44:48], r[:, 0:4], AF.Exp, [rk], [rk], bias=r[:, 37:38], accum_out=r[:, 38:39])
            p.op("dve", lambda e, r=r: e.reciprocal(out=r[:, 39:40], in_=r[:, 38:39]), [rk], [rk])
            p.ts("dve", r[:, 40:44], r[:, 0:4], r[:, 36:37], None, ALU.is_equal, None, [rk], [rk])
            p.ts("dve", r[:, 48:56], r[:, 4:12], r[:, 40:41], None, ALU.mult, None, [rk], [rk])
            for g in range(1, 4):
                p.stt(r[:, 48:56], r[:, 4 + 8 * g:12 + 8 * g], r[:, 40 + g:41 + g], r[:, 48:56], ALU.mult, ALU.add, [rk], [rk])
            Et = E[tb]
            ek = ("E", tb)
            p.tt("dve", Et[:, 2, 0:4], r[:, 40:44], iota4, ALU.mult, [rk, "cst2", ek], [ek])
            p.op("dve", lambda e, r=r, Et=Et: e.reduce_sum(out=r[:, 58:59], in_=Et[:, 2, 0:4], axis=AX.X), [rk, ek], [rk])
            m8 = r[:, 48:56]
            i8t = i8[tb]
            p.op("dve", lambda e, r=r, Et=Et: e.max(out=Et[:, 2, 8:16], in_=r[:, 48:56]), [rk, ek], [ek])
            p.op("dve", lambda e, r=r, Et=Et, i8t=i8t: e.max_index(out=i8t[:], in_max=Et[:, 2, 8:16], in_values=r[:, 48:56]),
                 [rk, ek], [("i8", tb)])
            g = gt[tb]
            gk = ("gt", tb)
            p.tt("dve", r[:, 56:57], Et[:, 2, 8:9], Et[:, 2, 9:10], ALU.subtract, [ek, rk], [rk])
            p.act(r[:, 57:58], r[:, 56:57], AF.Sigmoid, [rk], [rk])
            p.tt("dve", g[:, 0:1], r[:, 57:58], r[:, 39:40], ALU.mult, [rk, gk], [gk])
            p.tt("dve", g[:, 1:2], r[:, 39:40], g[:, 0:1], ALU.subtract, [rk, gk], [gk])
            p.copy("dve", r[:, 59:61], i8t[:, 0:2], [("i8", tb), rk], [rk])
            p.stt(r[:, 59:61], r[:, 58:59].to_broadcast([128, 2]), 8.0, r[:, 59:61], ALU.mult, ALU.add, [rk], [rk])
            for kk in range(2):
                p.ts("dve", Et[:, kk, :], iota32, r[:, 59 + kk:60 + kk], None, ALU.is_equal, None, ["cst2", rk, ek], [ek])
            p.tt("dve", Eb[tb][:], Et[:, 0, :], Et[:, 1, :], ALU.add, [ek], [("Eb", tb)])
            pc, pck = ring.next()
            p.mm(pc[:, 0:32], cstb[:, 0, :], Eb[tb][:], True, True, ["cstb", ("Eb", tb)], [pck])
            p.mm(pc[:, 32:64], cstb[:, 1, :], Eb[tb][:], True, True, ["cstb", ("Eb", tb)], [pck])
            p.tt("dve", Et[:, 2, :], pc[:, 0:32], base[:], ALU.add, [pck, "base", ek], [ek])
            for kk in range(2):
                p.tt("dve", Et[:, kk, :], Et[:, kk, :], Et[:, 2, :], ALU.mult, [ek], [ek])
                p.op("dve", lambda e, r=r, Et=Et, kk=kk: e.reduce_sum(out=r[:, 61 + kk:62 + kk], in_=Et[:, kk, :], axis=AX.X),
                     [ek, rk], [rk])
            p.tt("dve", base[:], base[:], pc[:, 32:64], ALU.add, [pck, "base"], ["base"])
            for kk in range(2):
                p.ts("dve", r[:, 63:64], r[:, 61 + kk:62 + kk], float(CAP), None, ALU.is_lt, None, [rk], [rk])
                p.stt(r[:, 61 + kk:62 + kk], r[:, 59 + kk:60 + kk], float(CAP), r[:, 61 + kk:62 + kk], ALU.mult, ALU.add,
                      [rk], [rk])
                p.tt("dve", r[:, 61 + kk:62 + kk], r[:, 61 + kk:62 + kk], trash, ALU.subtract, [rk, "cst2"], [rk])
                p.tt("dve", r[:, 61 + kk:62 + kk], r[:, 61 + kk:62 + kk], r[:, 63:64], ALU.mult, [rk], [rk])
                p.tt("dve", r[:, 61 + kk:62 + kk], r[:, 61 + kk:62 + kk], trash, ALU.add, [rk, "cst2"], [rk])
                p.tt("dve", g[:, kk:kk + 1], g[:, kk:kk + 1], r[:, 63:64], ALU.mult, [rk, gk], [gk])
            slt = sl[tb]
            slk = ("sl", tb)
            p.copy("dve", slt[:], r[:, 61:63], [rk], [slk])
            p.dma(d["slots"][tok0:tok0 + 128, :], slt[:], reads=[slk], writes=[("slo", tile_i)])
            p.dma(d["gates"][tok0:tok0 + 128, :], g[:], reads=[gk], writes=[("gto", tile_i)])
            fin.extend([("slo", tile_i), ("gto", tile_i)])
            for kk in range(2):
                p.dma(None, None, reads=[("x1b", tb), slk] + zk, writes=[("xdo", tile_i, kk)], queue="pool",
                      fn=lambda e, tb=tb, kk=kk, slt=slt: e.indirect_dma_start(
                          out=d["xdisp"], out_offset=bass.IndirectOffsetOnAxis(ap=slt[:, kk:kk + 1], axis=0),
                          in_=x1b[tb][:, :], in_offset=None))
                fin.append(("xdo", tile_i, kk))

        stage_a(0, T * 4)
        for s in range(4):
            if s + 1 < 4:
                stage_a(s + 1, T * 4 + s + 1)
            stage_b(s, T * 4 + s)
    return fin


def mix_consts():
    ident = np.eye(128, dtype=np.float32)
    tp = np.arange(128)[:, None]
    t = np.arange(128)[None, :]
    lower = (tp < t).astype(np.float32)
    ones = np.ones((128, 128), np.float32)
    cst = np.stack([ident, lower, ones], axis=1)
    cst2 = np.zeros((128, 40), np.float32)
    cst2[:, 0:32] = np.arange(32, dtype=np.float32)[None, :]
    cst2[:, 32:36] = np.arange(4, dtype=np.float32)[None, :]
    cst2[:, 36] = NSLOT + np.arange(128)
    return {"cst": np.ascontiguousarray(cst), "cst2": cst2}


def build_moe():
    nc = bass.Bass("TRN2", target_bir_lowering=False)
    d = {}
    d["xdisp"] = nc.dram_tensor("xdisp", [NROWS, 1024], BF16, kind="ExternalInput").ap()
    d["x1"] = nc.dram_tensor("x1", [TOK, 1024], F32, kind="ExternalInput").ap()
    d["slots"] = nc.dram_tensor("slots", [TOK, 2], I32, kind="ExternalInput").ap()
    d["gates"] = nc.dram_tensor("gates", [TOK, 2], F32, kind="ExternalInput").ap()
    d["w_g"] = nc.dram_tensor("w_g", [N_EXP, 1024, 512], F32, kind="ExternalInput").ap()
    d["w_u"] = nc.dram_tensor("w_u", [N_EXP, 1024, 512], F32, kind="ExternalInput").ap()
    d["w_d"] = nc.dram_tensor("w_d", [N_EXP, 512, 1024], F32, kind="ExternalInput").ap()
    d["ln"] = nc.dram_tensor("ln", [128, 2, 1024], F32, kind="ExternalInput").ap()
    d["ident"] = nc.dram_tensor("ident", [128, 128], F32, kind="ExternalInput").ap()
    d["ydisp"] = nc.dram_tensor("ydisp", [NROWS, 1024], F32).ap()
    d["x2"] = nc.dram_tensor("x2", [TOK, 1024], F32, kind="ExternalOutput").ap()
    p = Prog(nc)
    fk = emit_moe(p, d)
    p.finish(fk)
    return nc


def emit_moe(p, d, pfx="e"):
    NWB = 3
    wg = [p.sb(pfx + "wg%d" % i, [128, 8, 512], BF16) for i in range(NWB)]
    wu = [p.sb(pfx + "wu%d" % i, [128, 8, 512], BF16) for i in range(NWB)]
    wd = [p.sb(pfx + "wd%d" % i, [128, 4, 1024], BF16) for i in range(NWB)]
    xe = [p.sb(pfx + "xe%d" % i, [128, 1024], BF16) for i in range(2)]
    xeT = [p.sb(pfx + "xeT%d" % i, [128, 8, CAP], BF16) for i in range(2)]
    hT = [p.sb(pfx + "hT%d" % i, [128, 4, CAP], BF16) for i in range(2)]
    sg = [p.sb(pfx + "sg%d" % i, [128, CAP], F32) for i in range(2)]
    yt = [p.sb(pfx + "yt%d" % i, [128, 1024], F32) for i in range(2)]
    ident = p.sb(pfx + "ident", [128, 128], BF16)
    lnt = p.sb(pfx + "ln", [128, 1, 2, 1024], F32)
    zero = p.sb(pfx + "zero", [128, 1024], F32)
    sl = [p.sb(pfx + "sl%d" % i, [128, 2], I32) for i in range(2)]
    gt = [p.sb(pfx + "gt%d" % i, [128, 2], F32) for i in range(2)]
    x1t = [p.sb(pfx + "x1t%d" % i, [128, 1024], F32) for i in range(2)]
    ya = [p.sb(pfx + "ya%d" % i, [128, 1024], F32) for i in range(2)]
    yb = [p.sb(pfx + "yb%d" % i, [128, 1024], F32) for i in range(2)]
    yo = [p.sb(pfx + "yo%d" % i, [128, 1024], F32) for i in range(2)]
    st6 = p.sb(pfx + "st6", [128, 12], F32)
    mv = p.sb(pfx + "mv", [128, 2], F32)
    rs = p.sb(pfx + "rs", [128, 2], F32)
    if "xT_next" in d:
        xTn = [p.sb(pfx + "xTn%d" % i, [128, 8, 512], BF16) for i in range(2)]
        identf = p.sb(pfx + "identf", [128, 128], F32)
        p.dma(identf[:], d["ident"], writes=["identf"])
    ring = PsumRing(p, 6, pfx + "ps")
    ptr = [p.ps(pfx + "ptr%d" % i, [128, 1024], BF16) for i in range(2)]
    for i in range(2):
        p.exclusive.add((pfx + "ptr", i))

    p.dma(ident[:], d["ident"], writes=["ident"], queue="pool")
    p.dma(lnt[:, 0, :, :], d["ln"], writes=["ln"])
    p.memset("pool", zero[:], 0.0, ["zero"])
    p.dma(d["ydisp"][NSLOT:NROWS, :], zero[:], reads=["zero"], writes=[("yd", -1, 0)])
    ydk = [("yd", -1, 0)]
    nb = 0
    nstg = 0
    stg = [p.sb(pfx + "stg%d" % i, [128, 4096], F32) for i in range(2)]
    for e in range(N_EXP):
        b = e % NWB
        wk = ("w", b)
        for wi, (wsrc, wdst, wkey) in enumerate(((d["w_g"][e], wg[b], ("wg", b)), (d["w_u"][e], wu[b], ("wu", b)),
                                                 (d["w_d"][e], wd[b], ("wd", b)))):
            if wi < 2:
                p.dma(wdst[:], wsrc.rearrange("(c p) n -> p c n", p=128), writes=[wkey], queue="pool")
                continue
            sgi = nstg % 2
            nstg += 1
            nchunk = 4 if wi == 2 else 8
            p.dma(stg[sgi][:].rearrange("p (c n) -> p c n", c=nchunk), wsrc.rearrange("(c p) n -> p c n", p=128),
                  writes=[("stg", sgi)])
            dflat = wdst[:].rearrange("p c n -> p (c n)")
            if wi == 1:
                p.copy("pool", dflat, stg[sgi][:], [("stg", sgi)], [wkey])
            else:
                p.act(dflat, stg[sgi][:], AF.Copy, [("stg", sgi)], [wkey])
        for blk in range(CAP // 128):
            xb_ = xe[nb % 2]
            xk = ("xe", nb % 2)
            r0 = e * CAP + blk * 128
            p.dma(xb_[:], d["xdisp"][r0:r0 + 128, :], writes=[xk])
            pt = ptr[nb % 2]
            ptk = (pfx + "ptr", nb % 2)
            for k in range(8):
                p.tr(pt[:, k * 128:(k + 1) * 128], xb_[:, k * 128:(k + 1) * 128], ident[:], [xk, "ident"], [ptk])
            p.copy("dve" if blk else "act", xeT[e % 2][:, :, blk * 128:(blk + 1) * 128], pt[:].rearrange("p (c n) -> p c n", c=8),
                   [ptk], [("xeT", e % 2, blk)]) if blk else \
                p.act(xeT[e % 2][:, :, blk * 128:(blk + 1) * 128], pt[:].rearrange("p (c n) -> p c n", c=8), AF.Copy,
                      [ptk], [("xeT", e % 2, blk)])
            nb += 1
        xtk = [("xeT", e % 2, blk) for blk in range(CAP // 128)]
        for m in range(4):
            pg, pgk = ring.next()
            pu, puk = ring.next()
            for k in range(8):
                p.mm(pg[:, 0:CAP], wg[b][:, k, m * 128:(m + 1) * 128], xeT[e % 2][:, k, :], k == 0, k == 7, [("wg", b)] + xtk, [pgk])
            for k in range(8):
                p.mm(pu[:, 0:CAP], wu[b][:, k, m * 128:(m + 1) * 128], xeT[e % 2][:, k, :], k == 0, k == 7, [("wu", b)] + xtk, [puk])
            s_ = sg[m % 2]
            sk = ("sg", m % 2)
            p.act(s_[:], pg[:, 0:CAP], AF.Silu, [pgk], [sk])
            p.tt("dve", hT[e % 2][:, m, :], s_[:], pu[:, 0:CAP], ALU.mult, [sk, puk], [("hT", e % 2, m)])
        htk = [("hT", e % 2, m) for m in range(4)]
        for blk in range(CAP // 128):
            y_ = yt[blk % 2]
            yk = ("yt", blk % 2)
            for hh in range(2):
                py, pyk = ring.next()
                for k in range(4):
                    p.mm(py[:], hT[e % 2][:, k, blk * 128:(blk + 1) * 128], wd[b][:, k, hh * 512:(hh + 1) * 512], k == 0, k == 3,
                         htk + [("wd", b)], [pyk])
                if hh == 0:
                    p.act(y_[:, 0:512], py[:], AF.Copy, [pyk], [yk])
                else:
                    p.copy("dve", y_[:, 512:1024], py[:], [pyk], [yk])
            r0 = e * CAP + blk * 128
            p.dma(d["ydisp"][r0:r0 + 128, :], y_[:], reads=[yk], writes=[("yd", e, blk)])
            ydk.append(("yd", e, blk))
    fin = []

    def cload(t):
        b = t % 2
        tok0 = t * 128
        p.dma(sl[b][:], d["slots"][tok0:tok0 + 128, :], writes=[("sl", b)])
        p.dma(gt[b][:], d["gates"][tok0:tok0 + 128, :], writes=[("gt", b)])
        p.dma(x1t[b][:], d["x1"][tok0:tok0 + 128, :], writes=[("x1t", b)])
        for kk, dst in enumerate((ya[b], yb[b])):
            p.dma(None, None, reads=[("sl", b)] + ydk, writes=[("yab", b, kk)], queue="pool",
                  fn=lambda e, dst=dst, b=b, kk=kk: e.indirect_dma_start(
                      out=dst[:, :], out_offset=None, in_=d["ydisp"],
                      in_offset=bass.IndirectOffsetOnAxis(ap=sl[b][:, kk:kk + 1], axis=0)))
    cload(0)
    for t in range(TOK // 128):
        b = t % 2
        tok0 = t * 128
        if t + 1 < TOK // 128:
            cload(t + 1)
        fk_ = ("f", b)
        p.ts("dve", ya[b][:], ya[b][:], gt[b][:, 0:1], None, ALU.mult, None, [("yab", b, 0), ("gt", b)], [("yab", b, 0)])
        p.stt(ya[b][:], yb[b][:], gt[b][:, 1:2], ya[b][:], ALU.mult, ALU.add, [("yab", b, 0), ("yab", b, 1), ("gt", b)],
              [("yab", b, 0)])
        p.stt(ya[b][:], x1t[b][:], ALPHA, ya[b][:], ALU.mult, ALU.add, [("yab", b, 0), ("x1t", b)], [("yab", b, 0)])
        emit_ln(p, ya[b][:], yo[b][:], lnt, 0, (st6, mv, rs), [("yab", b, 0)], [("yo", b)], "b")
        p.dma(d["x2"][tok0:tok0 + 128, :], yo[b][:], reads=[("yo", b)], writes=[("x2o", t)])
        fin.append(("x2o", t))
        if "xT_next" in d:
            xn = xTn[(t // 4) % 2]
            xnk = ("xTn", (t // 4) % 2)
            for hh in range(2):
                pst, psk = ring.next()
                for c in range(4):
                    k = hh * 4 + c
                    p.tr(pst[:, c * 128:(c + 1) * 128], yo[b][:, k * 128:(k + 1) * 128], identf[:], [("yo", b), "identf"], [psk])
                p.copy("dve" if hh else "pool", xn[:, hh * 4:(hh + 1) * 4, (t % 4) * 128:(t % 4 + 1) * 128],
                       pst[:].rearrange("p (c n) -> p c n", c=4), [psk], [xnk]) if hh else \
                    p.act(xn[:, hh * 4:(hh + 1) * 4, (t % 4) * 128:(t % 4 + 1) * 128],
                          pst[:].rearrange("p (c n) -> p c n", c=4), AF.Copy, [psk], [xnk])
            if t % 4 == 3:
                T4 = t // 4
                p.dma(d["xT_next"].rearrange("(c p) n -> p c n", p=128)[:, :, T4 * 512:(T4 + 1) * 512], xn[:],
                      reads=[xnk], writes=[("xTn_out", T4)])
    return fin


_PROGS = {}


def _prog(name, builder):
    if name not in _PROGS:
        _PROGS[name] = builder()
    return _PROGS[name]


def _run(nc, in_maps):
    res = run_bass_kernel_spmd(nc, in_maps, core_ids=list(range(8)))
    return res.results


def kernel_unfused(x, w_in, w_sink, w_conv, b_conv, w_rec_gate, b_rec_gate, w_in_gate, b_in_gate, lru_lambda,
                   w_attn_o, w_rnn_o, w_out, ln_g, ln_b, w_router_group, b_router_group, w_router_expert, b_router_expert,
                   w_exp_gate, w_exp_up, w_exp_down):
    f = lambda a: np.asarray(a, dtype=np.float32)
    x = f(x)
    w_in, w_sink, w_conv, b_conv = f(w_in), f(w_sink), f(w_conv), f(b_conv)
    w_rec_gate, b_rec_gate, w_in_gate, b_in_gate, lru_lambda = f(w_rec_gate), f(b_rec_gate), f(w_in_gate), f(b_in_gate), f(lru_lambda)
    w_attn_o, w_rnn_o, w_out, ln_g, ln_b = f(w_attn_o), f(w_rnn_o), f(w_out), f(ln_g), f(ln_b)
    w_router_group, b_router_group = f(w_router_group), f(b_router_group)
    w_router_expert, b_router_expert = f(w_router_expert), f(b_router_expert)
    w_exp_gate, w_exp_up, w_exp_down = f(w_exp_gate), f(w_exp_up), f(w_exp_down)
    depth = w_in.shape[0]
    nc_r = _prog("rnn", build_rnn)
    nc_a = _prog("attn", build_attn)
    nc_m = _prog("mix", build_mix)
    nc_e = _prog("moe", build_moe)
    aconst = [attn_consts(j) for j in range(4)]
    mconst = mix_consts()
    ident = np.eye(128, dtype=np.float32)
    cores = [(c // 4, c % 4) for c in range(8)]
    for l in range(depth):
        xTs = [np.ascontiguousarray(x[b].T) for b in range(2)]
        maps = []
        for (b, j) in cores:
            m = pack_rnn_inputs(l, j, w_in, w_conv, b_conv, w_rec_gate, b_rec_gate, w_in_gate, b_in_gate, lru_lambda)
            m["xT"] = xTs[b]
            maps.append(m)
        res = _run(nc_r, maps)
        hgT = [np.concatenate([np.asarray(res[b * 4 + j]["hgT"]) for j in range(4)], axis=0) for b in range(2)]
        w_qkv = np.ascontiguousarray(w_in[l][:, 0:1536])
        sink = np.ascontiguousarray(np.broadcast_to(w_sink[l][None, :], (128, 8)))
        maps = []
        for (b, j) in cores:
            m = dict(aconst[j])
            m["xT"] = halo_xT(x[b], j)
            m["w_qkv"] = w_qkv
            m["sink"] = sink
            maps.append(m)
        res = _run(nc_a, maps)
        oT = [np.asarray(res[c]["oT"]) for c in range(8)]
        w4 = np.ascontiguousarray(np.stack([w_attn_o[l], w_rnn_o[l], w_in[l][:, 3584:4608], w_in[l][:, 4608:5632]]))
        ln1 = np.ascontiguousarray(np.broadcast_to(np.stack([ln_g[l, 0], ln_b[l, 0]])[None], (128, 2, 1024)))
        w_rt = np.ascontiguousarray(np.concatenate([w_router_group[l], w_router_expert[l]], axis=1))
        b_rt = np.ascontiguousarray(np.broadcast_to(np.concatenate([b_router_group[l], b_router_expert[l]])[None], (128, 36)))
        wo = np.ascontiguousarray(w_out[l])
        maps = []
        for c, (b, j) in enumerate(cores):
            m = dict(mconst)
            m["xT"] = np.ascontiguousarray(xTs[b][:, j * TOK:(j + 1) * TOK])
            m["x_tok"] = np.ascontiguousarray(x[b][j * TOK:(j + 1) * TOK])
            m["oT"] = oT[c]
            m["hgT"] = np.ascontiguousarray(hgT[b][:, j * TOK:(j + 1) * TOK])
            m["w4"] = w4
            m["w_out"] = wo
            m["ln"] = ln1
            m["w_rt"] = w_rt
            m["b_rt"] = b_rt
            maps.append(m)
        res = _run(nc_m, maps)
        ln2 = np.ascontiguousarray(np.broadcast_to(np.stack([ln_g[l, 1], ln_b[l, 1]])[None], (128, 2, 1024)))
        wg_, wu_, wd_ = np.ascontiguousarray(w_exp_gate[l]), np.ascontiguousarray(w_exp_up[l]), np.ascontiguousarray(w_exp_down[l])
        maps = []
        for c in range(8):
            maps.append({"xdisp": np.asarray(res[c]["xdisp"]), "x1": np.asarray(res[c]["x1"]),
                         "slots": np.asarray(res[c]["slots"]), "gates": np.asarray(res[c]["gates"]),
                         "w_g": wg_, "w_u": wu_, "w_d": wd_, "ln": ln2, "ident": ident})
        res = _run(nc_e, maps)
        x = np.stack([np.concatenate([np.asarray(res[b * 4 + j]["x2"]) for j in range(4)], axis=0) for b in range(2)])
    return np.ascontiguousarray(x.astype(np.float32))


GROUPS4 = [[0, 1, 2, 3], [4, 5, 6, 7]]


def build_fused(depth=4):
    nc = bass.Bass("TRN2", target_bir_lowering=False)
    L = depth

    def ext(name, shape, dt=F32):
        return nc.dram_tensor(name, list(shape), dt, kind="ExternalInput").ap()

    def internal(name, shape, dt):
        return nc.dram_tensor(name, list(shape), dt).ap()
    I = {}
    I["x_tok0"] = ext("x_tok0", [TOK, 1024])
    I["xT0"] = ext("xT0", [1024, TOK])
    I["w_r"] = ext("w_r", [L, 1024, 512])
    I["w_gt"] = ext("w_gt", [L, 128, 8, 128])
    I["small"] = ext("small", [L, 128, 2, NSM])
    I["w_qkv"] = ext("w_qkv", [L, 1024, 1536])
    I["cosT"] = ext("cosT", [128, TH])
    I["sinT"] = ext("sinT", [128, TH])
    I["masks"] = ext("masks", [128, 3, 384])
    I["cmats"] = ext("cmats", [128, 2, 128])
    I["sink"] = ext("sink", [L, 128, 8])
    I["w4"] = ext("w4", [L, 4, 1024, 1024])
    I["w_out"] = ext("w_out", [L, 1024, 1024])
    I["ln1"] = ext("ln1", [L, 128, 2, 1024])
    I["ln2"] = ext("ln2", [L, 128, 2, 1024])
    I["w_rt"] = ext("w_rt", [L, 1024, 36])
    I["b_rt"] = ext("b_rt", [L, 128, 36])
    I["cst"] = ext("cst", [128, 3, 128])
    I["cst2"] = ext("cst2", [128, 40])
    I["w_eg"] = ext("w_eg", [L, N_EXP, 1024, 512])
    I["w_eu"] = ext("w_eu", [L, N_EXP, 1024, 512])
    I["w_ed"] = ext("w_ed", [L, N_EXP, 512, 1024])
    I["ident"] = ext("ident", [128, 128])
    out = nc.dram_tensor("out", [TOK, 1024], F32, kind="ExternalOutput").ap()
    xT_own = [internal("xT_own%d" % i, [1024, TOK], BF16) for i in range(2)]
    xT_all = [internal("xT_all%d" % i, [128 + 4096, TOK], BF16) for i in range(2)]
    x_tok_i = [internal("x_tok_i%d" % i, [TOK, 1024], F32) for i in range(2)]
    hg_own = [internal("hg_own%d" % i, [4, 256, TOK], BF16) for i in range(2)]
    hg_all = [internal("hg_all%d" % i, [128 + 4096, TOK], BF16) for i in range(2)]
    halo_l = internal("halo_l", [1024, HALO], BF16)
    halo_r = internal("halo_r", [1024, HALO], BF16)
    hg_mine = internal("hg_mine", [1024, TOK], BF16)
    oT = internal("oT_i", [1024, TOK], BF16)
    x1 = internal("x1_i", [TOK, 1024], F32)
    xdisp = internal("xdisp_i", [NROWS, 1024], BF16)
    slots = internal("slots_i", [TOK, 2], I32)
    gates = internal("gates_i", [TOK, 2], F32)
    ydisp = internal("ydisp_i", [NROWS, 1024], F32)

    p = Prog(nc)
    p.begin_phase()
    st = [p.sb("pro%d" % i, [128, 8, 512], BF16) for i in range(2)]
    src = I["xT0"].rearrange("(c p) n -> p c n", p=128)
    dst = xT_own[0].rearrange("(c p) n -> p c n", p=128)
    for t in range(TOK // 512):
        p.dma(st[t % 2][:], src[:, :, t * 512:(t + 1) * 512], writes=[("pro", t % 2)], queue="pool")
        p.dma(dst[:, :, t * 512:(t + 1) * 512], st[t % 2][:], reads=[("pro", t % 2)], writes=[("xT_own_w", t)])
    for k in range(4):
        p.coll("AllGather", GROUPS4, xT_own[0][k * 256:(k + 1) * 256, :], xT_all[0][128 + k * 1024:128 + (k + 1) * 1024, :],
               reads=[("xT_own_w", t) for t in range(TOK // 512)], writes=[("xT_all", k)])
    p.end_phase()
    for l in range(L):
        par = l % 2
        last = (l == L - 1)
        p.begin_phase()
        if l > 0:
            for k in range(4):
                p.coll("AllGather", GROUPS4, xT_own[par][k * 256:(k + 1) * 256, :],
                       xT_all[par][128 + k * 1024:128 + (k + 1) * 1024, :], reads=[], writes=[("xT_all", k)])
        emit_attn(p, {"xT_own": xT_own[par], "xT_all": xT_all[par], "w_qkv": I["w_qkv"][l], "cosT": I["cosT"],
                      "sinT": I["sinT"], "masks": I["masks"], "cmats": I["cmats"], "sink": I["sink"][l], "oT": oT,
                      "halo_l": halo_l, "halo_r": halo_r},
                  pfx="a%d" % l)
        p.end_phase()
        p.begin_phase()
        emit_rnn(p, xT_all[par], I["w_r"][l], I["w_gt"][l], I["small"][l], hg_own[par], pfx="r%d" % l)
        for t in range(4):
            p.coll("AllGather", GROUPS4, hg_own[par][t], hg_all[par][128 + t * 1024:128 + (t + 1) * 1024, :],
                   reads=[("hg_out", b, t) for b in range(2)], writes=[("hg_all", t)])
        p.end_phase()
        p.begin_phase()
        emit_mix(p, {"xT_own": xT_own[par], "x_tok": (I["x_tok0"] if l == 0 else x_tok_i[par]), "oT": oT,
                     "hg_all": hg_all[par], "hg_mine": hg_mine, "w4": I["w4"][l], "w_out": I["w_out"][l], "ln": I["ln1"][l],
                     "w_rt": I["w_rt"][l], "b_rt": I["b_rt"][l], "cst": I["cst"], "cst2": I["cst2"],
                     "x1": x1, "xdisp": xdisp, "slots": slots, "gates": gates}, pfx="m%d" % l)
        p.end_phase()
        p.begin_phase()
        dd = {"xdisp": xdisp, "x1": x1, "slots": slots, "gates": gates, "w_g": I["w_eg"][l], "w_u": I["w_eu"][l],
              "w_d": I["w_ed"][l], "ln": I["ln2"][l], "ident": I["ident"], "ydisp": ydisp,
              "x2": (out if last else x_tok_i[1 - par])}
        if not last:
            dd["xT_next"] = xT_own[1 - par]
        emit_moe(p, dd, pfx="e%d" % l)
        p.end_phase()
    p.finish([])
    return nc


def fused_inputs(depth, x, w_in, w_sink, w_conv, b_conv, w_rec_gate, b_rec_gate, w_in_gate, b_in_gate, lru_lambda,
                 w_attn_o, w_rnn_o, w_out, ln_g, ln_b, w_router_group, b_router_group, w_router_expert, b_router_expert,
                 w_exp_gate, w_exp_up, w_exp_down):
    L = depth
    ca = np.ascontiguousarray
    shared = {}
    shared["w_qkv"] = ca(w_in[:L, :, 0:1536])
    shared["sink"] = ca(np.broadcast_to(w_sink[:L, None, :], (L, 128, 8)))
    shared["w4"] = ca(np.stack([np.stack([w_attn_o[l], w_rnn_o[l], w_in[l][:, 3584:4608], w_in[l][:, 4608:5632]]) for l in range(L)]))
    shared["w_out"] = ca(w_out[:L])
    shared["ln1"] = ca(np.broadcast_to(np.stack([ln_g[:L, 0], ln_b[:L, 0]], axis=1)[:, None], (L, 128, 2, 1024)))
    shared["ln2"] = ca(np.broadcast_to(np.stack([ln_g[:L, 1], ln_b[:L, 1]], axis=1)[:, None], (L, 128, 2, 1024)))
    shared["w_rt"] = ca(np.concatenate([w_router_group[:L], w_router_expert[:L]], axis=2))
    shared["b_rt"] = ca(np.broadcast_to(np.concatenate([b_router_group[:L], b_router_expert[:L]], axis=1)[:, None, :], (L, 128, 36)))
    shared["w_eg"] = ca(w_exp_gate[:L])
    shared["w_eu"] = ca(w_exp_up[:L])
    shared["w_ed"] = ca(w_exp_down[:L])
    shared["ident"] = np.eye(128, dtype=np.float32)
    shared.update(mix_consts())
    rn = []
    for j in range(4):
        packs = [pack_rnn_inputs(l, j, w_in, w_conv, b_conv, w_rec_gate, b_rec_gate, w_in_gate, b_in_gate, lru_lambda)
                 for l in range(L)]
        rn.append({"w_r": ca(np.stack([q["w_r"] for q in packs])), "w_gt": ca(np.stack([q["w_g"] for q in packs])),
                   "small": ca(np.stack([q["small"] for q in packs]))})
    maps = []
    for c in range(8):
        b, j = c // 4, c % 4
        m = dict(shared)
        m.update(rn[j])
        m.update(attn_consts(j))
        xs = x[b][j * TOK:(j + 1) * TOK]
        m["x_tok0"] = ca(xs)
        m["xT0"] = ca(xs.T)
        maps.append(m)
    return maps


def kernel_fused(depth, **inp):
    key = "fused%d" % depth
    nc = _prog(key, lambda: build_fused(depth))
    names = ["x", "w_in", "w_sink", "w_conv", "b_conv", "w_rec_gate", "b_rec_gate", "w_in_gate", "b_in_gate", "lru_lambda",
             "w_attn_o", "w_rnn_o", "w_out", "ln_g", "ln_b", "w_router_group", "b_router_group", "w_router_expert",
             "b_router_expert", "w_exp_gate", "w_exp_up", "w_exp_down"]
    args = [np.asarray(inp[n], dtype=np.float32) for n in names]
    maps = fused_inputs(depth, *args)
    res = _run(nc, maps)
    x = np.stack([np.concatenate([np.asarray(res[b * 4 + j]["out"]) for j in range(4)], axis=0) for b in range(2)])
    return np.ascontiguousarray(x.astype(np.float32))


def kernel(**inputs):
    return kernel_fused(4, **inputs)
```

```python
import contextlib
import numpy as np
import concourse.bass as bass
import concourse.mybir as mybir
from concourse.bass_utils import run_bass_kernel_spmd

F32 = mybir.dt.float32
BF16 = mybir.dt.bfloat16
I32 = mybir.dt.int32
U32 = mybir.dt.uint32
AF = mybir.ActivationFunctionType
ALU = mybir.AluOpType
AX = mybir.AxisListType


class Prog:
    ENG = ("pe", "dve", "act", "pool", "sp")

    def __init__(self, nc, n_slots=8):
        self.nc = nc
        self.es = contextlib.ExitStack()
        self.q = {e: [] for e in self.ENG}
        self.cnt = {e: 0 for e in self.ENG}
        self.sem = {e: self.es.enter_context(nc.semaphore("s_" + e)) for e in self.ENG}
        self.n_slots = n_slots
        self.pool_slots = 2
        self.dq = ("sp", "act", "pool")
        self.dsem = {q: [self.es.enter_context(nc.semaphore("d_%s%d" % (q, i))) for i in range(n_slots)]
                     for q in self.dq}
        self.dn = {q: 0 for q in self.dq}
        self.sems = {}
        for e in self.ENG:
            self.sems[("c", e)] = self.sem[e]
        for q in self.dq:
            for i in range(n_slots):
                self.sems[("d", q, i)] = self.dsem[q][i]
        self.lastw = {}
        self.readers = {}
        self.waited = {e: {} for e in self.ENG}
        self.n_ops = 0
        self.exclusive = set()
        self.csem = []
        self.ph = None
        self.latest = {}
        self._cidx = {}

    def sb(self, name, shape, dtype):
        st = self.ph if self.ph is not None else self.es
        return st.enter_context(self.nc.sbuf_tensor(name, list(shape), dtype))

    def ps(self, name, shape, dtype):
        st = self.ph if self.ph is not None else self.es
        return st.enter_context(self.nc.psum_tensor(name, list(shape), dtype))

    def core_idx(self, e, which):
        key = id(e)
        if key not in self._cidx:
            pid = e.partition_id()
            vals = {}
            for name, off in (("j", 0), ("jl", 3), ("jr", 5)):
                vals[name] = e.snap((pid + off) % 4, min_val=0, max_val=3)
            self._cidx[key] = vals
        return self._cidx[key][which]

    def begin_phase(self):
        assert self.ph is None
        self.ph = contextlib.ExitStack()
        self.exclusive = set()

    def _barrier(self):
        for e in self.ENG:
            waits = []
            for sk, v in self.latest.items():
                if sk == ("c", e):
                    continue
                if self.waited[e].get(sk, 0) < v:
                    self.waited[e][sk] = v
                    waits.append((sk, v))
            if waits:
                self.q[e].append((waits, None, None, 0))
        self.lastw = {}
        self.readers = {}

    def _emit_block(self):
        nc = self.nc
        engobj = {"pe": "tensor", "dve": "vector", "act": "scalar", "pool": "gpsimd", "sp": "sync"}
        with nc.Block() as block:
            for e in self.ENG:
                items = self.q[e]

                def body(eng, items=items):
                    for waits, fn, sk, amt in items:
                        for wsk, v in waits:
                            eng.wait_ge(self.sems[wsk], v)
                        if fn is not None:
                            ins = fn(eng)
                            if amt is None:
                                ins.then_inc(self.sems[sk])
                            else:
                                ins.then_inc(self.sems[sk], amt)
                getattr(block, engobj[e])(body)
        self.q = {e: [] for e in self.ENG}

    def end_phase(self):
        self._barrier()
        self._emit_block()
        self.ph.close()
        self.ph = None

    def _deps(self, eng, reads, writes, is_dma):
        need = {}

        def add(d):
            for sk, v in d.items():
                if need.get(sk, 0) < v:
                    need[sk] = v
        for k in reads:
            if k in self.lastw:
                add(self.lastw[k])
        for k in writes:
            if k in self.lastw:
                add(self.lastw[k])
            if k in self.readers:
                add(self.readers[k])
        out = []
        for sk, v in need.items():
            if (not is_dma) and eng == "pe" and sk == ("c", "pe"):
                continue
            if self.waited[eng].get(sk, 0) >= v:
                continue
            self.waited[eng][sk] = v
            out.append((sk, v))
        return out

    def _mark(self, reads, writes, tok):
        sk, v = tok
        if self.latest.get(sk, 0) < v:
            self.latest[sk] = v
        for k in reads:
            r = self.readers.setdefault(k, {})
            if r.get(sk, 0) < v:
                r[sk] = v
        for k in writes:
            self.lastw[k] = {sk: v}
            self.readers[k] = {}

    def _excl(self, reads, writes):
        ex = [k for k in reads if k in self.exclusive]
        if ex:
            writes = list(writes) + [k for k in ex if k not in writes]
        return reads, writes

    def op(self, eng, fn, reads=(), writes=()):
        reads, writes = self._excl(reads, writes)
        waits = self._deps(eng, reads, writes, False)
        self.cnt[eng] += 1
        tok = (("c", eng), self.cnt[eng])
        self.q[eng].append((waits, fn, tok[0], 1))
        self._mark(reads, writes, tok)
        self.n_ops += 1

    def dma(self, out, in_, reads=(), writes=(), queue="sp", fn=None):
        n = self.dn[queue]
        self.dn[queue] += 1
        ns = self.n_slots if queue != "pool" else min(self.n_slots, self.pool_slots)
        slot = n % ns
        sk = ("d", queue, slot)
        waits = self._deps(queue, reads, writes, True)
        prev = 16 * (n // ns)
        if prev > 0 and self.waited[queue].get(sk, 0) < prev:
            self.waited[queue][sk] = prev
            waits.append((sk, prev))
        if fn is None:
            def fn(e, out=out, in_=in_):
                return e.dma_start(out=out, in_=in_)
        tok = (sk, prev + 16)
        self.q[queue].append((waits, fn, sk, 16))
        self._mark(reads, writes, tok)
        self.n_ops += 1

    def mm(self, out, lhsT, rhs, start, stop, reads, writes):
        self.op("pe", lambda e: e.matmul(out, lhsT=lhsT, rhs=rhs, start=start, stop=stop), reads, writes)

    def tr(self, out, in_, ident, reads, writes):
        self.op("pe", lambda e: e.transpose(out=out, in_=in_, identity=ident), reads, writes)

    def act(self, out, in_, func, reads, writes, scale=1.0, bias=None, accum_out=None):
        kw = {}
        if bias is not None:
            kw["bias"] = bias
        if accum_out is not None:
            kw["accum_out"] = accum_out
        self.op("act", lambda e: e.activation(out=out, in_=in_, func=func, scale=scale, **kw), reads, writes)

    def ts(self, eng, out, in0, s1, s2, op0, op1, reads, writes, accum_out=None):
        kw = {}
        if accum_out is not None:
            kw["accum_out"] = accum_out
        if op1 is None:
            self.op(eng, lambda e: e.tensor_scalar(out=out, in0=in0, scalar1=s1, scalar2=None, op0=op0, **kw), reads, writes)
        else:
            self.op(eng, lambda e: e.tensor_scalar(out=out, in0=in0, scalar1=s1, scalar2=s2, op0=op0, op1=op1, **kw),
                    reads, writes)

    def tt(self, eng, out, in0, in1, op, reads, writes):
        self.op(eng, lambda e: e.tensor_tensor(out=out, in0=in0, in1=in1, op=op), reads, writes)

    def stt(self, out, in0, scalar, in1, op0, op1, reads, writes):
        self.op("dve", lambda e: e.scalar_tensor_tensor(out=out, in0=in0, scalar=scalar, in1=in1, op0=op0, op1=op1),
                reads, writes)

    def scan(self, out, d0, d1, init, reads, writes):
        self.op("dve", lambda e: e.tensor_tensor_scan(out=out, data0=d0, data1=d1, initial=init, op0=ALU.mult, op1=ALU.add),
                reads, writes)

    def copy(self, eng, out, in_, reads, writes):
        self.op(eng, lambda e: e.tensor_copy(out=out, in_=in_), reads, writes)

    def memset(self, eng, ap, val, writes):
        self.op(eng, lambda e: e.memset(ap, val), (), writes)

    def coll(self, kind, groups, in_ap, out_ap, reads=(), writes=()):
        idx = len(self.csem)
        sem = self.es.enter_context(self.nc.semaphore("cc%d" % idx))
        self.csem.append(sem)
        sk = ("k", idx)
        self.sems[sk] = sem
        waits = self._deps("pool", reads, writes, True)

        def fn(e):
            return e.collective_compute(kind, ALU.bypass, replica_groups=groups, ins=[in_ap], outs=[out_ap])
        self.q["pool"].append((waits, fn, sk, None))
        self._mark(reads, writes, (sk, 1))
        self.n_ops += 1

    def finish(self, final_keys):
        self._barrier()
        self._emit_block()
        if self.ph is not None:
            self.ph.close()
            self.ph = None
        self.es.close()


def _rev(ap2d, n):
    apl = [list(s) for s in ap2d.ap]
    assert len(apl) == 2 and apl[1][1] == n and apl[1][0] == 1, apl
    from concourse.ap import AP
    return AP(ap2d.tensor, ap2d.offset + (n - 1), [apl[0], [-1, n]])


class PsumRing:
    def __init__(self, p, n=8, name="ps"):
        self.p = p
        self.t = [p.ps("%s%d" % (name, i), [128, 512], F32) for i in range(n)]
        for i in range(n):
            p.exclusive.add((name, i))
        self.i = 0
        self.n = n
        self.name = name

    def next(self):
        i = self.i
        self.i = (self.i + 1) % self.n
        return self.t[i], (self.name, i)


S_LEN = 8192
NSM = 11
GELU_C = 1.5957691216057308


def build_rnn(x_dtype=F32):
    nc = bass.Bass("TRN2", target_bir_lowering=False)
    xT = nc.dram_tensor("xT", [1024, S_LEN], x_dtype, kind="ExternalInput").ap()
    w_r = nc.dram_tensor("w_r", [1024, 512], F32, kind="ExternalInput").ap()
    w_g = nc.dram_tensor("w_g", [128, 8, 128], F32, kind="ExternalInput").ap()
    small = nc.dram_tensor("small", [128, 2, NSM], F32, kind="ExternalInput").ap()
    hgT = nc.dram_tensor("hgT", [256, S_LEN], BF16, kind="ExternalOutput").ap()
    p = Prog(nc)
    emit_rnn(p, xT, w_r, w_g, small, hgT)
    p.finish([("hg_out", b, c) for b in range(2) for c in range(4)])
    return nc


def emit_rnn(p, xT, w_r, w_g, small, hgT, pfx="r"):
    T = S_LEN
    CH = 2048
    NCH = T // CH
    wb = p.sb(pfx + "wb", [128, 8, 512], BF16)
    wg = p.sb(pfx + "wg", [128, 8, 128], BF16)
    sm = p.sb(pfx + "sm", [128, 2, NSM], F32)
    cl = p.sb(pfx + "cl", [128, 2, 2, 2], F32)
    zt = p.sb(pfx + "zt", [128, 4], F32)
    pt = p.sb(pfx + "pt", [128, 4], F32)
    xb = [p.sb(pfx + "xb%d" % i, [128, 8, 512], BF16) for i in range(2)]
    xr_full = p.sb(pfx + "xrf", [128, T + 4], F32)
    gy = p.sb(pfx + "gy", [128, T], BF16)
    xc = p.sb(pfx + "xc", [128, T], F32)
    xcb = p.sb(pfx + "xcb", [128, T], BF16)
    g1 = [p.sb(pfx + "g1_%d" % i, [128, 512], F32) for i in range(2)]
    g2 = [p.sb(pfx + "g2_%d" % i, [128, 512], F32) for i in range(2)]
    rtL = [p.sb(pfx + "rt%d" % i, [128, CH], F32) for i in range(2)]
    itL = [p.sb(pfx + "it%d" % i, [128, CH], F32) for i in range(2)]
    atL = [p.sb(pfx + "at%d" % i, [128, CH], F32) for i in range(2)]
    ccn = [0]
    hb = [p.sb(pfx + "hb%d" % i, [128, CH], F32) for i in range(2)]
    hgb = [p.sb(pfx + "hgb%d" % i, [128, CH], BF16) for i in range(2)]
    ring = PsumRing(p, 8, pfx + "ps")

    p.dma(wb[:], w_r.rearrange("(c p) n -> p c n", p=128), writes=["wb"], queue="pool")
    p.dma(wg[:], w_g, writes=["wg"], queue="pool")
    p.dma(sm[:], small, writes=["sm"])
    for b in range(2):
        for d in range(2):
            j = b * 2 + d
            p.act(zt[:, j:j + 1], sm[:, b, 5 + 3 * d + 2: 5 + 3 * d + 3], AF.Exp, ["sm"], ["zt"], scale=-1.0)
    p.ts("dve", pt[:], zt[:], -1.0 / 6, 1.0 / 5, ALU.mult, ALU.add, ["zt"], ["pt"])
    for cst in (-1.0 / 4, 1.0 / 3, -1.0 / 2, 1.0):
        p.tt("dve", pt[:], pt[:], zt[:], ALU.mult, ["pt", "zt"], ["pt"])
        p.ts("dve", pt[:], pt[:], cst, None, ALU.add, None, ["pt"], ["pt"])
    p.tt("dve", pt[:], pt[:], zt[:], ALU.mult, ["pt", "zt"], ["pt"])
    for b in range(2):
        for d in range(2):
            j = b * 2 + d
            p.ts("dve", cl[:, b, d, 0:1], pt[:, j:j + 1], -8.0, None, ALU.mult, None, ["pt"], ["cl"])
            p.ts("dve", cl[:, b, d, 1:2], pt[:, j:j + 1], -16.0, None, ALU.mult, None, ["pt"], ["cl"])
    chunked = (xT.shape[0] == 4096 + 128)
    if chunked:
        xTv = xT[128:128 + 4096, :].rearrange("(k r c p) n -> p r k c n", k=4, r=4, c=2, p=128)
    else:
        xTv = xT.rearrange("(c p) n -> p c n", p=128)

    def xrk(c):
        return [("xr", t) for t in range(4 * c, 4 * c + 4)]

    for blk in range(2):
        p.memset("pool", xr_full[:, 0:2], 0.0, [("xr", -1)])
        p.memset("pool", xr_full[:, T + 2:T + 4], 0.0, [("xr", 16)])
        for t in range(T // 512):
            xbt = xb[t % 2]
            xbk = ("xb", t % 2)
            if chunked:
                for k in range(4):
                    p.dma(xbt[:, 2 * k:2 * k + 2, :], xTv[:, t // 4, k, :, (t % 4) * 512:(t % 4 + 1) * 512],
                          reads=["xT_all"], writes=[xbk])
            else:
                p.dma(xbt[:], xTv[:, :, t * 512:(t + 1) * 512], writes=[xbk], queue="pool")
            for m in range(2):
                pst, psk = ring.next()
                col = (0 if m == 0 else 256) + blk * 128
                for k in range(8):
                    p.mm(pst[:], wb[:, k, col:col + 128], xbt[:, k, :], k == 0, k == 7, ["wb", xbk], [psk])
                if m == 0:
                    p.act(xr_full[:, 2 + t * 512: 2 + (t + 1) * 512], pst[:], AF.Copy, [psk], [("xr", t)])
                else:
                    a1, a2 = g1[t % 2], g2[t % 2]
                    k1, k2 = ("g1", t % 2), ("g2", t % 2)
                    p.act(a1[:], pst[:], AF.Square, [psk], [k1])
                    p.ts("dve", a1[:], a1[:], 0.044715, 1.0, ALU.mult, ALU.add, [k1], [k1])
                    p.tt("dve", a1[:], a1[:], pst[:], ALU.mult, [k1, psk], [k1])
                    p.act(a2[:], a1[:], AF.Sigmoid, [k1], [k2], scale=GELU_C)
                    p.tt("dve", gy[:, t * 512:(t + 1) * 512], a2[:], pst[:], ALU.mult, [k2, psk], [("gy", t)])
        for c in range(NCH):
            o = c * CH
            rk = [("xr", t) for t in range(4 * c - 1, 4 * c + 5)]
            p.ts("dve", xc[:, o:o + CH], xr_full[:, o:o + CH], sm[:, blk, 0:1], sm[:, blk, 4:5], ALU.mult, ALU.add,
                 rk + ["sm"], [("xc", c)])
            for tap in range(1, 4):
                p.stt(xc[:, o:o + CH], xr_full[:, o + tap:o + tap + CH], sm[:, blk, tap:tap + 1], xc[:, o:o + CH],
                      ALU.mult, ALU.add, rk + ["sm", ("xc", c)], [("xc", c)])
            p.copy("pool", xcb[:, o:o + CH], xc[:, o:o + CH], [("xc", c)], [("xcb", c)])
        hf = xr_full
        for d in range(2):
            order = list(range(NCH)) if d == 0 else list(range(NCH - 1, -1, -1))
            prev = None
            for ci, c in enumerate(order):
                o = c * CH
                cp = ccn[0] % 2
                ccn[0] += 1
                rt, it, at = rtL[cp], itL[cp], atL[cp]
                atk = ("at", cp)
                for s in range(CH // 512):
                    for g in range(2):
                        pst, psk = ring.next()
                        p.mm(pst[:], wg[:, blk * 4 + d * 2 + g, :], xcb[:, o + s * 512: o + (s + 1) * 512], True, True,
                             ["wg", ("xcb", c)], [psk])
                        dst = rt if g == 0 else it
                        p.act(dst[:, s * 512:(s + 1) * 512], pst[:], AF.Sigmoid, [psk, "sm"],
                              [("rt" if g == 0 else "it", cp, s)], bias=sm[:, blk, 5 + 3 * d + g: 5 + 3 * d + g + 1])
                rtk = [("rt", cp, s) for s in range(4)]
                itk = [("it", cp, s) for s in range(4)]
                p.act(at[:], rt[:], AF.Exp, rtk + ["cl"], [atk], scale=cl[:, blk, d, 0:1])
                p.act(rt[:], rt[:], AF.Exp, rtk + ["cl"], rtk, scale=cl[:, blk, d, 1:2])
                p.act(rt[:], rt[:], AF.Sqrt, rtk, rtk, scale=-1.0, bias=1.0)
                p.tt("pool", it[:], it[:], xc[:, o:o + CH], ALU.mult, itk + [("xc", c)], itk)
                p.tt("pool", it[:], it[:], rt[:], ALU.mult, itk + rtk, itk)
                if d == 0:
                    init = 0.0 if ci == 0 else hf[:, 2 + o - 1: 2 + o]
                    p.scan(hf[:, 2 + o: 2 + o + CH], at[:], it[:], init, [atk] + itk + [("xr", 4 * c - 1)], xrk(c))
                else:
                    hbt = hb[ci % 2]
                    init = 0.0 if ci == 0 else prev[:, 0:1]
                    p.scan(_rev(hbt[:], CH), _rev(at[:], CH), _rev(it[:], CH), init,
                           [atk] + itk + [("hb", (ci + 1) % 2)], [("hb", ci % 2)])
                    prev = hbt
                    hg = hgb[ci % 2]
                    p.tt("pool", at[:], hbt[:], hf[:, 2 + o: 2 + o + CH], ALU.add, [("hb", ci % 2), atk] + xrk(c), [atk])
                    p.tt("dve", hg[:], at[:], gy[:, o:o + CH], ALU.mult, [atk] + [("gy", t) for t in range(4 * c, 4 * c + 4)],
                         [("hgb", ci % 2)])
                    hdst = hgT[c, blk * 128:(blk + 1) * 128, :] if len(hgT.shape) == 3 else hgT[blk * 128:(blk + 1) * 128, o:o + CH]
                    p.dma(hdst, hg[:], reads=[("hgb", ci % 2)], writes=[("hg_out", blk, c)])


def pack_rnn_inputs(l, j, w_in, w_conv, b_conv, w_rec_gate, b_rec_gate, w_in_gate, b_in_gate, lru_lambda):
    XR0 = 1024 + 512
    YR0 = XR0 + 1024
    c0 = 2 * j * 128
    w_r = np.concatenate([w_in[l][:, XR0 + c0: XR0 + c0 + 256], w_in[l][:, YR0 + c0: YR0 + c0 + 256]], axis=1)
    w_g = np.empty((128, 8, 128), np.float32)
    small = np.empty((128, 2, NSM), np.float32)
    for b in range(2):
        cb = 2 * j + b
        sl = slice(cb * 128, (cb + 1) * 128)
        small[:, b, 0:4] = w_conv[l][:, sl].T
        small[:, b, 4] = b_conv[l][sl]
        for d in range(2):
            w_g[:, b * 4 + d * 2 + 0, :] = w_rec_gate[l, d, cb]
            w_g[:, b * 4 + d * 2 + 1, :] = w_in_gate[l, d, cb]
            small[:, b, 5 + 3 * d + 0] = b_rec_gate[l, d][sl]
            small[:, b, 5 + 3 * d + 1] = b_in_gate[l, d][sl]
            small[:, b, 5 + 3 * d + 2] = lru_lambda[l, d][sl]
    return {"w_r": np.ascontiguousarray(w_r), "w_g": w_g, "small": small}


TOK = 2048
HALO = 128
TH = TOK + 2 * HALO
ATT_SCALE = 128 ** -0.5


def build_attn(x_dtype=F32):
    nc = bass.Bass("TRN2", target_bir_lowering=False)
    d = {}
    d["xT"] = nc.dram_tensor("xT", [1024, TH], x_dtype, kind="ExternalInput").ap()
    d["w_qkv"] = nc.dram_tensor("w_qkv", [1024, 1536], F32, kind="ExternalInput").ap()
    d["cosT"] = nc.dram_tensor("cosT", [128, TH], F32, kind="ExternalInput").ap()
    d["sinT"] = nc.dram_tensor("sinT", [128, TH], F32, kind="ExternalInput").ap()
    d["masks"] = nc.dram_tensor("masks", [128, 3, 384], F32, kind="ExternalInput").ap()
    d["cmats"] = nc.dram_tensor("cmats", [128, 2, 128], F32, kind="ExternalInput").ap()
    d["sink"] = nc.dram_tensor("sink", [128, 8], F32, kind="ExternalInput").ap()
    d["oT"] = nc.dram_tensor("oT", [1024, TOK], BF16, kind="ExternalOutput").ap()
    p = Prog(nc)
    emit_attn(p, d)
    p.finish([("o_out", h, g) for h in range(8) for g in range(4)])
    return nc


def emit_attn(p, d, pfx="a", hook=None):
    xb = p.sb(pfx + "xb", [128, 8, TH], BF16)
    wq = [p.sb(pfx + "wq%d" % i, [128, 8, 128], BF16) for i in range(3)]
    wv = p.sb(pfx + "wv", [128, 8, 256], BF16)
    cosT = p.sb(pfx + "cos", [128, TH], F32)
    sinT = p.sb(pfx + "sin", [128, TH], F32)
    masks = p.sb(pfx + "masks", [128, 3, 384], BF16)
    cm0 = p.sb(pfx + "cm0", [128, 128], BF16)
    cm1 = p.sb(pfx + "cm1", [128, 128], BF16)
    sink = p.sb(pfx + "sink", [128, 8], F32)
    nsink = p.sb(pfx + "nsink", [128, 8], F32)
    qT = p.sb(pfx + "qT", [128, 8, TOK], BF16)
    kT = p.sb(pfx + "kT", [128, 2, TH], BF16)
    V = p.sb(pfx + "V", [128, TH // 128, 256], BF16)
    qraw = [p.sb(pfx + "qraw%d" % i, [128, 512], BF16) for i in range(2)]
    r1 = [p.sb(pfx + "r1_%d" % i, [128, 512], F32) for i in range(2)]
    r2 = [p.sb(pfx + "r2_%d" % i, [128, 512], F32) for i in range(2)]
    P = [p.sb(pfx + "P%d" % i, [128, 384], BF16) for i in range(2)]
    PT = [p.sb(pfx + "PT%d" % i, [128, 384], BF16) for i in range(2)]
    D = [p.sb(pfx + "D%d" % i, [128, 128], BF16) for i in range(2)]
    cols = [p.sb(pfx + "cols%d" % i, [128, 8], F32) for i in range(2)]
    ring = PsumRing(p, 6, pfx + "ps")
    oring = PsumRing(p, 2, pfx + "po")
    ident = cm0[:]
    pswap = cm1[:]

    if "xT_own" in d:
        own = d["xT_own"].rearrange("(c p) n -> p c n", p=128)
        allv = d["xT_all"][128:128 + 4096, :].rearrange("(k r c p) n -> p r k c n", k=4, r=4, c=2, p=128)
        nc_ = p.nc

        def emit_halo():
            allr = d["xT_all"][128:128 + 4096, :].rearrange("(k r q) n -> r k q n", k=4, r=4, q=256)

            def halo(e, left):
                jn = p.core_idx(e, "jl" if left else "jr")
                if left:
                    return e.dma_start(out=d["halo_l"].rearrange("(k q) n -> k q n", k=4), in_=allr[jn, :, :, TOK - HALO:TOK])
                return e.dma_start(out=d["halo_r"].rearrange("(k q) n -> k q n", k=4), in_=allr[jn, :, :, 0:HALO])
            agk = [("xT_all", k) for k in range(4)]
            p.dma(None, None, reads=agk, writes=["halo_l"], fn=lambda e: halo(e, True))
            p.dma(None, None, reads=agk, writes=["halo_r"], fn=lambda e: halo(e, False))
            p.dma(xb[:, :, 0:HALO], d["halo_l"].rearrange("(c p) n -> p c n", p=128), reads=["halo_l"],
                  writes=[("xbh", 0, k) for k in range(4)])
            p.dma(xb[:, :, HALO + TOK:TH], d["halo_r"].rearrange("(c p) n -> p c n", p=128), reads=["halo_r"],
                  writes=[("xbh", 1, k) for k in range(4)])
        if hook is None:
            emit_halo()
        for t0 in range(0, TOK, 512):
            keys = sorted(set([(HALO + t0) // 512, (HALO + t0 + 511) // 512]))
            p.dma(xb[:, :, HALO + t0:HALO + t0 + 512], own[:, :, t0:t0 + 512], reads=["xT_own"],
                  writes=[("xb", k) for k in keys])
    else:
        xTv = d["xT"].rearrange("(c p) n -> p c n", p=128)
        for t0 in range(0, TH, 512):
            w = min(512, TH - t0)
            p.dma(xb[:, :, t0:t0 + w], xTv[:, :, t0:t0 + w], writes=[("xb", t0 // 512)], queue="pool")

    def xbk(a, b):
        ks = [("xb", t) for t in range(a // 512, (b - 1) // 512 + 1)]
        if a < HALO:
            ks += [("xbh", 0, k) for k in range(4)]
        if b > HALO + TOK:
            ks += [("xbh", 1, k) for k in range(4)]
        return ks
    p.dma(wv[:], d["w_qkv"].rearrange("(c p) n -> p c n", p=128)[:, :, 1280:1536], writes=["wv"], queue="pool")
    p.dma(masks[:], d["masks"], writes=["masks"], queue="pool")
    p.dma(cm0[:], d["cmats"][:, 0, :], writes=["cm"], queue="pool")
    p.dma(cm1[:], d["cmats"][:, 1, :], writes=["cm"], queue="pool")
    p.dma(cosT[:], d["cosT"], writes=["cos"])
    p.dma(sinT[:], d["sinT"], writes=["sin"])
    p.dma(sink[:], d["sink"], writes=["sink"])
    p.ts("dve", nsink[:], sink[:], -1.0, None, ALU.mult, None, ["sink"], ["nsink"])
    wqv = d["w_qkv"].rearrange("(c p) n -> p c n", p=128)
    def emit_v():
        for tt in range(TH // 128):
            pst, psk = ring.next()
            for k in range(8):
                p.mm(pst[:, 0:256], xb[:, k, tt * 128:(tt + 1) * 128], wv[:, k, :], k == 0, k == 7, xbk(tt * 128, (tt + 1) * 128) + ["wv"], [psk])
            p.copy("dve" if tt % 2 else "act", V[:, tt, :], pst[:, 0:256], [psk], [("V", tt)]) if tt % 2 else \
                p.act(V[:, tt, :], pst[:, 0:256], AF.Copy, [psk], [("V", tt)])
    it = 0
    for m in list(range(8)) + ['v', 8, 9]:
        if m == 'v':
            emit_v()
            continue
        wt = wq[m % 3]
        wk = ("wq", m % 3)
        p.dma(wt[:], wqv[:, :, m * 128:(m + 1) * 128], writes=[wk], queue="pool")
        if m == 2 and hook is not None:
            hook()
            emit_halo()
        isq = m < 8
        ntok = TOK if isq else TH
        off = HALO if isq else 0
        t0 = 0
        while t0 < ntok:
            w = min(512, ntok - t0)
            pst, psk = ring.next()
            for k in range(8):
                p.mm(pst[:, 0:w], wt[:, k, :], xb[:, k, off + t0: off + t0 + w], k == 0, k == 7, xbk(off + t0, off + t0 + w) + [wk], [psk])
            qr = qraw[it % 2]
            qk = ("qraw", it % 2)
            p.act(qr[:, 0:w], pst[:, 0:w], AF.Copy, [psk], [qk])
            ps2, ps2k = ring.next()
            p.mm(ps2[:, 0:w], pswap, qr[:, 0:w], True, True, ["cm", qk], [ps2k])
            a1, a2 = r1[it % 2], r2[it % 2]
            k1, k2 = ("r1", it % 2), ("r2", it % 2)
            p.tt("dve", a1[:, 0:w], cosT[:, off + t0: off + t0 + w], pst[:, 0:w], ALU.mult, [psk, "cos"], [k1])
            p.tt("dve", a2[:, 0:w], sinT[:, off + t0: off + t0 + w], ps2[:, 0:w], ALU.mult, [ps2k, "sin"], [k2])
            if isq:
                dst = qT[:, m, t0:t0 + w]
                dk = [("qT", m, t0 // 512)]
            else:
                dst = kT[:, m - 8, t0:t0 + w]
                dk = [("kT", m - 8, t0 // 512)]
            p.tt("dve", dst, a1[:, 0:w], a2[:, 0:w], ALU.add, [k1, k2], dk)
            it += 1
            t0 += w
    its = [(grp, h, qi) for grp in range(4) for h in range(8) for qi in range(4)]
    NB = 5
    colsN = cols + [p.sb(pfx + "colsx%d" % i, [128, 8], F32) for i in range(NB - 2)]
    PN = P + [p.sb(pfx + "Px%d" % i, [128, 384], BF16) for i in range(NB - 2)]
    state = {}

    def stage_a(n):
        grp, h, qi = its[n]
        g = h // 4
        qb = grp * 4 + qi
        if qi == 0:
            state[(grp, h)] = oring.next()
        mi = 0 if qb == 0 else (2 if qb == 15 else 1)
        pss, pssk = ring.next()
        kkeys = [("kT", g, t) for t in sorted(set([(qb * 128) // 512, (qb * 128 + 383) // 512]))]
        p.mm(pss[:, 0:384], qT[:, h, qb * 128:(qb + 1) * 128], kT[:, g, qb * 128: qb * 128 + 384], True, False,
             [("qT", h, grp)] + kkeys, [pssk])
        p.mm(pss[:, 0:384], ident, masks[:, mi, :], False, True, ["cm", "masks"], [pssk])
        cl = colsN[n % NB]
        ck = ("cols", n % NB)
        p.op("dve", lambda e, cl=cl, pss=pss: e.reduce_max(out=cl[:, 0:1], in_=pss[:, 0:384], axis=AX.X), [pssk], [ck])
        p.ts("dve", cl[:, 1:2], cl[:, 0:1], -ATT_SCALE, nsink[:, h:h + 1], ALU.mult, ALU.min, [ck, "nsink"], [ck])
        Pt = PN[n % NB]
        pk = ("P", n % NB)
        p.act(Pt[:], pss[:, 0:384], AF.Exp, [pssk, ck], [pk, ck], scale=ATT_SCALE, bias=cl[:, 1:2], accum_out=cl[:, 2:3])
        p.act(cl[:, 3:4], cl[:, 1:2], AF.Exp, [ck, "sink"], [ck], bias=sink[:, h:h + 1])

    def stage_b(n):
        grp, h, qi = its[n]
        g = h // 4
        qb = grp * 4 + qi
        po, pok = state[(grp, h)]
        cl = colsN[n % NB]
        ck = ("cols", n % NB)
        Pt = PN[n % NB]
        pk = ("P", n % NB)
        p.tt("dve", cl[:, 4:5], cl[:, 2:3], cl[:, 3:4], ALU.add, [ck], [ck])
        p.op("dve", lambda e, cl=cl: e.reciprocal(out=cl[:, 5:6], in_=cl[:, 4:5]), [ck], [ck])
        Dt = D[n % 2]
        dk = ("D", n % 2)
        p.ts("dve", Dt[:], ident, cl[:, 5:6], None, ALU.mult, None, ["cm", ck], [dk])
        ppt, pptk = ring.next()
        for kb in range(3):
            p.mm(ppt[:, kb * 128:(kb + 1) * 128], Pt[:, kb * 128:(kb + 1) * 128], Dt[:], True, True, [pk, dk], [pptk])
        PTt = PT[n % 2]
        ptk = ("PT", n % 2)
        p.act(PTt[:], ppt[:, 0:384], AF.Copy, [pptk], [ptk])
        for kb in range(3):
            p.mm(po[:, qi * 128:(qi + 1) * 128], V[:, qb + kb, g * 128:(g + 1) * 128], PTt[:, kb * 128:(kb + 1) * 128],
                 kb == 0, kb == 2, [ptk, ("V", qb + kb)], [pok])
        if qi == 3:
            p.copy("dve", qT[:, h, grp * 512:(grp + 1) * 512], po[:], [pok], [("qT", h, grp)])
            p.dma(d["oT"][h * 128:(h + 1) * 128, grp * 512:(grp + 1) * 512], qT[:, h, grp * 512:(grp + 1) * 512],
                  reads=[("qT", h, grp)], writes=[("o_out", h, grp)])

    SKEW = 3
    for n in range(len(its) + SKEW):
        if n < len(its):
            stage_a(n)
        if n - SKEW >= 0:
            stage_b(n - SKEW)


def attn_consts(j):
    inv = (10000.0 ** (-np.arange(0, 128, 2, dtype=np.float32) / 128)).astype(np.float32)
    pos = (j * TOK - HALO + np.arange(TH)).astype(np.float32)
    ang = pos[:, None] * inv[None, :]
    cos = np.cos(ang).astype(np.float32).T
    sin = np.sin(ang).astype(np.float32).T
    cosT = np.concatenate([cos, cos], axis=0)
    sinT = np.concatenate([-sin, sin], axis=0)
    qi = np.arange(128)[:, None]
    kj = np.arange(384)[None, :]
    rel = kj - 128 - qi
    base = np.where(np.abs(rel) <= 128, 0.0, -30000.0).astype(np.float32)
    first = base.copy(); first[:, 0:128] = -30000.0
    last = base.copy(); last[:, 256:384] = -30000.0
    masks = np.stack([first if j == 0 else base, base, last if j == 3 else base], axis=1)
    ident = np.eye(128, dtype=np.float32)
    swap = np.zeros((128, 128), np.float32)
    mm = np.arange(128)
    swap[(mm + 64) % 128, mm] = 1.0
    cmats = np.stack([ident, swap], axis=1)
    return {"cosT": np.ascontiguousarray(cosT), "sinT": np.ascontiguousarray(sinT),
            "masks": np.ascontiguousarray(masks), "cmats": np.ascontiguousarray(cmats)}


def halo_xT(x_b, j):
    out = np.zeros((1024, TH), x_b.dtype)
    lo, hi = j * TOK - HALO, (j + 1) * TOK + HALO
    slo, shi = max(lo, 0), min(hi, S_LEN)
    out[:, slo - lo: shi - lo] = x_b[slo:shi].T
    return out


ALPHA = 8 ** 0.25
LN_EPS = 1e-5
N_EXP = 32
CAP = 256
NSLOT = N_EXP * CAP
NROWS = NSLOT + 128


def build_mix(x_dtype=F32):
    nc = bass.Bass("TRN2", target_bir_lowering=False)
    d = {}
    d["xT"] = nc.dram_tensor("xT", [1024, TOK], x_dtype, kind="ExternalInput").ap()
    d["x_tok"] = nc.dram_tensor("x_tok", [TOK, 1024], F32, kind="ExternalInput").ap()
    d["oT"] = nc.dram_tensor("oT", [1024, TOK], BF16, kind="ExternalInput").ap()
    d["hgT"] = nc.dram_tensor("hgT", [1024, TOK], BF16, kind="ExternalInput").ap()
    d["w4"] = nc.dram_tensor("w4", [4, 1024, 1024], F32, kind="ExternalInput").ap()
    d["w_out"] = nc.dram_tensor("w_out", [1024, 1024], F32, kind="ExternalInput").ap()
    d["ln"] = nc.dram_tensor("ln", [128, 2, 1024], F32, kind="ExternalInput").ap()
    d["w_rt"] = nc.dram_tensor("w_rt", [1024, 36], F32, kind="ExternalInput").ap()
    d["b_rt"] = nc.dram_tensor("b_rt", [128, 36], F32, kind="ExternalInput").ap()
    d["cst"] = nc.dram_tensor("cst", [128, 3, 128], F32, kind="ExternalInput").ap()
    d["cst2"] = nc.dram_tensor("cst2", [128, 40], F32, kind="ExternalInput").ap()
    d["x1"] = nc.dram_tensor("x1", [TOK, 1024], F32, kind="ExternalOutput").ap()
    d["xdisp"] = nc.dram_tensor("xdisp", [NROWS, 1024], BF16, kind="ExternalOutput").ap()
    d["slots"] = nc.dram_tensor("slots", [TOK, 2], I32, kind="ExternalOutput").ap()
    d["gates"] = nc.dram_tensor("gates", [TOK, 2], F32, kind="ExternalOutput").ap()
    p = Prog(nc)
    fk = emit_mix(p, d)
    p.finish(fk)
    return nc


def emit_ln(p, y, out, lnt, which, stat, reads, writes, tag):
    st6, mv, rs = stat
    sk = ("lnstat", tag)
    for hh in range(2):
        p.op("dve", lambda e, hh=hh: e.bn_stats(out=st6[:, hh * 6:(hh + 1) * 6], in_=y[:, hh * 512:(hh + 1) * 512]),
             reads + [sk], [sk])
    p.op("dve", lambda e: e.bn_aggr(out=mv[:, 0:2], in_=st6[:, 0:12]), [sk], [sk])
    p.act(rs[:, 0:1], mv[:, 1:2], AF.Sqrt, [sk], [sk], bias=LN_EPS)
    p.op("dve", lambda e: e.reciprocal(out=rs[:, 1:2], in_=rs[:, 0:1]), [sk], [sk])
    p.ts("dve", out, y, mv[:, 0:1], rs[:, 1:2], ALU.subtract, ALU.mult, reads + [sk], writes)
    p.tt("pool", out, out, lnt[:, which, 0, :], ALU.mult, writes + ["ln"], writes)
    p.tt("pool", out, out, lnt[:, which, 1, :], ALU.add, writes + ["ln"], writes)


def emit_mix(p, d, pfx="m", hook=None):
    ot = [p.sb(pfx + "ot%d" % i, [128, 8, 512], BF16) for i in range(1)] * 2
    hg = [p.sb(pfx + "hg%d" % i, [128, 8, 512], BF16) for i in range(1)] * 2
    xb = [p.sb(pfx + "xb%d" % i, [128, 8, 512], BF16) for i in range(1)] * 2
    w4r = p.sb(pfx + "w4r", [128, 4, 8, 1024], BF16)
    wo = p.sb(pfx + "wo", [128, 8, 1024], BF16)
    mg = [p.sb(pfx + "mg%d" % i, [128, 8, 512], BF16) for i in range(2)]
    t1 = [p.sb(pfx + "t1_%d" % i, [128, 512], F32) for i in range(2)]
    t2 = [p.sb(pfx + "t2_%d" % i, [128, 512], F32) for i in range(2)]
    lnt = p.sb(pfx + "ln", [128, 1, 2, 1024], F32)
    wrt = p.sb(pfx + "wrt", [128, 8, 36], F32)
    brt = p.sb(pfx + "brt", [128, 36], F32)
    cst = p.sb(pfx + "cst", [128, 3, 128], F32)
    cstb = p.sb(pfx + "cstb", [128, 2, 128], BF16)
    cst2 = p.sb(pfx + "cst2", [128, 40], F32)
    zero = p.sb(pfx + "zero", [128, 1024], BF16)
    xt = [p.sb(pfx + "xt%d" % i, [128, 1024], F32) for i in range(2)]
    y = [p.sb(pfx + "y%d" % i, [128, 1024], F32) for i in range(2)]
    x1 = [p.sb(pfx + "x1_%d" % i, [128, 1024], F32) for i in range(2)]
    x1b = [p.sb(pfx + "x1b%d" % i, [128, 1024], BF16) for i in range(2)]
    x1T = [p.sb(pfx + "x1T%d" % i, [128, 8, 128], F32) for i in range(2)]
    st6 = p.sb(pfx + "st6", [128, 12], F32)
    mv = p.sb(pfx + "mv", [128, 2], F32)
    rs = p.sb(pfx + "rs", [128, 2], F32)
    rt = [p.sb(pfx + "rt%d" % i, [128, 64], F32) for i in range(2)]
    E = [p.sb(pfx + "E%d" % i, [128, 3, 32], F32) for i in range(2)]
    Eb = [p.sb(pfx + "Eb%d" % i, [128, 32], BF16) for i in range(2)]
    base = p.sb(pfx + "base", [128, 32], F32)
    sl = [p.sb(pfx + "sl%d" % i, [128, 2], I32) for i in range(2)]
    gt = [p.sb(pfx + "gt%d" % i, [128, 2], F32) for i in range(2)]
    i8 = [p.sb(pfx + "i8_%d" % i, [128, 8], U32) for i in range(2)]
    ring = PsumRing(p, 8, pfx + "ps")
    ident = cst[:, 0, :]
    iota32 = cst2[:, 0:32]
    iota4 = cst2[:, 32:36]
    trash = cst2[:, 36:37]

    p.dma(lnt[:, 0, :, :], d["ln"], writes=["ln"])
    p.dma(wrt[:], d["w_rt"].rearrange("(c p) n -> p c n", p=128), writes=["wrt"])
    p.dma(brt[:], d["b_rt"], writes=["brt"])
    p.dma(cst[:], d["cst"], writes=["cst"])
    p.dma(cst2[:], d["cst2"], writes=["cst2"])
    p.copy("dve", cstb[:, 0, :], cst[:, 1, :], ["cst"], ["cstb"])
    p.copy("dve", cstb[:, 1, :], cst[:, 2, :], ["cst"], ["cstb"])
    p.memset("pool", zero[:], 0.0, ["zero"])
    p.memset("pool", base[:], 0.0, ["base"])
    zk = []
    for r0 in range(0, NROWS, 1024):
        nr = min(1024, NROWS - r0)
        p.dma(d["xdisp"][r0:r0 + nr, :].rearrange("(a p) n -> p a n", p=128),
              zero[:].partition_broadcast(128) if False else zero[:, None, :].to_broadcast([128, nr // 128, 1024]),
              reads=["zero"], writes=[("xdz", r0)])
        zk.append(("xdz", r0))
    for q in range(4):
        for c0 in range(0, 1024, 512):
            p.dma(w4r[:, q, :, c0:c0 + 512], d["w4"][q].rearrange("(c p) n -> p c n", p=128)[:, :, c0:c0 + 512],
                  writes=[("w4r", q, c0)], queue="pool")
    w4k = [[("w4r", q, 0), ("w4r", q, 512)] for q in range(4)]
    for c0 in range(0, 1024, 512):
        p.dma(wo[:, :, c0:c0 + 512], d["w_out"].rearrange("(c p) n -> p c n", p=128)[:, :, c0:c0 + 512], writes=[("wo", c0)], queue="pool")
    if hook is not None:
        hook()
    oTv = d["oT"].rearrange("(c p) n -> p c n", p=128)
    hgv = d["hgT"].rearrange("(c p) n -> p c n", p=128) if "hgT" in d else None
    xTv = (d["xT_own"] if "xT_own" in d else d["xT"]).rearrange("(c p) n -> p c n", p=128)
    fin = []
    wn = 0
    tile_i = 0
    for T in range(TOK // 512):
        b = T % 2
        p.dma(ot[b][:], oTv[:, :, T * 512:(T + 1) * 512], writes=[("ot", 0)])
        if "hg_all" in d:
            if T == 0:
                hat = d["hg_all"][128:128 + 4096, :].rearrange("(t q) n -> t q n", t=4)
                p.dma(None, None, reads=[("hg_all", t_) for t_ in range(4)], writes=["hg_mine"], queue="act",
                      fn=lambda e: e.dma_start(out=d["hg_mine"], in_=hat[p.core_idx(e, "j")]))
            p.dma(hg[b][:], d["hg_mine"].rearrange("(c p) n -> p c n", p=128)[:, :, T * 512:(T + 1) * 512],
                  reads=["hg_mine"], writes=[("hg", 0)])
            p.dma(xb[b][:], xTv[:, :, T * 512:(T + 1) * 512], reads=["xT_own"], writes=[("xb", 0)])
        else:
            p.dma(hg[b][:], hgv[:, :, T * 512:(T + 1) * 512], writes=[("hg", 0)])
            p.dma(xb[b][:], xTv[:, :, T * 512:(T + 1) * 512], writes=[("xb", 0)], queue="pool")
        for m in range(8):
            banks = [ring.next() for _ in range(4)]
            srcs = [ot[b], hg[b], xb[b], xb[b]]
            skeys = [("ot", 0), ("hg", 0), ("xb", 0), ("xb", 0)]
            for q in range(4):
                pst, psk = banks[q]
                for k in range(8):
                    p.mm(pst[:], w4r[:, q, k, m * 128:(m + 1) * 128], srcs[q][:, k, :], k == 0, k == 7,
                         w4k[q] + [skeys[q]], [psk])
            a1, a2 = t1[m % 2], t2[m % 2]
            k1, k2 = ("t1", m % 2), ("t2", m % 2)
            p.act(a1[:], banks[2][0][:], AF.Sigmoid, [banks[2][1]], [k1])
            p.act(a2[:], banks[3][0][:], AF.Sigmoid, [banks[3][1]], [k2])
            p.tt("dve", a1[:], a1[:], banks[0][0][:], ALU.mult, [k1, banks[0][1]], [k1])
            p.tt("dve", a2[:], a2[:], banks[1][0][:], ALU.mult, [k2, banks[1][1]], [k2])
            p.tt("pool", mg[b][:, m, :], a1[:], a2[:], ALU.add, [k1, k2], [("mg", b, m)])
        mgk = [("mg", b, m) for m in range(8)]
        def stage_a(s, tile_i):
            tb = tile_i % 2
            tok0 = T * 512 + s * 128
            x1k = ("x1", tb)
            p.dma(xt[tb][:], d["x_tok"][tok0:tok0 + 128, :], writes=[("xt", tb)])
            for hh in range(2):
                pst, psk = ring.next()
                for k in range(8):
                    p.mm(pst[:], mg[b][:, k, s * 128:(s + 1) * 128], wo[:, k, hh * 512:(hh + 1) * 512], k == 0, k == 7,
                         mgk + [("wo", hh * 512)], [psk])
                p.stt(y[tb][:, hh * 512:(hh + 1) * 512], xt[tb][:, hh * 512:(hh + 1) * 512], ALPHA, pst[:], ALU.mult, ALU.add,
                      [("xt", tb), psk], [("y", tb, hh)])
            x1k = ("x1", tb)
            emit_ln(p, y[tb][:], x1[tb][:], lnt, 0, (st6, mv, rs), [("y", tb, 0), ("y", tb, 1)], [x1k], "a")
            p.dma(d["x1"][tok0:tok0 + 128, :], x1[tb][:], reads=[x1k], writes=[("x1o", tile_i)])
            fin.append(("x1o", tile_i))
            p.act(x1b[tb][:], x1[tb][:], AF.Copy, [x1k], [("x1b", tb)])

        def stage_b(s, tile_i):
            tb = tile_i % 2
            tok0 = T * 512 + s * 128
            x1k = ("x1", tb)
            for hh in range(2):
                pst, psk = ring.next()
                for c in range(4):
                    k = hh * 4 + c
                    p.tr(pst[:, c * 128:(c + 1) * 128], x1[tb][:, k * 128:(k + 1) * 128], ident, [x1k, "cst"], [psk])
                p.copy("dve", x1T[tb][:, hh * 4:(hh + 1) * 4, :], pst[:].rearrange("p (c n) -> p c n", c=4), [psk],
                       [("x1T", tb, hh)])
            pl, plk = ring.next()
            for k in range(8):
                p.mm(pl[:, 0:36], x1T[tb][:, k, :], wrt[:, k, :], k == 0, k == 7, [("x1T", tb, 0), ("x1T", tb, 1), "wrt"], [plk])
            r = rt[tb]
            rk = ("rt", tb)
            p.tt("dve", r[:, 0:36], pl[:, 0:36], brt[:], ALU.add, [plk, "brt", rk], [rk])
            p.op("dve", lambda e, r=r: e.reduce_max(out=r[:, 36:37], in_=r[:, 0:4], axis=AX.X), [rk], [rk])
            p.ts("dve", r[:, 37:38], r[:, 36:37], -1.0, None, ALU.mult, None, [rk], [rk])
            p.act(r[:, 44:48], r[:, 0:4], AF.Exp, [rk], [rk], bias=r[:, 37:38], accum_out=r[:, 38:39])
            p.op("dve", lambda e, r=r: e.reciprocal(out=r[:, 39:40], in_=r[:, 38:39]), [rk], [rk])
            p.ts("dve", r[:, 40:44], r[:, 0:4], r[:, 36:37], None, ALU.is_equal, None, [rk], [rk])
            p.ts("dve", r[:, 48:56], r[:, 4:12], r[:, 40:41], None, ALU.mult, None, [rk], [rk])
            for g in range(1, 4):
                p.stt(r[:, 48:56], r[:, 4 + 8 * g:12 + 8 * g], r[:, 40 + g:41 + g], r[:, 48:56], ALU.mult, ALU.add, [rk], [rk])
            Et = E[tb]
            ek = ("E", tb)
            p.tt("dve", Et[:, 2, 0:4], r[:, 40:44], iota4, ALU.mult, [rk, "cst2", ek], [ek])
            p.op("dve", lambda e, r=r, Et=Et: e.reduce_sum(out=r[:, 58:59], in_=Et[:, 2, 0:4], axis=AX.X), [rk, ek], [rk])
            m8 = r[:, 48:56]
            i8t = i8[tb]
            p.op("dve", lambda e, r=r, Et=Et: e.max(out=Et[:, 2, 8:16], in_=r[:, 48:56]), [rk, ek], [ek])
            p.op("dve", lambda e, r=r, Et=Et, i8t=i8t: e.max_index(out=i8t[:], in_max=Et[:, 2, 8:16], in_values=r[:, 48:56]),
                 [rk, ek], [("i8", tb)])
            g = gt[tb]
            gk = ("gt", tb)
            p.tt("dve", r[:, 56:57], Et[:, 2, 8:9], Et[:, 2, 9:10], ALU.subtract, [ek, rk], [rk])
            p.act(r[:, 57:58], r[:, 56:57], AF.Sigmoid, [rk], [rk])
            p.tt("dve", g[:, 0:1], r[:, 57:58], r[:, 39:40], ALU.mult, [rk, gk], [gk])
            p.tt("dve", g[:, 1:2], r[:, 39:40], g[:, 0:1], ALU.subtract, [rk, gk], [gk])
            p.copy("dve", r[:, 59:61], i8t[:, 0:2], [("i8", tb), rk], [rk])
            p.stt(r[:, 59:61], r[:, 58:59].to_broadcast([128, 2]), 8.0, r[:, 59:61], ALU.mult, ALU.add, [rk], [rk])
            for kk in range(2):
                p.ts("dve", Et[:, kk, :], iota32, r[:, 59 + kk:60 + kk], None, ALU.is_equal, None, ["cst2", rk, ek], [ek])
            p.tt("dve", Eb[tb][:], Et[:, 0, :], Et[:, 1, :], ALU.add, [ek], [("Eb", tb)])
            pc, pck = ring.next()
            p.mm(pc[:, 0:32], cstb[:, 0, :], Eb[tb][:], True, True, ["cstb", ("Eb", tb)], [pck])
            p.mm(pc[:, 32:64], cstb[:, 1, :], Eb[tb][:], True, True, ["cstb", ("Eb", tb)], [pck])
            p.tt("dve", Et[:, 2, :], pc[:, 0:32], base[:], ALU.add, [pck, "base", ek], [ek])
            for kk in range(2):
                p.tt("dve", Et[:, kk, :], Et[:, kk, :], Et[:, 2, :], ALU.mult, [ek], [ek])
                p.op("dve", lambda e, r=r, Et=Et, kk=kk: e.reduce_sum(out=r[:, 61 + kk:62 + kk], in_=Et[:, kk, :], axis=AX.X),
                     [ek, rk], [rk])
            p.tt("dve", base[:], base[:], pc[:, 32:64], ALU.add, [pck, "base"], ["base"])
            for kk in range(2):
                p.ts("dve", r[:, 63:64], r[:, 61 + kk:62 + kk], float(CAP), None, ALU.is_lt, None, [rk], [rk])
                p.stt(r[:, 61 + kk:62 + kk], r[:, 59 + kk:60 + kk], float(CAP), r[:, 61 + kk:62 + kk], ALU.mult, ALU.add,
                      [rk], [rk])
                p.tt("dve", r[:, 61 + kk:62 + kk], r[:, 61 + kk:62 + kk], trash, ALU.subtract, [rk, "cst2"], [rk])
                p.tt("dve", r[:, 61 + kk:62 + kk], r[:, 61 + kk:62 + kk], r[:, 63:64], ALU.mult, [rk], [rk])
                p.tt("dve", r[:, 61 + kk:62 + kk], r[:, 61 + kk:62 + kk], trash, ALU.add, [rk, "cst2"], [rk])
                p.tt("dve", g[:, kk:kk + 1], g[:, kk:kk + 1], r[:, 63:64], ALU.mult, [rk, gk], [gk])
            slt = sl[tb]
            slk = ("sl", tb)
            p.copy("dve", slt[:], r[:, 61:63], [rk], [slk])
            p.dma(d["slots"][tok0:tok0 + 128, :], slt[:], reads=[slk], writes=[("slo", tile_i)])
            p.dma(d["gates"][tok0:tok0 + 128, :], g[:], reads=[gk], writes=[("gto", tile_i)])
            fin.extend([("slo", tile_i), ("gto", tile_i)])
            for kk in range(2):
                p.dma(None, None, reads=[("x1b", tb), slk] + zk, writes=[("xdo", tile_i, kk)], queue="pool",
                      fn=lambda e, tb=tb, kk=kk, slt=slt: e.indirect_dma_start(
                          out=d["xdisp"], out_offset=bass.IndirectOffsetOnAxis(ap=slt[:, kk:kk + 1], axis=0),
                          in_=x1b[tb][:, :], in_offset=None))
                fin.append(("xdo", tile_i, kk))

        stage_a(0, T * 4)
        for s in range(4):
            if s + 1 < 4:
                stage_a(s + 1, T * 4 + s + 1)
            stage_b(s, T * 4 + s)
    return fin


def mix_consts():
    ident = np.eye(128, dtype=np.float32)
    tp = np.arange(128)[:, None]
    t = np.arange(128)[None, :]
    lower = (tp < t).astype(np.float32)
    ones = np.ones((128, 128), np.float32)
    cst = np.stack([ident, lower, ones], axis=1)
    cst2 = np.zeros((128, 40), np.float32)
    cst2[:, 0:32] = np.arange(32, dtype=np.float32)[None, :]
    cst2[:, 32:36] = np.arange(4, dtype=np.float32)[None, :]
    cst2[:, 36] = NSLOT + np.arange(128)
    return {"cst": np.ascontiguousarray(cst), "cst2": cst2}


def build_moe():
    nc = bass.Bass("TRN2", target_bir_lowering=False)
    d = {}
    d["xdisp"] = nc.dram_tensor("xdisp", [NROWS, 1024], BF16, kind="ExternalInput").ap()
    d["x1"] = nc.dram_tensor("x1", [TOK, 1024], F32, kind="ExternalInput").ap()
    d["slots"] = nc.dram_tensor("slots", [TOK, 2], I32, kind="ExternalInput").ap()
    d["gates"] = nc.dram_tensor("gates", [TOK, 2], F32, kind="ExternalInput").ap()
    d["w_g"] = nc.dram_tensor("w_g", [N_EXP, 1024, 512], F32, kind="ExternalInput").ap()
    d["w_u"] = nc.dram_tensor("w_u", [N_EXP, 1024, 512], F32, kind="ExternalInput").ap()
    d["w_d"] = nc.dram_tensor("w_d", [N_EXP, 512, 1024], F32, kind="ExternalInput").ap()
    d["ln"] = nc.dram_tensor("ln", [128, 2, 1024], F32, kind="ExternalInput").ap()
    d["ident"] = nc.dram_tensor("ident", [128, 128], F32, kind="ExternalInput").ap()
    d["ydisp"] = nc.dram_tensor("ydisp", [NROWS, 1024], F32).ap()
    d["x2"] = nc.dram_tensor("x2", [TOK, 1024], F32, kind="ExternalOutput").ap()
    p = Prog(nc)
    fk = emit_moe(p, d)
    p.finish(fk)
    return nc


def emit_moe(p, d, pfx="e"):
    NWB = 3
    wg = [p.sb(pfx + "wg%d" % i, [128, 8, 512], BF16) for i in range(NWB)]
    wu = [p.sb(pfx + "wu%d" % i, [128, 8, 512], BF16) for i in range(NWB)]
    wd = [p.sb(pfx + "wd%d" % i, [128, 4, 1024], BF16) for i in range(NWB)]
    xe = [p.sb(pfx + "xe%d" % i, [128, 1024], BF16) for i in range(2)]
    xeT = [p.sb(pfx + "xeT%d" % i, [128, 8, CAP], BF16) for i in range(2)]
    hT = [p.sb(pfx + "hT%d" % i, [128, 4, CAP], BF16) for i in range(2)]
    sg = [p.sb(pfx + "sg%d" % i, [128, CAP], F32) for i in range(2)]
    yt = [p.sb(pfx + "yt%d" % i, [128, 1024], F32) for i in range(2)]
    ident = p.sb(pfx + "ident", [128, 128], BF16)
    lnt = p.sb(pfx + "ln", [128, 1, 2, 1024], F32)
    zero = p.sb(pfx + "zero", [128, 1024], F32)
    sl = [p.sb(pfx + "sl%d" % i, [128, 2], I32) for i in range(2)]
    gt = [p.sb(pfx + "gt%d" % i, [128, 2], F32) for i in range(2)]
    x1t = [p.sb(pfx + "x1t%d" % i, [128, 1024], F32) for i in range(2)]
    ya = [p.sb(pfx + "ya%d" % i, [128, 1024], F32) for i in range(2)]
    yb = [p.sb(pfx + "yb%d" % i, [128, 1024], F32) for i in range(2)]
    yo = [p.sb(pfx + "yo%d" % i, [128, 1024], F32) for i in range(2)]
    st6 = p.sb(pfx + "st6", [128, 12], F32)
    mv = p.sb(pfx + "mv", [128, 2], F32)
    rs = p.sb(pfx + "rs", [128, 2], F32)
    if "xT_next" in d:
        xTn = [p.sb(pfx + "xTn%d" % i, [128, 8, 512], BF16) for i in range(2)]
        identf = p.sb(pfx + "identf", [128, 128], F32)
        p.dma(identf[:], d["ident"], writes=["identf"])
    ring = PsumRing(p, 6, pfx + "ps")
    ptr = [p.ps(pfx + "ptr%d" % i, [128, 1024], BF16) for i in range(2)]
    for i in range(2):
        p.exclusive.add((pfx + "ptr", i))

    p.dma(ident[:], d["ident"], writes=["ident"], queue="pool")
    p.dma(lnt[:, 0, :, :], d["ln"], writes=["ln"])
    p.memset("pool", zero[:], 0.0, ["zero"])
    p.dma(d["ydisp"][NSLOT:NROWS, :], zero[:], reads=["zero"], writes=[("yd", -1, 0)])
    ydk = [("yd", -1, 0)]
    nb = 0
    nstg = 0
    stg = [p.sb(pfx + "stg%d" % i, [128, 4096], F32) for i in range(2)]
    for e in range(N_EXP):
        b = e % NWB
        wk = ("w", b)
        for wi, (wsrc, wdst, wkey) in enumerate(((d["w_g"][e], wg[b], ("wg", b)), (d["w_u"][e], wu[b], ("wu", b)),
                                                 (d["w_d"][e], wd[b], ("wd", b)))):
            if wi < 2:
                p.dma(wdst[:], wsrc.rearrange("(c p) n -> p c n", p=128), writes=[wkey], queue="pool")
                continue
            sgi = nstg % 2
            nstg += 1
            nchunk = 4 if wi == 2 else 8
            p.dma(stg[sgi][:].rearrange("p (c n) -> p c n", c=nchunk), wsrc.rearrange("(c p) n -> p c n", p=128),
                  writes=[("stg", sgi)])
            dflat = wdst[:].rearrange("p c n -> p (c n)")
            if wi == 1:
                p.copy("pool", dflat, stg[sgi][:], [("stg", sgi)], [wkey])
            else:
                p.act(dflat, stg[sgi][:], AF.Copy, [("stg", sgi)], [wkey])
        for blk in range(CAP // 128):
            xb_ = xe[nb % 2]
            xk = ("xe", nb % 2)
            r0 = e * CAP + blk * 128
            p.dma(xb_[:], d["xdisp"][r0:r0 + 128, :], writes=[xk])
            pt = ptr[nb % 2]
            ptk = (pfx + "ptr", nb % 2)
            for k in range(8):
                p.tr(pt[:, k * 128:(k + 1) * 128], xb_[:, k * 128:(k + 1) * 128], ident[:], [xk, "ident"], [ptk])
            p.copy("dve" if blk else "act", xeT[e % 2][:, :, blk * 128:(blk + 1) * 128], pt[:].rearrange("p (c n) -> p c n", c=8),
                   [ptk], [("xeT", e % 2, blk)]) if blk else \
                p.act(xeT[e % 2][:, :, blk * 128:(blk + 1) * 128], pt[:].rearrange("p (c n) -> p c n", c=8), AF.Copy,
                      [ptk], [("xeT", e % 2, blk)])
            nb += 1
        xtk = [("xeT", e % 2, blk) for blk in range(CAP // 128)]
        for m in range(4):
            pg, pgk = ring.next()
            pu, puk = ring.next()
            for k in range(8):
                p.mm(pg[:, 0:CAP], wg[b][:, k, m * 128:(m + 1) * 128], xeT[e % 2][:, k, :], k == 0, k == 7, [("wg", b)] + xtk, [pgk])
            for k in range(8):
                p.mm(pu[:, 0:CAP], wu[b][:, k, m * 128:(m + 1) * 128], xeT[e % 2][:, k, :], k == 0, k == 7, [("wu", b)] + xtk, [puk])
            s_ = sg[m % 2]
            sk = ("sg", m % 2)
            p.act(s_[:], pg[:, 0:CAP], AF.Silu, [pgk], [sk])
            p.tt("dve", hT[e % 2][:, m, :], s_[:], pu[:, 0:CAP], ALU.mult, [sk, puk], [("hT", e % 2, m)])
        htk = [("hT", e % 2, m) for m in range(4)]
        for blk in range(CAP // 128):
            y_ = yt[blk % 2]
            yk = ("yt", blk % 2)
            for hh in range(2):
                py, pyk = ring.next()
                for k in range(4):
                    p.mm(py[:], hT[e % 2][:, k, blk * 128:(blk + 1) * 128], wd[b][:, k, hh * 512:(hh + 1) * 512], k == 0, k == 3,
                         htk + [("wd", b)], [pyk])
                if hh == 0:
                    p.act(y_[:, 0:512], py[:], AF.Copy, [pyk], [yk])
                else:
                    p.copy("dve", y_[:, 512:1024], py[:], [pyk], [yk])
            r0 = e * CAP + blk * 128
            p.dma(d["ydisp"][r0:r0 + 128, :], y_[:], reads=[yk], writes=[("yd", e, blk)])
            ydk.append(("yd", e, blk))
    fin = []

    def cload(t):
        b = t % 2
        tok0 = t * 128
        p.dma(sl[b][:], d["slots"][tok0:tok0 + 128, :], writes=[("sl", b)])
        p.dma(gt[b][:], d["gates"][tok0:tok0 + 128, :], writes=[("gt", b)])
        p.dma(x1t[b][:], d["x1"][tok0:tok0 + 128, :], writes=[("x1t", b)])
        for kk, dst in enumerate((ya[b], yb[b])):
            p.dma(None, None, reads=[("sl", b)] + ydk, writes=[("yab", b, kk)], queue="pool",
                  fn=lambda e, dst=dst, b=b, kk=kk: e.indirect_dma_start(
                      out=dst[:, :], out_offset=None, in_=d["ydisp"],
                      in_offset=bass.IndirectOffsetOnAxis(ap=sl[b][:, kk:kk + 1], axis=0)))
    cload(0)
    for t in range(TOK // 128):
        b = t % 2
        tok0 = t * 128
        if t + 1 < TOK // 128:
            cload(t + 1)
        fk_ = ("f", b)
        p.ts("dve", ya[b][:], ya[b][:], gt[b][:, 0:1], None, ALU.mult, None, [("yab", b, 0), ("gt", b)], [("yab", b, 0)])
        p.stt(ya[b][:], yb[b][:], gt[b][:, 1:2], ya[b][:], ALU.mult, ALU.add, [("yab", b, 0), ("yab", b, 1), ("gt", b)],
              [("yab", b, 0)])
        p.stt(ya[b][:], x1t[b][:], ALPHA, ya[b][:], ALU.mult, ALU.add, [("yab", b, 0), ("x1t", b)], [("yab", b, 0)])
        emit_ln(p, ya[b][:], yo[b][:], lnt, 0, (st6, mv, rs), [("yab", b, 0)], [("yo", b)], "b")
        p.dma(d["x2"][tok0:tok0 + 128, :], yo[b][:], reads=[("yo", b)], writes=[("x2o", t)])
        fin.append(("x2o", t))
        if "xT_next" in d:
            xn = xTn[(t // 4) % 2]
            xnk = ("xTn", (t // 4) % 2)
            for hh in range(2):
                pst, psk = ring.next()
                for c in range(4):
                    k = hh * 4 + c
                    p.tr(pst[:, c * 128:(c + 1) * 128], yo[b][:, k * 128:(k + 1) * 128], identf[:], [("yo", b), "identf"], [psk])
                p.copy("dve" if hh else "pool", xn[:, hh * 4:(hh + 1) * 4, (t % 4) * 128:(t % 4 + 1) * 128],
                       pst[:].rearrange("p (c n) -> p c n", c=4), [psk], [xnk]) if hh else \
                    p.act(xn[:, hh * 4:(hh + 1) * 4, (t % 4) * 128:(t % 4 + 1) * 128],
                          pst[:].rearrange("p (c n) -> p c n", c=4), AF.Copy, [psk], [xnk])
            if t % 4 == 3:
                T4 = t // 4
                p.dma(d["xT_next"].rearrange("(c p) n -> p c n", p=128)[:, :, T4 * 512:(T4 + 1) * 512], xn[:],
                      reads=[xnk], writes=[("xTn_out", T4)])
    return fin


_PROGS = {}


def _prog(name, builder):
    if name not in _PROGS:
        _PROGS[name] = builder()
    return _PROGS[name]


def _run(nc, in_maps):
    res = run_bass_kernel_spmd(nc, in_maps, core_ids=list(range(8)))
    return res.results


def kernel_unfused(x, w_in, w_sink, w_conv, b_conv, w_rec_gate, b_rec_gate, w_in_gate, b_in_gate, lru_lambda,
                   w_attn_o, w_rnn_o, w_out, ln_g, ln_b, w_router_group, b_router_group, w_router_expert, b_router_expert,
                   w_exp_gate, w_exp_up, w_exp_down):
    f = lambda a: np.asarray(a, dtype=np.float32)
    x = f(x)
    w_in, w_sink, w_conv, b_conv = f(w_in), f(w_sink), f(w_conv), f(b_conv)
    w_rec_gate, b_rec_gate, w_in_gate, b_in_gate, lru_lambda = f(w_rec_gate), f(b_rec_gate), f(w_in_gate), f(b_in_gate), f(lru_lambda)
    w_attn_o, w_rnn_o, w_out, ln_g, ln_b = f(w_attn_o), f(w_rnn_o), f(w_out), f(ln_g), f(ln_b)
    w_router_group, b_router_group = f(w_router_group), f(b_router_group)
    w_router_expert, b_router_expert = f(w_router_expert), f(b_router_expert)
    w_exp_gate, w_exp_up, w_exp_down = f(w_exp_gate), f(w_exp_up), f(w_exp_down)
    depth = w_in.shape[0]
    nc_r = _prog("rnn", build_rnn)
    nc_a = _prog("attn", build_attn)
    nc_m = _prog("mix", build_mix)
    nc_e = _prog("moe", build_moe)
    aconst = [attn_consts(j) for j in range(4)]
    mconst = mix_consts()
    ident = np.eye(128, dtype=np.float32)
    cores = [(c // 4, c % 4) for c in range(8)]
    for l in range(depth):
        xTs = [np.ascontiguousarray(x[b].T) for b in range(2)]
        maps = []
        for (b, j) in cores:
            m = pack_rnn_inputs(l, j, w_in, w_conv, b_conv, w_rec_gate, b_rec_gate, w_in_gate, b_in_gate, lru_lambda)
            m["xT"] = xTs[b]
            maps.append(m)
        res = _run(nc_r, maps)
        hgT = [np.concatenate([np.asarray(res[b * 4 + j]["hgT"]) for j in range(4)], axis=0) for b in range(2)]
        w_qkv = np.ascontiguousarray(w_in[l][:, 0:1536])
        sink = np.ascontiguousarray(np.broadcast_to(w_sink[l][None, :], (128, 8)))
        maps = []
        for (b, j) in cores:
            m = dict(aconst[j])
            m["xT"] = halo_xT(x[b], j)
            m["w_qkv"] = w_qkv
            m["sink"] = sink
            maps.append(m)
        res = _run(nc_a, maps)
        oT = [np.asarray(res[c]["oT"]) for c in range(8)]
        w4 = np.ascontiguousarray(np.stack([w_attn_o[l], w_rnn_o[l], w_in[l][:, 3584:4608], w_in[l][:, 4608:5632]]))
        ln1 = np.ascontiguousarray(np.broadcast_to(np.stack([ln_g[l, 0], ln_b[l, 0]])[None], (128, 2, 1024)))
        w_rt = np.ascontiguousarray(np.concatenate([w_router_group[l], w_router_expert[l]], axis=1))
        b_rt = np.ascontiguousarray(np.broadcast_to(np.concatenate([b_router_group[l], b_router_expert[l]])[None], (128, 36)))
        wo = np.ascontiguousarray(w_out[l])
        maps = []
        for c, (b, j) in enumerate(cores):
            m = dict(mconst)
            m["xT"] = np.ascontiguousarray(xTs[b][:, j * TOK:(j + 1) * TOK])
            m["x_tok"] = np.ascontiguousarray(x[b][j * TOK:(j + 1) * TOK])
            m["oT"] = oT[c]
            m["hgT"] = np.ascontiguousarray(hgT[b][:, j * TOK:(j + 1) * TOK])
            m["w4"] = w4
            m["w_out"] = wo
            m["ln"] = ln1
            m["w_rt"] = w_rt
            m["b_rt"] = b_rt
            maps.append(m)
        res = _run(nc_m, maps)
        ln2 = np.ascontiguousarray(np.broadcast_to(np.stack([ln_g[l, 1], ln_b[l, 1]])[None], (128, 2, 1024)))
        wg_, wu_, wd_ = np.ascontiguousarray(w_exp_gate[l]), np.ascontiguousarray(w_exp_up[l]), np.ascontiguousarray(w_exp_down[l])
        maps = []
        for c in range(8):
            maps.append({"xdisp": np.asarray(res[c]["xdisp"]), "x1": np.asarray(res[c]["x1"]),
                         "slots": np.asarray(res[c]["slots"]), "gates": np.asarray(res[c]["gates"]),
                         "w_g": wg_, "w_u": wu_, "w_d": wd_, "ln": ln2, "ident": ident})
        res = _run(nc_e, maps)
        x = np.stack([np.concatenate([np.asarray(res[b * 4 + j]["x2"]) for j in range(4)], axis=0) for b in range(2)])
    return np.ascontiguousarray(x.astype(np.float32))


GROUPS4 = [[0, 1, 2, 3], [4, 5, 6, 7]]


def build_fused(depth=4):
    nc = bass.Bass("TRN2", target_bir_lowering=False)
    L = depth

    def ext(name, shape, dt=F32):
        return nc.dram_tensor(name, list(shape), dt, kind="ExternalInput").ap()

    def internal(name, shape, dt):
        return nc.dram_tensor(name, list(shape), dt).ap()
    I = {}
    I["x_tok0"] = ext("x_tok0", [TOK, 1024])
    I["xT0"] = ext("xT0", [1024, TOK])
    I["w_r"] = ext("w_r", [L, 1024, 512])
    I["w_gt"] = ext("w_gt", [L, 128, 8, 128])
    I["small"] = ext("small", [L, 128, 2, NSM])
    I["w_qkv"] = ext("w_qkv", [L, 1024, 1536])
    I["cosT"] = ext("cosT", [128, TH])
    I["sinT"] = ext("sinT", [128, TH])
    I["masks"] = ext("masks", [128, 3, 384])
    I["cmats"] = ext("cmats", [128, 2, 128])
    I["sink"] = ext("sink", [L, 128, 8])
    I["w4"] = ext("w4", [L, 4, 1024, 1024])
    I["w_out"] = ext("w_out", [L, 1024, 1024])
    I["ln1"] = ext("ln1", [L, 128, 2, 1024])
    I["ln2"] = ext("ln2", [L, 128, 2, 1024])
    I["w_rt"] = ext("w_rt", [L, 1024, 36])
    I["b_rt"] = ext("b_rt", [L, 128, 36])
    I["cst"] = ext("cst", [128, 3, 128])
    I["cst2"] = ext("cst2", [128, 40])
    I["w_eg"] = ext("w_eg", [L, N_EXP, 1024, 512])
    I["w_eu"] = ext("w_eu", [L, N_EXP, 1024, 512])
    I["w_ed"] = ext("w_ed", [L, N_EXP, 512, 1024])
    I["ident"] = ext("ident", [128, 128])
    out = nc.dram_tensor("out", [TOK, 1024], F32, kind="ExternalOutput").ap()
    xT_own = [internal("xT_own%d" % i, [1024, TOK], BF16) for i in range(2)]
    xT_all = [internal("xT_all%d" % i, [128 + 4096, TOK], BF16) for i in range(2)]
    x_tok_i = [internal("x_tok_i%d" % i, [TOK, 1024], F32) for i in range(2)]
    hg_own = [internal("hg_own%d" % i, [4, 256, TOK], BF16) for i in range(2)]
    hg_all = [internal("hg_all%d" % i, [128 + 4096, TOK], BF16) for i in range(2)]
    halo_l = internal("halo_l", [1024, HALO], BF16)
    halo_r = internal("halo_r", [1024, HALO], BF16)
    hg_mine = internal("hg_mine", [1024, TOK], BF16)
    oT = internal("oT_i", [1024, TOK], BF16)
    x1 = internal("x1_i", [TOK, 1024], F32)
    xdisp = internal("xdisp_i", [NROWS, 1024], BF16)
    slots = internal("slots_i", [TOK, 2], I32)
    gates = internal("gates_i", [TOK, 2], F32)
    ydisp = internal("ydisp_i", [NROWS, 1024], F32)

    p = Prog(nc)
    p.begin_phase()
    st = [p.sb("pro%d" % i, [128, 8, 512], BF16) for i in range(2)]
    src = I["xT0"].rearrange("(c p) n -> p c n", p=128)
    dst = xT_own[0].rearrange("(c p) n -> p c n", p=128)
    for t in range(TOK // 512):
        p.dma(st[t % 2][:], src[:, :, t * 512:(t + 1) * 512], writes=[("pro", t % 2)], queue="pool")
        p.dma(dst[:, :, t * 512:(t + 1) * 512], st[t % 2][:], reads=[("pro", t % 2)], writes=[("xT_own_w", t)])
    for k in range(4):
        p.coll("AllGather", GROUPS4, xT_own[0][k * 256:(k + 1) * 256, :], xT_all[0][128 + k * 1024:128 + (k + 1) * 1024, :],
               reads=[("xT_own_w", t) for t in range(TOK // 512)], writes=[("xT_all", k)])
    p.end_phase()
    for l in range(L):
        par = l % 2
        last = (l == L - 1)
        p.begin_phase()

        def ag_x(par=par):
            for k in range(4):
                p.coll("AllGather", GROUPS4, xT_own[par][k * 256:(k + 1) * 256, :],
                       xT_all[par][128 + k * 1024:128 + (k + 1) * 1024, :], reads=[], writes=[("xT_all", k)])
        emit_attn(p, {"xT_own": xT_own[par], "xT_all": xT_all[par], "w_qkv": I["w_qkv"][l], "cosT": I["cosT"],
                      "sinT": I["sinT"], "masks": I["masks"], "cmats": I["cmats"], "sink": I["sink"][l], "oT": oT,
                      "halo_l": halo_l, "halo_r": halo_r},
                  pfx="a%d" % l, hook=(ag_x if l > 0 else None))
        p.end_phase()
        p.begin_phase()
        emit_rnn(p, xT_all[par], I["w_r"][l], I["w_gt"][l], I["small"][l], hg_own[par], pfx="r%d" % l)
        p.end_phase()
        p.begin_phase()

        def ag_h(par=par):
            for t in range(4):
                p.coll("AllGather", GROUPS4, hg_own[par][t], hg_all[par][128 + t * 1024:128 + (t + 1) * 1024, :],
                       reads=[], writes=[("hg_all", t)])
        emit_mix(p, {"xT_own": xT_own[par], "x_tok": (I["x_tok0"] if l == 0 else x_tok_i[par]), "oT": oT,
                     "hg_all": hg_all[par], "hg_mine": hg_mine, "w4": I["w4"][l], "w_out": I["w_out"][l], "ln": I["ln1"][l],
                     "w_rt": I["w_rt"][l], "b_rt": I["b_rt"][l], "cst": I["cst"], "cst2": I["cst2"],
                     "x1": x1, "xdisp": xdisp, "slots": slots, "gates": gates}, pfx="m%d" % l, hook=ag_h)
        p.end_phase()
        p.begin_phase()
        dd = {"xdisp": xdisp, "x1": x1, "slots": slots, "gates": gates, "w_g": I["w_eg"][l], "w_u": I["w_eu"][l],
              "w_d": I["w_ed"][l], "ln": I["ln2"][l], "ident": I["ident"], "ydisp": ydisp,
              "x2": (out if last else x_tok_i[1 - par])}
        if not last:
            dd["xT_next"] = xT_own[1 - par]
        emit_moe(p, dd, pfx="e%d" % l)
        p.end_phase()
    p.finish([])
    return nc


def fused_inputs(depth, x, w_in, w_sink, w_conv, b_conv, w_rec_gate, b_rec_gate, w_in_gate, b_in_gate, lru_lambda,
                 w_attn_o, w_rnn_o, w_out, ln_g, ln_b, w_router_group, b_router_group, w_router_expert, b_router_expert,
                 w_exp_gate, w_exp_up, w_exp_down):
    L = depth
    ca = np.ascontiguousarray
    shared = {}
    shared["w_qkv"] = ca(w_in[:L, :, 0:1536])
    shared["sink"] = ca(np.broadcast_to(w_sink[:L, None, :], (L, 128, 8)))
    shared["w4"] = ca(np.stack([np.stack([w_attn_o[l], w_rnn_o[l], w_in[l][:, 3584:4608], w_in[l][:, 4608:5632]]) for l in range(L)]))
    shared["w_out"] = ca(w_out[:L])
    shared["ln1"] = ca(np.broadcast_to(np.stack([ln_g[:L, 0], ln_b[:L, 0]], axis=1)[:, None], (L, 128, 2, 1024)))
    shared["ln2"] = ca(np.broadcast_to(np.stack([ln_g[:L, 1], ln_b[:L, 1]], axis=1)[:, None], (L, 128, 2, 1024)))
    shared["w_rt"] = ca(np.concatenate([w_router_group[:L], w_router_expert[:L]], axis=2))
    shared["b_rt"] = ca(np.broadcast_to(np.concatenate([b_router_group[:L], b_router_expert[:L]], axis=1)[:, None, :], (L, 128, 36)))
    shared["w_eg"] = ca(w_exp_gate[:L])
    shared["w_eu"] = ca(w_exp_up[:L])
    shared["w_ed"] = ca(w_exp_down[:L])
    shared["ident"] = np.eye(128, dtype=np.float32)
    shared.update(mix_consts())
    rn = []
    for j in range(4):
        packs = [pack_rnn_inputs(l, j, w_in, w_conv, b_conv, w_rec_gate, b_rec_gate, w_in_gate, b_in_gate, lru_lambda)
                 for l in range(L)]
        rn.append({"w_r": ca(np.stack([q["w_r"] for q in packs])), "w_gt": ca(np.stack([q["w_g"] for q in packs])),
                   "small": ca(np.stack([q["small"] for q in packs]))})
    maps = []
    for c in range(8):
        b, j = c // 4, c % 4
        m = dict(shared)
        m.update(rn[j])
        m.update(attn_consts(j))
        xs = x[b][j * TOK:(j + 1) * TOK]
        m["x_tok0"] = ca(xs)
        m["xT0"] = ca(xs.T)
        maps.append(m)
    return maps


def kernel_fused(depth, **inp):
    key = "fused%d" % depth
    nc = _prog(key, lambda: build_fused(depth))
    names = ["x", "w_in", "w_sink", "w_conv", "b_conv", "w_rec_gate", "b_rec_gate", "w_in_gate", "b_in_gate", "lru_lambda",
             "w_attn_o", "w_rnn_o", "w_out", "ln_g", "ln_b", "w_router_group", "b_router_group", "w_router_expert",
             "b_router_expert", "w_exp_gate", "w_exp_up", "w_exp_down"]
    args = [np.asarray(inp[n], dtype=np.float32) for n in names]
    maps = fused_inputs(depth, *args)
    res = _run(nc, maps)
    x = np.stack([np.concatenate([np.asarray(res[b * 4 + j]["out"]) for j in range(4)], axis=0) for b in range(2)])
    return np.ascontiguousarray(x.astype(np.float32))


def kernel(**inputs):
    return kernel_fused(4, **inputs)
```

```python
import contextlib
import numpy as np
import concourse.bass as bass
import concourse.mybir as mybir
from concourse.bass_utils import run_bass_kernel_spmd

F32 = mybir.dt.float32
BF16 = mybir.dt.bfloat16
I32 = mybir.dt.int32
U32 = mybir.dt.uint32
AF = mybir.ActivationFunctionType
ALU = mybir.AluOpType
AX = mybir.AxisListType


class Prog:
    ENG = ("pe", "dve", "act", "pool", "sp")

    def __init__(self, nc, n_slots=8):
        self.nc = nc
        self.es = contextlib.ExitStack()
        self.q = {e: [] for e in self.ENG}
        self.cnt = {e: 0 for e in self.ENG}
        self.sem = {e: self.es.enter_context(nc.semaphore("s_" + e)) for e in self.ENG}
        self.n_slots = n_slots
        self.pool_slots = 2
        self.dq = ("sp", "act", "pool")
        self.dsem = {q: [self.es.enter_context(nc.semaphore("d_%s%d" % (q, i))) for i in range(n_slots)]
                     for q in self.dq}
        self.dn = {q: 0 for q in self.dq}
        self.sems = {}
        for e in self.ENG:
            self.sems[("c", e)] = self.sem[e]
        for q in self.dq:
            for i in range(n_slots):
                self.sems[("d", q, i)] = self.dsem[q][i]
        self.lastw = {}
        self.readers = {}
        self.waited = {e: {} for e in self.ENG}
        self.n_ops = 0
        self.exclusive = set()
        self.csem = []
        self.ph = None
        self.latest = {}
        self._cidx = {}

    def sb(self, name, shape, dtype):
        st = self.ph if self.ph is not None else self.es
        return st.enter_context(self.nc.sbuf_tensor(name, list(shape), dtype))

    def ps(self, name, shape, dtype):
        st = self.ph if self.ph is not None else self.es
        return st.enter_context(self.nc.psum_tensor(name, list(shape), dtype))

    def core_idx(self, e, which):
        key = id(e)
        if key not in self._cidx:
            pid = e.partition_id()
            vals = {}
            for name, off in (("j", 0), ("jl", 3), ("jr", 5)):
                vals[name] = e.snap((pid + off) % 4, min_val=0, max_val=3)
            self._cidx[key] = vals
        return self._cidx[key][which]

    def begin_phase(self):
        assert self.ph is None
        self.ph = contextlib.ExitStack()
        self.exclusive = set()

    def _barrier(self):
        for e in self.ENG:
            waits = []
            for sk, v in self.latest.items():
                if sk == ("c", e):
                    continue
                if self.waited[e].get(sk, 0) < v:
                    self.waited[e][sk] = v
                    waits.append((sk, v))
            if waits:
                self.q[e].append((waits, None, None, 0))
        self.lastw = {}
        self.readers = {}

    def _emit_block(self):
        nc = self.nc
        engobj = {"pe": "tensor", "dve": "vector", "act": "scalar", "pool": "gpsimd", "sp": "sync"}
        with nc.Block() as block:
            for e in self.ENG:
                items = self.q[e]

                def body(eng, items=items):
                    for waits, fn, sk, amt in items:
                        for wsk, v in waits:
                            eng.wait_ge(self.sems[wsk], v)
                        if fn is not None:
                            ins = fn(eng)
                            if amt is None:
                                ins.then_inc(self.sems[sk])
                            else:
                                ins.then_inc(self.sems[sk], amt)
                getattr(block, engobj[e])(body)
        self.q = {e: [] for e in self.ENG}

    def end_phase(self):
        self._barrier()
        self._emit_block()
        self.ph.close()
        self.ph = None

    def _deps(self, eng, reads, writes, is_dma):
        need = {}

        def add(d):
            for sk, v in d.items():
                if need.get(sk, 0) < v:
                    need[sk] = v
        for k in reads:
            if k in self.lastw:
                add(self.lastw[k])
        for k in writes:
            if k in self.lastw:
                add(self.lastw[k])
            if k in self.readers:
                add(self.readers[k])
        out = []
        for sk, v in need.items():
            if (not is_dma) and eng == "pe" and sk == ("c", "pe"):
                continue
            if self.waited[eng].get(sk, 0) >= v:
                continue
            self.waited[eng][sk] = v
            out.append((sk, v))
        return out

    def _mark(self, reads, writes, tok):
        sk, v = tok
        if self.latest.get(sk, 0) < v:
            self.latest[sk] = v
        for k in reads:
            r = self.readers.setdefault(k, {})
            if r.get(sk, 0) < v:
                r[sk] = v
        for k in writes:
            self.lastw[k] = {sk: v}
            self.readers[k] = {}

    def _excl(self, reads, writes):
        ex = [k for k in reads if k in self.exclusive]
        if ex:
            writes = list(writes) + [k for k in ex if k not in writes]
        return reads, writes

    def op(self, eng, fn, reads=(), writes=()):
        reads, writes = self._excl(reads, writes)
        waits = self._deps(eng, reads, writes, False)
        self.cnt[eng] += 1
        tok = (("c", eng), self.cnt[eng])
        self.q[eng].append((waits, fn, tok[0], 1))
        self._mark(reads, writes, tok)
        self.n_ops += 1

    def dma(self, out, in_, reads=(), writes=(), queue="sp", fn=None):
        n = self.dn[queue]
        self.dn[queue] += 1
        ns = self.n_slots if queue != "pool" else min(self.n_slots, self.pool_slots)
        slot = n % ns
        sk = ("d", queue, slot)
        waits = self._deps(queue, reads, writes, True)
        prev = 16 * (n // ns)
        if prev > 0 and self.waited[queue].get(sk, 0) < prev:
            self.waited[queue][sk] = prev
            waits.append((sk, prev))
        if fn is None:
            def fn(e, out=out, in_=in_):
                return e.dma_start(out=out, in_=in_)
        tok = (sk, prev + 16)
        self.q[queue].append((waits, fn, sk, 16))
        self._mark(reads, writes, tok)
        self.n_ops += 1

    def mm(self, out, lhsT, rhs, start, stop, reads, writes):
        self.op("pe", lambda e: e.matmul(out, lhsT=lhsT, rhs=rhs, start=start, stop=stop), reads, writes)

    def tr(self, out, in_, ident, reads, writes):
        self.op("pe", lambda e: e.transpose(out=out, in_=in_, identity=ident), reads, writes)

    def act(self, out, in_, func, reads, writes, scale=1.0, bias=None, accum_out=None):
        kw = {}
        if bias is not None:
            kw["bias"] = bias
        if accum_out is not None:
            kw["accum_out"] = accum_out
        self.op("act", lambda e: e.activation(out=out, in_=in_, func=func, scale=scale, **kw), reads, writes)

    def ts(self, eng, out, in0, s1, s2, op0, op1, reads, writes, accum_out=None):
        kw = {}
        if accum_out is not None:
            kw["accum_out"] = accum_out
        if op1 is None:
            self.op(eng, lambda e: e.tensor_scalar(out=out, in0=in0, scalar1=s1, scalar2=None, op0=op0, **kw), reads, writes)
        else:
            self.op(eng, lambda e: e.tensor_scalar(out=out, in0=in0, scalar1=s1, scalar2=s2, op0=op0, op1=op1, **kw),
                    reads, writes)

    def tt(self, eng, out, in0, in1, op, reads, writes):
        self.op(eng, lambda e: e.tensor_tensor(out=out, in0=in0, in1=in1, op=op), reads, writes)

    def stt(self, out, in0, scalar, in1, op0, op1, reads, writes):
        self.op("dve", lambda e: e.scalar_tensor_tensor(out=out, in0=in0, scalar=scalar, in1=in1, op0=op0, op1=op1),
                reads, writes)

    def scan(self, out, d0, d1, init, reads, writes):
        self.op("dve", lambda e: e.tensor_tensor_scan(out=out, data0=d0, data1=d1, initial=init, op0=ALU.mult, op1=ALU.add),
                reads, writes)

    def copy(self, eng, out, in_, reads, writes):
        self.op(eng, lambda e: e.tensor_copy(out=out, in_=in_), reads, writes)

    def memset(self, eng, ap, val, writes):
        self.op(eng, lambda e: e.memset(ap, val), (), writes)

    def coll(self, kind, groups, in_ap, out_ap, reads=(), writes=()):
        idx = len(self.csem)
        sem = self.es.enter_context(self.nc.semaphore("cc%d" % idx))
        self.csem.append(sem)
        sk = ("k", idx)
        self.sems[sk] = sem
        waits = self._deps("pool", reads, writes, True)

        def fn(e):
            return e.collective_compute(kind, ALU.bypass, replica_groups=groups, ins=[in_ap], outs=[out_ap])
        self.q["pool"].append((waits, fn, sk, None))
        self._mark(reads, writes, (sk, 1))
        self.n_ops += 1

    def finish(self, final_keys):
        self._barrier()
        self._emit_block()
        if self.ph is not None:
            self.ph.close()
            self.ph = None
        self.es.close()


def _rev(ap2d, n):
    apl = [list(s) for s in ap2d.ap]
    assert len(apl) == 2 and apl[1][1] == n and apl[1][0] == 1, apl
    from concourse.ap import AP
    return AP(ap2d.tensor, ap2d.offset + (n - 1), [apl[0], [-1, n]])


class PsumRing:
    def __init__(self, p, n=8, name="ps"):
        self.p = p
        self.t = [p.ps("%s%d" % (name, i), [128, 512], F32) for i in range(n)]
        for i in range(n):
            p.exclusive.add((name, i))
        self.i = 0
        self.n = n
        self.name = name

    def next(self):
        i = self.i
        self.i = (self.i + 1) % self.n
        return self.t[i], (self.name, i)


S_LEN = 8192
NSM = 11
GELU_C = 1.5957691216057308


def build_rnn(x_dtype=F32):
    nc = bass.Bass("TRN2", target_bir_lowering=False)
    xT = nc.dram_tensor("xT", [1024, S_LEN], x_dtype, kind="ExternalInput").ap()
    w_r = nc.dram_tensor("w_r", [1024, 512], F32, kind="ExternalInput").ap()
    w_g = nc.dram_tensor("w_g", [128, 8, 128], F32, kind="ExternalInput").ap()
    small = nc.dram_tensor("small", [128, 2, NSM], F32, kind="ExternalInput").ap()
    hgT = nc.dram_tensor("hgT", [256, S_LEN], BF16, kind="ExternalOutput").ap()
    p = Prog(nc)
    emit_rnn(p, xT, w_r, w_g, small, hgT)
    p.finish([("hg_out", b, c) for b in range(2) for c in range(4)])
    return nc


def emit_rnn(p, xT, w_r, w_g, small, hgT, pfx="r"):
    T = S_LEN
    CH = 2048
    NCH = T // CH
    wb = p.sb(pfx + "wb", [128, 8, 512], BF16)
    wg = p.sb(pfx + "wg", [128, 8, 128], BF16)
    sm = p.sb(pfx + "sm", [128, 2, NSM], F32)
    cl = p.sb(pfx + "cl", [128, 2, 2, 2], F32)
    zt = p.sb(pfx + "zt", [128, 4], F32)
    pt = p.sb(pfx + "pt", [128, 4], F32)
    xb = [p.sb(pfx + "xb%d" % i, [128, 8, 512], BF16) for i in range(2)]
    xr_full = p.sb(pfx + "xrf", [128, T + 4], F32)
    gy = p.sb(pfx + "gy", [128, T], BF16)
    xc = p.sb(pfx + "xc", [128, T], F32)
    xcb = p.sb(pfx + "xcb", [128, T], BF16)
    g1 = [p.sb(pfx + "g1_%d" % i, [128, 512], F32) for i in range(2)]
    g2 = [p.sb(pfx + "g2_%d" % i, [128, 512], F32) for i in range(2)]
    rtL = [p.sb(pfx + "rt%d" % i, [128, CH], F32) for i in range(2)]
    itL = [p.sb(pfx + "it%d" % i, [128, CH], F32) for i in range(2)]
    atL = [p.sb(pfx + "at%d" % i, [128, CH], F32) for i in range(2)]
    ccn = [0]
    hb = [p.sb(pfx + "hb%d" % i, [128, CH], F32) for i in range(2)]
    hgb = [p.sb(pfx + "hgb%d" % i, [128, CH], BF16) for i in range(2)]
    ring = PsumRing(p, 8, pfx + "ps")

    p.dma(wb[:], w_r.rearrange("(c p) n -> p c n", p=128), writes=["wb"], queue="pool")
    p.dma(wg[:], w_g, writes=["wg"], queue="pool")
    p.dma(sm[:], small, writes=["sm"])
    for b in range(2):
        for d in range(2):
            j = b * 2 + d
            p.act(zt[:, j:j + 1], sm[:, b, 5 + 3 * d + 2: 5 + 3 * d + 3], AF.Exp, ["sm"], ["zt"], scale=-1.0)
    p.ts("dve", pt[:], zt[:], -1.0 / 6, 1.0 / 5, ALU.mult, ALU.add, ["zt"], ["pt"])
    for cst in (-1.0 / 4, 1.0 / 3, -1.0 / 2, 1.0):
        p.tt("dve", pt[:], pt[:], zt[:], ALU.mult, ["pt", "zt"], ["pt"])
        p.ts("dve", pt[:], pt[:], cst, None, ALU.add, None, ["pt"], ["pt"])
    p.tt("dve", pt[:], pt[:], zt[:], ALU.mult, ["pt", "zt"], ["pt"])
    for b in range(2):
        for d in range(2):
            j = b * 2 + d
            p.ts("dve", cl[:, b, d, 0:1], pt[:, j:j + 1], -8.0, None, ALU.mult, None, ["pt"], ["cl"])
            p.ts("dve", cl[:, b, d, 1:2], pt[:, j:j + 1], -16.0, None, ALU.mult, None, ["pt"], ["cl"])
    chunked = (xT.shape[0] == 4096 + 128)
    if chunked:
        xTv = xT[128:128 + 4096, :].rearrange("(k r c p) n -> p r k c n", k=4, r=4, c=2, p=128)
    else:
        xTv = xT.rearrange("(c p) n -> p c n", p=128)

    def xrk(c):
        return [("xr", t) for t in range(4 * c, 4 * c + 4)]

    for blk in range(2):
        p.memset("pool", xr_full[:, 0:2], 0.0, [("xr", -1)])
        p.memset("pool", xr_full[:, T + 2:T + 4], 0.0, [("xr", 16)])
        for t in range(T // 512):
            xbt = xb[t % 2]
            xbk = ("xb", t % 2)
            if chunked:
                for k in range(4):
                    p.dma(xbt[:, 2 * k:2 * k + 2, :], xTv[:, t // 4, k, :, (t % 4) * 512:(t % 4 + 1) * 512],
                          reads=["xT_all"], writes=[xbk])
            else:
                p.dma(xbt[:], xTv[:, :, t * 512:(t + 1) * 512], writes=[xbk], queue="pool")
            for m in range(2):
                pst, psk = ring.next()
                col = (0 if m == 0 else 256) + blk * 128
                for k in range(8):
                    p.mm(pst[:], wb[:, k, col:col + 128], xbt[:, k, :], k == 0, k == 7, ["wb", xbk], [psk])
                if m == 0:
                    p.act(xr_full[:, 2 + t * 512: 2 + (t + 1) * 512], pst[:], AF.Copy, [psk], [("xr", t)])
                else:
                    a1, a2 = g1[t % 2], g2[t % 2]
                    k1, k2 = ("g1", t % 2), ("g2", t % 2)
                    p.act(a1[:], pst[:], AF.Square, [psk], [k1])
                    p.ts("dve", a1[:], a1[:], 0.044715, 1.0, ALU.mult, ALU.add, [k1], [k1])
                    p.tt("dve", a1[:], a1[:], pst[:], ALU.mult, [k1, psk], [k1])
                    p.act(a2[:], a1[:], AF.Sigmoid, [k1], [k2], scale=GELU_C)
                    p.tt("dve", gy[:, t * 512:(t + 1) * 512], a2[:], pst[:], ALU.mult, [k2, psk], [("gy", t)])
        for c in range(NCH):
            o = c * CH
            rk = [("xr", t) for t in range(4 * c - 1, 4 * c + 5)]
            p.ts("dve", xc[:, o:o + CH], xr_full[:, o:o + CH], sm[:, blk, 0:1], sm[:, blk, 4:5], ALU.mult, ALU.add,
                 rk + ["sm"], [("xc", c)])
            for tap in range(1, 4):
                p.stt(xc[:, o:o + CH], xr_full[:, o + tap:o + tap + CH], sm[:, blk, tap:tap + 1], xc[:, o:o + CH],
                      ALU.mult, ALU.add, rk + ["sm", ("xc", c)], [("xc", c)])
            p.copy("pool", xcb[:, o:o + CH], xc[:, o:o + CH], [("xc", c)], [("xcb", c)])
        hf = xr_full
        for d in range(2):
            order = list(range(NCH)) if d == 0 else list(range(NCH - 1, -1, -1))
            prev = None
            for ci, c in enumerate(order):
                o = c * CH
                cp = ccn[0] % 2
                ccn[0] += 1
                rt, it, at = rtL[cp], itL[cp], atL[cp]
                atk = ("at", cp)
                for s in range(CH // 512):
                    for g in range(2):
                        pst, psk = ring.next()
                        p.mm(pst[:], wg[:, blk * 4 + d * 2 + g, :], xcb[:, o + s * 512: o + (s + 1) * 512], True, True,
                             ["wg", ("xcb", c)], [psk])
                        dst = rt if g == 0 else it
                        p.act(dst[:, s * 512:(s + 1) * 512], pst[:], AF.Sigmoid, [psk, "sm"],
                              [("rt" if g == 0 else "it", cp, s)], bias=sm[:, blk, 5 + 3 * d + g: 5 + 3 * d + g + 1])
                rtk = [("rt", cp, s) for s in range(4)]
                itk = [("it", cp, s) for s in range(4)]
                p.act(at[:], rt[:], AF.Exp, rtk + ["cl"], [atk], scale=cl[:, blk, d, 0:1])
                p.act(rt[:], rt[:], AF.Exp, rtk + ["cl"], rtk, scale=cl[:, blk, d, 1:2])
                p.act(rt[:], rt[:], AF.Sqrt, rtk, rtk, scale=-1.0, bias=1.0)
                p.tt("pool", it[:], it[:], xc[:, o:o + CH], ALU.mult, itk + [("xc", c)], itk)
                p.tt("pool", it[:], it[:], rt[:], ALU.mult, itk + rtk, itk)
                if d == 0:
                    init = 0.0 if ci == 0 else hf[:, 2 + o - 1: 2 + o]
                    p.scan(hf[:, 2 + o: 2 + o + CH], at[:], it[:], init, [atk] + itk + [("xr", 4 * c - 1)], xrk(c))
                else:
                    hbt = hb[ci % 2]
                    init = 0.0 if ci == 0 else prev[:, 0:1]
                    p.scan(_rev(hbt[:], CH), _rev(at[:], CH), _rev(it[:], CH), init,
                           [atk] + itk + [("hb", (ci + 1) % 2)], [("hb", ci % 2)])
                    prev = hbt
                    hg = hgb[ci % 2]
                    p.tt("pool", at[:], hbt[:], hf[:, 2 + o: 2 + o + CH], ALU.add, [("hb", ci % 2), atk] + xrk(c), [atk])
                    p.tt("dve", hg[:], at[:], gy[:, o:o + CH], ALU.mult, [atk] + [("gy", t) for t in range(4 * c, 4 * c + 4)],
                         [("hgb", ci % 2)])
                    hdst = hgT[c, blk * 128:(blk + 1) * 128, :] if len(hgT.shape) == 3 else hgT[blk * 128:(blk + 1) * 128, o:o + CH]
                    p.dma(hdst, hg[:], reads=[("hgb", ci % 2)], writes=[("hg_out", blk, c)])


def pack_rnn_inputs(l, j, w_in, w_conv, b_conv, w_rec_gate, b_rec_gate, w_in_gate, b_in_gate, lru_lambda):
    XR0 = 1024 + 512
    YR0 = XR0 + 1024
    c0 = 2 * j * 128
    w_r = np.concatenate([w_in[l][:, XR0 + c0: XR0 + c0 + 256], w_in[l][:, YR0 + c0: YR0 + c0 + 256]], axis=1)
    w_g = np.empty((128, 8, 128), np.float32)
    small = np.empty((128, 2, NSM), np.float32)
    for b in range(2):
        cb = 2 * j + b
        sl = slice(cb * 128, (cb + 1) * 128)
        small[:, b, 0:4] = w_conv[l][:, sl].T
        small[:, b, 4] = b_conv[l][sl]
        for d in range(2):
            w_g[:, b * 4 + d * 2 + 0, :] = w_rec_gate[l, d, cb]
            w_g[:, b * 4 + d * 2 + 1, :] = w_in_gate[l, d, cb]
            small[:, b, 5 + 3 * d + 0] = b_rec_gate[l, d][sl]
            small[:, b, 5 + 3 * d + 1] = b_in_gate[l, d][sl]
            small[:, b, 5 + 3 * d + 2] = lru_lambda[l, d][sl]
    return {"w_r": np.ascontiguousarray(w_r), "w_g": w_g, "small": small}


TOK = 2048
HALO = 128
TH = TOK + 2 * HALO
ATT_SCALE = 128 ** -0.5


def build_attn(x_dtype=F32):
    nc = bass.Bass("TRN2", target_bir_lowering=False)
    d = {}
    d["xT"] = nc.dram_tensor("xT", [1024, TH], x_dtype, kind="ExternalInput").ap()
    d["w_qkv"] = nc.dram_tensor("w_qkv", [1024, 1536], F32, kind="ExternalInput").ap()
    d["cosT"] = nc.dram_tensor("cosT", [128, TH], F32, kind="ExternalInput").ap()
    d["sinT"] = nc.dram_tensor("sinT", [128, TH], F32, kind="ExternalInput").ap()
    d["masks"] = nc.dram_tensor("masks", [128, 3, 384], F32, kind="ExternalInput").ap()
    d["cmats"] = nc.dram_tensor("cmats", [128, 2, 128], F32, kind="ExternalInput").ap()
    d["sink"] = nc.dram_tensor("sink", [128, 8], F32, kind="ExternalInput").ap()
    d["oT"] = nc.dram_tensor("oT", [1024, TOK], BF16, kind="ExternalOutput").ap()
    p = Prog(nc)
    emit_attn(p, d)
    p.finish([("o_out", h, g) for h in range(8) for g in range(4)])
    return nc


def emit_attn(p, d, pfx="a", hook=None):
    xb = p.sb(pfx + "xb", [128, 8, TH], BF16)
    wq = [p.sb(pfx + "wq%d" % i, [128, 8, 128], BF16) for i in range(3)]
    wv = p.sb(pfx + "wv", [128, 8, 256], BF16)
    cosT = p.sb(pfx + "cos", [128, TH], F32)
    sinT = p.sb(pfx + "sin", [128, TH], F32)
    masks = p.sb(pfx + "masks", [128, 3, 384], BF16)
    cm0 = p.sb(pfx + "cm0", [128, 128], BF16)
    cm1 = p.sb(pfx + "cm1", [128, 128], BF16)
    sink = p.sb(pfx + "sink", [128, 8], F32)
    nsink = p.sb(pfx + "nsink", [128, 8], F32)
    qT = p.sb(pfx + "qT", [128, 8, TOK], BF16)
    kT = p.sb(pfx + "kT", [128, 2, TH], BF16)
    V = p.sb(pfx + "V", [128, TH // 128, 256], BF16)
    qraw = [p.sb(pfx + "qraw%d" % i, [128, 512], BF16) for i in range(2)]
    r1 = [p.sb(pfx + "r1_%d" % i, [128, 512], F32) for i in range(2)]
    r2 = [p.sb(pfx + "r2_%d" % i, [128, 512], F32) for i in range(2)]
    P = [p.sb(pfx + "P%d" % i, [128, 384], BF16) for i in range(2)]
    PT = [p.sb(pfx + "PT%d" % i, [128, 384], BF16) for i in range(2)]
    D = [p.sb(pfx + "D%d" % i, [128, 128], BF16) for i in range(2)]
    cols = [p.sb(pfx + "cols%d" % i, [128, 8], F32) for i in range(2)]
    ring = PsumRing(p, 6, pfx + "ps")
    oring = PsumRing(p, 2, pfx + "po")
    ident = cm0[:]
    pswap = cm1[:]

    if "xT_own" in d:
        own = d["xT_own"].rearrange("(c p) n -> p c n", p=128)
        allv = d["xT_all"][128:128 + 4096, :].rearrange("(k r c p) n -> p r k c n", k=4, r=4, c=2, p=128)
        nc_ = p.nc

        def emit_halo():
            allr = d["xT_all"][128:128 + 4096, :].rearrange("(k r q) n -> r k q n", k=4, r=4, q=256)

            def halo(e, left):
                jn = p.core_idx(e, "jl" if left else "jr")
                if left:
                    return e.dma_start(out=d["halo_l"].rearrange("(k q) n -> k q n", k=4), in_=allr[jn, :, :, TOK - HALO:TOK])
                return e.dma_start(out=d["halo_r"].rearrange("(k q) n -> k q n", k=4), in_=allr[jn, :, :, 0:HALO])
            agk = [("xT_all", k) for k in range(4)]
            p.dma(None, None, reads=agk, writes=["halo_l"], fn=lambda e: halo(e, True))
            p.dma(None, None, reads=agk, writes=["halo_r"], fn=lambda e: halo(e, False))
            p.dma(xb[:, :, 0:HALO], d["halo_l"].rearrange("(c p) n -> p c n", p=128), reads=["halo_l"],
                  writes=[("xbh", 0, k) for k in range(4)])
            p.dma(xb[:, :, HALO + TOK:TH], d["halo_r"].rearrange("(c p) n -> p c n", p=128), reads=["halo_r"],
                  writes=[("xbh", 1, k) for k in range(4)])
        if hook is None:
            emit_halo()
        for t0 in range(0, TOK, 512):
            keys = sorted(set([(HALO + t0) // 512, (HALO + t0 + 511) // 512]))
            p.dma(xb[:, :, HALO + t0:HALO + t0 + 512], own[:, :, t0:t0 + 512], reads=["xT_own"],
                  writes=[("xb", k) for k in keys])
    else:
        xTv = d["xT"].rearrange("(c p) n -> p c n", p=128)
        for t0 in range(0, TH, 512):
            w = min(512, TH - t0)
            p.dma(xb[:, :, t0:t0 + w], xTv[:, :, t0:t0 + w], writes=[("xb", t0 // 512)], queue="pool")

    def xbk(a, b):
        ks = [("xb", t) for t in range(a // 512, (b - 1) // 512 + 1)]
        if a < HALO:
            ks += [("xbh", 0, k) for k in range(4)]
        if b > HALO + TOK:
            ks += [("xbh", 1, k) for k in range(4)]
        return ks
    p.dma(wv[:], d["w_qkv"].rearrange("(c p) n -> p c n", p=128)[:, :, 1280:1536], writes=["wv"], queue="pool")
    p.dma(masks[:], d["masks"], writes=["masks"], queue="pool")
    p.dma(cm0[:], d["cmats"][:, 0, :], writes=["cm"], queue="pool")
    p.dma(cm1[:], d["cmats"][:, 1, :], writes=["cm"], queue="pool")
    p.dma(cosT[:], d["cosT"], writes=["cos"])
    p.dma(sinT[:], d["sinT"], writes=["sin"])
    p.dma(sink[:], d["sink"], writes=["sink"])
    p.ts("dve", nsink[:], sink[:], -1.0, None, ALU.mult, None, ["sink"], ["nsink"])
    wqv = d["w_qkv"].rearrange("(c p) n -> p c n", p=128)
    def emit_v():
        for tt in range(TH // 128):
            pst, psk = ring.next()
            for k in range(8):
                p.mm(pst[:, 0:256], xb[:, k, tt * 128:(tt + 1) * 128], wv[:, k, :], k == 0, k == 7, xbk(tt * 128, (tt + 1) * 128) + ["wv"], [psk])
            p.copy("dve" if tt % 2 else "act", V[:, tt, :], pst[:, 0:256], [psk], [("V", tt)]) if tt % 2 else \
                p.act(V[:, tt, :], pst[:, 0:256], AF.Copy, [psk], [("V", tt)])
    it = 0
    for m in list(range(8)) + ['v', 8, 9]:
        if m == 'v':
            emit_v()
            continue
        wt = wq[m % 3]
        wk = ("wq", m % 3)
        p.dma(wt[:], wqv[:, :, m * 128:(m + 1) * 128], writes=[wk], queue="pool")
        if m == 2 and hook is not None:
            hook()
            emit_halo()
        isq = m < 8
        ntok = TOK if isq else TH
        off = HALO if isq else 0
        t0 = 0
        while t0 < ntok:
            w = min(512, ntok - t0)
            pst, psk = ring.next()
            for k in range(8):
                p.mm(pst[:, 0:w], wt[:, k, :], xb[:, k, off + t0: off + t0 + w], k == 0, k == 7, xbk(off + t0, off + t0 + w) + [wk], [psk])
            qr = qraw[it % 2]
            qk = ("qraw", it % 2)
            p.act(qr[:, 0:w], pst[:, 0:w], AF.Copy, [psk], [qk])
            ps2, ps2k = ring.next()
            p.mm(ps2[:, 0:w], pswap, qr[:, 0:w], True, True, ["cm", qk], [ps2k])
            a1, a2 = r1[it % 2], r2[it % 2]
            k1, k2 = ("r1", it % 2), ("r2", it % 2)
            p.tt("dve", a1[:, 0:w], cosT[:, off + t0: off + t0 + w], pst[:, 0:w], ALU.mult, [psk, "cos"], [k1])
            p.tt("dve", a2[:, 0:w], sinT[:, off + t0: off + t0 + w], ps2[:, 0:w], ALU.mult, [ps2k, "sin"], [k2])
            if isq:
                dst = qT[:, m, t0:t0 + w]
                dk = [("qT", m, t0 // 512)]
            else:
                dst = kT[:, m - 8, t0:t0 + w]
                dk = [("kT", m - 8, t0 // 512)]
            p.tt("dve", dst, a1[:, 0:w], a2[:, 0:w], ALU.add, [k1, k2], dk)
            it += 1
            t0 += w
    its = [(grp, h, qi) for grp in range(4) for h in range(8) for qi in range(4)]
    NB = 5
    colsN = cols + [p.sb(pfx + "colsx%d" % i, [128, 8], F32) for i in range(NB - 2)]
    PN = P + [p.sb(pfx + "Px%d" % i, [128, 384], BF16) for i in range(NB - 2)]
    state = {}

    def stage_a(n):
        grp, h, qi = its[n]
        g = h // 4
        qb = grp * 4 + qi
        if qi == 0:
            state[(grp, h)] = oring.next()
        mi = 0 if qb == 0 else (2 if qb == 15 else 1)
        pss, pssk = ring.next()
        kkeys = [("kT", g, t) for t in sorted(set([(qb * 128) // 512, (qb * 128 + 383) // 512]))]
        p.mm(pss[:, 0:384], qT[:, h, qb * 128:(qb + 1) * 128], kT[:, g, qb * 128: qb * 128 + 384], True, False,
             [("qT", h, grp)] + kkeys, [pssk])
        p.mm(pss[:, 0:384], ident, masks[:, mi, :], False, True, ["cm", "masks"], [pssk])
        cl = colsN[n % NB]
        ck = ("cols", n % NB)
        p.op("dve", lambda e, cl=cl, pss=pss: e.reduce_max(out=cl[:, 0:1], in_=pss[:, 0:384], axis=AX.X), [pssk], [ck])
        p.ts("dve", cl[:, 1:2], cl[:, 0:1], -ATT_SCALE, nsink[:, h:h + 1], ALU.mult, ALU.min, [ck, "nsink"], [ck])
        Pt = PN[n % NB]
        pk = ("P", n % NB)
        p.act(Pt[:], pss[:, 0:384], AF.Exp, [pssk, ck], [pk, ck], scale=ATT_SCALE, bias=cl[:, 1:2], accum_out=cl[:, 2:3])
        p.act(cl[:, 3:4], cl[:, 1:2], AF.Exp, [ck, "sink"], [ck], bias=sink[:, h:h + 1])

    def stage_b(n):
        grp, h, qi = its[n]
        g = h // 4
        qb = grp * 4 + qi
        po, pok = state[(grp, h)]
        cl = colsN[n % NB]
        ck = ("cols", n % NB)
        Pt = PN[n % NB]
        pk = ("P", n % NB)
        p.tt("dve", cl[:, 4:5], cl[:, 2:3], cl[:, 3:4], ALU.add, [ck], [ck])
        p.op("dve", lambda e, cl=cl: e.reciprocal(out=cl[:, 5:6], in_=cl[:, 4:5]), [ck], [ck])
        Dt = D[n % 2]
        dk = ("D", n % 2)
        p.ts("dve", Dt[:], ident, cl[:, 5:6], None, ALU.mult, None, ["cm", ck], [dk])
        ppt, pptk = ring.next()
        for kb in range(3):
            p.mm(ppt[:, kb * 128:(kb + 1) * 128], Pt[:, kb * 128:(kb + 1) * 128], Dt[:], True, True, [pk, dk], [pptk])
        PTt = PT[n % 2]
        ptk = ("PT", n % 2)
        p.act(PTt[:], ppt[:, 0:384], AF.Copy, [pptk], [ptk])
        for kb in range(3):
            p.mm(po[:, qi * 128:(qi + 1) * 128], V[:, qb + kb, g * 128:(g + 1) * 128], PTt[:, kb * 128:(kb + 1) * 128],
                 kb == 0, kb == 2, [ptk, ("V", qb + kb)], [pok])
        if qi == 3:
            p.copy("dve", qT[:, h, grp * 512:(grp + 1) * 512], po[:], [pok], [("qT", h, grp)])
            p.dma(d["oT"][h * 128:(h + 1) * 128, grp * 512:(grp + 1) * 512], qT[:, h, grp * 512:(grp + 1) * 512],
                  reads=[("qT", h, grp)], writes=[("o_out", h, grp)])

    SKEW = 3
    for n in range(len(its) + SKEW):
        if n < len(its):
            stage_a(n)
        if n - SKEW >= 0:
            stage_b(n - SKEW)


def attn_consts(j):
    inv = (10000.0 ** (-np.arange(0, 128, 2, dtype=np.float32) / 128)).astype(np.float32)
    pos = (j * TOK - HALO + np.arange(TH)).astype(np.float32)
    ang = pos[:, None] * inv[None, :]
    cos = np.cos(ang).astype(np.float32).T
    sin = np.sin(ang).astype(np.float32).T
    cosT = np.concatenate([cos, cos], axis=0)
    sinT = np.concatenate([-sin, sin], axis=0)
    qi = np.arange(128)[:, None]
    kj = np.arange(384)[None, :]
    rel = kj - 128 - qi
    base = np.where(np.abs(rel) <= 128, 0.0, -30000.0).astype(np.float32)
    first = base.copy(); first[:, 0:128] = -30000.0
    last = base.copy(); last[:, 256:384] = -30000.0
    masks = np.stack([first if j == 0 else base, base, last if j == 3 else base], axis=1)
    ident = np.eye(128, dtype=np.float32)
    swap = np.zeros((128, 128), np.float32)
    mm = np.arange(128)
    swap[(mm + 64) % 128, mm] = 1.0
    cmats = np.stack([ident, swap], axis=1)
    return {"cosT": np.ascontiguousarray(cosT), "sinT": np.ascontiguousarray(sinT),
            "masks": np.ascontiguousarray(masks), "cmats": np.ascontiguousarray(cmats)}


def halo_xT(x_b, j):
    out = np.zeros((1024, TH), x_b.dtype)
    lo, hi = j * TOK - HALO, (j + 1) * TOK + HALO
    slo, shi = max(lo, 0), min(hi, S_LEN)
    out[:, slo - lo: shi - lo] = x_b[slo:shi].T
    return out


ALPHA = 8 ** 0.25
LN_EPS = 1e-5
N_EXP = 32
CAP = 256
NSLOT = N_EXP * CAP
NROWS = NSLOT + 128


def build_mix(x_dtype=F32):
    nc = bass.Bass("TRN2", target_bir_lowering=False)
    d = {}
    d["xT"] = nc.dram_tensor("xT", [1024, TOK], x_dtype, kind="ExternalInput").ap()
    d["x_tok"] = nc.dram_tensor("x_tok", [TOK, 1024], F32, kind="ExternalInput").ap()
    d["oT"] = nc.dram_tensor("oT", [1024, TOK], BF16, kind="ExternalInput").ap()
    d["hgT"] = nc.dram_tensor("hgT", [1024, TOK], BF16, kind="ExternalInput").ap()
    d["w4"] = nc.dram_tensor("w4", [4, 1024, 1024], F32, kind="ExternalInput").ap()
    d["w_out"] = nc.dram_tensor("w_out", [1024, 1024], F32, kind="ExternalInput").ap()
    d["ln"] = nc.dram_tensor("ln", [128, 2, 1024], F32, kind="ExternalInput").ap()
    d["w_rt"] = nc.dram_tensor("w_rt", [1024, 36], F32, kind="ExternalInput").ap()
    d["b_rt"] = nc.dram_tensor("b_rt", [128, 36], F32, kind="ExternalInput").ap()
    d["cst"] = nc.dram_tensor("cst", [128, 3, 128], F32, kind="ExternalInput").ap()
    d["cst2"] = nc.dram_tensor("cst2", [128, 40], F32, kind="ExternalInput").ap()
    d["x1"] = nc.dram_tensor("x1", [TOK, 1024], F32, kind="ExternalOutput").ap()
    d["xdisp"] = nc.dram_tensor("xdisp", [NROWS, 1024], BF16, kind="ExternalOutput").ap()
    d["slots"] = nc.dram_tensor("slots", [TOK, 2], I32, kind="ExternalOutput").ap()
    d["gates"] = nc.dram_tensor("gates", [TOK, 2], F32, kind="ExternalOutput").ap()
    p = Prog(nc)
    fk = emit_mix(p, d)
    p.finish(fk)
    return nc


def emit_ln(p, y, out, lnt, which, stat, reads, writes, tag):
    st6, mv, rs = stat
    sk = ("lnstat", tag)
    for hh in range(2):
        p.op("dve", lambda e, hh=hh: e.bn_stats(out=st6[:, hh * 6:(hh + 1) * 6], in_=y[:, hh * 512:(hh + 1) * 512]),
             reads + [sk], [sk])
    p.op("dve", lambda e: e.bn_aggr(out=mv[:, 0:2], in_=st6[:, 0:12]), [sk], [sk])
    p.act(rs[:, 0:1], mv[:, 1:2], AF.Sqrt, [sk], [sk], bias=LN_EPS)
    p.op("dve", lambda e: e.reciprocal(out=rs[:, 1:2], in_=rs[:, 0:1]), [sk], [sk])
    p.ts("dve", out, y, mv[:, 0:1], rs[:, 1:2], ALU.subtract, ALU.mult, reads + [sk], writes)
    p.tt("pool", out, out, lnt[:, which, 0, :], ALU.mult, writes + ["ln"], writes)
    p.tt("pool", out, out, lnt[:, which, 1, :], ALU.add, writes + ["ln"], writes)


def emit_mix(p, d, pfx="m", hook=None):
    ot = [p.sb(pfx + "ot%d" % i, [128, 8, 512], BF16) for i in range(1)] * 2
    hg = [p.sb(pfx + "hg%d" % i, [128, 8, 512], BF16) for i in range(1)] * 2
    xb = [p.sb(pfx + "xb%d" % i, [128, 8, 512], BF16) for i in range(1)] * 2
    w4r = p.sb(pfx + "w4r", [128, 4, 8, 1024], BF16)
    wo = p.sb(pfx + "wo", [128, 8, 1024], BF16)
    mg = [p.sb(pfx + "mg%d" % i, [128, 8, 512], BF16) for i in range(2)]
    t1 = [p.sb(pfx + "t1_%d" % i, [128, 512], F32) for i in range(2)]
    t2 = [p.sb(pfx + "t2_%d" % i, [128, 512], F32) for i in range(2)]
    lnt = p.sb(pfx + "ln", [128, 1, 2, 1024], F32)
    wrt = p.sb(pfx + "wrt", [128, 8, 36], F32)
    brt = p.sb(pfx + "brt", [128, 36], F32)
    cst = p.sb(pfx + "cst", [128, 3, 128], F32)
    cstb = p.sb(pfx + "cstb", [128, 2, 128], BF16)
    cst2 = p.sb(pfx + "cst2", [128, 40], F32)
    zero = p.sb(pfx + "zero", [128, 1024], BF16)
    xt = [p.sb(pfx + "xt%d" % i, [128, 1024], F32) for i in range(2)]
    y = [p.sb(pfx + "y%d" % i, [128, 1024], F32) for i in range(2)]
    x1 = [p.sb(pfx + "x1_%d" % i, [128, 1024], F32) for i in range(2)]
    x1b = [p.sb(pfx + "x1b%d" % i, [128, 1024], BF16) for i in range(2)]
    x1T = [p.sb(pfx + "x1T%d" % i, [128, 8, 128], F32) for i in range(2)]
    st6 = p.sb(pfx + "st6", [128, 12], F32)
    mv = p.sb(pfx + "mv", [128, 2], F32)
    rs = p.sb(pfx + "rs", [128, 2], F32)
    rt = [p.sb(pfx + "rt%d" % i, [128, 64], F32) for i in range(2)]
    E = [p.sb(pfx + "E%d" % i, [128, 3, 32], F32) for i in range(2)]
    Eb = [p.sb(pfx + "Eb%d" % i, [128, 32], BF16) for i in range(2)]
    base = p.sb(pfx + "base", [128, 32], F32)
    sl = [p.sb(pfx + "sl%d" % i, [128, 2], I32) for i in range(2)]
    gt = [p.sb(pfx + "gt%d" % i, [128, 2], F32) for i in range(2)]
    i8 = [p.sb(pfx + "i8_%d" % i, [128, 8], U32) for i in range(2)]
    ring = PsumRing(p, 8, pfx + "ps")
    ident = cst[:, 0, :]
    iota32 = cst2[:, 0:32]
    iota4 = cst2[:, 32:36]
    trash = cst2[:, 36:37]

    p.dma(lnt[:, 0, :, :], d["ln"], writes=["ln"])
    p.dma(wrt[:], d["w_rt"].rearrange("(c p) n -> p c n", p=128), writes=["wrt"])
    p.dma(brt[:], d["b_rt"], writes=["brt"])
    p.dma(cst[:], d["cst"], writes=["cst"])
    p.dma(cst2[:], d["cst2"], writes=["cst2"])
    p.copy("dve", cstb[:, 0, :], cst[:, 1, :], ["cst"], ["cstb"])
    p.copy("dve", cstb[:, 1, :], cst[:, 2, :], ["cst"], ["cstb"])
    p.memset("pool", zero[:], 0.0, ["zero"])
    p.memset("pool", base[:], 0.0, ["base"])
    zk = []
    for r0 in range(0, NROWS, 1024):
        nr = min(1024, NROWS - r0)
        p.dma(d["xdisp"][r0:r0 + nr, :].rearrange("(a p) n -> p a n", p=128),
              zero[:].partition_broadcast(128) if False else zero[:, None, :].to_broadcast([128, nr // 128, 1024]),
              reads=["zero"], writes=[("xdz", r0)])
        zk.append(("xdz", r0))
    for q in range(4):
        for c0 in range(0, 1024, 512):
            p.dma(w4r[:, q, :, c0:c0 + 512], d["w4"][q].rearrange("(c p) n -> p c n", p=128)[:, :, c0:c0 + 512],
                  writes=[("w4r", q, c0)], queue="pool")
    w4k = [[("w4r", q, 0), ("w4r", q, 512)] for q in range(4)]
    for c0 in range(0, 1024, 512):
        p.dma(wo[:, :, c0:c0 + 512], d["w_out"].rearrange("(c p) n -> p c n", p=128)[:, :, c0:c0 + 512], writes=[("wo", c0)], queue="pool")
    if hook is not None:
        hook()
    oTv = d["oT"].rearrange("(c p) n -> p c n", p=128)
    hgv = d["hgT"].rearrange("(c p) n -> p c n", p=128) if "hgT" in d else None
    xTv = (d["xT_own"] if "xT_own" in d else d["xT"]).rearrange("(c p) n -> p c n", p=128)
    fin = []
    wn = 0
    tile_i = 0
    for T in range(TOK // 512):
        b = T % 2
        p.dma(ot[b][:], oTv[:, :, T * 512:(T + 1) * 512], writes=[("ot", 0)])
        if "hg_all" in d:
            if T == 0:
                hat = d["hg_all"][128:128 + 4096, :].rearrange("(t q) n -> t q n", t=4)
                p.dma(None, None, reads=[("hg_all", t_) for t_ in range(4)], writes=["hg_mine"], queue="act",
                      fn=lambda e: e.dma_start(out=d["hg_mine"], in_=hat[p.core_idx(e, "j")]))
            p.dma(hg[b][:], d["hg_mine"].rearrange("(c p) n -> p c n", p=128)[:, :, T * 512:(T + 1) * 512],
                  reads=["hg_mine"], writes=[("hg", 0)])
            p.dma(xb[b][:], xTv[:, :, T * 512:(T + 1) * 512], reads=["xT_own"], writes=[("xb", 0)])
        else:
            p.dma(hg[b][:], hgv[:, :, T * 512:(T + 1) * 512], writes=[("hg", 0)])
            p.dma(xb[b][:], xTv[:, :, T * 512:(T + 1) * 512], writes=[("xb", 0)], queue="pool")
        for m in range(8):
            banks = [ring.next() for _ in range(4)]
            srcs = [ot[b], hg[b], xb[b], xb[b]]
            skeys = [("ot", 0), ("hg", 0), ("xb", 0), ("xb", 0)]
            for q in range(4):
                pst, psk = banks[q]
                for k in range(8):
                    p.mm(pst[:], w4r[:, q, k, m * 128:(m + 1) * 128], srcs[q][:, k, :], k == 0, k == 7,
                         w4k[q] + [skeys[q]], [psk])
            a1, a2 = t1[m % 2], t2[m % 2]
            k1, k2 = ("t1", m % 2), ("t2", m % 2)
            p.act(a1[:], banks[2][0][:], AF.Sigmoid, [banks[2][1]], [k1])
            p.act(a2[:], banks[3][0][:], AF.Sigmoid, [banks[3][1]], [k2])
            p.tt("dve", a1[:], a1[:], banks[0][0][:], ALU.mult, [k1, banks[0][1]], [k1])
            p.tt("dve", a2[:], a2[:], banks[1][0][:], ALU.mult, [k2, banks[1][1]], [k2])
            p.tt("pool", mg[b][:, m, :], a1[:], a2[:], ALU.add, [k1, k2], [("mg", b, m)])
        mgk = [("mg", b, m) for m in range(8)]
        def stage_a(s, tile_i):
            tb = tile_i % 2
            tok0 = T * 512 + s * 128
            x1k = ("x1", tb)
            p.dma(xt[tb][:], d["x_tok"][tok0:tok0 + 128, :], writes=[("xt", tb)])
            for hh in range(2):
                pst, psk = ring.next()
                for k in range(8):
                    p.mm(pst[:], mg[b][:, k, s * 128:(s + 1) * 128], wo[:, k, hh * 512:(hh + 1) * 512], k == 0, k == 7,
                         mgk + [("wo", hh * 512)], [psk])
                p.stt(y[tb][:, hh * 512:(hh + 1) * 512], xt[tb][:, hh * 512:(hh + 1) * 512], ALPHA, pst[:], ALU.mult, ALU.add,
                      [("xt", tb), psk], [("y", tb, hh)])
            x1k = ("x1", tb)
            emit_ln(p, y[tb][:], x1[tb][:], lnt, 0, (st6, mv, rs), [("y", tb, 0), ("y", tb, 1)], [x1k], "a")
            p.dma(d["x1"][tok0:tok0 + 128, :], x1[tb][:], reads=[x1k], writes=[("x1o", tile_i)])
            fin.append(("x1o", tile_i))
            p.act(x1b[tb][:], x1[tb][:], AF.Copy, [x1k], [("x1b", tb)])

        def stage_b(s, tile_i):
            tb = tile_i % 2
            tok0 = T * 512 + s * 128
            x1k = ("x1", tb)
            for hh in range(2):
                pst, psk = ring.next()
                for c in range(4):
                    k = hh * 4 + c
                    p.tr(pst[:, c * 128:(c + 1) * 128], x1[tb][:, k * 128:(k + 1) * 128], ident, [x1k, "cst"], [psk])
                p.copy("dve", x1T[tb][:, hh * 4:(hh + 1) * 4, :], pst[:].rearrange("p (c n) -> p c n", c=4), [psk],
                       [("x1T", tb, hh)])
            pl, plk = ring.next()
            for k in range(8):
                p.mm(pl[:, 0:36], x1T[tb][:, k, :], wrt[:, k, :], k == 0, k == 7, [("x1T", tb, 0), ("x1T", tb, 1), "wrt"], [plk])
            r = rt[tb]
            rk = ("rt", tb)
            p.tt("dve", r[:, 0:36], pl[:, 0:36], brt[:], ALU.add, [plk, "brt", rk], [rk])
            p.op("dve", lambda e, r=r: e.reduce_max(out=r[:, 36:37], in_=r[:, 0:4], axis=AX.X), [rk], [rk])
            p.ts("dve", r[:, 37:38], r[:, 36:37], -1.0, None, ALU.mult, None, [rk], [rk])
            p.act(r[:, 44:48], r[:, 0:4], AF.Exp, [rk], [rk], bias=r[:, 37:38], accum_out=r[:, 38:39])
            p.op("dve", lambda e, r=r: e.reciprocal(out=r[:, 39:40], in_=r[:, 38:39]), [rk], [rk])
            p.ts("dve", r[:, 40:44], r[:, 0:4], r[:, 36:37], None, ALU.is_equal, None, [rk], [rk])
            p.ts("dve", r[:, 48:56], r[:, 4:12], r[:, 40:41], None, ALU.mult, None, [rk], [rk])
            for g in range(1, 4):
                p.stt(r[:, 48:56], r[:, 4 + 8 * g:12 + 8 * g], r[:, 40 + g:41 + g], r[:, 48:56], ALU.mult, ALU.add, [rk], [rk])
            Et = E[tb]
            ek = ("E", tb)
            p.tt("dve", Et[:, 2, 0:4], r[:, 40:44], iota4, ALU.mult, [rk, "cst2", ek], [ek])
            p.op("dve", lambda e, r=r, Et=Et: e.reduce_sum(out=r[:, 58:59], in_=Et[:, 2, 0:4], axis=AX.X), [rk, ek], [rk])
            m8 = r[:, 48:56]
            i8t = i8[tb]
            p.op("dve", lambda e, r=r, Et=Et: e.max(out=Et[:, 2, 8:16], in_=r[:, 48:56]), [rk, ek], [ek])
            p.op("dve", lambda e, r=r, Et=Et, i8t=i8t: e.max_index(out=i8t[:], in_max=Et[:, 2, 8:16], in_values=r[:, 48:56]),
                 [rk, ek], [("i8", tb)])
            g = gt[tb]
            gk = ("gt", tb)
            p.tt("dve", r[:, 56:57], Et[:, 2, 8:9], Et[:, 2, 9:10], ALU.subtract, [ek, rk], [rk])
            p.act(r[:, 57:58], r[:, 56:57], AF.Sigmoid, [rk], [rk])
            p.tt("dve", g[:, 0:1], r[:, 57:58], r[:, 39:40], ALU.mult, [rk, gk], [gk])
            p.tt("dve", g[:, 1:2], r[:, 39:40], g[:, 0:1], ALU.subtract, [rk, gk], [gk])
            p.copy("dve", r[:, 59:61], i8t[:, 0:2], [("i8", tb), rk], [rk])
            p.stt(r[:, 59:61], r[:, 58:59].to_broadcast([128, 2]), 8.0, r[:, 59:61], ALU.mult, ALU.add, [rk], [rk])
            for kk in range(2):
                p.ts("dve", Et[:, kk, :], iota32, r[:, 59 + kk:60 + kk], None, ALU.is_equal, None, ["cst2", rk, ek], [ek])
            p.tt("dve", Eb[tb][:], Et[:, 0, :], Et[:, 1, :], ALU.add, [ek], [("Eb", tb)])
            pc, pck = ring.next()
            p.mm(pc[:, 0:32], cstb[:, 0, :], Eb[tb][:], True, True, ["cstb", ("Eb", tb)], [pck])
            p.mm(pc[:, 32:64], cstb[:, 1, :], Eb[tb][:], True, True, ["cstb", ("Eb", tb)], [pck])
            p.tt("dve", Et[:, 2, :], pc[:, 0:32], base[:], ALU.add, [pck, "base", ek], [ek])
            for kk in range(2):
                p.tt("dve", Et[:, kk, :], Et[:, kk, :], Et[:, 2, :], ALU.mult, [ek], [ek])
                p.op("dve", lambda e, r=r, Et=Et, kk=kk: e.reduce_sum(out=r[:, 61 + kk:62 + kk], in_=Et[:, kk, :], axis=AX.X),
                     [ek, rk], [rk])
            p.tt("dve", base[:], base[:], pc[:, 32:64], ALU.add, [pck, "base"], ["base"])
            for kk in range(2):
                p.ts("dve", r[:, 63:64], r[:, 61 + kk:62 + kk], float(CAP), None, ALU.is_lt, None, [rk], [rk])
                p.stt(r[:, 61 + kk:62 + kk], r[:, 59 + kk:60 + kk], float(CAP), r[:, 61 + kk:62 + kk], ALU.mult, ALU.add,
                      [rk], [rk])
                p.tt("dve", r[:, 61 + kk:62 + kk], r[:, 61 + kk:62 + kk], trash, ALU.subtract, [rk, "cst2"], [rk])
                p.tt("dve", r[:, 61 + kk:62 + kk], r[:, 61 + kk:62 + kk], r[:, 63:64], ALU.mult, [rk], [rk])
                p.tt("dve", r[:, 61 + kk:62 + kk], r[:, 61 + kk:62 + kk], trash, ALU.add, [rk, "cst2"], [rk])
                p.tt("dve", g[:, kk:kk + 1], g[:, kk:kk + 1], r[:, 63:64], ALU.mult, [rk, gk], [gk])
            slt = sl[tb]
            slk = ("sl", tb)
            p.copy("dve", slt[:], r[:, 61:63], [rk], [slk])
            p.dma(d["slots"][tok0:tok0 + 128, :], slt[:], reads=[slk], writes=[("slo", tile_i)])
            p.dma(d["gates"][tok0:tok0 + 128, :], g[:], reads=[gk], writes=[("gto", tile_i)])
            fin.extend([("slo", tile_i), ("gto", tile_i)])
            for kk in range(2):
                p.dma(None, None, reads=[("x1b", tb), slk] + zk, writes=[("xdo", tile_i, kk)], queue="pool",
                      fn=lambda e, tb=tb, kk=kk, slt=slt: e.indirect_dma_start(
                          out=d["xdisp"], out_offset=bass.IndirectOffsetOnAxis(ap=slt[:, kk:kk + 1], axis=0),
                          in_=x1b[tb][:, :], in_offset=None))
                fin.append(("xdo", tile_i, kk))

        stage_a(0, T * 4)
        for s in range(4):
            if s + 1 < 4:
                stage_a(s + 1, T * 4 + s + 1)
            stage_b(s, T * 4 + s)
    return fin


def mix_consts():
    ident = np.eye(128, dtype=np.float32)
    tp = np.arange(128)[:, None]
    t = np.arange(128)[None, :]
    lower = (tp < t).astype(np.float32)
    ones = np.ones((128, 128), np.float32)
    cst = np.stack([ident, lower, ones], axis=1)
    cst2 = np.zeros((128, 40), np.float32)
    cst2[:, 0:32] = np.arange(32, dtype=np.float32)[None, :]
    cst2[:, 32:36] = np.arange(4, dtype=np.float32)[None, :]
    cst2[:, 36] = NSLOT + np.arange(128)
    return {"cst": np.ascontiguousarray(cst), "cst2": cst2}


def build_moe():
    nc = bass.Bass("TRN2", target_bir_lowering=False)
    d = {}
    d["xdisp"] = nc.dram_tensor("xdisp", [NROWS, 1024], BF16, kind="ExternalInput").ap()
    d["x1"] = nc.dram_tensor("x1", [TOK, 1024], F32, kind="ExternalInput").ap()
    d["slots"] = nc.dram_tensor("slots", [TOK, 2], I32, kind="ExternalInput").ap()
    d["gates"] = nc.dram_tensor("gates", [TOK, 2], F32, kind="ExternalInput").ap()
    d["w_g"] = nc.dram_tensor("w_g", [N_EXP, 1024, 512], F32, kind="ExternalInput").ap()
    d["w_u"] = nc.dram_tensor("w_u", [N_EXP, 1024, 512], F32, kind="ExternalInput").ap()
    d["w_d"] = nc.dram_tensor("w_d", [N_EXP, 512, 1024], F32, kind="ExternalInput").ap()
    d["ln"] = nc.dram_tensor("ln", [128, 2, 1024], F32, kind="ExternalInput").ap()
    d["ident"] = nc.dram_tensor("ident", [128, 128], F32, kind="ExternalInput").ap()
    d["ydisp"] = nc.dram_tensor("ydisp", [NROWS, 1024], F32).ap()
    d["x2"] = nc.dram_tensor("x2", [TOK, 1024], F32, kind="ExternalOutput").ap()
    p = Prog(nc)
    fk = emit_moe(p, d)
    p.finish(fk)
    return nc


def emit_moe(p, d, pfx="e"):
    NWB = 3
    wg = [p.sb(pfx + "wg%d" % i, [128, 8, 512], BF16) for i in range(NWB)]
    wu = [p.sb(pfx + "wu%d" % i, [128, 8, 512], BF16) for i in range(NWB)]
    wd = [p.sb(pfx + "wd%d" % i, [128, 4, 1024], BF16) for i in range(NWB)]
    xe = [p.sb(pfx + "xe%d" % i, [128, 1024], BF16) for i in range(2)]
    xeT = [p.sb(pfx + "xeT%d" % i, [128, 8, CAP], BF16) for i in range(2)]
    hT = [p.sb(pfx + "hT%d" % i, [128, 4, CAP], BF16) for i in range(2)]
    sg = [p.sb(pfx + "sg%d" % i, [128, CAP], F32) for i in range(2)]
    yt = [p.sb(pfx + "yt%d" % i, [128, 1024], F32) for i in range(2)]
    ident = p.sb(pfx + "ident", [128, 128], BF16)
    lnt = p.sb(pfx + "ln", [128, 1, 2, 1024], F32)
    zero = p.sb(pfx + "zero", [128, 1024], F32)
    sl = [p.sb(pfx + "sl%d" % i, [128, 2], I32) for i in range(2)]
    gt = [p.sb(pfx + "gt%d" % i, [128, 2], F32) for i in range(2)]
    x1t = [p.sb(pfx + "x1t%d" % i, [128, 1024], F32) for i in range(2)]
    ya = [p.sb(pfx + "ya%d" % i, [128, 1024], F32) for i in range(2)]
    yb = [p.sb(pfx + "yb%d" % i, [128, 1024], F32) for i in range(2)]
    yo = [p.sb(pfx + "yo%d" % i, [128, 1024], F32) for i in range(2)]
    st6 = p.sb(pfx + "st6", [128, 12], F32)
    mv = p.sb(pfx + "mv", [128, 2], F32)
    rs = p.sb(pfx + "rs", [128, 2], F32)
    if "xT_next" in d:
        xTn = [p.sb(pfx + "xTn%d" % i, [128, 8, 512], BF16) for i in range(2)]
        identf = p.sb(pfx + "identf", [128, 128], F32)
        p.dma(identf[:], d["ident"], writes=["identf"])
    ring = PsumRing(p, 6, pfx + "ps")
    ptr = [p.ps(pfx + "ptr%d" % i, [128, 1024], BF16) for i in range(2)]
    for i in range(2):
        p.exclusive.add((pfx + "ptr", i))

    p.dma(ident[:], d["ident"], writes=["ident"], queue="pool")
    p.dma(lnt[:, 0, :, :], d["ln"], writes=["ln"])
    p.memset("pool", zero[:], 0.0, ["zero"])
    p.dma(d["ydisp"][NSLOT:NROWS, :], zero[:], reads=["zero"], writes=[("yd", -1, 0)])
    ydk = [("yd", -1, 0)]
    nb = 0
    nstg = 0
    stg = [p.sb(pfx + "stg%d" % i, [128, 4096], F32) for i in range(2)]
    for e in range(N_EXP):
        b = e % NWB
        wk = ("w", b)
        for wi, (wsrc, wdst, wkey) in enumerate(((d["w_g"][e], wg[b], ("wg", b)), (d["w_u"][e], wu[b], ("wu", b)),
                                                 (d["w_d"][e], wd[b], ("wd", b)))):
            if True:
                p.dma(wdst[:], wsrc.rearrange("(c p) n -> p c n", p=128), writes=[wkey], queue="pool")
                continue
            sgi = nstg % 2
            nstg += 1
            nchunk = 4 if wi == 2 else 8
            p.dma(stg[sgi][:].rearrange("p (c n) -> p c n", c=nchunk), wsrc.rearrange("(c p) n -> p c n", p=128),
                  writes=[("stg", sgi)])
            dflat = wdst[:].rearrange("p c n -> p (c n)")
            if wi == 1:
                p.copy("pool", dflat, stg[sgi][:], [("stg", sgi)], [wkey])
            else:
                p.act(dflat, stg[sgi][:], AF.Copy, [("stg", sgi)], [wkey])
        for blk in range(CAP // 128):
            xb_ = xe[nb % 2]
            xk = ("xe", nb % 2)
            r0 = e * CAP + blk * 128
            p.dma(xb_[:], d["xdisp"][r0:r0 + 128, :], writes=[xk])
            pt = ptr[nb % 2]
            ptk = (pfx + "ptr", nb % 2)
            for k in range(8):
                p.tr(pt[:, k * 128:(k + 1) * 128], xb_[:, k * 128:(k + 1) * 128], ident[:], [xk, "ident"], [ptk])
            p.copy("dve" if blk else "act", xeT[e % 2][:, :, blk * 128:(blk + 1) * 128], pt[:].rearrange("p (c n) -> p c n", c=8),
                   [ptk], [("xeT", e % 2, blk)]) if blk else \
                p.act(xeT[e % 2][:, :, blk * 128:(blk + 1) * 128], pt[:].rearrange("p (c n) -> p c n", c=8), AF.Copy,
                      [ptk], [("xeT", e % 2, blk)])
            nb += 1
        xtk = [("xeT", e % 2, blk) for blk in range(CAP // 128)]
        for m in range(4):
            pg, pgk = ring.next()
            pu, puk = ring.next()
            for k in range(8):
                p.mm(pg[:, 0:CAP], wg[b][:, k, m * 128:(m + 1) * 128], xeT[e % 2][:, k, :], k == 0, k == 7, [("wg", b)] + xtk, [pgk])
            for k in range(8):
                p.mm(pu[:, 0:CAP], wu[b][:, k, m * 128:(m + 1) * 128], xeT[e % 2][:, k, :], k == 0, k == 7, [("wu", b)] + xtk, [puk])
            s_ = sg[m % 2]
            sk = ("sg", m % 2)
            p.act(s_[:], pg[:, 0:CAP], AF.Silu, [pgk], [sk])
            p.tt("dve", hT[e % 2][:, m, :], s_[:], pu[:, 0:CAP], ALU.mult, [sk, puk], [("hT", e % 2, m)])
        htk = [("hT", e % 2, m) for m in range(4)]
        for blk in range(CAP // 128):
            y_ = yt[blk % 2]
            yk = ("yt", blk % 2)
            for hh in range(2):
                py, pyk = ring.next()
                for k in range(4):
                    p.mm(py[:], hT[e % 2][:, k, blk * 128:(blk + 1) * 128], wd[b][:, k, hh * 512:(hh + 1) * 512], k == 0, k == 3,
                         htk + [("wd", b)], [pyk])
                if hh == 0:
                    p.act(y_[:, 0:512], py[:], AF.Copy, [pyk], [yk])
                else:
                    p.copy("dve", y_[:, 512:1024], py[:], [pyk], [yk])
            r0 = e * CAP + blk * 128
            p.dma(d["ydisp"][r0:r0 + 128, :], y_[:], reads=[yk], writes=[("yd", e, blk)])
            ydk.append(("yd", e, blk))
    fin = []

    def cload(t):
        b = t % 2
        tok0 = t * 128
        p.dma(sl[b][:], d["slots"][tok0:tok0 + 128, :], writes=[("sl", b)])
        p.dma(gt[b][:], d["gates"][tok0:tok0 + 128, :], writes=[("gt", b)])
        p.dma(x1t[b][:], d["x1"][tok0:tok0 + 128, :], writes=[("x1t", b)])
        for kk, dst in enumerate((ya[b], yb[b])):
            p.dma(None, None, reads=[("sl", b)] + ydk, writes=[("yab", b, kk)], queue="pool",
                  fn=lambda e, dst=dst, b=b, kk=kk: e.indirect_dma_start(
                      out=dst[:, :], out_offset=None, in_=d["ydisp"],
                      in_offset=bass.IndirectOffsetOnAxis(ap=sl[b][:, kk:kk + 1], axis=0)))
    cload(0)
    for t in range(TOK // 128):
        b = t % 2
        tok0 = t * 128
        if t + 1 < TOK // 128:
            cload(t + 1)
        fk_ = ("f", b)
        p.ts("dve", ya[b][:], ya[b][:], gt[b][:, 0:1], None, ALU.mult, None, [("yab", b, 0), ("gt", b)], [("yab", b, 0)])
        p.stt(ya[b][:], yb[b][:], gt[b][:, 1:2], ya[b][:], ALU.mult, ALU.add, [("yab", b, 0), ("yab", b, 1), ("gt", b)],
              [("yab", b, 0)])
        p.stt(ya[b][:], x1t[b][:], ALPHA, ya[b][:], ALU.mult, ALU.add, [("yab", b, 0), ("x1t", b)], [("yab", b, 0)])
        emit_ln(p, ya[b][:], yo[b][:], lnt, 0, (st6, mv, rs), [("yab", b, 0)], [("yo", b)], "b")
        p.dma(d["x2"][tok0:tok0 + 128, :], yo[b][:], reads=[("yo", b)], writes=[("x2o", t)])
        fin.append(("x2o", t))
        if "xT_next" in d:
            xn = xTn[(t // 4) % 2]
            xnk = ("xTn", (t // 4) % 2)
            for hh in range(2):
                pst, psk = ring.next()
                for c in range(4):
                    k = hh * 4 + c
                    p.tr(pst[:, c * 128:(c + 1) * 128], yo[b][:, k * 128:(k + 1) * 128], identf[:], [("yo", b), "identf"], [psk])
                p.copy("dve" if hh else "pool", xn[:, hh * 4:(hh + 1) * 4, (t % 4) * 128:(t % 4 + 1) * 128],
                       pst[:].rearrange("p (c n) -> p c n", c=4), [psk], [xnk]) if hh else \
                    p.act(xn[:, hh * 4:(hh + 1) * 4, (t % 4) * 128:(t % 4 + 1) * 128],
                          pst[:].rearrange("p (c n) -> p c n", c=4), AF.Copy, [psk], [xnk])
            if t % 4 == 3:
                T4 = t // 4
                p.dma(d["xT_next"].rearrange("(c p) n -> p c n", p=128)[:, :, T4 * 512:(T4 + 1) * 512], xn[:],
                      reads=[xnk], writes=[("xTn_out", T4)])
    return fin


_PROGS = {}


def _prog(name, builder):
    if name not in _PROGS:
        _PROGS[name] = builder()
    return _PROGS[name]


def _run(nc, in_maps):
    res = run_bass_kernel_spmd(nc, in_maps, core_ids=list(range(8)))
    return res.results


def kernel_unfused(x, w_in, w_sink, w_conv, b_conv, w_rec_gate, b_rec_gate, w_in_gate, b_in_gate, lru_lambda,
                   w_attn_o, w_rnn_o, w_out, ln_g, ln_b, w_router_group, b_router_group, w_router_expert, b_router_expert,
                   w_exp_gate, w_exp_up, w_exp_down):
    f = lambda a: np.asarray(a, dtype=np.float32)
    x = f(x)
    w_in, w_sink, w_conv, b_conv = f(w_in), f(w_sink), f(w_conv), f(b_conv)
    w_rec_gate, b_rec_gate, w_in_gate, b_in_gate, lru_lambda = f(w_rec_gate), f(b_rec_gate), f(w_in_gate), f(b_in_gate), f(lru_lambda)
    w_attn_o, w_rnn_o, w_out, ln_g, ln_b = f(w_attn_o), f(w_rnn_o), f(w_out), f(ln_g), f(ln_b)
    w_router_group, b_router_group = f(w_router_group), f(b_router_group)
    w_router_expert, b_router_expert = f(w_router_expert), f(b_router_expert)
    w_exp_gate, w_exp_up, w_exp_down = f(w_exp_gate), f(w_exp_up), f(w_exp_down)
    depth = w_in.shape[0]
    nc_r = _prog("rnn", build_rnn)
    nc_a = _prog("attn", build_attn)
    nc_m = _prog("mix", build_mix)
    nc_e = _prog("moe", build_moe)
    aconst = [attn_consts(j) for j in range(4)]
    mconst = mix_consts()
    ident = np.eye(128, dtype=np.float32)
    cores = [(c // 4, c % 4) for c in range(8)]
    for l in range(depth):
        xTs = [np.ascontiguousarray(x[b].T) for b in range(2)]
        maps = []
        for (b, j) in cores:
            m = pack_rnn_inputs(l, j, w_in, w_conv, b_conv, w_rec_gate, b_rec_gate, w_in_gate, b_in_gate, lru_lambda)
            m["xT"] = xTs[b]
            maps.append(m)
        res = _run(nc_r, maps)
        hgT = [np.concatenate([np.asarray(res[b * 4 + j]["hgT"]) for j in range(4)], axis=0) for b in range(2)]
        w_qkv = np.ascontiguousarray(w_in[l][:, 0:1536])
        sink = np.ascontiguousarray(np.broadcast_to(w_sink[l][None, :], (128, 8)))
        maps = []
        for (b, j) in cores:
            m = dict(aconst[j])
            m["xT"] = halo_xT(x[b], j)
            m["w_qkv"] = w_qkv
            m["sink"] = sink
            maps.append(m)
        res = _run(nc_a, maps)
        oT = [np.asarray(res[c]["oT"]) for c in range(8)]
        w4 = np.ascontiguousarray(np.stack([w_attn_o[l], w_rnn_o[l], w_in[l][:, 3584:4608], w_in[l][:, 4608:5632]]))
        ln1 = np.ascontiguousarray(np.broadcast_to(np.stack([ln_g[l, 0], ln_b[l, 0]])[None], (128, 2, 1024)))
        w_rt = np.ascontiguousarray(np.concatenate([w_router_group[l], w_router_expert[l]], axis=1))
        b_rt = np.ascontiguousarray(np.broadcast_to(np.concatenate([b_router_group[l], b_router_expert[l]])[None], (128, 36)))
        wo = np.ascontiguousarray(w_out[l])
        maps = []
        for c, (b, j) in enumerate(cores):
            m = dict(mconst)
            m["xT"] = np.ascontiguousarray(xTs[b][:, j * TOK:(j + 1) * TOK])
            m["x_tok"] = np.ascontiguousarray(x[b][j * TOK:(j + 1) * TOK])
            m["oT"] = oT[c]
            m["hgT"] = np.ascontiguousarray(hgT[b][:, j * TOK:(j + 1) * TOK])
            m["w4"] = w4
            m["w_out"] = wo
            m["ln"] = ln1
            m["w_rt"] = w_rt
            m["b_rt"] = b_rt
            maps.append(m)
        res = _run(nc_m, maps)
        ln2 = np.ascontiguousarray(np.broadcast_to(np.stack([ln_g[l, 1], ln_b[l, 1]])[None], (128, 2, 1024)))
        wg_, wu_, wd_ = np.ascontiguousarray(w_exp_gate[l]), np.ascontiguousarray(w_exp_up[l]), np.ascontiguousarray(w_exp_down[l])
        maps = []
        for c in range(8):
            maps.append({"xdisp": np.asarray(res[c]["xdisp"]), "x1": np.asarray(res[c]["x1"]),
                         "slots": np.asarray(res[c]["slots"]), "gates": np.asarray(res[c]["gates"]),
                         "w_g": wg_, "w_u": wu_, "w_d": wd_, "ln": ln2, "ident": ident})
        res = _run(nc_e, maps)
        x = np.stack([np.concatenate([np.asarray(res[b * 4 + j]["x2"]) for j in range(4)], axis=0) for b in range(2)])
    return np.ascontiguousarray(x.astype(np.float32))


GROUPS4 = [[0, 1, 2, 3], [4, 5, 6, 7]]


def build_fused(depth=4):
    nc = bass.Bass("TRN2", target_bir_lowering=False)
    L = depth

    def ext(name, shape, dt=F32):
        return nc.dram_tensor(name, list(shape), dt, kind="ExternalInput").ap()

    def internal(name, shape, dt):
        return nc.dram_tensor(name, list(shape), dt).ap()
    I = {}
    I["x_tok0"] = ext("x_tok0", [TOK, 1024])
    I["xT0"] = ext("xT0", [1024, TOK])
    I["w_r"] = ext("w_r", [L, 1024, 512])
    I["w_gt"] = ext("w_gt", [L, 128, 8, 128])
    I["small"] = ext("small", [L, 128, 2, NSM])
    I["w_qkv"] = ext("w_qkv", [L, 1024, 1536])
    I["cosT"] = ext("cosT", [128, TH])
    I["sinT"] = ext("sinT", [128, TH])
    I["masks"] = ext("masks", [128, 3, 384])
    I["cmats"] = ext("cmats", [128, 2, 128])
    I["sink"] = ext("sink", [L, 128, 8])
    I["w4"] = ext("w4", [L, 4, 1024, 1024])
    I["w_out"] = ext("w_out", [L, 1024, 1024])
    I["ln1"] = ext("ln1", [L, 128, 2, 1024])
    I["ln2"] = ext("ln2", [L, 128, 2, 1024])
    I["w_rt"] = ext("w_rt", [L, 1024, 36])
    I["b_rt"] = ext("b_rt", [L, 128, 36])
    I["cst"] = ext("cst", [128, 3, 128])
    I["cst2"] = ext("cst2", [128, 40])
    I["w_eg"] = ext("w_eg", [L, N_EXP, 1024, 512])
    I["w_eu"] = ext("w_eu", [L, N_EXP, 1024, 512])
    I["w_ed"] = ext("w_ed", [L, N_EXP, 512, 1024])
    I["ident"] = ext("ident", [128, 128])
    out = nc.dram_tensor("out", [TOK, 1024], F32, kind="ExternalOutput").ap()
    xT_own = [internal("xT_own%d" % i, [1024, TOK], BF16) for i in range(2)]
    xT_all = [internal("xT_all%d" % i, [128 + 4096, TOK], BF16) for i in range(2)]
    x_tok_i = [internal("x_tok_i%d" % i, [TOK, 1024], F32) for i in range(2)]
    hg_own = [internal("hg_own%d" % i, [4, 256, TOK], BF16) for i in range(2)]
    hg_all = [internal("hg_all%d" % i, [128 + 4096, TOK], BF16) for i in range(2)]
    halo_l = internal("halo_l", [1024, HALO], BF16)
    halo_r = internal("halo_r", [1024, HALO], BF16)
    hg_mine = internal("hg_mine", [1024, TOK], BF16)
    oT = internal("oT_i", [1024, TOK], BF16)
    x1 = internal("x1_i", [TOK, 1024], F32)
    xdisp = internal("xdisp_i", [NROWS, 1024], BF16)
    slots = internal("slots_i", [TOK, 2], I32)
    gates = internal("gates_i", [TOK, 2], F32)
    ydisp = internal("ydisp_i", [NROWS, 1024], F32)

    p = Prog(nc)
    p.begin_phase()
    st = [p.sb("pro%d" % i, [128, 8, 512], BF16) for i in range(2)]
    src = I["xT0"].rearrange("(c p) n -> p c n", p=128)
    dst = xT_own[0].rearrange("(c p) n -> p c n", p=128)
    for t in range(TOK // 512):
        p.dma(st[t % 2][:], src[:, :, t * 512:(t + 1) * 512], writes=[("pro", t % 2)], queue="pool")
        p.dma(dst[:, :, t * 512:(t + 1) * 512], st[t % 2][:], reads=[("pro", t % 2)], writes=[("xT_own_w", t)])
    for k in range(4):
        p.coll("AllGather", GROUPS4, xT_own[0][k * 256:(k + 1) * 256, :], xT_all[0][128 + k * 1024:128 + (k + 1) * 1024, :],
               reads=[("xT_own_w", t) for t in range(TOK // 512)], writes=[("xT_all", k)])
    p.end_phase()
    for l in range(L):
        par = l % 2
        last = (l == L - 1)
        p.begin_phase()

        def ag_x(par=par):
            for k in range(4):
                p.coll("AllGather", GROUPS4, xT_own[par][k * 256:(k + 1) * 256, :],
                       xT_all[par][128 + k * 1024:128 + (k + 1) * 1024, :], reads=[], writes=[("xT_all", k)])
        emit_attn(p, {"xT_own": xT_own[par], "xT_all": xT_all[par], "w_qkv": I["w_qkv"][l], "cosT": I["cosT"],
                      "sinT": I["sinT"], "masks": I["masks"], "cmats": I["cmats"], "sink": I["sink"][l], "oT": oT,
                      "halo_l": halo_l, "halo_r": halo_r},
                  pfx="a%d" % l, hook=(ag_x if l > 0 else None))
        p.end_phase()
        p.begin_phase()
        emit_rnn(p, xT_all[par], I["w_r"][l], I["w_gt"][l], I["small"][l], hg_own[par], pfx="r%d" % l)
        p.end_phase()
        p.begin_phase()

        def ag_h(par=par):
            for t in range(4):
                p.coll("AllGather", GROUPS4, hg_own[par][t], hg_all[par][128 + t * 1024:128 + (t + 1) * 1024, :],
                       reads=[], writes=[("hg_all", t)])
        emit_mix(p, {"xT_own": xT_own[par], "x_tok": (I["x_tok0"] if l == 0 else x_tok_i[par]), "oT": oT,
                     "hg_all": hg_all[par], "hg_mine": hg_mine, "w4": I["w4"][l], "w_out": I["w_out"][l], "ln": I["ln1"][l],
                     "w_rt": I["w_rt"][l], "b_rt": I["b_rt"][l], "cst": I["cst"], "cst2": I["cst2"],
                     "x1": x1, "xdisp": xdisp, "slots": slots, "gates": gates}, pfx="m%d" % l, hook=ag_h)
        p.end_phase()
        p.begin_phase()
        dd = {"xdisp": xdisp, "x1": x1, "slots": slots, "gates": gates, "w_g": I["w_eg"][l], "w_u": I["w_eu"][l],
              "w_d": I["w_ed"][l], "ln": I["ln2"][l], "ident": I["ident"], "ydisp": ydisp,
              "x2": (out if last else x_tok_i[1 - par])}
        if not last:
            dd["xT_next"] = xT_own[1 - par]
        emit_moe(p, dd, pfx="e%d" % l)
        p.end_phase()
    p.finish([])
    return nc


def fused_inputs(depth, x, w_in, w_sink, w_conv, b_conv, w_rec_gate, b_rec_gate, w_in_gate, b_in_gate, lru_lambda,
                 w_attn_o, w_rnn_o, w_out, ln_g, ln_b, w_router_group, b_router_group, w_router_expert, b_router_expert,
                 w_exp_gate, w_exp_up, w_exp_down):
    L = depth
    ca = np.ascontiguousarray
    shared = {}
    shared["w_qkv"] = ca(w_in[:L, :, 0:1536])
    shared["sink"] = ca(np.broadcast_to(w_sink[:L, None, :], (L, 128, 8)))
    shared["w4"] = ca(np.stack([np.stack([w_attn_o[l], w_rnn_o[l], w_in[l][:, 3584:4608], w_in[l][:, 4608:5632]]) for l in range(L)]))
    shared["w_out"] = ca(w_out[:L])
    shared["ln1"] = ca(np.broadcast_to(np.stack([ln_g[:L, 0], ln_b[:L, 0]], axis=1)[:, None], (L, 128, 2, 1024)))
    shared["ln2"] = ca(np.broadcast_to(np.stack([ln_g[:L, 1], ln_b[:L, 1]], axis=1)[:, None], (L, 128, 2, 1024)))
    shared["w_rt"] = ca(np.concatenate([w_router_group[:L], w_router_expert[:L]], axis=2))
    shared["b_rt"] = ca(np.broadcast_to(np.concatenate([b_router_group[:L], b_router_expert[:L]], axis=1)[:, None, :], (L, 128, 36)))
    shared["w_eg"] = ca(w_exp_gate[:L])
    shared["w_eu"] = ca(w_exp_up[:L])
    shared["w_ed"] = ca(w_exp_down[:L])
    shared["ident"] = np.eye(128, dtype=np.float32)
    shared.update(mix_consts())
    rn = []
    for j in range(4):
        packs = [pack_rnn_inputs(l, j, w_in, w_conv, b_conv, w_rec_gate, b_rec_gate, w_in_gate, b_in_gate, lru_lambda)
                 for l in range(L)]
        rn.append({"w_r": ca(np.stack([q["w_r"] for q in packs])), "w_gt": ca(np.stack([q["w_g"] for q in packs])),
                   "small": ca(np.stack([q["small"] for q in packs]))})
    maps = []
    for c in range(8):
        b, j = c // 4, c % 4
        m = dict(shared)
        m.update(rn[j])
        m.update(attn_consts(j))
        xs = x[b][j * TOK:(j + 1) * TOK]
        m["x_tok0"] = ca(xs)
        m["xT0"] = ca(xs.T)
        maps.append(m)
    return maps


def kernel_fused(depth, **inp):
    key = "fused%d" % depth
    nc = _prog(key, lambda: build_fused(depth))
    names = ["x", "w_in", "w_sink", "w_conv", "b_conv", "w_rec_gate", "b_rec_gate", "w_in_gate", "b_in_gate", "lru_lambda",
             "w_attn_o", "w_rnn_o", "w_out", "ln_g", "ln_b", "w_router_group", "b_router_group", "w_router_expert",
             "b_router_expert", "w_exp_gate", "w_exp_up", "w_exp_down"]
    args = [np.asarray(inp[n], dtype=np.float32) for n in names]
    maps = fused_inputs(depth, *args)
    res = _run(nc, maps)
    x = np.stack([np.concatenate([np.asarray(res[b * 4 + j]["out"]) for j in range(4)], axis=0) for b in range(2)])
    return np.ascontiguousarray(x.astype(np.float32))


def kernel(**inputs):
    return kernel_fused(4, **inputs)
```

```python
import contextlib
import numpy as np
import concourse.bass as bass
import concourse.mybir as mybir
from concourse.bass_utils import run_bass_kernel_spmd

F32 = mybir.dt.float32
BF16 = mybir.dt.bfloat16
I32 = mybir.dt.int32
U32 = mybir.dt.uint32
AF = mybir.ActivationFunctionType
ALU = mybir.AluOpType
AX = mybir.AxisListType


class Prog:
    ENG = ("pe", "dve", "act", "pool", "sp")

    def __init__(self, nc, n_slots=8):
        self.nc = nc
        self.es = contextlib.ExitStack()
        self.q = {e: [] for e in self.ENG}
        self.cnt = {e: 0 for e in self.ENG}
        self.sem = {e: self.es.enter_context(nc.semaphore("s_" + e)) for e in self.ENG}
        self.n_slots = n_slots
        self.pool_slots = 6
        self.dq = ("sp", "act", "pool")
        self.dsem = {q: [self.es.enter_context(nc.semaphore("d_%s%d" % (q, i))) for i in range(n_slots)]
                     for q in self.dq}
        self.dn = {q: 0 for q in self.dq}
        self.sems = {}
        for e in self.ENG:
            self.sems[("c", e)] = self.sem[e]
        for q in self.dq:
            for i in range(n_slots):
                self.sems[("d", q, i)] = self.dsem[q][i]
        self.lastw = {}
        self.readers = {}
        self.waited = {e: {} for e in self.ENG}
        self.n_ops = 0
        self.exclusive = set()
        self.csem = []
        self.ph = None
        self.latest = {}
        self._cidx = {}

    def sb(self, name, shape, dtype):
        st = self.ph if self.ph is not None else self.es
        return st.enter_context(self.nc.sbuf_tensor(name, list(shape), dtype))

    def ps(self, name, shape, dtype):
        st = self.ph if self.ph is not None else self.es
        return st.enter_context(self.nc.psum_tensor(name, list(shape), dtype))

    def core_idx(self, e, which):
        key = id(e)
        if key not in self._cidx:
            pid = e.partition_id()
            vals = {}
            for name, off in (("j", 0), ("jl", 3), ("jr", 5)):
                vals[name] = e.snap((pid + off) % 4, min_val=0, max_val=3)
            self._cidx[key] = vals
        return self._cidx[key][which]

    def begin_phase(self):
        assert self.ph is None
        self.ph = contextlib.ExitStack()
        self.exclusive = set()

    def _barrier(self):
        for e in self.ENG:
            waits = []
            for sk, v in self.latest.items():
                if sk == ("c", e):
                    continue
                if self.waited[e].get(sk, 0) < v:
                    self.waited[e][sk] = v
                    waits.append((sk, v))
            if waits:
                self.q[e].append((waits, None, None, 0))
        self.lastw = {}
        self.readers = {}

    def _emit_block(self):
        nc = self.nc
        engobj = {"pe": "tensor", "dve": "vector", "act": "scalar", "pool": "gpsimd", "sp": "sync"}
        with nc.Block() as block:
            for e in self.ENG:
                items = self.q[e]

                def body(eng, items=items):
                    for waits, fn, sk, amt in items:
                        for wsk, v in waits:
                            eng.wait_ge(self.sems[wsk], v)
                        if fn is not None:
                            ins = fn(eng)
                            if amt is None:
                                ins.then_inc(self.sems[sk])
                            else:
                                ins.then_inc(self.sems[sk], amt)
                getattr(block, engobj[e])(body)
        self.q = {e: [] for e in self.ENG}

    def end_phase(self):
        self._barrier()
        self._emit_block()
        self.ph.close()
        self.ph = None

    def _deps(self, eng, reads, writes, is_dma):
        need = {}

        def add(d):
            for sk, v in d.items():
                if need.get(sk, 0) < v:
                    need[sk] = v
        for k in reads:
            if k in self.lastw:
                add(self.lastw[k])
        for k in writes:
            if k in self.lastw:
                add(self.lastw[k])
            if k in self.readers:
                add(self.readers[k])
        out = []
        for sk, v in need.items():
            if (not is_dma) and eng == "pe" and sk == ("c", "pe"):
                continue
            if self.waited[eng].get(sk, 0) >= v:
                continue
            self.waited[eng][sk] = v
            out.append((sk, v))
        return out

    def _mark(self, reads, writes, tok):
        sk, v = tok
        if self.latest.get(sk, 0) < v:
            self.latest[sk] = v
        for k in reads:
            r = self.readers.setdefault(k, {})
            if r.get(sk, 0) < v:
                r[sk] = v
        for k in writes:
            self.lastw[k] = {sk: v}
            self.readers[k] = {}

    def _excl(self, reads, writes):
        ex = [k for k in reads if k in self.exclusive]
        if ex:
            writes = list(writes) + [k for k in ex if k not in writes]
        return reads, writes

    def op(self, eng, fn, reads=(), writes=()):
        reads, writes = self._excl(reads, writes)
        waits = self._deps(eng, reads, writes, False)
        self.cnt[eng] += 1
        tok = (("c", eng), self.cnt[eng])
        self.q[eng].append((waits, fn, tok[0], 1))
        self._mark(reads, writes, tok)
        self.n_ops += 1

    def dma(self, out, in_, reads=(), writes=(), queue="sp", fn=None):
        n = self.dn[queue]
        self.dn[queue] += 1
        ns = self.n_slots if queue != "pool" else min(self.n_slots, self.pool_slots)
        slot = n % ns
        sk = ("d", queue, slot)
        waits = self._deps(queue, reads, writes, True)
        prev = 16 * (n // ns)
        if prev > 0 and self.waited[queue].get(sk, 0) < prev:
            self.waited[queue][sk] = prev
            waits.append((sk, prev))
        if fn is None:
            def fn(e, out=out, in_=in_):
                return e.dma_start(out=out, in_=in_)
        tok = (sk, prev + 16)
        self.q[queue].append((waits, fn, sk, 16))
        self._mark(reads, writes, tok)
        self.n_ops += 1

    def mm(self, out, lhsT, rhs, start, stop, reads, writes):
        self.op("pe", lambda e: e.matmul(out, lhsT=lhsT, rhs=rhs, start=start, stop=stop), reads, writes)

    def tr(self, out, in_, ident, reads, writes):
        self.op("pe", lambda e: e.transpose(out=out, in_=in_, identity=ident), reads, writes)

    def act(self, out, in_, func, reads, writes, scale=1.0, bias=None, accum_out=None):
        kw = {}
        if bias is not None:
            kw["bias"] = bias
        if accum_out is not None:
            kw["accum_out"] = accum_out
        self.op("act", lambda e: e.activation(out=out, in_=in_, func=func, scale=scale, **kw), reads, writes)

    def ts(self, eng, out, in0, s1, s2, op0, op1, reads, writes, accum_out=None):
        kw = {}
        if accum_out is not None:
            kw["accum_out"] = accum_out
        if op1 is None:
            self.op(eng, lambda e: e.tensor_scalar(out=out, in0=in0, scalar1=s1, scalar2=None, op0=op0, **kw), reads, writes)
        else:
            self.op(eng, lambda e: e.tensor_scalar(out=out, in0=in0, scalar1=s1, scalar2=s2, op0=op0, op1=op1, **kw),
                    reads, writes)

    def tt(self, eng, out, in0, in1, op, reads, writes):
        self.op(eng, lambda e: e.tensor_tensor(out=out, in0=in0, in1=in1, op=op), reads, writes)

    def stt(self, out, in0, scalar, in1, op0, op1, reads, writes):
        self.op("dve", lambda e: e.scalar_tensor_tensor(out=out, in0=in0, scalar=scalar, in1=in1, op0=op0, op1=op1),
                reads, writes)

    def scan(self, out, d0, d1, init, reads, writes):
        self.op("dve", lambda e: e.tensor_tensor_scan(out=out, data0=d0, data1=d1, initial=init, op0=ALU.mult, op1=ALU.add),
                reads, writes)

    def copy(self, eng, out, in_, reads, writes):
        self.op(eng, lambda e: e.tensor_copy(out=out, in_=in_), reads, writes)

    def memset(self, eng, ap, val, writes):
        self.op(eng, lambda e: e.memset(ap, val), (), writes)

    def coll(self, kind, groups, in_ap, out_ap, reads=(), writes=()):
        idx = len(self.csem)
        sem = self.es.enter_context(self.nc.semaphore("cc%d" % idx))
        self.csem.append(sem)
        sk = ("k", idx)
        self.sems[sk] = sem
        waits = self._deps("pool", reads, writes, True)

        def fn(e):
            return e.collective_compute(kind, ALU.bypass, replica_groups=groups, ins=[in_ap], outs=[out_ap])
        self.q["pool"].append((waits, fn, sk, None))
        self._mark(reads, writes, (sk, 1))
        self.n_ops += 1

    def finish(self, final_keys):
        self._barrier()
        self._emit_block()
        if self.ph is not None:
            self.ph.close()
            self.ph = None
        self.es.close()


def _rev(ap2d, n):
    apl = [list(s) for s in ap2d.ap]
    assert len(apl) == 2 and apl[1][1] == n and apl[1][0] == 1, apl
    from concourse.ap import AP
    return AP(ap2d.tensor, ap2d.offset + (n - 1), [apl[0], [-1, n]])


class PsumRing:
    def __init__(self, p, n=8, name="ps"):
        self.p = p
        self.t = [p.ps("%s%d" % (name, i), [128, 512], F32) for i in range(n)]
        for i in range(n):
            p.exclusive.add((name, i))
        self.i = 0
        self.n = n
        self.name = name

    def next(self):
        i = self.i
        self.i = (self.i + 1) % self.n
        return self.t[i], (self.name, i)


S_LEN = 8192
NSM = 11
GELU_C = 1.5957691216057308


def build_rnn(x_dtype=F32):
    nc = bass.Bass("TRN2", target_bir_lowering=False)
    xT = nc.dram_tensor("xT", [1024, S_LEN], x_dtype, kind="ExternalInput").ap()
    w_r = nc.dram_tensor("w_r", [1024, 512], F32, kind="ExternalInput").ap()
    w_g = nc.dram_tensor("w_g", [128, 8, 128], F32, kind="ExternalInput").ap()
    small = nc.dram_tensor("small", [128, 2, NSM], F32, kind="ExternalInput").ap()
    hgT = nc.dram_tensor("hgT", [256, S_LEN], BF16, kind="ExternalOutput").ap()
    p = Prog(nc)
    emit_rnn(p, xT, w_r, w_g, small, hgT)
    p.finish([("hg_out", b, c) for b in range(2) for c in range(4)])
    return nc


def emit_rnn(p, xT, w_r, w_g, small, hgT, pfx="r"):
    T = S_LEN
    CH = 2048
    NCH = T // CH
    wb = p.sb(pfx + "wb", [128, 8, 512], BF16)
    wg = p.sb(pfx + "wg", [128, 8, 128], BF16)
    sm = p.sb(pfx + "sm", [128, 2, NSM], F32)
    cl = p.sb(pfx + "cl", [128, 2, 2, 2], F32)
    zt = p.sb(pfx + "zt", [128, 4], F32)
    pt = p.sb(pfx + "pt", [128, 4], F32)
    xb = [p.sb(pfx + "xb%d" % i, [128, 8, 512], BF16) for i in range(2)]
    xr_full = p.sb(pfx + "xrf", [128, T + 4], F32)
    gy = p.sb(pfx + "gy", [128, T], BF16)
    xc = p.sb(pfx + "xc", [128, T], F32)
    xcb = p.sb(pfx + "xcb", [128, T], BF16)
    g1 = [p.sb(pfx + "g1_%d" % i, [128, 512], F32) for i in range(2)]
    g2 = [p.sb(pfx + "g2_%d" % i, [128, 512], F32) for i in range(2)]
    rtL = [p.sb(pfx + "rt%d" % i, [128, CH], F32) for i in range(2)]
    itL = [p.sb(pfx + "it%d" % i, [128, CH], F32) for i in range(2)]
    atL = [p.sb(pfx + "at%d" % i, [128, CH], F32) for i in range(2)]
    ccn = [0]
    hb = [p.sb(pfx + "hb%d" % i, [128, CH], F32) for i in range(2)]
    hgb = [p.sb(pfx + "hgb%d" % i, [128, CH], BF16) for i in range(2)]
    ring = PsumRing(p, 8, pfx + "ps")

    p.dma(wb[:], w_r.rearrange("(c p) n -> p c n", p=128), writes=["wb"], queue="pool")
    p.dma(wg[:], w_g, writes=["wg"], queue="pool")
    p.dma(sm[:], small, writes=["sm"])
    for b in range(2):
        for d in range(2):
            j = b * 2 + d
            p.act(zt[:, j:j + 1], sm[:, b, 5 + 3 * d + 2: 5 + 3 * d + 3], AF.Exp, ["sm"], ["zt"], scale=-1.0)
    p.ts("dve", pt[:], zt[:], -1.0 / 6, 1.0 / 5, ALU.mult, ALU.add, ["zt"], ["pt"])
    for cst in (-1.0 / 4, 1.0 / 3, -1.0 / 2, 1.0):
        p.tt("dve", pt[:], pt[:], zt[:], ALU.mult, ["pt", "zt"], ["pt"])
        p.ts("dve", pt[:], pt[:], cst, None, ALU.add, None, ["pt"], ["pt"])
    p.tt("dve", pt[:], pt[:], zt[:], ALU.mult, ["pt", "zt"], ["pt"])
    for b in range(2):
        for d in range(2):
            j = b * 2 + d
            p.ts("dve", cl[:, b, d, 0:1], pt[:, j:j + 1], -8.0, None, ALU.mult, None, ["pt"], ["cl"])
            p.ts("dve", cl[:, b, d, 1:2], pt[:, j:j + 1], -16.0, None, ALU.mult, None, ["pt"], ["cl"])
    chunked = (xT.shape[0] == 4096 + 128)
    if chunked:
        xTv = xT[128:128 + 4096, :].rearrange("(k r c p) n -> p r k c n", k=4, r=4, c=2, p=128)
    else:
        xTv = xT.rearrange("(c p) n -> p c n", p=128)

    def xrk(c):
        return [("xr", t) for t in range(4 * c, 4 * c + 4)]

    for blk in range(2):
        p.memset("pool", xr_full[:, 0:2], 0.0, [("xr", -1)])
        p.memset("pool", xr_full[:, T + 2:T + 4], 0.0, [("xr", 16)])
        for t in range(T // 512):
            xbt = xb[t % 2]
            xbk = ("xb", t % 2)
            if chunked:
                for k in range(4):
                    p.dma(xbt[:, 2 * k:2 * k + 2, :], xTv[:, t // 4, k, :, (t % 4) * 512:(t % 4 + 1) * 512],
                          reads=["xT_all"], writes=[xbk])
            else:
                p.dma(xbt[:], xTv[:, :, t * 512:(t + 1) * 512], writes=[xbk], queue="pool")
            for m in range(2):
                pst, psk = ring.next()
                col = (0 if m == 0 else 256) + blk * 128
                for k in range(8):
                    p.mm(pst[:], wb[:, k, col:col + 128], xbt[:, k, :], k == 0, k == 7, ["wb", xbk], [psk])
                if m == 0:
                    p.act(xr_full[:, 2 + t * 512: 2 + (t + 1) * 512], pst[:], AF.Copy, [psk], [("xr", t)])
                else:
                    a1, a2 = g1[t % 2], g2[t % 2]
                    k1, k2 = ("g1", t % 2), ("g2", t % 2)
                    p.act(a1[:], pst[:], AF.Square, [psk], [k1])
                    p.ts("dve", a1[:], a1[:], 0.044715, 1.0, ALU.mult, ALU.add, [k1], [k1])
                    p.tt("dve", a1[:], a1[:], pst[:], ALU.mult, [k1, psk], [k1])
                    p.act(a2[:], a1[:], AF.Sigmoid, [k1], [k2], scale=GELU_C)
                    p.tt("dve", gy[:, t * 512:(t + 1) * 512], a2[:], pst[:], ALU.mult, [k2, psk], [("gy", t)])
        for c in range(NCH):
            o = c * CH
            rk = [("xr", t) for t in range(4 * c - 1, 4 * c + 5)]
            p.ts("dve", xc[:, o:o + CH], xr_full[:, o:o + CH], sm[:, blk, 0:1], sm[:, blk, 4:5], ALU.mult, ALU.add,
                 rk + ["sm"], [("xc", c)])
            for tap in range(1, 4):
                p.stt(xc[:, o:o + CH], xr_full[:, o + tap:o + tap + CH], sm[:, blk, tap:tap + 1], xc[:, o:o + CH],
                      ALU.mult, ALU.add, rk + ["sm", ("xc", c)], [("xc", c)])
            p.copy("pool", xcb[:, o:o + CH], xc[:, o:o + CH], [("xc", c)], [("xcb", c)])
        hf = xr_full
        for d in range(2):
            order = list(range(NCH)) if d == 0 else list(range(NCH - 1, -1, -1))
            prev = None
            for ci, c in enumerate(order):
                o = c * CH
                cp = ccn[0] % 2
                ccn[0] += 1
                rt, it, at = rtL[cp], itL[cp], atL[cp]
                atk = ("at", cp)
                for s in range(CH // 512):
                    for g in range(2):
                        pst, psk = ring.next()
                        p.mm(pst[:], wg[:, blk * 4 + d * 2 + g, :], xcb[:, o + s * 512: o + (s + 1) * 512], True, True,
                             ["wg", ("xcb", c)], [psk])
                        dst = rt if g == 0 else it
                        p.act(dst[:, s * 512:(s + 1) * 512], pst[:], AF.Sigmoid, [psk, "sm"],
                              [("rt" if g == 0 else "it", cp, s)], bias=sm[:, blk, 5 + 3 * d + g: 5 + 3 * d + g + 1])
                rtk = [("rt", cp, s) for s in range(4)]
                itk = [("it", cp, s) for s in range(4)]
                p.act(at[:], rt[:], AF.Exp, rtk + ["cl"], [atk], scale=cl[:, blk, d, 0:1])
                p.act(rt[:], rt[:], AF.Exp, rtk + ["cl"], rtk, scale=cl[:, blk, d, 1:2])
                p.act(rt[:], rt[:], AF.Sqrt, rtk, rtk, scale=-1.0, bias=1.0)
                p.tt("pool", it[:], it[:], xc[:, o:o + CH], ALU.mult, itk + [("xc", c)], itk)
                p.tt("pool", it[:], it[:], rt[:], ALU.mult, itk + rtk, itk)
                if d == 0:
                    init = 0.0 if ci == 0 else hf[:, 2 + o - 1: 2 + o]
                    p.scan(hf[:, 2 + o: 2 + o + CH], at[:], it[:], init, [atk] + itk + [("xr", 4 * c - 1)], xrk(c))
                else:
                    hbt = hb[ci % 2]
                    init = 0.0 if ci == 0 else prev[:, 0:1]
                    p.scan(_rev(hbt[:], CH), _rev(at[:], CH), _rev(it[:], CH), init,
                           [atk] + itk + [("hb", (ci + 1) % 2)], [("hb", ci % 2)])
                    prev = hbt
                    hg = hgb[ci % 2]
                    p.tt("pool", at[:], hbt[:], hf[:, 2 + o: 2 + o + CH], ALU.add, [("hb", ci % 2), atk] + xrk(c), [atk])
                    p.tt("dve", hg[:], at[:], gy[:, o:o + CH], ALU.mult, [atk] + [("gy", t) for t in range(4 * c, 4 * c + 4)],
                         [("hgb", ci % 2)])
                    hdst = hgT[c, blk * 128:(blk + 1) * 128, :] if len(hgT.shape) == 3 else hgT[blk * 128:(blk + 1) * 128, o:o + CH]
                    p.dma(hdst, hg[:], reads=[("hgb", ci % 2)], writes=[("hg_out", blk, c)])


def pack_rnn_inputs(l, j, w_in, w_conv, b_conv, w_rec_gate, b_rec_gate, w_in_gate, b_in_gate, lru_lambda):
    XR0 = 1024 + 512
    YR0 = XR0 + 1024
    c0 = 2 * j * 128
    w_r = np.concatenate([w_in[l][:, XR0 + c0: XR0 + c0 + 256], w_in[l][:, YR0 + c0: YR0 + c0 + 256]], axis=1)
    w_g = np.empty((128, 8, 128), np.float32)
    small = np.empty((128, 2, NSM), np.float32)
    for b in range(2):
        cb = 2 * j + b
        sl = slice(cb * 128, (cb + 1) * 128)
        small[:, b, 0:4] = w_conv[l][:, sl].T
        small[:, b, 4] = b_conv[l][sl]
        for d in range(2):
            w_g[:, b * 4 + d * 2 + 0, :] = w_rec_gate[l, d, cb]
            w_g[:, b * 4 + d * 2 + 1, :] = w_in_gate[l, d, cb]
            small[:, b, 5 + 3 * d + 0] = b_rec_gate[l, d][sl]
            small[:, b, 5 + 3 * d + 1] = b_in_gate[l, d][sl]
            small[:, b, 5 + 3 * d + 2] = lru_lambda[l, d][sl]
    return {"w_r": np.ascontiguousarray(w_r), "w_g": w_g, "small": small}


TOK = 2048
HALO = 128
TH = TOK + 2 * HALO
ATT_SCALE = 128 ** -0.5


def build_attn(x_dtype=F32):
    nc = bass.Bass("TRN2", target_bir_lowering=False)
    d = {}
    d["xT"] = nc.dram_tensor("xT", [1024, TH], x_dtype, kind="ExternalInput").ap()
    d["w_qkv"] = nc.dram_tensor("w_qkv", [1024, 1536], F32, kind="ExternalInput").ap()
    d["cosT"] = nc.dram_tensor("cosT", [128, TH], F32, kind="ExternalInput").ap()
    d["sinT"] = nc.dram_tensor("sinT", [128, TH], F32, kind="ExternalInput").ap()
    d["masks"] = nc.dram_tensor("masks", [128, 3, 384], F32, kind="ExternalInput").ap()
    d["cmats"] = nc.dram_tensor("cmats", [128, 2, 128], F32, kind="ExternalInput").ap()
    d["sink"] = nc.dram_tensor("sink", [128, 8], F32, kind="ExternalInput").ap()
    d["oT"] = nc.dram_tensor("oT", [1024, TOK], BF16, kind="ExternalOutput").ap()
    p = Prog(nc)
    emit_attn(p, d)
    p.finish([("o_out", h, g) for h in range(8) for g in range(4)])
    return nc


def emit_attn(p, d, pfx="a", hook=None):
    xb = p.sb(pfx + "xb", [128, 8, TH], BF16)
    wq = [p.sb(pfx + "wq%d" % i, [128, 8, 128], BF16) for i in range(3)]
    wv = p.sb(pfx + "wv", [128, 8, 256], BF16)
    cosT = p.sb(pfx + "cos", [128, TH], F32)
    sinT = p.sb(pfx + "sin", [128, TH], F32)
    masks = p.sb(pfx + "masks", [128, 3, 384], BF16)
    cm0 = p.sb(pfx + "cm0", [128, 128], BF16)
    cm1 = p.sb(pfx + "cm1", [128, 128], BF16)
    sink = p.sb(pfx + "sink", [128, 8], F32)
    nsink = p.sb(pfx + "nsink", [128, 8], F32)
    qT = p.sb(pfx + "qT", [128, 8, TOK], BF16)
    kT = p.sb(pfx + "kT", [128, 2, TH], BF16)
    V = p.sb(pfx + "V", [128, TH // 128, 256], BF16)
    qraw = [p.sb(pfx + "qraw%d" % i, [128, 512], BF16) for i in range(2)]
    r1 = [p.sb(pfx + "r1_%d" % i, [128, 512], F32) for i in range(2)]
    r2 = [p.sb(pfx + "r2_%d" % i, [128, 512], F32) for i in range(2)]
    P = [p.sb(pfx + "P%d" % i, [128, 384], BF16) for i in range(2)]
    PT = [p.sb(pfx + "PT%d" % i, [128, 384], BF16) for i in range(2)]
    D = [p.sb(pfx + "D%d" % i, [128, 128], BF16) for i in range(2)]
    cols = [p.sb(pfx + "cols%d" % i, [128, 8], F32) for i in range(2)]
    ring = PsumRing(p, 6, pfx + "ps")
    oring = PsumRing(p, 2, pfx + "po")
    ident = cm0[:]
    pswap = cm1[:]

    if "xT_own" in d:
        own = d["xT_own"].rearrange("(c p) n -> p c n", p=128)
        allv = d["xT_all"][128:128 + 4096, :].rearrange("(k r c p) n -> p r k c n", k=4, r=4, c=2, p=128)
        nc_ = p.nc

        def emit_halo():
            allr = d["xT_all"][128:128 + 4096, :].rearrange("(k r q) n -> r k q n", k=4, r=4, q=256)

            def halo(e, left):
                jn = p.core_idx(e, "jl" if left else "jr")
                if left:
                    return e.dma_start(out=d["halo_l"].rearrange("(k q) n -> k q n", k=4), in_=allr[jn, :, :, TOK - HALO:TOK])
                return e.dma_start(out=d["halo_r"].rearrange("(k q) n -> k q n", k=4), in_=allr[jn, :, :, 0:HALO])
            agk = [("xT_all", k) for k in range(4)]
            p.dma(None, None, reads=agk, writes=["halo_l"], fn=lambda e: halo(e, True))
            p.dma(None, None, reads=agk, writes=["halo_r"], fn=lambda e: halo(e, False))
            p.dma(xb[:, :, 0:HALO], d["halo_l"].rearrange("(c p) n -> p c n", p=128), reads=["halo_l"],
                  writes=[("xbh", 0, k) for k in range(4)])
            p.dma(xb[:, :, HALO + TOK:TH], d["halo_r"].rearrange("(c p) n -> p c n", p=128), reads=["halo_r"],
                  writes=[("xbh", 1, k) for k in range(4)])
        if hook is None:
            emit_halo()
        for t0 in range(0, TOK, 512):
            keys = sorted(set([(HALO + t0) // 512, (HALO + t0 + 511) // 512]))
            p.dma(xb[:, :, HALO + t0:HALO + t0 + 512], own[:, :, t0:t0 + 512], reads=["xT_own"],
                  writes=[("xb", k) for k in keys])
    else:
        xTv = d["xT"].rearrange("(c p) n -> p c n", p=128)
        for t0 in range(0, TH, 512):
            w = min(512, TH - t0)
            p.dma(xb[:, :, t0:t0 + w], xTv[:, :, t0:t0 + w], writes=[("xb", t0 // 512)], queue="pool")

    def xbk(a, b):
        ks = [("xb", t) for t in range(a // 512, (b - 1) // 512 + 1)]
        if a < HALO:
            ks += [("xbh", 0, k) for k in range(4)]
        if b > HALO + TOK:
            ks += [("xbh", 1, k) for k in range(4)]
        return ks
    p.dma(wv[:], d["w_qkv"].rearrange("(c p) n -> p c n", p=128)[:, :, 1280:1536], writes=["wv"], queue="pool")
    p.dma(masks[:], d["masks"], writes=["masks"], queue="pool")
    p.dma(cm0[:], d["cmats"][:, 0, :], writes=["cm"], queue="pool")
    p.dma(cm1[:], d["cmats"][:, 1, :], writes=["cm"], queue="pool")
    p.dma(cosT[:], d["cosT"], writes=["cos"])
    p.dma(sinT[:], d["sinT"], writes=["sin"])
    p.dma(sink[:], d["sink"], writes=["sink"])
    p.ts("dve", nsink[:], sink[:], -1.0, None, ALU.mult, None, ["sink"], ["nsink"])
    wqv = d["w_qkv"].rearrange("(c p) n -> p c n", p=128)
    def emit_v():
        for tt in range(TH // 128):
            pst, psk = ring.next()
            for k in range(8):
                p.mm(pst[:, 0:256], xb[:, k, tt * 128:(tt + 1) * 128], wv[:, k, :], k == 0, k == 7, xbk(tt * 128, (tt + 1) * 128) + ["wv"], [psk])
            p.copy("dve" if tt % 2 else "act", V[:, tt, :], pst[:, 0:256], [psk], [("V", tt)]) if tt % 2 else \
                p.act(V[:, tt, :], pst[:, 0:256], AF.Copy, [psk], [("V", tt)])
    it = 0
    for m in list(range(8)) + ['v', 8, 9]:
        if m == 'v':
            emit_v()
            continue
        wt = wq[m % 3]
        wk = ("wq", m % 3)
        p.dma(wt[:], wqv[:, :, m * 128:(m + 1) * 128], writes=[wk], queue="pool")
        if m == 2 and hook is not None:
            hook()
            emit_halo()
        isq = m < 8
        ntok = TOK if isq else TH
        off = HALO if isq else 0
        t0 = 0
        while t0 < ntok:
            w = min(512, ntok - t0)
            pst, psk = ring.next()
            for k in range(8):
                p.mm(pst[:, 0:w], wt[:, k, :], xb[:, k, off + t0: off + t0 + w], k == 0, k == 7, xbk(off + t0, off + t0 + w) + [wk], [psk])
            qr = qraw[it % 2]
            qk = ("qraw", it % 2)
            p.act(qr[:, 0:w], pst[:, 0:w], AF.Copy, [psk], [qk])
            ps2, ps2k = ring.next()
            p.mm(ps2[:, 0:w], pswap, qr[:, 0:w], True, True, ["cm", qk], [ps2k])
            a1, a2 = r1[it % 2], r2[it % 2]
            k1, k2 = ("r1", it % 2), ("r2", it % 2)
            p.tt("dve", a1[:, 0:w], cosT[:, off + t0: off + t0 + w], pst[:, 0:w], ALU.mult, [psk, "cos"], [k1])
            p.tt("dve", a2[:, 0:w], sinT[:, off + t0: off + t0 + w], ps2[:, 0:w], ALU.mult, [ps2k, "sin"], [k2])
            if isq:
                dst = qT[:, m, t0:t0 + w]
                dk = [("qT", m, t0 // 512)]
            else:
                dst = kT[:, m - 8, t0:t0 + w]
                dk = [("kT", m - 8, t0 // 512)]
            p.tt("dve", dst, a1[:, 0:w], a2[:, 0:w], ALU.add, [k1, k2], dk)
            it += 1
            t0 += w
    its = [(grp, h, qi) for grp in range(4) for h in range(8) for qi in range(4)]
    NB = 5
    colsN = cols + [p.sb(pfx + "colsx%d" % i, [128, 8], F32) for i in range(NB - 2)]
    PN = P + [p.sb(pfx + "Px%d" % i, [128, 384], BF16) for i in range(NB - 2)]
    state = {}

    def stage_a(n):
        grp, h, qi = its[n]
        g = h // 4
        qb = grp * 4 + qi
        if qi == 0:
            state[(grp, h)] = oring.next()
        mi = 0 if qb == 0 else (2 if qb == 15 else 1)
        pss, pssk = ring.next()
        kkeys = [("kT", g, t) for t in sorted(set([(qb * 128) // 512, (qb * 128 + 383) // 512]))]
        p.mm(pss[:, 0:384], qT[:, h, qb * 128:(qb + 1) * 128], kT[:, g, qb * 128: qb * 128 + 384], True, False,
             [("qT", h, grp)] + kkeys, [pssk])
        p.mm(pss[:, 0:384], ident, masks[:, mi, :], False, True, ["cm", "masks"], [pssk])
        cl = colsN[n % NB]
        ck = ("cols", n % NB)
        p.op("dve", lambda e, cl=cl, pss=pss: e.reduce_max(out=cl[:, 0:1], in_=pss[:, 0:384], axis=AX.X), [pssk], [ck])
        p.ts("dve", cl[:, 1:2], cl[:, 0:1], -ATT_SCALE, nsink[:, h:h + 1], ALU.mult, ALU.min, [ck, "nsink"], [ck])
        Pt = PN[n % NB]
        pk = ("P", n % NB)
        p.act(Pt[:], pss[:, 0:384], AF.Exp, [pssk, ck], [pk, ck], scale=ATT_SCALE, bias=cl[:, 1:2], accum_out=cl[:, 2:3])
        p.act(cl[:, 3:4], cl[:, 1:2], AF.Exp, [ck, "sink"], [ck], bias=sink[:, h:h + 1])

    def stage_b(n):
        grp, h, qi = its[n]
        g = h // 4
        qb = grp * 4 + qi
        po, pok = state[(grp, h)]
        cl = colsN[n % NB]
        ck = ("cols", n % NB)
        Pt = PN[n % NB]
        pk = ("P", n % NB)
        p.tt("dve", cl[:, 4:5], cl[:, 2:3], cl[:, 3:4], ALU.add, [ck], [ck])
        p.op("dve", lambda e, cl=cl: e.reciprocal(out=cl[:, 5:6], in_=cl[:, 4:5]), [ck], [ck])
        Dt = D[n % 2]
        dk = ("D", n % 2)
        p.ts("dve", Dt[:], ident, cl[:, 5:6], None, ALU.mult, None, ["cm", ck], [dk])
        ppt, pptk = ring.next()
        for kb in range(3):
            p.mm(ppt[:, kb * 128:(kb + 1) * 128], Pt[:, kb * 128:(kb + 1) * 128], Dt[:], True, True, [pk, dk], [pptk])
        PTt = PT[n % 2]
        ptk = ("PT", n % 2)
        p.act(PTt[:], ppt[:, 0:384], AF.Copy, [pptk], [ptk])
        for kb in range(3):
            p.mm(po[:, qi * 128:(qi + 1) * 128], V[:, qb + kb, g * 128:(g + 1) * 128], PTt[:, kb * 128:(kb + 1) * 128],
                 kb == 0, kb == 2, [ptk, ("V", qb + kb)], [pok])
        if qi == 3:
            p.copy("dve", qT[:, h, grp * 512:(grp + 1) * 512], po[:], [pok], [("qT", h, grp)])
            p.dma(d["oT"][h * 128:(h + 1) * 128, grp * 512:(grp + 1) * 512], qT[:, h, grp * 512:(grp + 1) * 512],
                  reads=[("qT", h, grp)], writes=[("o_out", h, grp)])

    SKEW = 3
    for n in range(len(its) + SKEW):
        if n < len(its):
            stage_a(n)
        if n - SKEW >= 0:
            stage_b(n - SKEW)


def attn_consts(j):
    inv = (10000.0 ** (-np.arange(0, 128, 2, dtype=np.float32) / 128)).astype(np.float32)
    pos = (j * TOK - HALO + np.arange(TH)).astype(np.float32)
    ang = pos[:, None] * inv[None, :]
    cos = np.cos(ang).astype(np.float32).T
    sin = np.sin(ang).astype(np.float32).T
    cosT = np.concatenate([cos, cos], axis=0)
    sinT = np.concatenate([-sin, sin], axis=0)
    qi = np.arange(128)[:, None]
    kj = np.arange(384)[None, :]
    rel = kj - 128 - qi
    base = np.where(np.abs(rel) <= 128, 0.0, -30000.0).astype(np.float32)
    first = base.copy(); first[:, 0:128] = -30000.0
    last = base.copy(); last[:, 256:384] = -30000.0
    masks = np.stack([first if j == 0 else base, base, last if j == 3 else base], axis=1)
    ident = np.eye(128, dtype=np.float32)
    swap = np.zeros((128, 128), np.float32)
    mm = np.arange(128)
    swap[(mm + 64) % 128, mm] = 1.0
    cmats = np.stack([ident, swap], axis=1)
    return {"cosT": np.ascontiguousarray(cosT), "sinT": np.ascontiguousarray(sinT),
            "masks": np.ascontiguousarray(masks), "cmats": np.ascontiguousarray(cmats)}


def halo_xT(x_b, j):
    out = np.zeros((1024, TH), x_b.dtype)
    lo, hi = j * TOK - HALO, (j + 1) * TOK + HALO
    slo, shi = max(lo, 0), min(hi, S_LEN)
    out[:, slo - lo: shi - lo] = x_b[slo:shi].T
    return out


ALPHA = 8 ** 0.25
LN_EPS = 1e-5
N_EXP = 32
CAP = 256
NSLOT = N_EXP * CAP
NROWS = NSLOT + 128


def build_mix(x_dtype=F32):
    nc = bass.Bass("TRN2", target_bir_lowering=False)
    d = {}
    d["xT"] = nc.dram_tensor("xT", [1024, TOK], x_dtype, kind="ExternalInput").ap()
    d["x_tok"] = nc.dram_tensor("x_tok", [TOK, 1024], F32, kind="ExternalInput").ap()
    d["oT"] = nc.dram_tensor("oT", [1024, TOK], BF16, kind="ExternalInput").ap()
    d["hgT"] = nc.dram_tensor("hgT", [1024, TOK], BF16, kind="ExternalInput").ap()
    d["w4"] = nc.dram_tensor("w4", [4, 1024, 1024], F32, kind="ExternalInput").ap()
    d["w_out"] = nc.dram_tensor("w_out", [1024, 1024], F32, kind="ExternalInput").ap()
    d["ln"] = nc.dram_tensor("ln", [128, 2, 1024], F32, kind="ExternalInput").ap()
    d["w_rt"] = nc.dram_tensor("w_rt", [1024, 36], F32, kind="ExternalInput").ap()
    d["b_rt"] = nc.dram_tensor("b_rt", [128, 36], F32, kind="ExternalInput").ap()
    d["cst"] = nc.dram_tensor("cst", [128, 3, 128], F32, kind="ExternalInput").ap()
    d["cst2"] = nc.dram_tensor("cst2", [128, 40], F32, kind="ExternalInput").ap()
    d["x1"] = nc.dram_tensor("x1", [TOK, 1024], F32, kind="ExternalOutput").ap()
    d["xdisp"] = nc.dram_tensor("xdisp", [NROWS, 1024], BF16, kind="ExternalOutput").ap()
    d["slots"] = nc.dram_tensor("slots", [TOK, 2], I32, kind="ExternalOutput").ap()
    d["gates"] = nc.dram_tensor("gates", [TOK, 2], F32, kind="ExternalOutput").ap()
    p = Prog(nc)
    fk = emit_mix(p, d)
    p.finish(fk)
    return nc


def emit_ln(p, y, out, lnt, which, stat, reads, writes, tag):
    st6, mv, rs = stat
    sk = ("lnstat", tag)
    for hh in range(2):
        p.op("dve", lambda e, hh=hh: e.bn_stats(out=st6[:, hh * 6:(hh + 1) * 6], in_=y[:, hh * 512:(hh + 1) * 512]),
             reads + [sk], [sk])
    p.op("dve", lambda e: e.bn_aggr(out=mv[:, 0:2], in_=st6[:, 0:12]), [sk], [sk])
    p.act(rs[:, 0:1], mv[:, 1:2], AF.Sqrt, [sk], [sk], bias=LN_EPS)
    p.op("dve", lambda e: e.reciprocal(out=rs[:, 1:2], in_=rs[:, 0:1]), [sk], [sk])
    p.ts("dve", out, y, mv[:, 0:1], rs[:, 1:2], ALU.subtract, ALU.mult, reads + [sk], writes)
    p.tt("pool", out, out, lnt[:, which, 0, :], ALU.mult, writes + ["ln"], writes)
    p.tt("pool", out, out, lnt[:, which, 1, :], ALU.add, writes + ["ln"], writes)


def emit_mix(p, d, pfx="m", hook=None):
    ot = [p.sb(pfx + "ot%d" % i, [128, 8, 512], BF16) for i in range(1)] * 2
    hg = [p.sb(pfx + "hg%d" % i, [128, 8, 512], BF16) for i in range(1)] * 2
    xb = [p.sb(pfx + "xb%d" % i, [128, 8, 512], BF16) for i in range(1)] * 2
    w4r = p.sb(pfx + "w4r", [128, 4, 8, 1024], BF16)
    wo = p.sb(pfx + "wo", [128, 8, 1024], BF16)
    mg = [p.sb(pfx + "mg%d" % i, [128, 8, 512], BF16) for i in range(2)]
    t1 = [p.sb(pfx + "t1_%d" % i, [128, 512], F32) for i in range(2)]
    t2 = [p.sb(pfx + "t2_%d" % i, [128, 512], F32) for i in range(2)]
    lnt = p.sb(pfx + "ln", [128, 1, 2, 1024], F32)
    wrt = p.sb(pfx + "wrt", [128, 8, 36], F32)
    brt = p.sb(pfx + "brt", [128, 36], F32)
    cst = p.sb(pfx + "cst", [128, 3, 128], F32)
    cstb = p.sb(pfx + "cstb", [128, 2, 128], BF16)
    cst2 = p.sb(pfx + "cst2", [128, 40], F32)
    zero = p.sb(pfx + "zero", [128, 1024], BF16)
    xt = [p.sb(pfx + "xt%d" % i, [128, 1024], F32) for i in range(2)]
    y = [p.sb(pfx + "y%d" % i, [128, 1024], F32) for i in range(2)]
    x1 = [p.sb(pfx + "x1_%d" % i, [128, 1024], F32) for i in range(2)]
    x1b = [p.sb(pfx + "x1b%d" % i, [128, 1024], BF16) for i in range(2)]
    x1T = [p.sb(pfx + "x1T%d" % i, [128, 8, 128], F32) for i in range(2)]
    st6 = p.sb(pfx + "st6", [128, 12], F32)
    mv = p.sb(pfx + "mv", [128, 2], F32)
    rs = p.sb(pfx + "rs", [128, 2], F32)
    rt = [p.sb(pfx + "rt%d" % i, [128, 64], F32) for i in range(2)]
    E = [p.sb(pfx + "E%d" % i, [128, 3, 32], F32) for i in range(2)]
    Eb = [p.sb(pfx + "Eb%d" % i, [128, 32], BF16) for i in range(2)]
    base = p.sb(pfx + "base", [128, 32], F32)
    sl = [p.sb(pfx + "sl%d" % i, [128, 2], I32) for i in range(2)]
    gt = [p.sb(pfx + "gt%d" % i, [128, 2], F32) for i in range(2)]
    i8 = [p.sb(pfx + "i8_%d" % i, [128, 8], U32) for i in range(2)]
    ring = PsumRing(p, 8, pfx + "ps")
    ident = cst[:, 0, :]
    iota32 = cst2[:, 0:32]
    iota4 = cst2[:, 32:36]
    trash = cst2[:, 36:37]

    p.dma(lnt[:, 0, :, :], d["ln"], writes=["ln"])
    p.dma(wrt[:], d["w_rt"].rearrange("(c p) n -> p c n", p=128), writes=["wrt"])
    p.dma(brt[:], d["b_rt"], writes=["brt"])
    p.dma(cst[:], d["cst"], writes=["cst"])
    p.dma(cst2[:], d["cst2"], writes=["cst2"])
    p.copy("dve", cstb[:, 0, :], cst[:, 1, :], ["cst"], ["cstb"])
    p.copy("dve", cstb[:, 1, :], cst[:, 2, :], ["cst"], ["cstb"])
    p.memset("pool", zero[:], 0.0, ["zero"])
    p.memset("pool", base[:], 0.0, ["base"])
    zk = []
    for r0 in range(0, NROWS, 1024):
        nr = min(1024, NROWS - r0)
        p.dma(d["xdisp"][r0:r0 + nr, :].rearrange("(a p) n -> p a n", p=128),
              zero[:].partition_broadcast(128) if False else zero[:, None, :].to_broadcast([128, nr // 128, 1024]),
              reads=["zero"], writes=[("xdz", r0)])
        zk.append(("xdz", r0))
    for q in range(4):
        for c0 in range(0, 1024, 512):
            p.dma(w4r[:, q, :, c0:c0 + 512], d["w4"][q].rearrange("(c p) n -> p c n", p=128)[:, :, c0:c0 + 512],
                  writes=[("w4r", q, c0)], queue="pool")
    w4k = [[("w4r", q, 0), ("w4r", q, 512)] for q in range(4)]
    for c0 in range(0, 1024, 512):
        p.dma(wo[:, :, c0:c0 + 512], d["w_out"].rearrange("(c p) n -> p c n", p=128)[:, :, c0:c0 + 512], writes=[("wo", c0)], queue="pool")
    if hook is not None:
        hook()
    oTv = d["oT"].rearrange("(c p) n -> p c n", p=128)
    hgv = d["hgT"].rearrange("(c p) n -> p c n", p=128) if "hgT" in d else None
    xTv = (d["xT_own"] if "xT_own" in d else d["xT"]).rearrange("(c p) n -> p c n", p=128)
    fin = []
    wn = 0
    tile_i = 0
    for T in range(TOK // 512):
        b = T % 2
        p.dma(ot[b][:], oTv[:, :, T * 512:(T + 1) * 512], writes=[("ot", 0)])
        if "hg_all" in d:
            if T == 0:
                hat = d["hg_all"][128:128 + 4096, :].rearrange("(t q) n -> t q n", t=4)
                p.dma(None, None, reads=[("hg_all", t_) for t_ in range(4)], writes=["hg_mine"], queue="act",
                      fn=lambda e: e.dma_start(out=d["hg_mine"], in_=hat[p.core_idx(e, "j")]))
            p.dma(hg[b][:], d["hg_mine"].rearrange("(c p) n -> p c n", p=128)[:, :, T * 512:(T + 1) * 512],
                  reads=["hg_mine"], writes=[("hg", 0)])
            p.dma(xb[b][:], xTv[:, :, T * 512:(T + 1) * 512], reads=["xT_own"], writes=[("xb", 0)])
        else:
            p.dma(hg[b][:], hgv[:, :, T * 512:(T + 1) * 512], writes=[("hg", 0)])
            p.dma(xb[b][:], xTv[:, :, T * 512:(T + 1) * 512], writes=[("xb", 0)], queue="pool")
        for m in range(8):
            banks = [ring.next() for _ in range(4)]
            srcs = [ot[b], hg[b], xb[b], xb[b]]
            skeys = [("ot", 0), ("hg", 0), ("xb", 0), ("xb", 0)]
            for q in range(4):
                pst, psk = banks[q]
                for k in range(8):
                    p.mm(pst[:], w4r[:, q, k, m * 128:(m + 1) * 128], srcs[q][:, k, :], k == 0, k == 7,
                         w4k[q] + [skeys[q]], [psk])
            a1, a2 = t1[m % 2], t2[m % 2]
            k1, k2 = ("t1", m % 2), ("t2", m % 2)
            p.act(a1[:], banks[2][0][:], AF.Sigmoid, [banks[2][1]], [k1])
            p.act(a2[:], banks[3][0][:], AF.Sigmoid, [banks[3][1]], [k2])
            p.tt("dve", a1[:], a1[:], banks[0][0][:], ALU.mult, [k1, banks[0][1]], [k1])
            p.tt("dve", a2[:], a2[:], banks[1][0][:], ALU.mult, [k2, banks[1][1]], [k2])
            p.tt("pool", mg[b][:, m, :], a1[:], a2[:], ALU.add, [k1, k2], [("mg", b, m)])
        mgk = [("mg", b, m) for m in range(8)]
        def stage_a(s, tile_i):
            tb = tile_i % 2
            tok0 = T * 512 + s * 128
            x1k = ("x1", tb)
            p.dma(xt[tb][:], d["x_tok"][tok0:tok0 + 128, :], writes=[("xt", tb)])
            for hh in range(2):
                pst, psk = ring.next()
                for k in range(8):
                    p.mm(pst[:], mg[b][:, k, s * 128:(s + 1) * 128], wo[:, k, hh * 512:(hh + 1) * 512], k == 0, k == 7,
                         mgk + [("wo", hh * 512)], [psk])
                p.stt(y[tb][:, hh * 512:(hh + 1) * 512], xt[tb][:, hh * 512:(hh + 1) * 512], ALPHA, pst[:], ALU.mult, ALU.add,
                      [("xt", tb), psk], [("y", tb, hh)])
            x1k = ("x1", tb)
            emit_ln(p, y[tb][:], x1[tb][:], lnt, 0, (st6, mv, rs), [("y", tb, 0), ("y", tb, 1)], [x1k], "a")
            p.dma(d["x1"][tok0:tok0 + 128, :], x1[tb][:], reads=[x1k], writes=[("x1o", tile_i)])
            fin.append(("x1o", tile_i))
            p.act(x1b[tb][:], x1[tb][:], AF.Copy, [x1k], [("x1b", tb)])

        def stage_b(s, tile_i):
            tb = tile_i % 2
            tok0 = T * 512 + s * 128
            x1k = ("x1", tb)
            for hh in range(2):
                pst, psk = ring.next()
                for c in range(4):
                    k = hh * 4 + c
                    p.tr(pst[:, c * 128:(c + 1) * 128], x1[tb][:, k * 128:(k + 1) * 128], ident, [x1k, "cst"], [psk])
                p.copy("dve", x1T[tb][:, hh * 4:(hh + 1) * 4, :], pst[:].rearrange("p (c n) -> p c n", c=4), [psk],
                       [("x1T", tb, hh)])
            pl, plk = ring.next()
            for k in range(8):
                p.mm(pl[:, 0:36], x1T[tb][:, k, :], wrt[:, k, :], k == 0, k == 7, [("x1T", tb, 0), ("x1T", tb, 1), "wrt"], [plk])
            r = rt[tb]
            rk = ("rt", tb)
            p.tt("dve", r[:, 0:36], pl[:, 0:36], brt[:], ALU.add, [plk, "brt", rk], [rk])
            p.op("dve", lambda e, r=r: e.reduce_max(out=r[:, 36:37], in_=r[:, 0:4], axis=AX.X), [rk], [rk])
            p.ts("dve", r[:, 37:38], r[:, 36:37], -1.0, None, ALU.mult, None, [rk], [rk])
            p.act(r[:, 44:48], r[:, 0:4], AF.Exp, [rk], [rk], bias=r[:, 37:38], accum_out=r[:, 38:39])
            p.op("dve", lambda e, r=r: e.reciprocal(out=r[:, 39:40], in_=r[:, 38:39]), [rk], [rk])
            p.ts("dve", r[:, 40:44], r[:, 0:4], r[:, 36:37], None, ALU.is_equal, None, [rk], [rk])
            p.ts("dve", r[:, 48:56], r[:, 4:12], r[:, 40:41], None, ALU.mult, None, [rk], [rk])
            for g in range(1, 4):
                p.stt(r[:, 48:56], r[:, 4 + 8 * g:12 + 8 * g], r[:, 40 + g:41 + g], r[:, 48:56], ALU.mult, ALU.add, [rk], [rk])
            Et = E[tb]
            ek = ("E", tb)
            p.tt("dve", Et[:, 2, 0:4], r[:, 40:44], iota4, ALU.mult, [rk, "cst2", ek], [ek])
            p.op("dve", lambda e, r=r, Et=Et: e.reduce_sum(out=r[:, 58:59], in_=Et[:, 2, 0:4], axis=AX.X), [rk, ek], [rk])
            m8 = r[:, 48:56]
            i8t = i8[tb]
            p.op("dve", lambda e, r=r, Et=Et: e.max(out=Et[:, 2, 8:16], in_=r[:, 48:56]), [rk, ek], [ek])
            p.op("dve", lambda e, r=r, Et=Et, i8t=i8t: e.max_index(out=i8t[:], in_max=Et[:, 2, 8:16], in_values=r[:, 48:56]),
                 [rk, ek], [("i8", tb)])
            g = gt[tb]
            gk = ("gt", tb)
            p.tt("dve", r[:, 56:57], Et[:, 2, 8:9], Et[:, 2, 9:10], ALU.subtract, [ek, rk], [rk])
            p.act(r[:, 57:58], r[:, 56:57], AF.Sigmoid, [rk], [rk])
            p.tt("dve", g[:, 0:1], r[:, 57:58], r[:, 39:40], ALU.mult, [rk, gk], [gk])
            p.tt("dve", g[:, 1:2], r[:, 39:40], g[:, 0:1], ALU.subtract, [rk, gk], [gk])
            p.copy("dve", r[:, 59:61], i8t[:, 0:2], [("i8", tb), rk], [rk])
            p.stt(r[:, 59:61], r[:, 58:59].to_broadcast([128, 2]), 8.0, r[:, 59:61], ALU.mult, ALU.add, [rk], [rk])
            for kk in range(2):
                p.ts("dve", Et[:, kk, :], iota32, r[:, 59 + kk:60 + kk], None, ALU.is_equal, None, ["cst2", rk, ek], [ek])
            p.tt("dve", Eb[tb][:], Et[:, 0, :], Et[:, 1, :], ALU.add, [ek], [("Eb", tb)])
            pc, pck = ring.next()
            p.mm(pc[:, 0:32], cstb[:, 0, :], Eb[tb][:], True, True, ["cstb", ("Eb", tb)], [pck])
            p.mm(pc[:, 32:64], cstb[:, 1, :], Eb[tb][:], True, True, ["cstb", ("Eb", tb)], [pck])
            p.tt("dve", Et[:, 2, :], pc[:, 0:32], base[:], ALU.add, [pck, "base", ek], [ek])
            for kk in range(2):
                p.tt("dve", Et[:, kk, :], Et[:, kk, :], Et[:, 2, :], ALU.mult, [ek], [ek])
                p.op("dve", lambda e, r=r, Et=Et, kk=kk: e.reduce_sum(out=r[:, 61 + kk:62 + kk], in_=Et[:, kk, :], axis=AX.X),
                     [ek, rk], [rk])
            p.tt("dve", base[:], base[:], pc[:, 32:64], ALU.add, [pck, "base"], ["base"])
            for kk in range(2):
                p.ts("dve", r[:, 63:64], r[:, 61 + kk:62 + kk], float(CAP), None, ALU.is_lt, None, [rk], [rk])
                p.stt(r[:, 61 + kk:62 + kk], r[:, 59 + kk:60 + kk], float(CAP), r[:, 61 + kk:62 + kk], ALU.mult, ALU.add,
                      [rk], [rk])
                p.tt("dve", r[:, 61 + kk:62 + kk], r[:, 61 + kk:62 + kk], trash, ALU.subtract, [rk, "cst2"], [rk])
                p.tt("dve", r[:, 61 + kk:62 + kk], r[:, 61 + kk:62 + kk], r[:, 63:64], ALU.mult, [rk], [rk])
                p.tt("dve", r[:, 61 + kk:62 + kk], r[:, 61 + kk:62 + kk], trash, ALU.add, [rk, "cst2"], [rk])
                p.tt("dve", g[:, kk:kk + 1], g[:, kk:kk + 1], r[:, 63:64], ALU.mult, [rk, gk], [gk])
            slt = sl[tb]
            slk = ("sl", tb)
            p.copy("dve", slt[:], r[:, 61:63], [rk], [slk])
            p.dma(d["slots"][tok0:tok0 + 128, :], slt[:], reads=[slk], writes=[("slo", tile_i)])
            p.dma(d["gates"][tok0:tok0 + 128, :], g[:], reads=[gk], writes=[("gto", tile_i)])
            fin.extend([("slo", tile_i), ("gto", tile_i)])
            for kk in range(2):
                p.dma(None, None, reads=[("x1b", tb), slk] + zk, writes=[("xdo", tile_i, kk)], queue="pool",
                      fn=lambda e, tb=tb, kk=kk, slt=slt: e.indirect_dma_start(
                          out=d["xdisp"], out_offset=bass.IndirectOffsetOnAxis(ap=slt[:, kk:kk + 1], axis=0),
                          in_=x1b[tb][:, :], in_offset=None))
                fin.append(("xdo", tile_i, kk))

        stage_a(0, T * 4)
        for s in range(4):
            if s + 1 < 4:
                stage_a(s + 1, T * 4 + s + 1)
            stage_b(s, T * 4 + s)
    return fin


def mix_consts():
    ident = np.eye(128, dtype=np.float32)
    tp = np.arange(128)[:, None]
    t = np.arange(128)[None, :]
    lower = (tp < t).astype(np.float32)
    ones = np.ones((128, 128), np.float32)
    cst = np.stack([ident, lower, ones], axis=1)
    cst2 = np.zeros((128, 40), np.float32)
    cst2[:, 0:32] = np.arange(32, dtype=np.float32)[None, :]
    cst2[:, 32:36] = np.arange(4, dtype=np.float32)[None, :]
    cst2[:, 36] = NSLOT + np.arange(128)
    return {"cst": np.ascontiguousarray(cst), "cst2": cst2}


def build_moe():
    nc = bass.Bass("TRN2", target_bir_lowering=False)
    d = {}
    d["xdisp"] = nc.dram_tensor("xdisp", [NROWS, 1024], BF16, kind="ExternalInput").ap()
    d["x1"] = nc.dram_tensor("x1", [TOK, 1024], F32, kind="ExternalInput").ap()
    d["slots"] = nc.dram_tensor("slots", [TOK, 2], I32, kind="ExternalInput").ap()
    d["gates"] = nc.dram_tensor("gates", [TOK, 2], F32, kind="ExternalInput").ap()
    d["w_g"] = nc.dram_tensor("w_g", [N_EXP, 1024, 512], F32, kind="ExternalInput").ap()
    d["w_u"] = nc.dram_tensor("w_u", [N_EXP, 1024, 512], F32, kind="ExternalInput").ap()
    d["w_d"] = nc.dram_tensor("w_d", [N_EXP, 512, 1024], F32, kind="ExternalInput").ap()
    d["ln"] = nc.dram_tensor("ln", [128, 2, 1024], F32, kind="ExternalInput").ap()
    d["ident"] = nc.dram_tensor("ident", [128, 128], F32, kind="ExternalInput").ap()
    d["ydisp"] = nc.dram_tensor("ydisp", [NROWS, 1024], F32).ap()
    d["x2"] = nc.dram_tensor("x2", [TOK, 1024], F32, kind="ExternalOutput").ap()
    p = Prog(nc)
    fk = emit_moe(p, d)
    p.finish(fk)
    return nc


def emit_moe(p, d, pfx="e"):
    NWB = 3
    wg = [p.sb(pfx + "wg%d" % i, [128, 8, 512], BF16) for i in range(NWB)]
    wu = [p.sb(pfx + "wu%d" % i, [128, 8, 512], BF16) for i in range(NWB)]
    wd = [p.sb(pfx + "wd%d" % i, [128, 4, 1024], BF16) for i in range(NWB)]
    xe = [p.sb(pfx + "xe%d" % i, [128, 1024], BF16) for i in range(2)]
    xeT = [p.sb(pfx + "xeT%d" % i, [128, 8, CAP], BF16) for i in range(2)]
    hT = [p.sb(pfx + "hT%d" % i, [128, 4, CAP], BF16) for i in range(2)]
    sg = [p.sb(pfx + "sg%d" % i, [128, CAP], F32) for i in range(2)]
    yt = [p.sb(pfx + "yt%d" % i, [128, 1024], F32) for i in range(2)]
    ident = p.sb(pfx + "ident", [128, 128], BF16)
    lnt = p.sb(pfx + "ln", [128, 1, 2, 1024], F32)
    zero = p.sb(pfx + "zero", [128, 1024], F32)
    sl = [p.sb(pfx + "sl%d" % i, [128, 2], I32) for i in range(2)]
    gt = [p.sb(pfx + "gt%d" % i, [128, 2], F32) for i in range(2)]
    x1t = [p.sb(pfx + "x1t%d" % i, [128, 1024], F32) for i in range(2)]
    ya = [p.sb(pfx + "ya%d" % i, [128, 1024], F32) for i in range(2)]
    yb = [p.sb(pfx + "yb%d" % i, [128, 1024], F32) for i in range(2)]
    yo = [p.sb(pfx + "yo%d" % i, [128, 1024], F32) for i in range(2)]
    st6 = p.sb(pfx + "st6", [128, 12], F32)
    mv = p.sb(pfx + "mv", [128, 2], F32)
    rs = p.sb(pfx + "rs", [128, 2], F32)
    if "xT_next" in d:
        xTn = [p.sb(pfx + "xTn%d" % i, [128, 8, 512], BF16) for i in range(2)]
        identf = p.sb(pfx + "identf", [128, 128], F32)
        p.dma(identf[:], d["ident"], writes=["identf"])
    ring = PsumRing(p, 6, pfx + "ps")
    ptr = [p.ps(pfx + "ptr%d" % i, [128, 1024], BF16) for i in range(2)]
    for i in range(2):
        p.exclusive.add((pfx + "ptr", i))

    p.dma(ident[:], d["ident"], writes=["ident"], queue="pool")
    p.dma(lnt[:, 0, :, :], d["ln"], writes=["ln"])
    p.memset("pool", zero[:], 0.0, ["zero"])
    p.dma(d["ydisp"][NSLOT:NROWS, :], zero[:], reads=["zero"], writes=[("yd", -1, 0)])
    ydk = [("yd", -1, 0)]
    nb = 0
    nstg = 0
    stg = [p.sb(pfx + "stg%d" % i, [128, 4096], F32) for i in range(2)]
    for e in range(N_EXP):
        b = e % NWB
        wk = ("w", b)
        for wi, (wsrc, wdst, wkey) in enumerate(((d["w_g"][e], wg[b], ("wg", b)), (d["w_u"][e], wu[b], ("wu", b)),
                                                 (d["w_d"][e], wd[b], ("wd", b)))):
            if True:
                p.dma(wdst[:], wsrc.rearrange("(c p) n -> p c n", p=128), writes=[wkey], queue="pool")
                continue
            sgi = nstg % 2
            nstg += 1
            nchunk = 4 if wi == 2 else 8
            p.dma(stg[sgi][:].rearrange("p (c n) -> p c n", c=nchunk), wsrc.rearrange("(c p) n -> p c n", p=128),
                  writes=[("stg", sgi)])
            dflat = wdst[:].rearrange("p c n -> p (c n)")
            if wi == 1:
                p.copy("pool", dflat, stg[sgi][:], [("stg", sgi)], [wkey])
            else:
                p.act(dflat, stg[sgi][:], AF.Copy, [("stg", sgi)], [wkey])
        for blk in range(CAP // 128):
            xb_ = xe[nb % 2]
            xk = ("xe", nb % 2)
            r0 = e * CAP + blk * 128
            p.dma(xb_[:], d["xdisp"][r0:r0 + 128, :], writes=[xk])
            pt = ptr[nb % 2]
            ptk = (pfx + "ptr", nb % 2)
            for k in range(8):
                p.tr(pt[:, k * 128:(k + 1) * 128], xb_[:, k * 128:(k + 1) * 128], ident[:], [xk, "ident"], [ptk])
            p.copy("dve" if blk else "act", xeT[e % 2][:, :, blk * 128:(blk + 1) * 128], pt[:].rearrange("p (c n) -> p c n", c=8),
                   [ptk], [("xeT", e % 2, blk)]) if blk else \
                p.act(xeT[e % 2][:, :, blk * 128:(blk + 1) * 128], pt[:].rearrange("p (c n) -> p c n", c=8), AF.Copy,
                      [ptk], [("xeT", e % 2, blk)])
            nb += 1
        xtk = [("xeT", e % 2, blk) for blk in range(CAP // 128)]
        for m in range(4):
            pg, pgk = ring.next()
            pu, puk = ring.next()
            for k in range(8):
                p.mm(pg[:, 0:CAP], wg[b][:, k, m * 128:(m + 1) * 128], xeT[e % 2][:, k, :], k == 0, k == 7, [("wg", b)] + xtk, [pgk])
            for k in range(8):
                p.mm(pu[:, 0:CAP], wu[b][:, k, m * 128:(m + 1) * 128], xeT[e % 2][:, k, :], k == 0, k == 7, [("wu", b)] + xtk, [puk])
            s_ = sg[m % 2]
            sk = ("sg", m % 2)
            p.act(s_[:], pg[:, 0:CAP], AF.Silu, [pgk], [sk])
            p.tt("dve", hT[e % 2][:, m, :], s_[:], pu[:, 0:CAP], ALU.mult, [sk, puk], [("hT", e % 2, m)])
        htk = [("hT", e % 2, m) for m in range(4)]
        for blk in range(CAP // 128):
            y_ = yt[blk % 2]
            yk = ("yt", blk % 2)
            for hh in range(2):
                py, pyk = ring.next()
                for k in range(4):
                    p.mm(py[:], hT[e % 2][:, k, blk * 128:(blk + 1) * 128], wd[b][:, k, hh * 512:(hh + 1) * 512], k == 0, k == 3,
                         htk + [("wd", b)], [pyk])
                if hh == 0:
                    p.act(y_[:, 0:512], py[:], AF.Copy, [pyk], [yk])
                else:
                    p.copy("dve", y_[:, 512:1024], py[:], [pyk], [yk])
            r0 = e * CAP + blk * 128
            p.dma(d["ydisp"][r0:r0 + 128, :], y_[:], reads=[yk], writes=[("yd", e, blk)])
            ydk.append(("yd", e, blk))
    fin = []

    def cload(t):
        b = t % 2
        tok0 = t * 128
        p.dma(sl[b][:], d["slots"][tok0:tok0 + 128, :], writes=[("sl", b)])
        p.dma(gt[b][:], d["gates"][tok0:tok0 + 128, :], writes=[("gt", b)])
        p.dma(x1t[b][:], d["x1"][tok0:tok0 + 128, :], writes=[("x1t", b)])
        for kk, dst in enumerate((ya[b], yb[b])):
            p.dma(None, None, reads=[("sl", b)] + ydk, writes=[("yab", b, kk)], queue="pool",
                  fn=lambda e, dst=dst, b=b, kk=kk: e.indirect_dma_start(
                      out=dst[:, :], out_offset=None, in_=d["ydisp"],
                      in_offset=bass.IndirectOffsetOnAxis(ap=sl[b][:, kk:kk + 1], axis=0)))
    cload(0)
    for t in range(TOK // 128):
        b = t % 2
        tok0 = t * 128
        if t + 1 < TOK // 128:
            cload(t + 1)
        fk_ = ("f", b)
        p.ts("dve", ya[b][:], ya[b][:], gt[b][:, 0:1], None, ALU.mult, None, [("yab", b, 0), ("gt", b)], [("yab", b, 0)])
        p.stt(ya[b][:], yb[b][:], gt[b][:, 1:2], ya[b][:], ALU.mult, ALU.add, [("yab", b, 0), ("yab", b, 1), ("gt", b)],
              [("yab", b, 0)])
        p.stt(ya[b][:], x1t[b][:], ALPHA, ya[b][:], ALU.mult, ALU.add, [("yab", b, 0), ("x1t", b)], [("yab", b, 0)])
        emit_ln(p, ya[b][:], yo[b][:], lnt, 0, (st6, mv, rs), [("yab", b, 0)], [("yo", b)], "b")
        p.dma(d["x2"][tok0:tok0 + 128, :], yo[b][:], reads=[("yo", b)], writes=[("x2o", t)])
        fin.append(("x2o", t))
        if "xT_next" in d:
            xn = xTn[(t // 4) % 2]
            xnk = ("xTn", (t // 4) % 2)
            for hh in range(2):
                pst, psk = ring.next()
                for c in range(4):
                    k = hh * 4 + c
                    p.tr(pst[:, c * 128:(c + 1) * 128], yo[b][:, k * 128:(k + 1) * 128], identf[:], [("yo", b), "identf"], [psk])
                p.copy("dve" if hh else "pool", xn[:, hh * 4:(hh + 1) * 4, (t % 4) * 128:(t % 4 + 1) * 128],
                       pst[:].rearrange("p (c n) -> p c n", c=4), [psk], [xnk]) if hh else \
                    p.act(xn[:, hh * 4:(hh + 1) * 4, (t % 4) * 128:(t % 4 + 1) * 128],
                          pst[:].rearrange("p (c n) -> p c n", c=4), AF.Copy, [psk], [xnk])
            if t % 4 == 3:
                T4 = t // 4
                p.dma(d["xT_next"].rearrange("(c p) n -> p c n", p=128)[:, :, T4 * 512:(T4 + 1) * 512], xn[:],
                      reads=[xnk], writes=[("xTn_out", T4)])
    return fin


_PROGS = {}


def _prog(name, builder):
    if name not in _PROGS:
        _PROGS[name] = builder()
    return _PROGS[name]


def _run(nc, in_maps):
    res = run_bass_kernel_spmd(nc, in_maps, core_ids=list(range(8)))
    return res.results


def kernel_unfused(x, w_in, w_sink, w_conv, b_conv, w_rec_gate, b_rec_gate, w_in_gate, b_in_gate, lru_lambda,
                   w_attn_o, w_rnn_o, w_out, ln_g, ln_b, w_router_group, b_router_group, w_router_expert, b_router_expert,
                   w_exp_gate, w_exp_up, w_exp_down):
    f = lambda a: np.asarray(a, dtype=np.float32)
    x = f(x)
    w_in, w_sink, w_conv, b_conv = f(w_in), f(w_sink), f(w_conv), f(b_conv)
    w_rec_gate, b_rec_gate, w_in_gate, b_in_gate, lru_lambda = f(w_rec_gate), f(b_rec_gate), f(w_in_gate), f(b_in_gate), f(lru_lambda)
    w_attn_o, w_rnn_o, w_out, ln_g, ln_b = f(w_attn_o), f(w_rnn_o), f(w_out), f(ln_g), f(ln_b)
    w_router_group, b_router_group = f(w_router_group), f(b_router_group)
    w_router_expert, b_router_expert = f(w_router_expert), f(b_router_expert)
    w_exp_gate, w_exp_up, w_exp_down = f(w_exp_gate), f(w_exp_up), f(w_exp_down)
    depth = w_in.shape[0]
    nc_r = _prog("rnn", build_rnn)
    nc_a = _prog("attn", build_attn)
    nc_m = _prog("mix", build_mix)
    nc_e = _prog("moe", build_moe)
    aconst = [attn_consts(j) for j in range(4)]
    mconst = mix_consts()
    ident = np.eye(128, dtype=np.float32)
    cores = [(c // 4, c % 4) for c in range(8)]
    for l in range(depth):
        xTs = [np.ascontiguousarray(x[b].T) for b in range(2)]
        maps = []
        for (b, j) in cores:
            m = pack_rnn_inputs(l, j, w_in, w_conv, b_conv, w_rec_gate, b_rec_gate, w_in_gate, b_in_gate, lru_lambda)
            m["xT"] = xTs[b]
            maps.append(m)
        res = _run(nc_r, maps)
        hgT = [np.concatenate([np.asarray(res[b * 4 + j]["hgT"]) for j in range(4)], axis=0) for b in range(2)]
        w_qkv = np.ascontiguousarray(w_in[l][:, 0:1536])
        sink = np.ascontiguousarray(np.broadcast_to(w_sink[l][None, :], (128, 8)))
        maps = []
        for (b, j) in cores:
            m = dict(aconst[j])
            m["xT"] = halo_xT(x[b], j)
            m["w_qkv"] = w_qkv
            m["sink"] = sink
            maps.append(m)
        res = _run(nc_a, maps)
        oT = [np.asarray(res[c]["oT"]) for c in range(8)]
        w4 = np.ascontiguousarray(np.stack([w_attn_o[l], w_rnn_o[l], w_in[l][:, 3584:4608], w_in[l][:, 4608:5632]]))
        ln1 = np.ascontiguousarray(np.broadcast_to(np.stack([ln_g[l, 0], ln_b[l, 0]])[None], (128, 2, 1024)))
        w_rt = np.ascontiguousarray(np.concatenate([w_router_group[l], w_router_expert[l]], axis=1))
        b_rt = np.ascontiguousarray(np.broadcast_to(np.concatenate([b_router_group[l], b_router_expert[l]])[None], (128, 36)))
        wo = np.ascontiguousarray(w_out[l])
        maps = []
        for c, (b, j) in enumerate(cores):
            m = dict(mconst)
            m["xT"] = np.ascontiguousarray(xTs[b][:, j * TOK:(j + 1) * TOK])
            m["x_tok"] = np.ascontiguousarray(x[b][j * TOK:(j + 1) * TOK])
            m["oT"] = oT[c]
            m["hgT"] = np.ascontiguousarray(hgT[b][:, j * TOK:(j + 1) * TOK])
            m["w4"] = w4
            m["w_out"] = wo
            m["ln"] = ln1
            m["w_rt"] = w_rt
            m["b_rt"] = b_rt
            maps.append(m)
        res = _run(nc_m, maps)
        ln2 = np.ascontiguousarray(np.broadcast_to(np.stack([ln_g[l, 1], ln_b[l, 1]])[None], (128, 2, 1024)))
        wg_, wu_, wd_ = np.ascontiguousarray(w_exp_gate[l]), np.ascontiguousarray(w_exp_up[l]), np.ascontiguousarray(w_exp_down[l])
        maps = []
        for c in range(8):
            maps.append({"xdisp": np.asarray(res[c]["xdisp"]), "x1": np.asarray(res[c]["x1"]),
                         "slots": np.asarray(res[c]["slots"]), "gates": np.asarray(res[c]["gates"]),
                         "w_g": wg_, "w_u": wu_, "w_d": wd_, "ln": ln2, "ident": ident})
        res = _run(nc_e, maps)
        x = np.stack([np.concatenate([np.asarray(res[b * 4 + j]["x2"]) for j in range(4)], axis=0) for b in range(2)])
    return np.ascontiguousarray(x.astype(np.float32))


GROUPS4 = [[0, 1, 2, 3], [4, 5, 6, 7]]


def build_fused(depth=4):
    nc = bass.Bass("TRN2", target_bir_lowering=False)
    L = depth

    def ext(name, shape, dt=F32):
        return nc.dram_tensor(name, list(shape), dt, kind="ExternalInput").ap()

    def internal(name, shape, dt):
        return nc.dram_tensor(name, list(shape), dt).ap()
    I = {}
    I["x_tok0"] = ext("x_tok0", [TOK, 1024])
    I["xT0"] = ext("xT0", [1024, TOK])
    I["w_r"] = ext("w_r", [L, 1024, 512])
    I["w_gt"] = ext("w_gt", [L, 128, 8, 128])
    I["small"] = ext("small", [L, 128, 2, NSM])
    I["w_qkv"] = ext("w_qkv", [L, 1024, 1536])
    I["cosT"] = ext("cosT", [128, TH])
    I["sinT"] = ext("sinT", [128, TH])
    I["masks"] = ext("masks", [128, 3, 384])
    I["cmats"] = ext("cmats", [128, 2, 128])
    I["sink"] = ext("sink", [L, 128, 8])
    I["w4"] = ext("w4", [L, 4, 1024, 1024])
    I["w_out"] = ext("w_out", [L, 1024, 1024])
    I["ln1"] = ext("ln1", [L, 128, 2, 1024])
    I["ln2"] = ext("ln2", [L, 128, 2, 1024])
    I["w_rt"] = ext("w_rt", [L, 1024, 36])
    I["b_rt"] = ext("b_rt", [L, 128, 36])
    I["cst"] = ext("cst", [128, 3, 128])
    I["cst2"] = ext("cst2", [128, 40])
    I["w_eg"] = ext("w_eg", [L, N_EXP, 1024, 512])
    I["w_eu"] = ext("w_eu", [L, N_EXP, 1024, 512])
    I["w_ed"] = ext("w_ed", [L, N_EXP, 512, 1024])
    I["ident"] = ext("ident", [128, 128])
    out = nc.dram_tensor("out", [TOK, 1024], F32, kind="ExternalOutput").ap()
    xT_own = [internal("xT_own%d" % i, [1024, TOK], BF16) for i in range(2)]
    xT_all = [internal("xT_all%d" % i, [128 + 4096, TOK], BF16) for i in range(2)]
    x_tok_i = [internal("x_tok_i%d" % i, [TOK, 1024], F32) for i in range(2)]
    hg_own = [internal("hg_own%d" % i, [4, 256, TOK], BF16) for i in range(2)]
    hg_all = [internal("hg_all%d" % i, [128 + 4096, TOK], BF16) for i in range(2)]
    halo_l = internal("halo_l", [1024, HALO], BF16)
    halo_r = internal("halo_r", [1024, HALO], BF16)
    hg_mine = internal("hg_mine", [1024, TOK], BF16)
    oT = internal("oT_i", [1024, TOK], BF16)
    x1 = internal("x1_i", [TOK, 1024], F32)
    xdisp = internal("xdisp_i", [NROWS, 1024], BF16)
    slots = internal("slots_i", [TOK, 2], I32)
    gates = internal("gates_i", [TOK, 2], F32)
    ydisp = internal("ydisp_i", [NROWS, 1024], F32)

    p = Prog(nc)
    p.begin_phase()
    st = [p.sb("pro%d" % i, [128, 8, 512], BF16) for i in range(2)]
    src = I["xT0"].rearrange("(c p) n -> p c n", p=128)
    dst = xT_own[0].rearrange("(c p) n -> p c n", p=128)
    for t in range(TOK // 512):
        p.dma(st[t % 2][:], src[:, :, t * 512:(t + 1) * 512], writes=[("pro", t % 2)], queue="pool")
        p.dma(dst[:, :, t * 512:(t + 1) * 512], st[t % 2][:], reads=[("pro", t % 2)], writes=[("xT_own_w", t)])
    for k in range(4):
        p.coll("AllGather", GROUPS4, xT_own[0][k * 256:(k + 1) * 256, :], xT_all[0][128 + k * 1024:128 + (k + 1) * 1024, :],
               reads=[("xT_own_w", t) for t in range(TOK // 512)], writes=[("xT_all", k)])
    p.end_phase()
    for l in range(L):
        par = l % 2
        last = (l == L - 1)
        p.begin_phase()

        def ag_x(par=par):
            for k in range(4):
                p.coll("AllGather", GROUPS4, xT_own[par][k * 256:(k + 1) * 256, :],
                       xT_all[par][128 + k * 1024:128 + (k + 1) * 1024, :], reads=[], writes=[("xT_all", k)])
        emit_attn(p, {"xT_own": xT_own[par], "xT_all": xT_all[par], "w_qkv": I["w_qkv"][l], "cosT": I["cosT"],
                      "sinT": I["sinT"], "masks": I["masks"], "cmats": I["cmats"], "sink": I["sink"][l], "oT": oT,
                      "halo_l": halo_l, "halo_r": halo_r},
                  pfx="a%d" % l, hook=(ag_x if l > 0 else None))
        p.end_phase()
        p.begin_phase()
        emit_rnn(p, xT_all[par], I["w_r"][l], I["w_gt"][l], I["small"][l], hg_own[par], pfx="r%d" % l)
        p.end_phase()
        p.begin_phase()

        def ag_h(par=par):
            for t in range(4):
                p.coll("AllGather", GROUPS4, hg_own[par][t], hg_all[par][128 + t * 1024:128 + (t + 1) * 1024, :],
                       reads=[], writes=[("hg_all", t)])
        emit_mix(p, {"xT_own": xT_own[par], "x_tok": (I["x_tok0"] if l == 0 else x_tok_i[par]), "oT": oT,
                     "hg_all": hg_all[par], "hg_mine": hg_mine, "w4": I["w4"][l], "w_out": I["w_out"][l], "ln": I["ln1"][l],
                     "w_rt": I["w_rt"][l], "b_rt": I["b_rt"][l], "cst": I["cst"], "cst2": I["cst2"],
                     "x1": x1, "xdisp": xdisp, "slots": slots, "gates": gates}, pfx="m%d" % l, hook=ag_h)
        p.end_phase()
        p.begin_phase()
        dd = {"xdisp": xdisp, "x1": x1, "slots": slots, "gates": gates, "w_g": I["w_eg"][l], "w_u": I["w_eu"][l],
              "w_d": I["w_ed"][l], "ln": I["ln2"][l], "ident": I["ident"], "ydisp": ydisp,
              "x2": (out if last else x_tok_i[1 - par])}
        if not last:
            dd["xT_next"] = xT_own[1 - par]
        emit_moe(p, dd, pfx="e%d" % l)
        p.end_phase()
    p.finish([])
    return nc


def fused_inputs(depth, x, w_in, w_sink, w_conv, b_conv, w_rec_gate, b_rec_gate, w_in_gate, b_in_gate, lru_lambda,
                 w_attn_o, w_rnn_o, w_out, ln_g, ln_b, w_router_group, b_router_group, w_router_expert, b_router_expert,
                 w_exp_gate, w_exp_up, w_exp_down):
    L = depth
    ca = np.ascontiguousarray
    shared = {}
    shared["w_qkv"] = ca(w_in[:L, :, 0:1536])
    shared["sink"] = ca(np.broadcast_to(w_sink[:L, None, :], (L, 128, 8)))
    shared["w4"] = ca(np.stack([np.stack([w_attn_o[l], w_rnn_o[l], w_in[l][:, 3584:4608], w_in[l][:, 4608:5632]]) for l in range(L)]))
    shared["w_out"] = ca(w_out[:L])
    shared["ln1"] = ca(np.broadcast_to(np.stack([ln_g[:L, 0], ln_b[:L, 0]], axis=1)[:, None], (L, 128, 2, 1024)))
    shared["ln2"] = ca(np.broadcast_to(np.stack([ln_g[:L, 1], ln_b[:L, 1]], axis=1)[:, None], (L, 128, 2, 1024)))
    shared["w_rt"] = ca(np.concatenate([w_router_group[:L], w_router_expert[:L]], axis=2))
    shared["b_rt"] = ca(np.broadcast_to(np.concatenate([b_router_group[:L], b_router_expert[:L]], axis=1)[:, None, :], (L, 128, 36)))
    shared["w_eg"] = ca(w_exp_gate[:L])
    shared["w_eu"] = ca(w_exp_up[:L])
    shared["w_ed"] = ca(w_exp_down[:L])
    shared["ident"] = np.eye(128, dtype=np.float32)
    shared.update(mix_consts())
    rn = []
    for j in range(4):
        packs = [pack_rnn_inputs(l, j, w_in, w_conv, b_conv, w_rec_gate, b_rec_gate, w_in_gate, b_in_gate, lru_lambda)
                 for l in range(L)]
        rn.append({"w_r": ca(np.stack([q["w_r"] for q in packs])), "w_gt": ca(np.stack([q["w_g"] for q in packs])),
                   "small": ca(np.stack([q["small"] for q in packs]))})
    maps = []
    for c in range(8):
        b, j = c // 4, c % 4
        m = dict(shared)
        m.update(rn[j])
        m.update(attn_consts(j))
        xs = x[b][j * TOK:(j + 1) * TOK]
        m["x_tok0"] = ca(xs)
        m["xT0"] = ca(xs.T)
        maps.append(m)
    return maps


def kernel_fused(depth, **inp):
    key = "fused%d" % depth
    nc = _prog(key, lambda: build_fused(depth))
    names = ["x", "w_in", "w_sink", "w_conv", "b_conv", "w_rec_gate", "b_rec_gate", "w_in_gate", "b_in_gate", "lru_lambda",
             "w_attn_o", "w_rnn_o", "w_out", "ln_g", "ln_b", "w_router_group", "b_router_group", "w_router_expert",
             "b_router_expert", "w_exp_gate", "w_exp_up", "w_exp_down"]
    args = [np.asarray(inp[n], dtype=np.float32) for n in names]
    maps = fused_inputs(depth, *args)
    res = _run(nc, maps)
    x = np.stack([np.concatenate([np.asarray(res[b * 4 + j]["out"]) for j in range(4)], axis=0) for b in range(2)])
    return np.ascontiguousarray(x.astype(np.float32))


def kernel(**inputs):
    return kernel_fused(4, **inputs)
```
